# Optimizing a Trainium2 kernel written in Bass

```python
import jax, jax.numpy as jnp
from jax import lax
import numpy as np

D_MODEL = 1024
BATCH = 8
SEQ = 2048
DEPTH = 2
DEC_BATCH = 128
DEC_SEQ = 8
PAST_LEN = 16384
PAGE_SIZE = 128

N_HEADS_A = 8
HEAD_DK = 128
HEAD_DV = 128
D_QK = N_HEADS_A * HEAD_DK
D_VA = N_HEADS_A * HEAD_DV
D_CONV = 2 * D_QK + D_VA
CONV_W = 4
DELTA_CHUNK = 64
MLP_CHUNK = 128
N_GROUPS_B = 8
D_B = 1024
GROUP_DIM_B = D_B // N_GROUPS_B
D_FF = ((8 * D_MODEL // 3 + 255) // 256) * 256
SPLIT_POINTS = (D_CONV,
                D_CONV + D_VA,
                D_CONV + D_VA + N_HEADS_A,
                D_CONV + D_VA + 2 * N_HEADS_A,
                D_CONV + D_VA + 2 * N_HEADS_A + 2 * D_B)
D_IN = D_CONV + D_VA + 2 * N_HEADS_A + 2 * D_B + 2 * D_MODEL

kernel_name = "hybrid_gdn_chunkmlp_decoder_step"


def rms_norm(x, w, eps=1e-6):
    xf = x.astype(jnp.float32)
    y = xf * lax.rsqrt(jnp.mean(jnp.square(xf), axis=-1, keepdims=True) + eps)
    return (y * w.astype(jnp.float32)).astype(x.dtype)


def layer_norm(x, w, b, eps=1e-5):
    xf = x.astype(jnp.float32)
    xc = xf - jnp.mean(xf, axis=-1, keepdims=True)
    var = jnp.mean(jnp.square(xc), axis=-1, keepdims=True)
    return (xc * lax.rsqrt(var + eps) * w.astype(jnp.float32) + b.astype(jnp.float32)).astype(x.dtype)


def l2_normalize(x, eps=1e-6):
    xf = x.astype(jnp.float32)
    return xf * lax.rsqrt(jnp.sum(jnp.square(xf), axis=-1, keepdims=True) + eps)


def causal_short_conv(x, buf, w):
    t = x.shape[1]
    xp = jnp.concatenate([buf.astype(x.dtype), x], axis=1)
    y = sum(w[i] * xp[:, i:i + t] for i in range(CONV_W))
    return jax.nn.silu(y), xp[:, -(CONV_W - 1):]


def gated_delta_chunked(q, k, v, g, beta, s0):
    n, t, h, dk = q.shape
    dv = v.shape[-1]
    c = min(DELTA_CHUNK, t)
    pad = (-t) % c
    n_c = (t + pad) // c

    def prep(arr):
        arr = jnp.pad(arr, [(0, 0), (0, pad)] + [(0, 0)] * (arr.ndim - 2))
        arr = arr.reshape((n, n_c, c, h) + arr.shape[3:])
        return jnp.swapaxes(jnp.swapaxes(arr, 0, 1), 2, 3)

    q = prep(q * HEAD_DK ** -0.5)
    k = prep(k)
    v = prep(v)
    g = prep(g)
    beta = prep(beta)
    gc = jnp.cumsum(g, axis=-1)
    idx = jnp.arange(c)
    causal = idx[:, None] >= idx[None, :]
    strict = idx[:, None] > idx[None, :]
    decay = jnp.exp(jnp.where(causal, gc[..., :, None] - gc[..., None, :], -jnp.inf))
    k_beta = k * beta[..., None]
    a_mat = jnp.where(strict, jnp.einsum('mnhik,mnhjk->mnhij', k_beta, k) * decay, 0.0)
    rhs = jnp.concatenate([v * beta[..., None], k_beta * jnp.exp(gc)[..., None]], axis=-1)
    sol = lax.linalg.triangular_solve(a_mat + jnp.eye(c, dtype=jnp.float32), rhs,
                                      left_side=True, lower=True, unit_diagonal=True)
    u, w = sol[..., :dv], sol[..., dv:]
    qk = jnp.einsum('mnhik,mnhjk->mnhij', q, k) * decay

    def step(s, inp):
        q_c, k_c, u_c, w_c, g_c, qk_c = inp
        v_new = u_c - jnp.einsum('nhck,nhkv->nhcv', w_c, s)
        o_c = (jnp.einsum('nhck,nhkv->nhcv', q_c * jnp.exp(g_c)[..., None], s)
               + jnp.einsum('nhij,nhjv->nhiv', qk_c, v_new))
        g_last = g_c[..., -1:]
        s = (s * jnp.exp(g_last)[..., None]
             + jnp.einsum('nhck,nhcv->nhkv', k_c * jnp.exp(g_last - g_c)[..., None], v_new))
        return s, o_c

    s_final, o = lax.scan(step, s0, (q, k, u, w, gc, qk))
    o = jnp.swapaxes(jnp.swapaxes(o, 2, 3), 0, 1).reshape(n, n_c * c, h, dv)[:, :t]
    return o, s_final


def delta_branch(qkv, z, a, b, conv_buf, s0, conv_w, a_log, dt_bias, norm_w):
    n, t, _ = qkv.shape
    qkv_c, new_buf = causal_short_conv(qkv, conv_buf, conv_w)
    q, k, v = jnp.split(qkv_c, [D_QK, 2 * D_QK], axis=-1)
    q = l2_normalize(q.reshape(n, t, N_HEADS_A, HEAD_DK))
    k = l2_normalize(k.reshape(n, t, N_HEADS_A, HEAD_DK))
    v = v.reshape(n, t, N_HEADS_A, HEAD_DV).astype(jnp.float32)
    g = -jnp.exp(a_log.astype(jnp.float32)) * jax.nn.softplus(a.astype(jnp.float32) + dt_bias.astype(jnp.float32))
    beta = jax.nn.sigmoid(b.astype(jnp.float32))
    o, s_new = gated_delta_chunked(q, k, v, g, beta, s0.astype(jnp.float32))
    o = rms_norm(o, norm_w) * jax.nn.silu(z.reshape(n, t, N_HEADS_A, HEAD_DV).astype(jnp.float32))
    return o.reshape(n, t, D_VA).astype(qkv.dtype), new_buf, s_new.astype(s0.dtype)


def chunk_mlp_branch(uv, ln_w, ln_b, w_spatial, b_spatial):
    n, t, _ = uv.shape
    u, v = jnp.split(jax.nn.gelu(uv, approximate=False), 2, axis=-1)
    v = layer_norm(v, ln_w, ln_b)
    c = min(MLP_CHUNK, t)
    pad = (-t) % c
    n_c = (t + pad) // c
    vc = jnp.pad(v, [(0, 0), (0, pad), (0, 0)]).reshape(n, n_c, c, N_GROUPS_B, GROUP_DIM_B)
    idx = jnp.arange(c)
    ws = jnp.where(idx[:, None] >= idx[None, :], w_spatial[:, :c, :c], 0.0)
    mixed = (jnp.einsum('gij,nmjgd->nmigd', ws, vc)
             + jnp.swapaxes(b_spatial[:, :c], 0, 1)[:, :, None])
    mixed = mixed.reshape(n, n_c * c, D_B)[:, :t]
    last_start = ((t - 1) // MLP_CHUNK) * MLP_CHUNK
    return u * mixed, v[:, last_start:]


def trunk_layer(x, conv_buf, s0, norm_pre_mix, w_in, conv_w, a_log, dt_bias, delta_norm_w,
                sgu_ln_w, sgu_ln_b, w_spatial, b_spatial, w_proj_a, w_proj_b, w_out,
                norm_post_mix, norm_pre_ffn, w_ffn_in, w_ffn_out, norm_post_ffn):
    h = rms_norm(x, norm_pre_mix)
    proj = jnp.einsum('btd,de->bte', h, w_in)
    qkv, z, a, b, uv, gates = jnp.split(proj, SPLIT_POINTS, axis=-1)
    o_a, new_buf, s_new = delta_branch(qkv, z, a, b, conv_buf, s0, conv_w, a_log, dt_bias, delta_norm_w)
    o_b, v_rows = chunk_mlp_branch(uv, sgu_ln_w, sgu_ln_b, w_spatial, b_spatial)
    g_a, g_b = jnp.split(gates, 2, axis=-1)
    merged = (jax.nn.sigmoid(g_a) * jnp.einsum('bte,ed->btd', o_a, w_proj_a)
              + jax.nn.sigmoid(g_b) * jnp.einsum('bte,ed->btd', o_b, w_proj_b))
    x = x + rms_norm(jnp.einsum('btd,de->bte', merged, w_out), norm_post_mix)
    h = rms_norm(x, norm_pre_ffn)
    gate, up = jnp.split(jnp.einsum('btd,df->btf', h, w_ffn_in), 2, axis=-1)
    x = x + rms_norm(jnp.einsum('btf,fd->btd', jax.nn.silu(gate) * up, w_ffn_out), norm_post_ffn)
    return x, s_new, new_buf, v_rows


def setup_inputs(seed: int = 0) -> dict:
    key = jax.random.key(seed)
    ks = jax.random.split(key, 24)
    f32 = jnp.float32

    def nrm(k, shape, scale):
        return jax.random.normal(k, shape, f32) * scale

    def gain(k, shape):
        return 1.0 + 0.05 * jax.random.normal(k, shape, f32)

    dt = jax.random.uniform(ks[8], (DEPTH, N_HEADS_A), f32, 0.001, 0.1)
    return {
        "x_prompt": nrm(ks[0], (BATCH, SEQ, D_MODEL), 1.0),
        "x_sample": nrm(ks[1], (DEC_BATCH, DEC_SEQ, D_MODEL), 1.0),
        "state_delta": nrm(ks[2], (DEPTH, DEC_BATCH, N_HEADS_A, HEAD_DK, HEAD_DV), 0.1),
        "state_conv": nrm(ks[3], (DEPTH, DEC_BATCH, CONV_W - 1, D_CONV), 1.0),
        "norm_pre_mix": gain(ks[4], (DEPTH, D_MODEL)),
        "w_in": nrm(ks[5], (DEPTH, D_MODEL, D_IN), D_MODEL ** -0.5),
        "conv_w": nrm(ks[6], (DEPTH, CONV_W, D_CONV), CONV_W ** -0.5),
        "a_log": jnp.log(jax.random.uniform(ks[7], (DEPTH, N_HEADS_A), f32, 1.0, 16.0)),
        "dt_bias": jnp.log(jnp.expm1(dt)),
        "delta_norm_w": gain(ks[9], (DEPTH, HEAD_DV)),
        "sgu_ln_w": gain(ks[10], (DEPTH, D_B)),
        "sgu_ln_b": nrm(ks[11], (DEPTH, D_B), 0.02),
        "w_spatial": nrm(ks[12], (DEPTH, N_GROUPS_B, MLP_CHUNK, MLP_CHUNK), MLP_CHUNK ** -0.5),
        "b_spatial": 1.0 + 0.1 * jax.random.normal(ks[13], (DEPTH, N_GROUPS_B, MLP_CHUNK), f32),
        "w_proj_a": nrm(ks[14], (DEPTH, D_VA, D_MODEL), D_VA ** -0.5),
        "w_proj_b": nrm(ks[15], (DEPTH, D_B, D_MODEL), D_B ** -0.5),
        "w_out": nrm(ks[16], (DEPTH, D_MODEL, D_MODEL), D_MODEL ** -0.5),
        "norm_post_mix": gain(ks[17], (DEPTH, D_MODEL)),
        "norm_pre_ffn": gain(ks[18], (DEPTH, D_MODEL)),
        "w_ffn_in": nrm(ks[19], (DEPTH, D_MODEL, 2 * D_FF), D_MODEL ** -0.5),
        "w_ffn_out": nrm(ks[20], (DEPTH, D_FF, D_MODEL), D_FF ** -0.5),
        "norm_post_ffn": gain(ks[21], (DEPTH, D_MODEL)),
    }


def reference(x_prompt, x_sample, state_delta, state_conv, norm_pre_mix, w_in, conv_w, a_log,
              dt_bias, delta_norm_w, sgu_ln_w, sgu_ln_b, w_spatial, b_spatial, w_proj_a, w_proj_b,
              w_out, norm_post_mix, norm_pre_ffn, w_ffn_in, w_ffn_out, norm_post_ffn):
    y_p, y_s = x_prompt, x_sample
    conv0 = jnp.zeros((BATCH, CONV_W - 1, D_CONV), x_prompt.dtype)
    s_zero = jnp.zeros((BATCH, N_HEADS_A, HEAD_DK, HEAD_DV), state_delta.dtype)
    sd_p, sc_p, cv_p, sd_s, sc_s, cv_s = [], [], [], [], [], []
    for l in range(DEPTH):
        p = dict(norm_pre_mix=norm_pre_mix[l], w_in=w_in[l], conv_w=conv_w[l], a_log=a_log[l],
                 dt_bias=dt_bias[l], delta_norm_w=delta_norm_w[l], sgu_ln_w=sgu_ln_w[l],
                 sgu_ln_b=sgu_ln_b[l], w_spatial=w_spatial[l], b_spatial=b_spatial[l],
                 w_proj_a=w_proj_a[l], w_proj_b=w_proj_b[l], w_out=w_out[l],
                 norm_post_mix=norm_post_mix[l], norm_pre_ffn=norm_pre_ffn[l],
                 w_ffn_in=w_ffn_in[l], w_ffn_out=w_ffn_out[l], norm_post_ffn=norm_post_ffn[l])
        y_p, s_new, buf_new, v_rows = trunk_layer(y_p, conv0, s_zero, **p)
        sd_p.append(s_new)
        sc_p.append(buf_new)
        cv_p.append(v_rows)
        y_s, s_new, buf_new, v_rows = trunk_layer(y_s, state_conv[l], state_delta[l], **p)
        sd_s.append(s_new)
        sc_s.append(buf_new)
        cv_s.append(v_rows)
    return (y_p, y_s, jnp.stack(sd_p), jnp.stack(sc_p), jnp.stack(cv_p),
            jnp.stack(sd_s), jnp.stack(sc_s), jnp.stack(cv_s))
```

```python
import os
import numpy as np
from contextlib import ExitStack
import concourse.bass as bass
import concourse.mybir as mybir
from concourse.bass_utils import run_bass_kernel_spmd

F32 = mybir.dt.float32
BF16 = mybir.dt.bfloat16
AF = mybir.ActivationFunctionType
ALU = mybir.AluOpType

NCORES = 8
D = 1024
DEPTH = 2
H = 8
DIN = 8208
DFF = 2816
BIG = 30000.0
NSLOT = 3
ENGS = ["pe", "act", "dve", "pool", "sp"]
BNAME = {"pe": "tensor", "act": "scalar", "dve": "vector", "pool": "gpsimd", "sp": "sync"}
NEPOCH = 12
SCHED = True


class Buf:
    __slots__ = ("name", "lw", "rd")

    def __init__(self, name):
        self.name = name
        self.lw = None
        self.rd = []


def bufs(name, n):
    return [Buf(f"{name}{i}") for i in range(n)]


class Rec:
    def __init__(self):
        self.calls = []

    def __getattr__(self, name):
        def f(*a, **k):
            self.calls.append((name, a, k))
            return self
        return f


class Em:
    def __init__(self):
        self.ops = []
        self.chans = {}
        self.epoch = 0

    def op(self, eng, fn, reads=(), writes=(), chan=None, ndma=0):
        idx = len(self.ops)
        deps = set()
        soft = set()
        for b in reads:
            if b.lw is not None:
                deps.add(b.lw)
        for b in writes:
            if b.lw is not None:
                soft.add(b.lw)
            soft.update(b.rd)
        deps |= soft
        val = None
        if chan:
            c = self.chans.setdefault(chan, {"count": 0, "last": None, "eng": eng})
            assert c["eng"] == eng
            if c["last"] is not None:
                deps.add(c["last"])
            c["count"] += 16 * ndma
            c["last"] = idx
            val = c["count"]
        key = chan if chan else eng
        for b in writes:
            b.lw = idx
            b.rd = []
        for b in reads:
            b.rd.append(idx)
        rec = Rec()
        fn(rec)
        assert rec.calls
        self.ops.append(dict(eng=eng, calls=rec.calls, deps=deps, chan=chan, val=val, inc=False, ep=self.epoch))
        return idx

    @staticmethod
    def _est(o):
        eng = o["eng"]
        tot = 0.0
        for name, a, k in o["calls"]:
            out = k.get("out", a[0] if a else None)
            try:
                fs = float(out.free_size())
            except Exception:
                fs = 128.0
            if o["chan"]:
                try:
                    nb = float(out.nbytes())
                except Exception:
                    nb = 1e5
                tot += 2.0 + nb / 2.5e5
            elif eng == "pe":
                lhs = k.get("lhsT", a[1] if len(a) > 1 else None) if name == "matmul" else k.get("in_")
                f32 = getattr(lhs, "dtype", None) == F32
                tot += 0.005 + max(fs, 64.0) * (4.0 if (f32 and name == "matmul") else 1.0) / 2400.0 + (0.05 if fs <= 128 else 0.0)
            elif eng == "dve":
                tot += 0.12 + fs * 1.1e-3
            elif eng == "act":
                tot += 0.17 + fs * 0.9e-3
            else:
                tot += 0.25 + fs * 2.2e-3
        return tot

    def schedule(self):
        import heapq
        ops = self.ops
        n = len(ops)
        succ = [[] for _ in range(n)]
        indeg = [0] * n
        for i, o in enumerate(ops):
            for d in o["deps"]:
                succ[d].append(i)
                indeg[i] += 1
        dur = [self._est(o) for o in ops]
        bl = [0.0] * n
        for i in range(n - 1, -1, -1):
            m_ = 0.0
            for j in succ[i]:
                if bl[j] > m_:
                    m_ = bl[j]
            bl[i] = dur[i] + 0.3 + m_
        PRIO = os.environ.get("KPRIO", "bl")
        ready = [0.0] * n
        fin = [0.0] * n
        free = {e: 0.0 for e in ENGS}
        heaps = {e: [] for e in ENGS}
        for i in range(n):
            if indeg[i] == 0:
                heapq.heappush(heaps[ops[i]["eng"]], (0.0, i))
        order = []
        LAT = 0.3
        while len(order) < n:
            best = None
            for e in ENGS:
                hp = heaps[e]
                if not hp:
                    continue
                cand = hp[0]
                st = max(free[e], cand[0])
                key = (st, cand[1])
                if best is None or key < best[0]:
                    best = (key, e)
            (st, i), e = best
            hp = heaps[e]
            pool_ = []
            while hp and hp[0][0] <= st:
                pool_.append(heapq.heappop(hp))
            if PRIO == "bl":
                pool_.sort(key=lambda x: (-bl[x[1]], x[1]))
            else:
                pool_.sort(key=lambda x: x[1])
            _, i = pool_[0]
            for x in pool_[1:]:
                heapq.heappush(hp, x)
            o = ops[i]
            fin[i] = st + dur[i]
            free[e] = st + (min(dur[i], 1.0) if o["chan"] else dur[i])
            order.append(i)
            for j in succ[i]:
                r = fin[i] + (LAT if ops[j]["eng"] != e or o["chan"] else 0.1)
                if r > ready[j]:
                    ready[j] = r
                indeg[j] -= 1
                if indeg[j] == 0:
                    heapq.heappush(heaps[ops[j]["eng"]], (ready[j], j))
        pos = {old: new for new, old in enumerate(order)}
        new_ops = []
        for old in order:
            o = ops[old]
            o["deps"] = {pos[d] for d in o["deps"]}
            new_ops.append(o)
        self.ops = new_ops
        self.est_span = max(fin) if fin else 0.0
        if os.environ.get("KDBG"):
            cp = [0.0] * n
            for old in order:
                o = ops[old]
            newdur = [dur[old] for old in order]
            for i_, o in enumerate(new_ops):
                st_ = 0.0
                for d in o["deps"]:
                    st_ = max(st_, cp[d] + LAT)
                cp[i_] = st_ + newdur[i_]
            busy = {}
            for i_, o in enumerate(new_ops):
                busy[o["eng"]] = busy.get(o["eng"], 0.0) + (min(newdur[i_], 1.0) if o["chan"] else newdur[i_])
            print("critical_path_us", max(cp), "busy", {k: round(v) for k, v in busy.items()}, flush=True)
            i_ = max(range(n), key=lambda q: cp[q])
            agg = {}
            seq = []
            while True:
                o = new_ops[i_]
                nm = o["calls"][0][0] + ("/dma" if o["chan"] else "")
                outap = o["calls"][0][2].get("out", o["calls"][0][1][0] if o["calls"][0][1] else None)
                tn = getattr(getattr(outap, "tensor", None), "name", "?")
                key = (o["eng"], nm, tn)
                a_ = agg.setdefault(key, [0, 0.0])
                a_[0] += 1
                a_[1] += newdur[i_] + LAT
                seq.append(key)
                prev = None
                for d in o["deps"]:
                    if prev is None or cp[d] > cp[prev]:
                        prev = d
                if prev is None:
                    break
                i_ = prev
            for k_, v_ in sorted(agg.items(), key=lambda kv: -kv[1][1])[:28]:
                print("   CP", k_, v_[0], round(v_[1], 1), flush=True)

    def prune(self):
        for o in self.ops:
            best = {}
            for d in o["deps"]:
                p = self.ops[d]
                key = ("c", p["chan"]) if p["chan"] else ("e", p["eng"])
                if key not in best or d > best[key]:
                    best[key] = d
            o["deps"] = set(best.values())

    def finalize(self):
        if SCHED:
            self.schedule()
        self.prune()
        for o in self.ops:
            for d in o["deps"]:
                p = self.ops[d]
                if p["chan"] is None:
                    if p["eng"] == "pe" and o["eng"] == "pe" and o["chan"] is None:
                        continue
                    p["inc"] = True
        cnt = {}
        for o in self.ops:
            if o["chan"] is None and o["inc"]:
                k = (o["eng"], o["ep"])
                cnt[k] = cnt.get(k, 0) + 1
                o["val"] = cnt[k]

    def emit(self, block, sems, chansems):
        for e in ENGS:
            ops_e = [o for o in self.ops if o["eng"] == e]

            def body(eng, ops_e=ops_e, e=e):
                seen = {}
                for o in ops_e:
                    for d in sorted(o["deps"]):
                        p = self.ops[d]
                        if p["chan"] is None:
                            if not p["inc"]:
                                continue
                            if p["eng"] == "pe" and e == "pe" and o["chan"] is None:
                                continue
                            key = (p["eng"], p["ep"])
                            sem = sems[key]
                        else:
                            key = p["chan"]
                            sem = chansems[key]
                        v = p["val"]
                        if seen.get(key, 0) >= v:
                            continue
                        seen[key] = v
                        eng.wait_ge(sem, v)
                    ins = None
                    for name, a, k in o["calls"]:
                        ins = getattr(eng, name)(*a, **k)
                        if o["chan"]:
                            ins.then_inc(chansems[o["chan"]], 16)
                    if (not o["chan"]) and o["inc"]:
                        ins.then_inc(sems[(e, o["ep"])], 1)
                for ch, c in self.chans.items():
                    if c["eng"] == e and c["count"] > 0:
                        eng.wait_ge(chansems[ch], c["count"])

            getattr(block, BNAME[e])(body)


def _consts():
    i = np.arange(128)
    m = i[:, None]
    p = i[None, :]
    blk = i // 8
    same = blk[:, None] == blk[None, :]
    c = np.zeros((13, 128, 128), np.float32)
    c[0] = np.eye(128)
    c[1] = 1.0
    c[2] = m <= p
    c[3] = m > p
    c[4] = -BIG * (p >= m)
    c[5] = -BIG * (p < m)
    c[6] = (m <= p) & same
    c[7] = (m > p) & same
    c[8] = -BIG * (~((p < m) & same))
    c[9] = -BIG * (~((p >= m) & same))
    c[10] = same
    s = np.zeros((128, 16, 16), np.float32)
    for j in range(16):
        s[:, j, j] = 1.0
    c[11] = s.reshape(128, 256)[:, :128]
    c[12] = s.reshape(128, 256)[:, 128:]
    c32 = np.ascontiguousarray(c[[0, 1, 2, 3, 6, 7, 10]])
    md = ((m // 64) == (p // 64)).astype(np.float32)
    mo = ((m < 64) & (p >= 64)).astype(np.float32)
    c16 = np.ascontiguousarray(np.concatenate([c[[0, 1, 4, 5, 8, 9, 11, 12]], md[None], mo[None]], 0))
    return c32, c16


def _cb():
    i = np.arange(128)
    maskc = (i[None, :] // 8 == np.arange(16)[:, None]).astype(np.float32)
    maskc = np.broadcast_to(maskc[None], (128, 16, 128)).reshape(128, 2048)
    seqsel = (i[:, None] // 8 == np.arange(16)[None, :]).astype(np.float32)
    return np.concatenate([maskc, seqsel], axis=1).astype(np.float32)


def build(nblk=4, do_samp=True, stop=99):
    reqs = []
    _build(nblk, do_samp, stop, reqs, None)
    return _build(nblk, do_samp, stop, [], reqs)


LOOKAHEAD = 2


def _build(nblk, do_samp, stop, rec_reqs, all_reqs):
    nc = bass.Bass("TRN2", target_bir_lowering=False)

    def din(name, shape):
        return nc.dram_tensor(name, list(shape), F32, kind="ExternalInput").ap()

    def dout(name, shape):
        return nc.dram_tensor(name, list(shape), F32, kind="ExternalOutput").ap()

    xp = din("xp", [2048, D])
    xs = din("xs", [128, D])
    sd = din("sd", [DEPTH, 16, H, 128, 128])
    sc = din("sc", [DEPTH, 48, 3072])
    w_in = din("w_in", [DEPTH, D, DIN])
    w_pa = din("w_pa", [DEPTH, D, D])
    w_pb = din("w_pb", [DEPTH, D, D])
    w_o = din("w_o", [DEPTH, D, D])
    w_f1 = din("w_f1", [DEPTH, D, 2 * DFF])
    w_f2 = din("w_f2", [DEPTH, DFF, D])
    pp = din("pp", [128, 2 * 32 + 2 * 96 + 2 + 4])
    prow = din("prow", [1, DEPTH * 2 * 1024])
    lnw = din("lnw", [DEPTH, 2, D])
    wsp = din("wsp", [DEPTH, 2, 128, 1024])
    cst = din("cst", [7, 128, 128])
    cst16 = din("cst16", [10, 128, 128])
    cbd = din("cbd", [128, 2064])

    NSCR = 2 * 41
    wscr = nc.dram_tensor("wscr", [NSCR, 128, 4096], BF16, kind="Internal").ap()
    wscr_b = bufs("wscr", NSCR)
    y_p = dout("y_p", [2048, D])
    y_s = dout("y_s", [128, D])
    sd_p = dout("sd_p", [DEPTH, H, 128, 128])
    sc_p = dout("sc_p", [DEPTH, 3, 3072])
    cv_p = dout("cv_p", [DEPTH, 128, D])
    sd_s = dout("sd_s", [DEPTH, 16, H, 128, 128])
    sc_s = dout("sc_s", [DEPTH, 16, 3, 3072])
    cv_s = dout("cv_s", [DEPTH, 128, D])

    E = Em()
    es = ExitStack()
    with es:
        def sb(name, shape, dt):
            return es.enter_context(nc.sbuf_tensor(name, shape, dt))

        xT = sb("xT", [128, 8, 512], F32); xT_b = bufs("xT", 8)
        hT = sb("hT", [128, 8, 512], BF16); hT_b = bufs("hT", 8)
        R1 = sb("R1", [128, 24, 512], BF16); R1_b = bufs("R1", 24)
        sz = sb("sz", [128, 8, 512], BF16); sz_b = bufs("sz", 8)
        oa = sb("oa", [128, 8, 512], BF16); oa_b = bufs("oa", 8)
        uT_b = R1_b[0:8]; vT_b = R1_b[8:16]
        t32 = sb("t32", [128, 8, 512], F32); t32_b = bufs("t32", 8)
        pre = [sb(f"pre{i}", [128, 515], F32) for i in range(2)]; pre_b = bufs("pre", 2)
        acc = [sb(f"acc{i}", [128, 512], F32) for i in range(2)]; acc_b = bufs("acc", 2)
        sq = [sb(f"sq{i}", [128, 1024], BF16) for i in range(2)]; sq_b = bufs("sq", 2)
        rstd = sb("rstd", [128, 1024], F32); rstd_b = Buf("rstd")
        carry = sb("carry", [128, DEPTH, 24, 3], F32); carry_b = [bufs(f"car{l}_", 24) for l in range(DEPTH)]
        wsl = [sb(f"wsl{i}", [128, 8, 512], BF16) for i in range(NSLOT)]; wsl_b = bufs("wsl", NSLOT)
        ppt = sb("ppt", [128, 262], F32); ppt_b = Buf("ppt")
        nea = sb("nea", [128, 2], F32); nea_b = Buf("nea")
        dnw = sb("dnw", [128, 2], F32); dnw_b = Buf("dnw")
        c32 = sb("c32", [128, 7, 128], F32); c32_b = Buf("c32")
        c16 = sb("c16", [128, 10, 128], BF16); c16_b = Buf("c16")
        seq32 = sb("seq32", [128, 16], F32); cb32_b = Buf("cb32")
        mskc = sb("mskc", [128, 16, 128], BF16); seqs = sb("seqs", [128, 16], BF16); cb16_b = Buf("cb16")
        selT = sb("selT", [16, 16, 128], BF16); selT_b = Buf("selT")
        prw16 = sb("prw16", [1, 1024], BF16); prw_b = Buf("prw")
        t32f = t32[:].rearrange("p a b -> p (a b)")
        lnt = t32f[:, 0:2048].rearrange("p (a b) -> p a b", a=2)
        xin = t32f[:, 0:1024]
        yout = [t32f[:, 1024:2048], t32f[:, 2048:3072]]
        scrow = t32f[0:3, 0:3072]
        scin = sb("scin", [128, 24, 48], F32); scin_b = Buf("scin")
        wsT = sb("wsT", [128, 8, 128], BF16); wsT_b = Buf("wsT")
        gT = pre[0][0:8, 0:512]; gT_b = pre_b[0]
        bT = pre[1][0:8, 0:512]; bT_b = pre_b[1]
        rn = acc[1][0:16, :]; rn_b = acc_b[1]
        rnb = acc[0][0:16, 0:256].bitcast(BF16); rnb_b = acc_b[0]
        rnl = acc[0][0:16, 256:512].bitcast(BF16)
        S32 = sb("S32", [128, DEPTH, H, 128], F32); S32_b = bufs("S32_", DEPTH)
        S16 = sb("S16", [128, DEPTH, H, 128], BF16); S16_b = bufs("S16_", DEPTH)
        s0f_0 = S32[:].rearrange("p a b c -> p (a b) c"); s0f_b0 = S32_b
        s0b_0 = S16[:].rearrange("p a b c -> p (a b) c"); s0b_b0 = S16_b
        gbt = sb("gbt", [128, 16], F32); gbt_b = Buf("gbt")
        es24 = sb("es24", [128, 24], F32); es_b = Buf("es24")
        kbs = sb("kbs", [128, 8], F32); kbs_b = Buf("kbs")
        Gm1 = sb("Gm1", [128, 8, 128], F32); Gm1_b = Buf("Gm1")
        Gm2 = sb("Gm2", [128, 8, 128], F32); Gm2_b = Buf("Gm2")
        ws32 = Gm1[:].rearrange("p a b -> p (a b)"); ws32_b = Gm1_b
        Kd = sb("Kd", [128, 8, 128], F32); Kd_b = Buf("Kd")
        Vb = sb("Vb", [128, 8, 128], BF16); Vb_b = Buf("Vb")
        Bo = sb("Bo", [128, 8, 128], BF16); Bo_b = Buf("Bo")
        r16 = sb("r16", [128, 8, 128], BF16); r16_b = Buf("r16")
        neg = sb("neg", [128, 8], F32); neg_b = Buf("neg")
        dsm = Gm1; dsm_b = Gm1_b
        dTm = Gm2; dTm_b = Gm2_b
        reg = sb("reg", [128, 8, 128], BF16); reg_b = Buf("reg")
        AmT = sb("AmT", [128, 2, 8, 128], BF16); Am = [AmT[:, 0, :, :], AmT[:, 1, :, :]]; Am_b = bufs("Am", 2)
        BmT = sb("BmT", [128, 2, 8, 128], BF16); Bm = [BmT[:, 0, :, :], BmT[:, 1, :, :]]; Bm_b = bufs("Bm", 2)
        vn32 = AmT[:].rearrange("p a b c -> p (a b c)").bitcast(F32).rearrange("p (b c) -> p b c", c=128)
        r32 = BmT[:].rearrange("p a b c -> p (a b c)").bitcast(F32).rearrange("p (b c) -> p b c", c=128)
        Pm0 = sb("Pm0", [128, 8, 128], BF16); Pm_b0 = Buf("Pm")
        Pf = rstd[:].rearrange("p (b c) -> p b c", c=128)
        PTm = sb("PTm", [128, 8, 128], BF16); PTm_b = Buf("PTm")
        vn = sb("vn", [128, 8, 128], BF16); vn_b = Buf("vn")
        qt = sb("qt", [128, 8, 128], BF16); qt_b = Buf("qt")
        on32 = sb("on32", [128, 8, 128], F32); on32_b = Buf("on32")
        lnv32 = on32[:].rearrange("p a b -> p (a b)"); lnv32_b = on32_b
        vnb = sb("vnb", [128, D], BF16); vnb_b = Buf("vnb")
        bnst = sb("bnst", [128, 2, 6], F32); mv = sb("mv", [128, 4], F32); bn_b = Buf("bn")
        t16 = t32f.bitcast(BF16)

        def t16tile(c):
            return t16[:, c * 1024:(c + 1) * 1024].rearrange("p (a b) -> p a b", a=8)
        KdV = Kd[:].rearrange("p a b -> p (a b)").bitcast(BF16).rearrange("p (s a b) -> p s a b", s=2, a=8)
        Kd2 = [KdV[:, 0, :, :], KdV[:, 1, :, :]]; Kd2_b = bufs("Kd2_", 2)
        Vb2 = [Vb[:], t16tile(0)]; Vb2_b = [Vb_b, t32_b[0]]
        qt2 = [qt[:], t16tile(1)]; qt2_b = [qt_b, t32_b[1]]
        PTm2 = [PTm[:], t16tile(2)]; PTm2_b = [PTm_b, t32_b[2]]
        Pm2 = [Pm0[:], t16tile(3)]; Pm2_b = [Pm_b0, t32_b[3]]
        Bo2 = [Bo[:], t16tile(4)]; Bo2_b = [Bo_b, t32_b[4]]
        neg1 = sb("neg1", [128, 8], F32); neg2 = [neg, neg1]; neg2_b = [neg_b, Buf("neg1")]
        es1 = sb("es1", [128, 24], F32); es2 = [es24, es1]; es2_b = [es_b, Buf("es1")]
        rstd9 = t32f[:, 3072:4096]
        msk16 = sb("msk16", [128, 16, 128], BF16); msk16_b = Buf("msk16")
        egl = sb("egl", [128, 16, 8], F32); egl_b = Buf("egl")
        gsel = sb("gsel", [128, 16, 8], F32); gsel_b = Buf("gsel")
        snew = [sb(f"snew{i}", [128, 4, 128], F32) for i in range(2)]; snew_b = bufs("snew", 2)

        ps = es.enter_context(nc.psum_tensor("ps", [128, 8, 512], F32))
        ps_b = bufs("ps", 8)
        bptr = [0]

        def bank():
            b = bptr[0] % 8
            bptr[0] = (b + 1) % 8
            return b

        def pair():
            b = bptr[0] % 8
            if b % 2:
                b = (b + 1) % 8
            bptr[0] = (b + 2) % 8
            return b

        def P32(i, n=1):
            return ps[:, i:i + n, :].rearrange("p a b -> p (a b)") if n > 1 else ps[:, i, :]

        def P16(i):
            return ps[:, i, :].bitcast(BF16)

        def V(fn, r, w): return E.op("dve", fn, r, w)
        def A(fn, r, w): return E.op("act", fn, r, w)
        def T(fn, r, w): return E.op("pe", fn, r, w)
        def G(fn, r, w): return E.op("pool", fn, r, w)

        def dma(eng, chan, out, in_, r, w):
            return E.op(eng, lambda e: e.dma_start(out=out, in_=in_), r, w, chan=chan, ndma=1)

        slot_i = [0]
        issued = [0]
        WT = {"w_in": w_in, "w_pa": w_pa, "w_pb": w_pb, "w_o": w_o, "w_f1": w_f1, "w_f2": w_f2}

        scr_idx = {}

        def issue_w(k, desc):
            wn, l_, r0, nr, c0, ncol = desc
            s_ = k % NSLOT
            kc_ = nr // 128
            dst = wsl[s_][:, 0:kc_, 0:ncol]
            if desc not in scr_idx:
                i_ = len(scr_idx)
                scr_idx[desc] = i_
                src = WT[wn][l_, r0:r0 + nr, c0:c0 + ncol].rearrange("(c p) n -> p c n", p=128)
                dma("pool", f"w{s_}", dst, src, [], [wsl_b[s_]])
                sv = wscr[i_, :, 0:kc_ * ncol].rearrange("p (c n) -> p c n", c=kc_)
                dma("sp", f"scrw{i_ % 2}", sv, dst, [wsl_b[s_]], [wscr_b[i_]])
            else:
                i_ = scr_idx[desc]
                sv = wscr[i_, :, 0:kc_ * ncol].rearrange("p (c n) -> p c n", c=kc_)
                dma("pool", f"w{s_}", dst, sv, [wscr_b[i_]], [wsl_b[s_]])

        def load_w(desc, kc=None, ncols=None):
            k = slot_i[0]
            slot_i[0] += 1
            rec_reqs.append(desc)
            if all_reqs is None:
                issue_w(k, desc)
            else:
                assert all_reqs[k] == desc
                pump()
                assert issued[0] > k, "weight slot ring exhausted"
            return k % NSLOT

        wdone = [0]

        def pump():
            while issued[0] < len(all_reqs) and issued[0] - NSLOT < wdone[0]:
                issue_w(issued[0], all_reqs[issued[0]])
                issued[0] += 1

        def w_done(n=1):
            wdone[0] += n
            if all_reqs is not None:
                pump()

        def wview(wt, l, r0, nr, c0, ncol):
            return (wt, l, r0, nr, c0, ncol)

        I32 = c32[:, 0, :]; I16 = c16[:, 0, :]; ONES16 = c16[:, 1, :]; ONES32 = c32[:, 1, :]

        def mm_group(bk, col0, ncol, lhs_fn, rhs_fn, nk, rbufs, extra_r=(), mp=128):
            def fn(e):
                ins = None
                for k in range(nk):
                    ins = e.matmul(ps[0:mp, bk, col0:col0 + ncol], lhs_fn(k), rhs_fn(k),
                                   start=(k == 0), stop=(k == nk - 1))
                return ins
            return T(fn, list(rbufs) + list(extra_r), [ps_b[bk]])

        dma("sp", "par", ppt[:], pp, [], [ppt_b])
        dma("sp", "par", c32[:], cst.rearrange("k p n -> p k n"), [], [c32_b])
        dma("sp", "par", seq32[:], cbd[:, 2048:2064], [], [cb32_b])
        for hf in range(2):
            dma("pool", "cst16", mskc[:, hf * 8:(hf + 1) * 8, :].rearrange("p a b -> p (a b)"), cbd[:, hf * 1024:(hf + 1) * 1024], [], [cb16_b])
        dma("pool", "cst16", c16[:], cst16.rearrange("k p n -> p k n"), [], [c16_b])
        V(lambda e: e.tensor_copy(out=seqs[:], in_=seq32[:]), [cb32_b, cb16_b], [cb16_b])
        V(lambda e: e.tensor_copy(out=selT[:], in_=c32[0:16, 0, 0:16].unsqueeze(2).to_broadcast([16, 16, 128])),
          [c32_b], [selT_b])
        def NW(l, n): return ppt[:, (l * 4 + n) * 8:(l * 4 + n) * 8 + 8]
        def CW(l, tap, ch): return ppt[:, 64 + l * 96 + tap * 24 + ch:64 + l * 96 + tap * 24 + ch + 1]
        A(lambda e: e.activation(out=nea[0:8, :], in_=ppt[0:8, 258:260], func=AF.Exp), [ppt_b], [nea_b])
        V(lambda e: e.tensor_scalar(out=nea[0:8, :], in0=nea[0:8, :], scalar1=-1.0, scalar2=None, op0=ALU.mult), [nea_b], [nea_b])
        V(lambda e: e.tensor_scalar(out=dnw[:], in0=ppt[:, 256:258], scalar1=float(128 ** -0.5), scalar2=None, op0=ALU.mult),
          [ppt_b], [dnw_b])
        sel16 = c16[:, 6:8, :].rearrange("p a b -> p (a b)")

        def rsqrt_from_psum(out_ap, in_ap, scale, eps, r, w):
            A(lambda e: e.activation(out=out_ap, in_=in_ap, func=AF.Ln, scale=scale, bias=eps), r, w)
            A(lambda e: e.activation(out=out_ap, in_=out_ap, func=AF.Exp, scale=-0.5), w, w)

        def rmsnorm(src, src_b, W, gain, mode, l):
            bk = bank()
            for c in range(8):
                s = c % 2
                A(lambda e, c=c, s=s: e.activation(out=sq[s][:, 0:W], in_=src[:, c, 0:W], func=AF.Square),
                  [src_b[c]], [sq_b[s]])
                T(lambda e, c=c, s=s: e.matmul(ps[:, bk, 0:W], ONES16, sq[s][:, 0:W], start=(c == 0), stop=(c == 7)),
                  [sq_b[s], c16_b], [ps_b[bk]])
            rsqrt_from_psum(rstd[:, 0:W], ps[:, bk, 0:W], 1.0 / D, 1e-6, [ps_b[bk]], [rstd_b])
            for c in range(8):
                if mode == "h":
                    V(lambda e, c=c: e.scalar_tensor_tensor(out=hT[:, c, 0:W], in0=src[:, c, 0:W], scalar=gain[:, c:c + 1],
                                                            in1=rstd[:, 0:W], op0=ALU.mult, op1=ALU.mult),
                      [src_b[c], rstd_b, ppt_b], [hT_b[c]])
                else:
                    V(lambda e, c=c: e.scalar_tensor_tensor(out=src[:, c, 0:W], in0=src[:, c, 0:W], scalar=gain[:, c:c + 1],
                                                            in1=rstd[:, 0:W], op0=ALU.mult, op1=ALU.mult),
                      [src_b[c], rstd_b, ppt_b], [src_b[c]])
                    G(lambda e, c=c: e.tensor_tensor(out=xT[:, c, 0:W], in0=xT[:, c, 0:W], in1=src[:, c, 0:W], op=ALU.add),
                      [src_b[c], xT_b[c]], [xT_b[c]])

        def proj_chunks(wt, l, c0, nchunks, W, rhs, rhs_b, consume, kc=8, r0=0):
            j = 0
            pend = [None]
            while j < nchunks:
                nj = min(4, nchunks - j)
                s = load_w(wview(wt, l, r0, kc * 128, c0 + j * 128, nj * 128), kc, nj * 128)
                for jj in range(nj):
                    bk = bank()
                    mm_group(bk, 0, W, lambda k, s=s, jj=jj: wsl[s][:, k, jj * 128:(jj + 1) * 128],
                             lambda k: rhs[:, k, 0:W], kc, rhs_b, [wsl_b[s]])
                    d_ = consume(j + jj, bk)
                    if pend[0] is not None:
                        pend[0]()
                    pend[0] = d_ if callable(d_) else None
                w_done()
                j += nj
            if pend[0] is not None:
                pend[0]()

        def layer(l, W, samp, last_blk):
            NT = W // 128
            if stop <= 0:
                return
            rmsnorm(xT, xT_b, W, NW(l, 0), "h", l)
            if stop <= 1:
                return
            if samp:
                tin = t32f
                dma("sp", "scin", tin[0:48, 0:3072], sc[l], [], t32_b[0:6])
                for q4 in range(6):
                    bk = bank()
                    for cc in range(4):
                        ch = q4 * 4 + cc
                        T(lambda e, ch=ch, cc=cc, bk=bk: e.transpose(out=ps[:, bk, cc * 48:(cc + 1) * 48], in_=tin[0:48, ch * 128:(ch + 1) * 128],
                                                                    identity=c32[0:48, 0, 0:48]), t32_b[0:6] + [c32_b], [ps_b[bk]])
                    V(lambda e, q4=q4, bk=bk: e.tensor_copy(out=scin[:, q4 * 4:(q4 + 1) * 4, :],
                                                            in_=ps[:, bk, 0:192].rearrange("p (a b) -> p a b", a=4)),
                      [ps_b[bk]], [scin_b])
            tok_b = t32_b
            if samp:
                tokm = t32f[:, 0:3072]

            def qkv_consume(ch, bk):
                s = ch % 2
                if samp:
                    pv = pre[s][:, 0:176].rearrange("p (a b) -> p a b", a=16)
                    G(lambda e: e.tensor_copy(out=pv[:, :, 0:3], in_=scin[:, ch, :].rearrange("p (a b) -> p a b", a=16)),
                      [scin_b], [pre_b[s]])
                    A(lambda e: e.activation(out=pv[:, :, 3:11], in_=ps[:, bk, 0:128].rearrange("p (a b) -> p a b", a=16),
                                             func=AF.Copy), [ps_b[bk]], [pre_b[s]])
                    av = acc[s][:, 0:128].rearrange("p (a b) -> p a b", a=16)
                    A(lambda e: e.activation(out=acc[s][:, 0:128], in_=ps[:, bk, 0:128], func=AF.Copy, scale=CW(l, 3, ch)),
                      [ps_b[bk], ppt_b], [acc_b[s]])
                    for tap in (2, 1, 0):
                        V(lambda e, tap=tap: e.scalar_tensor_tensor(out=av, in0=pv[:, :, tap:tap + 8], scalar=CW(l, tap, ch),
                                                                    in1=av, op0=ALU.mult, op1=ALU.add),
                          [pre_b[s], acc_b[s], ppt_b], [acc_b[s]])
                    G(lambda e: e.tensor_copy(out=sq[s][:, 0:256].bitcast(F32).rearrange("p (a b) -> p a b", a=16),
                                              in_=pv[:, :, 3:11]), [pre_b[s]], [sq_b[s]])
                    b2 = bank()
                    T(lambda e: e.transpose(out=ps[:, b2, 0:128], in_=sq[s][:, 0:256].bitcast(F32), identity=I32),
                      [sq_b[s], c32_b], [ps_b[b2]])
                    V(lambda e: e.tensor_copy(out=tokm[:, ch * 128:(ch + 1) * 128], in_=ps[:, b2, 0:128]),
                      [ps_b[b2]], [tok_b[ch // 4]])
                else:
                    G(lambda e: e.tensor_copy(out=pre[s][:, 0:3], in_=carry[:, l, ch, :]), [carry_b[l][ch]], [pre_b[s]])
                    A(lambda e: e.activation(out=pre[s][:, 3:515], in_=ps[:, bk, :], func=AF.Copy), [ps_b[bk]], [pre_b[s]])
                    A(lambda e: e.activation(out=acc[s][:], in_=ps[:, bk, :], func=AF.Copy, scale=CW(l, 3, ch)),
                      [ps_b[bk], ppt_b], [acc_b[s]])
                    G(lambda e: e.tensor_copy(out=carry[:, l, ch, :], in_=pre[s][:, 512:515]), [pre_b[s]], [carry_b[l][ch]])
                    for tap in (2, 1, 0):
                        V(lambda e, tap=tap: e.scalar_tensor_tensor(out=acc[s][:], in0=pre[s][:, tap:tap + 512],
                                                                    scalar=CW(l, tap, ch), in1=acc[s][:],
                                                                    op0=ALU.mult, op1=ALU.add),
                          [pre_b[s], acc_b[s], ppt_b], [acc_b[s]])
                return lambda: A(lambda e: e.activation(out=R1[:, ch, 0:W], in_=acc[s][:, 0:W], func=AF.Silu), [acc_b[s]], [R1_b[ch]])

            proj_chunks("w_in", l, 0, 24, W, hT, hT_b, qkv_consume)
            if samp:
                for j in range(3):
                    dma("sp", "o_sc", sc_s[l, :, j, :], tokm[5 + j:128:8, :], tok_b[0:6], [])
            elif last_blk:
                for q4 in range(6):
                    bk = bank()
                    for cc in range(4):
                        ch = q4 * 4 + cc
                        T(lambda e, ch=ch, cc=cc: e.transpose(out=ps[0:3, bk, cc * 128:(cc + 1) * 128], in_=carry[:, l, ch, :],
                                                              identity=I32), [carry_b[l][ch], c32_b], [ps_b[bk]])
                    V(lambda e, q4=q4: e.tensor_copy(out=scrow[:, q4 * 512:(q4 + 1) * 512], in_=ps[0:3, bk, :]),
                      [ps_b[bk]], t32_b[0:6])
                dma("sp", "o_sc", sc_p[l], scrow, t32_b[0:6], [])

            if stop <= 2:
                return
            proj_chunks("w_in", l, 3072, 8, W, hT, hT_b,
                        lambda ch, bk: A(lambda e: e.activation(out=sz[:, ch, 0:W], in_=ps[:, bk, 0:W], func=AF.Silu),
                                         [ps_b[bk]], [sz_b[ch]]))
            s = load_w(wview("w_in", l, 0, 1024, 4096, 16), 8, 16)
            ba = bank()
            mm_group(ba, 0, W, lambda k: wsl[s][:, k, 0:8], lambda k: hT[:, k, 0:W], 8, hT_b, [wsl_b[s]], mp=8)
            bb = bank()
            mm_group(bb, 0, W, lambda k: wsl[s][:, k, 8:16], lambda k: hT[:, k, 0:W], 8, hT_b, [wsl_b[s]], mp=8)
            w_done()
            A(lambda e: e.activation(out=gT[:, 0:W], in_=ps[0:8, ba, 0:W], func=AF.Exp, bias=ppt[0:8, 260 + l:261 + l]),
              [ps_b[ba], ppt_b], [gT_b])
            A(lambda e: e.activation(out=gT[:, 0:W], in_=gT[:, 0:W], func=AF.Ln, bias=1.0), [gT_b], [gT_b])
            V(lambda e: e.tensor_scalar(out=gT[:, 0:W], in0=gT[:, 0:W], scalar1=nea[0:8, l:l + 1], scalar2=None, op0=ALU.mult),
              [gT_b, nea_b], [gT_b])
            A(lambda e: e.activation(out=bT[:, 0:W], in_=ps[0:8, bb, 0:W], func=AF.Sigmoid), [ps_b[bb]], [bT_b])
            bn_ = bank()
            for j in range(16):
                s2 = j % 2
                A(lambda e, j=j, s2=s2: e.activation(out=sq[s2][:, 0:W], in_=R1[:, j, 0:W], func=AF.Square),
                  [R1_b[j]], [sq_b[s2]])
                T(lambda e, j=j, s2=s2: e.matmul(ps[0:16, bn_, 0:W], sel16[:, j * 16:(j + 1) * 16], sq[s2][:, 0:W],
                                                 start=(j == 0), stop=(j == 15)), [sq_b[s2], c16_b], [ps_b[bn_]])
            rsqrt_from_psum(rn[:, 0:W], ps[0:16, bn_, 0:W], 1.0, 1e-6, [ps_b[bn_]], [rn_b])
            V(lambda e: e.tensor_copy(out=rnb[:, 0:W], in_=rn[:, 0:W]), [rn_b], [rnb_b])
            V(lambda e: e.tensor_tensor(out=rnl[:, 0:W], in0=rn[:, 0:W], in1=rnb[:, 0:W], op=ALU.subtract), [rn_b, rnb_b], [rnb_b])
            for j in range(16):
                bk = bank()
                def fbc(e, j=j, bk=bk):
                    e.matmul(ps[:, bk, 0:W], selT[:, j, :], rnb[:, 0:W], start=True, stop=False)
                    return e.matmul(ps[:, bk, 0:W], selT[:, j, :], rnl[:, 0:W], start=False, stop=True)
                T(fbc, [rnb_b, selT_b], [ps_b[bk]])
                V(lambda e, j=j, bk=bk: e.tensor_tensor(out=R1[:, j, 0:W], in0=R1[:, j, 0:W], in1=ps[:, bk, 0:W], op=ALU.mult),
                  [R1_b[j], ps_b[bk]], [R1_b[j]])

            if stop <= 3:
                return
            cU, cSL, cN1, cN2 = (4, 5, 4, 5) if samp else (2, 3, 2, 3)
            nlev = 3 if samp else 6

            def bc8(ap):
                return ap.unsqueeze(2).to_broadcast([128, 8, 128])

            def bcm(ap):
                return ap.unsqueeze(1).to_broadcast([128, 8, 128])

            def PV(i):
                return ps[:, i:i + 2, :].rearrange("p a (b c) -> p (a b) c", c=128)

            class Alloc:
                def __init__(self, lo, n):
                    self.lo, self.n, self.p = lo, n, 0

                def bank(self):
                    b_ = self.p % self.n
                    self.p = (b_ + 1) % self.n
                    return self.lo + b_

                def pair(self):
                    b_ = self.p % self.n
                    if b_ % 2:
                        b_ = (b_ + 1) % self.n
                    self.p = (b_ + 2) % self.n
                    return self.lo + b_

            a1 = Alloc(0, 4)
            a2 = Alloc(0, 8)
            qb = R1_b[0:8]

            def chunk_gen(t):
                cs = slice(t * 128, (t + 1) * 128)
                st = 0 if samp else t % 2
                Kd_, Kd_b_ = Kd2[st], Kd2_b[st]
                Vb_, Vb_b_ = Vb2[st], Vb2_b[st]
                qt_, qt_b_ = qt2[st], qt2_b[st]
                PTm_, PTm_b_ = PTm2[st], PTm2_b[st]
                TT, TT_b = Pm2[st], Pm2_b[st]
                Bo_, Bo_b_ = Bo2[st], Bo2_b[st]
                neg_, neg_b_ = neg2[st], neg2_b[st]
                es_, es_b_ = es2[st], es2_b[st]
                bK = a1.bank()
                for h in range(H):
                    T(lambda e, h=h: e.transpose(out=P16(bK)[:, h * 128:(h + 1) * 128], in_=R1[:, 8 + h, cs], identity=I16),
                      [R1_b[8 + h], c16_b], [ps_b[bK]])
                bV = a1.bank()
                for h in range(H):
                    T(lambda e, h=h: e.transpose(out=P16(bV)[:, h * 128:(h + 1) * 128], in_=R1[:, 16 + h, cs], identity=I16),
                      [R1_b[16 + h], c16_b], [ps_b[bV]])
                bG = a1.bank()
                T(lambda e: e.transpose(out=ps[:, bG, 0:8], in_=gT[:, cs], identity=c32[0:8, 0, 0:8]), [gT_b, c32_b], [ps_b[bG]])
                T(lambda e: e.transpose(out=ps[:, bG, 8:16], in_=bT[:, cs], identity=c32[0:8, 0, 0:8]), [bT_b, c32_b], [ps_b[bG]])
                V(lambda e: e.tensor_copy(out=gbt[:], in_=ps[:, bG, 0:16]), [ps_b[bG]], [gbt_b])
                yield 1
                bS = a1.bank()
                cLast = 6 if samp else 1
                for i3, ci in enumerate((cU, cSL, cLast)):
                    T(lambda e, i3=i3, ci=ci: e.matmul(ps[:, bS, i3 * 8:(i3 + 1) * 8], c32[:, ci, :], gbt[:, 0:8],
                                                       start=True, stop=True), [gbt_b, c32_b], [ps_b[bS]])
                A(lambda e: e.activation(out=es_[:], in_=ps[:, bS, 0:24], func=AF.Exp), [ps_b[bS]], [es_b_])
                V(lambda e: e.tensor_tensor(out=kbs[:], in0=gbt[:, 8:16], in1=es_[:, 0:8], op=ALU.mult), [gbt_b, es_b_], [kbs_b])
                yield 1
                KP = P16(bK).rearrange("p (a b) -> p a b", a=8)
                VP = P16(bV).rearrange("p (a b) -> p a b", a=8)
                V(lambda e: e.tensor_tensor(out=Kd_, in0=KP, in1=bc8(es_[:, 8:16]), op=ALU.mult), [ps_b[bK], es_b_], [Kd_b_])
                V(lambda e: e.tensor_tensor(out=Vb_, in0=VP, in1=bc8(gbt[:, 8:16]), op=ALU.mult), [ps_b[bV], gbt_b], [Vb_b_])
                V(lambda e: e.tensor_scalar(out=neg_[:], in0=kbs[:], scalar1=-1.0, scalar2=None, op0=ALU.mult), [kbs_b], [neg_b_])
                yield 1
                G(lambda e: e.tensor_tensor(out=Gm1[:], in0=bcm(c32[:, cSL, :]), in1=bc8(gbt[:, 0:8]), op=ALU.mult),
                  [c32_b, gbt_b], [Gm1_b])
                V(lambda e: e.tensor_tensor(out=Gm2[:], in0=bcm(c32[:, cU, :]), in1=bc8(gbt[:, 0:8]), op=ALU.mult),
                  [c32_b, gbt_b], [Gm2_b])
                yield 1
                pD = a1.pair()
                pT = a1.pair()
                for hh in range(2):
                    g1 = Gm1[:, hh * 4:(hh + 1) * 4, :].rearrange("p a b -> p (a b)")
                    g2 = Gm2[:, hh * 4:(hh + 1) * 4, :].rearrange("p a b -> p (a b)")

                    def fD(e, hh=hh, g1=g1):
                        e.matmul(ps[:, pD + hh, :], c32[:, cU, :], g1, start=True, stop=False)
                        ins = None
                        for q in range(4):
                            ins = e.matmul(ps[:, pD + hh, q * 128:(q + 1) * 128], I16, c16[:, cN1, :], start=False, stop=(q == 3))
                        return ins
                    T(fD, [Gm1_b, c32_b, c16_b], [ps_b[pD + hh]])

                    def fT(e, hh=hh, g2=g2):
                        e.matmul(ps[:, pT + hh, :], c32[:, cSL, :], g2, start=True, stop=False)
                        ins = None
                        for q in range(4):
                            ins = e.matmul(ps[:, pT + hh, q * 128:(q + 1) * 128], I16, c16[:, cN2, :], start=False, stop=(q == 3))
                        return ins
                    T(fT, [Gm2_b, c32_b, c16_b], [ps_b[pT + hh]])
                yield 1
                A(lambda e: e.activation(out=dsm[:].rearrange("p a b -> p (a b)"), in_=P32(pD, 2), func=AF.Exp),
                  [ps_b[pD], ps_b[pD + 1]], [dsm_b])
                pR = a1.pair()
                for hh in range(2):
                    g2 = Gm2[:, hh * 4:(hh + 1) * 4, :].rearrange("p a b -> p (a b)")
                    T(lambda e, hh=hh, g2=g2: e.matmul(ps[:, pR + hh, :], ONES32, g2, start=True, stop=True),
                      [Gm2_b, c32_b], [ps_b[pR + hh]])
                A(lambda e: e.activation(out=dTm[:].rearrange("p a b -> p (a b)"), in_=P32(pT, 2), func=AF.Exp),
                  [ps_b[pT], ps_b[pT + 1]], [dTm_b])
                A(lambda e: e.activation(out=reg[:].rearrange("p a b -> p (a b)"), in_=P32(pR, 2), func=AF.Exp),
                  [ps_b[pR], ps_b[pR + 1]], [reg_b])
                yield 1
                V(lambda e: e.tensor_tensor(out=qt_, in0=R1[:, 0:8, cs], in1=reg[:], op=ALU.mult), qb + [reg_b], [qt_b_])
                pG = a1.pair()
                pP = a1.pair()
                for h in range(H):
                    T(lambda e, h=h: e.matmul(PV(pG)[:, h, :], R1[:, 8 + h, cs], R1[:, 8 + h, cs], start=True, stop=True),
                      [R1_b[8 + h]], [ps_b[pG + h // 4]])
                for h in range(H):
                    T(lambda e, h=h: e.matmul(PV(pP)[:, h, :], R1[:, 8 + h, cs], R1[:, h, cs], start=True, stop=True),
                      [R1_b[8 + h], R1_b[h]], [ps_b[pP + h // 4]])
                yield 1
                for h in range(H):
                    V(lambda e, h=h: e.scalar_tensor_tensor(out=Am[0][:, h, :], in0=PV(pG)[:, h, :], scalar=gbt[:, 8 + h:9 + h],
                                                            in1=dsm[:, h, :], op0=ALU.mult, op1=ALU.mult),
                      [ps_b[pG + h // 4], gbt_b, dsm_b], [Am_b[0]])
                V(lambda e: e.tensor_tensor(out=PTm_, in0=PV(pP), in1=dTm[:], op=ALU.mult),
                  [ps_b[pP], ps_b[pP + 1], dTm_b], [PTm_b_])
                yield 1
                bB = a1.bank()
                for h in range(H):
                    T(lambda e, h=h: e.transpose(out=P16(bB)[:, h * 128:(h + 1) * 128], in_=Am[0][:, h, :], identity=I16),
                      [Am_b[0], c16_b], [ps_b[bB]])
                BP = P16(bB).rearrange("p (a b) -> p a b", a=8)
                if samp:
                    A(lambda e: e.activation(out=Bm[0], in_=BP, func=AF.Copy), [ps_b[bB]], [Bm_b[0]])
                else:
                    A(lambda e: e.activation(out=Bm[1], in_=BP, func=AF.Copy), [ps_b[bB]], [Bm_b[1]])
                    yield 1
                    G(lambda e: e.tensor_tensor(out=Bm[0], in0=Bm[1], in1=bcm(c16[:, 8, :]), op=ALU.mult), [Bm_b[1], c16_b], [Bm_b[0]])
                    G(lambda e: e.tensor_tensor(out=Am[0], in0=Am[0], in1=bcm(c16[:, 8, :]), op=ALU.mult), [Am_b[0], c16_b], [Am_b[0]])
                    G(lambda e: e.tensor_tensor(out=Bo_, in0=Bm[1], in1=bcm(c16[:, 9, :]), op=ALU.mult), [Bm_b[1], c16_b], [Bo_b_])
                yield 1
                V(lambda e: e.tensor_tensor(out=TT, in0=bcm(I32), in1=Bm[0], op=ALU.subtract), [Bm_b[0], c32_b], [TT_b])
                V(lambda e: e.tensor_tensor(out=Pf, in0=bcm(I32), in1=Bm[0], op=ALU.subtract), [Bm_b[0], c32_b], [rstd_b])
                ca = 0
                for lev in range(1, nlev):
                    na = 1 - ca
                    pA = a1.pair()
                    for h in range(H):
                        T(lambda e, h=h: e.matmul(PV(pA)[:, h, :], Bm[ca][:, h, :], Am[ca][:, h, :], start=True, stop=True),
                          [Am_b[ca], Bm_b[ca]], [ps_b[pA + h // 4]])
                    A(lambda e: e.activation(out=Am[na], in_=PV(pA), func=AF.Copy), [ps_b[pA], ps_b[pA + 1]], [Am_b[na]])
                    yield 1
                    if lev < nlev - 1:
                        pB = a1.pair()
                        for h in range(H):
                            T(lambda e, h=h: e.matmul(PV(pB)[:, h, :], Am[ca][:, h, :], Bm[ca][:, h, :], start=True, stop=True),
                              [Am_b[ca], Bm_b[ca]], [ps_b[pB + h // 4]])
                        A(lambda e: e.activation(out=Bm[na], in_=PV(pB), func=AF.Copy), [ps_b[pB], ps_b[pB + 1]], [Bm_b[na]])
                        yield 1
                    pU = a1.pair()
                    for h in range(H):
                        T(lambda e, h=h: e.matmul(PV(pU)[:, h, :], Am[na][:, h, :], TT[:, h, :], start=True, stop=True),
                          [Am_b[na], TT_b], [ps_b[pU + h // 4]])
                    V(lambda e: e.tensor_tensor(out=TT, in0=PV(pU), in1=Pf, op=ALU.add), [ps_b[pU], ps_b[pU + 1], rstd_b], [TT_b])
                    if lev < nlev - 1:
                        V(lambda e: e.tensor_tensor(out=Pf, in0=PV(pU), in1=Pf, op=ALU.add), [ps_b[pU], ps_b[pU + 1], rstd_b], [rstd_b])
                    yield 1
                    ca = na
                yield "S2"
                r32 = on32[:]
                R32b = [on32_b]
                if samp:
                    pK = a2.pair(); pN = a2.pair(); pO = a2.pair()
                    other = [b_ for b_ in range(8) if b_ not in (pK, pK + 1, pN, pN + 1, pO, pO + 1)]
                    V(lambda e: e.tensor_tensor(out=gsel[:], in0=gbt[:, 0:8].unsqueeze(1).to_broadcast([128, 16, 8]),
                                                in1=seq32[:].unsqueeze(2).to_broadcast([128, 16, 8]), op=ALU.mult),
                      [gbt_b, cb32_b], [gsel_b])
                    bE = other[0]
                    T(lambda e: e.matmul(ps[:, bE, 0:128], ONES32, gsel[:].rearrange("p a b -> p (a b)"), start=True, stop=True),
                      [gsel_b, c32_b], [ps_b[bE]])
                    A(lambda e: e.activation(out=egl[:].rearrange("p a b -> p (a b)"), in_=ps[:, bE, 0:128], func=AF.Exp),
                      [ps_b[bE]], [egl_b])
                    for h in range(H):
                        if h % 2 == 0:
                            s0f, s0f_b, s0b, s0b_b = s0f_0, s0f_b0, s0b_0, s0b_b0
                        else:
                            s0f = t32f[:, 0:2048].rearrange("p (a b) -> p a b", a=16); s0f_b = t32_b[0:4]
                            s0b = t16[:, 4096:6144].rearrange("p (a b) -> p a b", a=16); s0b_b = t32_b[4:6]
                        dma("sp", f"s0in{h % 2}", s0f, sd[l, :, h, :, :].rearrange("s k v -> k s v"), [], s0f_b)
                        A(lambda e: e.activation(out=s0b, in_=s0f, func=AF.Copy), s0f_b, s0b_b)
                        V(lambda e, h=h: e.tensor_tensor(out=msk16[:], in0=R1[:, 8 + h, cs].unsqueeze(1).to_broadcast([128, 16, 128]),
                                                         in1=mskc[:], op=ALU.mult), [R1_b[8 + h], cb16_b], [msk16_b])

                        def fks(e, h=h):
                            ins = None
                            for s_ in range(16):
                                ins = e.matmul(PV(pK)[:, h, :], msk16[:, s_, :], s0b[:, s_, :], start=(s_ == 0), stop=(s_ == 15))
                            return ins
                        T(fks, [msk16_b] + s0b_b, [ps_b[pK + h // 4]])
                        V(lambda e, h=h: e.scalar_tensor_tensor(out=r32[:, h, :], in0=PV(pK)[:, h, :], scalar=neg_[:, h:h + 1],
                                                                in1=Vb_[:, h, :], op0=ALU.mult, op1=ALU.add),
                          [ps_b[pK + h // 4], neg_b_, Vb_b_], R32b)
                        V(lambda e, h=h: e.tensor_copy(out=r16[:, h, :], in_=r32[:, h, :]), R32b, [r16_b])
                        T(lambda e, h=h: e.matmul(PV(pN)[:, h, :], TT[:, h, :], r16[:, h, :], start=True, stop=True),
                          [TT_b, r16_b], [ps_b[pN + h // 4]])
                        A(lambda e, h=h: e.activation(out=vn[:, h, :], in_=PV(pN)[:, h, :], func=AF.Copy), [ps_b[pN + h // 4]], [vn_b])
                        V(lambda e, h=h: e.tensor_tensor(out=msk16[:], in0=qt_[:, h, :].unsqueeze(1).to_broadcast([128, 16, 128]),
                                                         in1=mskc[:], op=ALU.mult), [qt_b_, cb16_b], [msk16_b])

                        def fo(e, h=h):
                            for s_ in range(16):
                                e.matmul(PV(pO)[:, h, :], s0b[:, s_, :], msk16[:, s_, :], start=(s_ == 0), stop=False)
                            return e.matmul(PV(pO)[:, h, :], vn[:, h, :], PTm_[:, h, :], start=False, stop=True)
                        T(fo, s0b_b + [msk16_b, vn_b, PTm_b_], [ps_b[pO + h // 4]])
                        V(lambda e, h=h: e.tensor_tensor(out=msk16[:], in0=Kd_[:, h, :].unsqueeze(1).to_broadcast([128, 16, 128]),
                                                         in1=seqs[:].unsqueeze(2).to_broadcast([128, 16, 128]), op=ALU.mult),
                          [Kd_b_, cb16_b], [msk16_b])
                        for q in range(4):
                            bq = other[1 + (h * 4 + q) % (len(other) - 1)]

                            def fs(e, h=h, q=q, bq=bq):
                                ins = None
                                for s4 in range(4):
                                    ins = e.matmul(ps[:, bq, s4 * 128:(s4 + 1) * 128], msk16[:, q * 4 + s4, :], vn[:, h, :],
                                                   start=True, stop=True)
                                return ins
                            T(fs, [msk16_b, vn_b], [ps_b[bq]])
                            for s4 in range(4):
                                s_ = q * 4 + s4
                                V(lambda e, h=h, s_=s_, s4=s4, bq=bq, q=q: e.scalar_tensor_tensor(
                                    out=snew[q % 2][:, s4, :], in0=s0f[:, s_, :], scalar=egl[:, s_, h:h + 1],
                                    in1=ps[:, bq, s4 * 128:(s4 + 1) * 128], op0=ALU.mult, op1=ALU.add),
                                  s0f_b + [egl_b, ps_b[bq]], [snew_b[q % 2]])
                            dma("sp", f"s0out{q % 2}", sd_s[l, q * 4:(q + 1) * 4, h, :, :].rearrange("s k v -> k s v"), snew[q % 2][:],
                                [snew_b[q % 2]], [])
                    pQ = a2.pair()
                else:
                    pK, pN, pO, pS, pQ = 4, 6, 4, 6, 6
                    pKv = PV(pK); pNv = PV(pN)
                    for h in range(H):
                        T(lambda e, h=h: e.matmul(PV(pK)[:, h, :], R1[:, 8 + h, cs], S16[:, l, h, :], start=True, stop=True),
                          [R1_b[8 + h], S16_b[l]], [ps_b[pK + h // 4]])
                    V(lambda e: e.tensor_tensor(out=r32, in0=pKv, in1=bc8(neg_[:]), op=ALU.mult), [ps_b[pK], ps_b[pK + 1], neg_b_], R32b)
                    yield 1
                    V(lambda e: e.tensor_tensor(out=r32, in0=r32, in1=Vb_, op=ALU.add), R32b + [Vb_b_], R32b)
                    A(lambda e: e.activation(out=r16[:], in_=r32, func=AF.Copy), R32b, [r16_b])
                    yield 1
                    for h in range(H):
                        T(lambda e, h=h: e.matmul(PV(pN)[:, h, :], TT[:, h, :], r16[:, h, :], start=True, stop=True),
                          [TT_b, r16_b], [ps_b[pN + h // 4]])
                    A(lambda e: e.activation(out=vn[:], in_=pNv, func=AF.Copy), [ps_b[pN], ps_b[pN + 1]], [vn_b])
                    yield 1
                    for h in range(H):
                        T(lambda e, h=h: e.matmul(PV(pK)[:, h, :], Bo_[:, h, :], vn[:, h, :], start=True, stop=True),
                          [Bo_b_, vn_b], [ps_b[pK + h // 4]])
                    V(lambda e: e.tensor_tensor(out=r16[:], in0=r32, in1=pKv, op=ALU.subtract), R32b + [ps_b[pK], ps_b[pK + 1]], [r16_b])
                    yield 1
                    for h in range(H):
                        T(lambda e, h=h: e.matmul(PV(pN)[:, h, :], TT[:, h, :], r16[:, h, :], start=True, stop=True),
                          [TT_b, r16_b], [ps_b[pN + h // 4]])
                    A(lambda e: e.activation(out=vn[:], in_=pNv, func=AF.Copy), [ps_b[pN], ps_b[pN + 1]], [vn_b])
                    yield 1
                    for h in range(H):
                        def fo(e, h=h):
                            e.matmul(PV(pO)[:, h, :], S16[:, l, h, :], qt_[:, h, :], start=True, stop=False)
                            return e.matmul(PV(pO)[:, h, :], vn[:, h, :], PTm_[:, h, :], start=False, stop=True)
                        T(fo, [S16_b[l], qt_b_, vn_b, PTm_b_], [ps_b[pO + h // 4]])
                    yield 1
                    for h in range(H):
                        T(lambda e, h=h: e.matmul(PV(pS)[:, h, :], Kd_[:, h, :], vn[:, h, :], start=True, stop=True),
                          [Kd_b_, vn_b], [ps_b[pS + h // 4]])
                    A(lambda e: e.activation(out=sq[0][:], in_=P32(pO, 2), func=AF.Square, scale=float(128 ** -0.5)),
                      [ps_b[pO], ps_b[pO + 1]], [sq_b[0]])
                    yield 1
                    for h in range(H):
                        V(lambda e, h=h: e.scalar_tensor_tensor(out=S32[:, l, h, :], in0=S32[:, l, h, :], scalar=es_[:, 16 + h:17 + h],
                                                                in1=PV(pS)[:, h, :], op0=ALU.mult, op1=ALU.add),
                          [S32_b[l], es_b_, ps_b[pS + h // 4]], [S32_b[l]])
                    A(lambda e: e.activation(out=S16[:, l, :, :], in_=S32[:, l, :, :], func=AF.Copy), [S32_b[l]], [S16_b[l]])
                    yield 1
                if samp:
                    A(lambda e: e.activation(out=sq[0][:], in_=P32(pO, 2), func=AF.Square, scale=float(128 ** -0.5)),
                      [ps_b[pO], ps_b[pO + 1]], [sq_b[0]])
                for hh in range(2):
                    T(lambda e, hh=hh: e.matmul(ps[:, pQ + hh, :], ONES16, sq[0][:, hh * 512:(hh + 1) * 512], start=True, stop=True),
                      [sq_b[0], c16_b], [ps_b[pQ + hh]])
                rsqrt_from_psum(rstd9, P32(pQ, 2), 1.0 / 128, 1e-6, [ps_b[pQ], ps_b[pQ + 1]], t32_b[6:8])
                yield 1
                V(lambda e: e.scalar_tensor_tensor(out=on32[:].rearrange("p a b -> p (a b)"), in0=P32(pO, 2), scalar=dnw[:, l:l + 1],
                                                   in1=rstd9, op0=ALU.mult, op1=ALU.mult),
                  [ps_b[pO], ps_b[pO + 1], dnw_b] + t32_b[6:8], [on32_b])
                G(lambda e: e.tensor_tensor(out=oa[:, :, cs], in0=on32[:], in1=sz[:, :, cs], op=ALU.mult),
                  [on32_b] + sz_b, oa_b)
                yield 1

            cur = None
            for g in [chunk_gen(t) for t in range(NT)] + [None]:
                g_done = g is None
                c_done = cur is None
                while not (g_done and c_done):
                    if not c_done:
                        try:
                            next(cur)
                        except StopIteration:
                            c_done = True
                    if not g_done:
                        if next(g) == "S2":
                            g_done = True
                cur = g

            if (not samp) and last_blk:
                dma("sp", "o_sd", sd_p[l].rearrange("h k v -> k h v"), S32[:, l, :, :], [S32_b[l]], [])

            if stop <= 4:
                return
            def uv_consume(ch, bk):
                A(lambda e: e.activation(out=R1[:, ch, 0:W], in_=ps[:, bk, 0:W], func=AF.Gelu), [ps_b[bk]], [R1_b[ch]])
            proj_chunks("w_in", l, 4112, 16, W, hT, hT_b, uv_consume)
            boff = (l * 2 + (1 if samp else 0)) * 1024
            dma("sp", "lnw", t32f[:, 0:2048], lnw[l].rearrange("b d -> (b d)").partition_broadcast(128), [], t32_b[0:4])
            dma("pool", "prw", prw16[:], prow[:, boff:boff + 1024], [], [prw_b])
            lnt16 = t16[:, 4096:6144].rearrange("p (a b) -> p a b", a=2)
            A(lambda e: e.activation(out=t16[:, 4096:6144], in_=t32f[:, 0:2048], func=AF.Copy), t32_b[0:4], t32_b[4:6])
            dma("sp", "wsp", ws32[:], wsp[l, 1 if samp else 0], [], [ws32_b])
            V(lambda e: e.tensor_tensor(out=wsT[:], in0=ws32[:].rearrange("p (a b) -> p a b", a=8), in1=bcm(c32[:, cU, :]),
                                        op=ALU.mult), [ws32_b, c32_b], [wsT_b])
            for t in range(NT):
                cs = slice(t * 128, (t + 1) * 128)
                bk = bank()
                for g in range(8):
                    T(lambda e, g=g: e.transpose(out=P16(bk)[:, g * 128:(g + 1) * 128], in_=R1[:, 8 + g, cs], identity=I16),
                      [vT_b[g], c16_b], [ps_b[bk]])
                for hh in range(2):
                    V(lambda e, hh=hh: e.bn_stats(out=bnst[:, hh, :], in_=P16(bk)[:, hh * 512:(hh + 1) * 512]), [ps_b[bk]], [bn_b])
                V(lambda e: e.bn_aggr(out=mv[:, 0:2], in_=bnst[:].rearrange("p a b -> p (a b)")), [bn_b], [bn_b])
                A(lambda e: e.activation(out=mv[:, 2:3], in_=mv[:, 1:2], func=AF.Ln, bias=1e-5), [bn_b], [bn_b])
                A(lambda e: e.activation(out=mv[:, 2:3], in_=mv[:, 2:3], func=AF.Exp, scale=-0.5), [bn_b], [bn_b])
                V(lambda e: e.tensor_scalar(out=vnb[:], in0=P16(bk), scalar1=mv[:, 0:1], scalar2=mv[:, 2:3],
                                            op0=ALU.subtract, op1=ALU.mult), [ps_b[bk], bn_b], [vnb_b])
                V(lambda e: e.tensor_tensor(out=vnb[:], in0=vnb[:], in1=lnt16[:, 0, :], op=ALU.mult), [vnb_b] + t32_b[4:6], [vnb_b])
                V(lambda e: e.tensor_tensor(out=vnb[:], in0=vnb[:], in1=lnt16[:, 1, :], op=ALU.add), [vnb_b] + t32_b[4:6], [vnb_b])
                if samp or (last_blk and t == NT - 1):
                    A(lambda e: e.activation(out=lnv32, in_=vnb[:], func=AF.Copy), [vnb_b], [lnv32_b])
                    dma("sp", "o_cv", cv_s[l] if samp else cv_p[l], lnv32, [lnv32_b], [])
                pM = pair()
                for g in range(8):
                    def fm(e, g=g):
                        e.matmul(PV(pM)[:, g, :], vnb[:, g * 128:(g + 1) * 128], wsT[:, g, :], start=True, stop=False)
                        return e.matmul(PV(pM)[:, g, :], c16[0:1, 1, :], prw16[0:1, g * 128:(g + 1) * 128],
                                        start=False, stop=True)
                    T(fm, [vnb_b, wsT_b, c16_b, prw_b], [ps_b[pM + g // 4]])
                V(lambda e: e.tensor_tensor(out=R1[:, 0:8, cs], in0=R1[:, 0:8, cs], in1=PV(pM), op=ALU.mult),
                  uT_b + [ps_b[pM], ps_b[pM + 1]], uT_b)

            if stop <= 5:
                return
            mg, mg_b = sz, sz_b
            for half in range(2):
                sA = load_w(wview("w_in", l, 0, 1024, 6160 + half * 512, 512), 8, 512)
                sPA = load_w(wview("w_pa", l, 0, 1024, half * 512, 512), 8, 512)
                for jj in range(4):
                    d = half * 4 + jj
                    b1 = bank()
                    mm_group(b1, 0, W, lambda k, jj=jj: wsl[sA][:, k, jj * 128:(jj + 1) * 128], lambda k: hT[:, k, 0:W], 8,
                             hT_b, [wsl_b[sA]])
                    b2 = bank()
                    mm_group(b2, 0, W, lambda k, jj=jj: wsl[sPA][:, k, jj * 128:(jj + 1) * 128], lambda k: oa[:, k, 0:W], 8,
                             oa_b, [wsl_b[sPA]])
                    A(lambda e, b1=b1: e.activation(out=acc[0][:, 0:W], in_=ps[:, b1, 0:W], func=AF.Sigmoid), [ps_b[b1]], [acc_b[0]])
                    V(lambda e, d=d, b2=b2: e.tensor_tensor(out=t32[:, d, 0:W], in0=acc[0][:, 0:W], in1=ps[:, b2, 0:W], op=ALU.mult),
                      [acc_b[0], ps_b[b2]], [t32_b[d]])
                w_done(2)
            for half in range(2):
                sB = load_w(wview("w_in", l, 0, 1024, 7184 + half * 512, 512), 8, 512)
                sPB = load_w(wview("w_pb", l, 0, 1024, half * 512, 512), 8, 512)
                for jj in range(4):
                    d = half * 4 + jj
                    b1 = bank()
                    mm_group(b1, 0, W, lambda k, jj=jj: wsl[sB][:, k, jj * 128:(jj + 1) * 128], lambda k: hT[:, k, 0:W], 8,
                             hT_b, [wsl_b[sB]])
                    b2 = bank()
                    mm_group(b2, 0, W, lambda k, jj=jj: wsl[sPB][:, k, jj * 128:(jj + 1) * 128], lambda k: R1[:, k, 0:W], 8,
                             uT_b, [wsl_b[sPB]])
                    A(lambda e, b1=b1: e.activation(out=acc[1][:, 0:W], in_=ps[:, b1, 0:W], func=AF.Sigmoid), [ps_b[b1]], [acc_b[1]])
                    V(lambda e, b2=b2: e.tensor_tensor(out=acc[1][:, 0:W], in0=acc[1][:, 0:W], in1=ps[:, b2, 0:W], op=ALU.mult),
                      [acc_b[1], ps_b[b2]], [acc_b[1]])
                    G(lambda e, d=d: e.tensor_tensor(out=mg[:, d, 0:W], in0=acc[1][:, 0:W], in1=t32[:, d, 0:W], op=ALU.add),
                      [acc_b[1], t32_b[d]], [mg_b[d]])
                w_done(2)
            proj_chunks("w_o", l, 0, 8, W, mg, mg_b,
                        lambda ch, bk: A(lambda e: e.activation(out=t32[:, ch, 0:W], in_=ps[:, bk, 0:W], func=AF.Copy),
                                         [ps_b[bk]], [t32_b[ch]]))
            rmsnorm(t32, t32_b, W, NW(l, 1), "x", l)
            if stop <= 6:
                return
            rmsnorm(xT, xT_b, W, NW(l, 2), "h", l)
            hid, hid_b = R1, R1_b
            proj_chunks("w_f1", l, 0, 22, W, hT, hT_b,
                        lambda ch, bk: A(lambda e: e.activation(out=hid[:, ch, 0:W], in_=ps[:, bk, 0:W], func=AF.Silu),
                                         [ps_b[bk]], [hid_b[ch]]))
            proj_chunks("w_f1", l, DFF, 22, W, hT, hT_b,
                        lambda ch, bk: V(lambda e: e.tensor_tensor(out=hid[:, ch, 0:W], in0=hid[:, ch, 0:W], in1=ps[:, bk, 0:W],
                                                                   op=ALU.mult), [hid_b[ch], ps_b[bk]], [hid_b[ch]]))
            for half in range(2):
                bks = [bank() for _ in range(4)]
                for kg, (k0, nk) in enumerate(((0, 8), (8, 8), (16, 6))):
                    s = load_w(wview("w_f2", l, k0 * 128, nk * 128, half * 512, 512), nk, 512)
                    for jj in range(4):
                        def ff(e, s=s, jj=jj, k0=k0, nk=nk, kg=kg):
                            ins = None
                            for k in range(nk):
                                ins = e.matmul(ps[:, bks[jj], 0:W], wsl[s][:, k, jj * 128:(jj + 1) * 128], hid[:, k0 + k, 0:W],
                                               start=(kg == 0 and k == 0), stop=(kg == 2 and k == nk - 1))
                            return ins
                        T(ff, hid_b[k0:k0 + nk] + [wsl_b[s]], [ps_b[bks[jj]]])
                    w_done()
                for jj in range(4):
                    d = half * 4 + jj
                    A(lambda e, d=d, jj=jj: e.activation(out=t32[:, d, 0:W], in_=ps[:, bks[jj], 0:W], func=AF.Copy),
                      [ps_b[bks[jj]]], [t32_b[d]])
            rmsnorm(t32, t32_b, W, NW(l, 3), "x", l)

        def load_x(src, W):
            for t in range(W // 128):
                dma("sp", "xin", xin, src[t * 128:(t + 1) * 128, :], [], t32_b[0:2])
                for hh in range(2):
                    bk = bank()
                    for c4 in range(4):
                        c = hh * 4 + c4
                        T(lambda e, c=c, c4=c4, bk=bk: e.transpose(out=ps[:, bk, c4 * 128:(c4 + 1) * 128], in_=xin[:, c * 128:(c + 1) * 128],
                                                                  identity=I32), t32_b[0:2] + [c32_b], [ps_b[bk]])
                    A(lambda e, hh=hh, bk=bk, t=t: e.activation(out=xT[:, hh * 4:(hh + 1) * 4, t * 128:(t + 1) * 128],
                                                               in_=ps[:, bk, :].rearrange("p (a b) -> p a b", a=4), func=AF.Copy),
                      [ps_b[bk]], xT_b[hh * 4:(hh + 1) * 4])

        def store_y(dst, W):
            for t in range(W // 128):
                yo = t % 2
                for hh in range(2):
                    bk = bank()
                    for c4 in range(4):
                        c = hh * 4 + c4
                        T(lambda e, c=c, c4=c4, bk=bk, t=t: e.transpose(out=ps[:, bk, c4 * 128:(c4 + 1) * 128],
                                                                       in_=xT[:, c, t * 128:(t + 1) * 128], identity=I32),
                          [xT_b[c], c32_b], [ps_b[bk]])
                    A(lambda e, hh=hh, bk=bk, yo=yo: e.activation(out=yout[yo][:, hh * 512:(hh + 1) * 512], in_=ps[:, bk, :],
                                                                  func=AF.Copy), [ps_b[bk]], t32_b[2 + 2 * yo:4 + 2 * yo])
                dma("sp", f"yo{yo}", dst[t * 128:(t + 1) * 128, :], yout[yo], t32_b[2 + 2 * yo:4 + 2 * yo], [])

        G(lambda e: e.memset(S32[:].rearrange("p a b c -> p (a b c)"), 0.0), [], S32_b)
        G(lambda e: e.memset(S16[:].rearrange("p a b c -> p (a b c)"), 0.0), [], S16_b)
        G(lambda e: e.memset(carry[:].rearrange("p a b c -> p (a b c)"), 0.0), [], carry_b[0] + carry_b[1])

        for blk in range(nblk):
            load_x(xp[blk * 512:(blk + 1) * 512, :], 512)
            for l in range(DEPTH):
                E.epoch += 1
                layer(l, 512, False, blk == nblk - 1)
            store_y(y_p[blk * 512:(blk + 1) * 512, :], 512)
        E.epoch += 1
        if do_samp:
            load_x(xs, 128)
            for l in range(DEPTH):
                E.epoch += 1
                layer(l, 128, True, False)
            store_y(y_s, 128)

        E.finalize()
        if os.environ.get("KDBG"):
            print("est_span_us", getattr(E, "est_span", None), "n_ops", len(E.ops), flush=True)
        sems = {}
        for e_ in ENGS:
            for ep in range(E.epoch + 1):
                sems[(e_, ep)] = es.enter_context(nc.semaphore(f"s_{e_}_{ep}"))
        chansems = {ch: es.enter_context(nc.semaphore(f"c_{ch}")) for ch in E.chans}
        block = es.enter_context(nc.Block())
        E.emit(block, sems, chansems)
    return nc


def _pack(inp):
    f = lambda a: np.ascontiguousarray(np.asarray(a, dtype=np.float32))
    pp = np.zeros((128, 262), np.float32)
    names = ["norm_pre_mix", "norm_post_mix", "norm_pre_ffn", "norm_post_ffn"]
    for l in range(DEPTH):
        for n, nm in enumerate(names):
            pp[:, (l * 4 + n) * 8:(l * 4 + n) * 8 + 8] = f(inp[nm])[l].reshape(8, 128).T
        cw = f(inp["conv_w"])[l]
        pp[:, 64 + l * 96:64 + (l + 1) * 96] = cw.reshape(4, 24, 128).transpose(2, 0, 1).reshape(128, 96)
        pp[:, 256 + l] = f(inp["delta_norm_w"])[l]
        pp[0:8, 258 + l] = f(inp["a_log"])[l]
        pp[0:8, 260 + l] = f(inp["dt_bias"])[l]
    bs = f(inp["b_spatial"])
    prow = np.zeros((DEPTH, 2, 8, 128), np.float32)
    prow[:, 0] = bs
    prow[:, 1] = np.tile(bs[:, :, :8], (1, 1, 16))
    prow = prow.reshape(1, -1)
    lnw = np.stack([f(inp["sgu_ln_w"]), f(inp["sgu_ln_b"])], axis=1)
    ws = f(inp["w_spatial"])
    wsT = ws.transpose(0, 3, 1, 2)
    wss = np.tile(ws[:, :, :8, :8], (1, 1, 16, 16)).transpose(0, 3, 1, 2)
    wsp = np.ascontiguousarray(np.stack([wsT, wss], axis=1).reshape(DEPTH, 2, 128, 1024))
    shared = dict(
        w_in=f(inp["w_in"]), w_pa=f(inp["w_proj_a"]), w_pb=f(inp["w_proj_b"]), w_o=f(inp["w_out"]),
        w_f1=f(inp["w_ffn_in"]), w_f2=f(inp["w_ffn_out"]), pp=pp, prow=prow, lnw=np.ascontiguousarray(lnw), wsp=wsp,
        cst=_consts()[0], cst16=_consts()[1], cbd=_cb(),
    )
    xpr = f(inp["x_prompt"]); xsm = f(inp["x_sample"]); sdl = f(inp["state_delta"]); scv = f(inp["state_conv"])
    maps = []
    for c in range(NCORES):
        m = dict(shared)
        m["xp"] = xpr[c]
        m["xs"] = np.ascontiguousarray(xsm[16 * c:16 * (c + 1)].reshape(128, D))
        m["sd"] = np.ascontiguousarray(sdl[:, 16 * c:16 * (c + 1)])
        m["sc"] = np.ascontiguousarray(scv[:, 16 * c:16 * (c + 1)].reshape(DEPTH, 48, 3072))
        maps.append(m)
    return maps


def kernel(**inputs):
    maps = _pack(inputs)
    nc = build()
    res = run_bass_kernel_spmd(nc, maps, core_ids=list(range(NCORES)))
    r = res.results
    y_p = np.stack([r[c]["y_p"] for c in range(NCORES)], 0).reshape(8, 2048, D)
    y_s = np.concatenate([r[c]["y_s"].reshape(16, 8, D) for c in range(NCORES)], 0)
    sd_p = np.stack([r[c]["sd_p"] for c in range(NCORES)], 1)
    sc_p = np.stack([r[c]["sc_p"] for c in range(NCORES)], 1)
    cv_p = np.stack([r[c]["cv_p"] for c in range(NCORES)], 1)
    sd_s = np.concatenate([r[c]["sd_s"] for c in range(NCORES)], 1)
    sc_s = np.concatenate([r[c]["sc_s"] for c in range(NCORES)], 1)
    cv_s = np.concatenate([r[c]["cv_s"].reshape(DEPTH, 16, 8, D) for c in range(NCORES)], 1)
    return tuple(np.ascontiguousarray(a.astype(np.float32)) for a in (y_p, y_s, sd_p, sc_p, cv_p, sd_s, sc_s, cv_s))
```

```python
import os
import numpy as np
from contextlib import ExitStack
import concourse.bass as bass
import concourse.mybir as mybir
from concourse.bass_utils import run_bass_kernel_spmd

F32 = mybir.dt.float32
BF16 = mybir.dt.bfloat16
AF = mybir.ActivationFunctionType
ALU = mybir.AluOpType

NCORES = 8
D = 1024
DEPTH = 2
H = 8
DIN = 8208
DFF = 2816
BIG = 30000.0
NSLOT = 3
ENGS = ["pe", "act", "dve", "pool", "sp"]
BNAME = {"pe": "tensor", "act": "scalar", "dve": "vector", "pool": "gpsimd", "sp": "sync"}
NEPOCH = 12
SCHED = True


class Buf:
    __slots__ = ("name", "lw", "rd")

    def __init__(self, name):
        self.name = name
        self.lw = None
        self.rd = []


def bufs(name, n):
    return [Buf(f"{name}{i}") for i in range(n)]


class Rec:
    def __init__(self):
        self.calls = []

    def __getattr__(self, name):
        def f(*a, **k):
            self.calls.append((name, a, k))
            return self
        return f


class Em:
    def __init__(self):
        self.ops = []
        self.chans = {}
        self.epoch = 0

    def op(self, eng, fn, reads=(), writes=(), chan=None, ndma=0):
        idx = len(self.ops)
        deps = set()
        soft = set()
        for b in reads:
            if b.lw is not None:
                deps.add(b.lw)
        for b in writes:
            if b.lw is not None:
                soft.add(b.lw)
            soft.update(b.rd)
        deps |= soft
        val = None
        if chan:
            c = self.chans.setdefault(chan, {"count": 0, "last": None, "eng": eng})
            assert c["eng"] == eng
            if c["last"] is not None:
                deps.add(c["last"])
            c["count"] += 16 * ndma
            c["last"] = idx
            val = c["count"]
        key = chan if chan else eng
        for b in writes:
            b.lw = idx
            b.rd = []
        for b in reads:
            b.rd.append(idx)
        rec = Rec()
        fn(rec)
        assert rec.calls
        self.ops.append(dict(eng=eng, calls=rec.calls, deps=deps, chan=chan, val=val, inc=False, ep=self.epoch))
        return idx

    @staticmethod
    def _est(o):
        eng = o["eng"]
        tot = 0.0
        for name, a, k in o["calls"]:
            out = k.get("out", a[0] if a else None)
            try:
                fs = float(out.free_size())
            except Exception:
                fs = 128.0
            if o["chan"]:
                try:
                    nb = float(out.nbytes())
                except Exception:
                    nb = 1e5
                tot += 2.0 + nb / 1.8e5
            elif eng == "pe":
                lhs = k.get("lhsT", a[1] if len(a) > 1 else None) if name == "matmul" else k.get("in_")
                f32 = getattr(lhs, "dtype", None) == F32
                tot += 0.045 + max(fs, 64.0) * (4.0 if (f32 and name == "matmul") else 1.0) / 2400.0 + (0.05 if fs <= 128 else 0.0)
            elif eng == "dve":
                tot += 0.12 + fs * 1.1e-3
            elif eng == "act":
                tot += 0.17 + fs * 0.9e-3
            else:
                tot += 0.25 + fs * 2.2e-3
        return tot

    def schedule(self):
        import heapq
        ops = self.ops
        n = len(ops)
        succ = [[] for _ in range(n)]
        indeg = [0] * n
        for i, o in enumerate(ops):
            for d in o["deps"]:
                succ[d].append(i)
                indeg[i] += 1
        dur = [self._est(o) for o in ops]
        bl = [0.0] * n
        for i in range(n - 1, -1, -1):
            m_ = 0.0
            for j in succ[i]:
                if bl[j] > m_:
                    m_ = bl[j]
            bl[i] = dur[i] + 0.3 + m_
        PRIO = os.environ.get("KPRIO", "bl")
        ready = [0.0] * n
        fin = [0.0] * n
        free = {e: 0.0 for e in ENGS}
        heaps = {e: [] for e in ENGS}
        for i in range(n):
            if indeg[i] == 0:
                heapq.heappush(heaps[ops[i]["eng"]], (0.0, i))
        order = []
        LAT = 0.3
        while len(order) < n:
            best = None
            for e in ENGS:
                hp = heaps[e]
                if not hp:
                    continue
                cand = hp[0]
                st = max(free[e], cand[0])
                key = (st, cand[1])
                if best is None or key < best[0]:
                    best = (key, e)
            (st, i), e = best
            hp = heaps[e]
            pool_ = []
            while hp and hp[0][0] <= st:
                pool_.append(heapq.heappop(hp))
            if PRIO == "bl":
                pool_.sort(key=lambda x: (-bl[x[1]], x[1]))
            else:
                pool_.sort(key=lambda x: x[1])
            _, i = pool_[0]
            for x in pool_[1:]:
                heapq.heappush(hp, x)
            o = ops[i]
            fin[i] = st + dur[i]
            free[e] = st + (min(dur[i], 1.0) if o["chan"] else dur[i])
            order.append(i)
            for j in succ[i]:
                r = fin[i] + (LAT if ops[j]["eng"] != e or o["chan"] else 0.1)
                if r > ready[j]:
                    ready[j] = r
                indeg[j] -= 1
                if indeg[j] == 0:
                    heapq.heappush(heaps[ops[j]["eng"]], (ready[j], j))
        pos = {old: new for new, old in enumerate(order)}
        new_ops = []
        for old in order:
            o = ops[old]
            o["deps"] = {pos[d] for d in o["deps"]}
            new_ops.append(o)
        self.ops = new_ops
        self.est_span = max(fin) if fin else 0.0
        if os.environ.get("KDBG"):
            cp = [0.0] * n
            for old in order:
                o = ops[old]
            newdur = [dur[old] for old in order]
            for i_, o in enumerate(new_ops):
                st_ = 0.0
                for d in o["deps"]:
                    st_ = max(st_, cp[d] + LAT)
                cp[i_] = st_ + newdur[i_]
            busy = {}
            for i_, o in enumerate(new_ops):
                busy[o["eng"]] = busy.get(o["eng"], 0.0) + (min(newdur[i_], 1.0) if o["chan"] else newdur[i_])
            print("critical_path_us", max(cp), "busy", {k: round(v) for k, v in busy.items()}, flush=True)
            i_ = max(range(n), key=lambda q: cp[q])
            agg = {}
            seq = []
            while True:
                o = new_ops[i_]
                nm = o["calls"][0][0] + ("/dma" if o["chan"] else "")
                outap = o["calls"][0][2].get("out", o["calls"][0][1][0] if o["calls"][0][1] else None)
                tn = getattr(getattr(outap, "tensor", None), "name", "?")
                key = (o["eng"], nm, tn)
                a_ = agg.setdefault(key, [0, 0.0])
                a_[0] += 1
                a_[1] += newdur[i_] + LAT
                seq.append(key)
                prev = None
                for d in o["deps"]:
                    if prev is None or cp[d] > cp[prev]:
                        prev = d
                if prev is None:
                    break
                i_ = prev
            for k_, v_ in sorted(agg.items(), key=lambda kv: -kv[1][1])[:28]:
                print("   CP", k_, v_[0], round(v_[1], 1), flush=True)

    def prune(self):
        for o in self.ops:
            best = {}
            for d in o["deps"]:
                p = self.ops[d]
                key = ("c", p["chan"]) if p["chan"] else ("e", p["eng"])
                if key not in best or d > best[key]:
                    best[key] = d
            o["deps"] = set(best.values())

    def finalize(self):
        if SCHED:
            self.schedule()
        self.prune()
        for o in self.ops:
            for d in o["deps"]:
                p = self.ops[d]
                if p["chan"] is None:
                    if p["eng"] == "pe" and o["eng"] == "pe" and o["chan"] is None:
                        continue
                    p["inc"] = True
        cnt = {}
        for o in self.ops:
            if o["chan"] is None and o["inc"]:
                k = (o["eng"], o["ep"])
                cnt[k] = cnt.get(k, 0) + 1
                o["val"] = cnt[k]

    def emit(self, block, sems, chansems):
        for e in ENGS:
            ops_e = [o for o in self.ops if o["eng"] == e]

            def body(eng, ops_e=ops_e, e=e):
                seen = {}
                for o in ops_e:
                    for d in sorted(o["deps"]):
                        p = self.ops[d]
                        if p["chan"] is None:
                            if not p["inc"]:
                                continue
                            if p["eng"] == "pe" and e == "pe" and o["chan"] is None:
                                continue
                            key = (p["eng"], p["ep"])
                            sem = sems[key]
                        else:
                            key = p["chan"]
                            sem = chansems[key]
                        v = p["val"]
                        if seen.get(key, 0) >= v:
                            continue
                        seen[key] = v
                        eng.wait_ge(sem, v)
                    ins = None
                    for name, a, k in o["calls"]:
                        ins = getattr(eng, name)(*a, **k)
                        if o["chan"]:
                            ins.then_inc(chansems[o["chan"]], 16)
                    if (not o["chan"]) and o["inc"]:
                        ins.then_inc(sems[(e, o["ep"])], 1)
                for ch, c in self.chans.items():
                    if c["eng"] == e and c["count"] > 0:
                        eng.wait_ge(chansems[ch], c["count"])

            getattr(block, BNAME[e])(body)


def _consts():
    i = np.arange(128)
    m = i[:, None]
    p = i[None, :]
    blk = i // 8
    same = blk[:, None] == blk[None, :]
    c = np.zeros((13, 128, 128), np.float32)
    c[0] = np.eye(128)
    c[1] = 1.0
    c[2] = m <= p
    c[3] = m > p
    c[4] = -BIG * (p >= m)
    c[5] = -BIG * (p < m)
    c[6] = (m <= p) & same
    c[7] = (m > p) & same
    c[8] = -BIG * (~((p < m) & same))
    c[9] = -BIG * (~((p >= m) & same))
    c[10] = same
    s = np.zeros((128, 16, 16), np.float32)
    for j in range(16):
        s[:, j, j] = 1.0
    c[11] = s.reshape(128, 256)[:, :128]
    c[12] = s.reshape(128, 256)[:, 128:]
    c32 = np.ascontiguousarray(c[[0, 1, 2, 3, 6, 7, 10]])
    md = ((m // 64) == (p // 64)).astype(np.float32)
    mo = ((m < 64) & (p >= 64)).astype(np.float32)
    c16 = np.ascontiguousarray(np.concatenate([c[[0, 1, 4, 5, 8, 9, 11, 12]], md[None], mo[None]], 0))
    return c32, c16


def _cb():
    i = np.arange(128)
    maskc = (i[None, :] // 8 == np.arange(16)[:, None]).astype(np.float32)
    maskc = np.broadcast_to(maskc[None], (128, 16, 128)).reshape(128, 2048)
    seqsel = (i[:, None] // 8 == np.arange(16)[None, :]).astype(np.float32)
    return np.concatenate([maskc, seqsel], axis=1).astype(np.float32)


def build(nblk=4, do_samp=True, stop=99):
    reqs = []
    _build(nblk, do_samp, stop, reqs, None)
    return _build(nblk, do_samp, stop, [], reqs)


LOOKAHEAD = 2


def _build(nblk, do_samp, stop, rec_reqs, all_reqs):
    nc = bass.Bass("TRN2", target_bir_lowering=False)

    def din(name, shape):
        return nc.dram_tensor(name, list(shape), F32, kind="ExternalInput").ap()

    def dout(name, shape):
        return nc.dram_tensor(name, list(shape), F32, kind="ExternalOutput").ap()

    xp = din("xp", [2048, D])
    xs = din("xs", [128, D])
    sd = din("sd", [DEPTH, 16, H, 128, 128])
    sc = din("sc", [DEPTH, 48, 3072])
    w_in = din("w_in", [DEPTH, D, DIN])
    w_pa = din("w_pa", [DEPTH, D, D])
    w_pb = din("w_pb", [DEPTH, D, D])
    w_o = din("w_o", [DEPTH, D, D])
    w_f1 = din("w_f1", [DEPTH, D, 2 * DFF])
    w_f2 = din("w_f2", [DEPTH, DFF, D])
    pp = din("pp", [128, 2 * 32 + 2 * 96 + 2 + 4])
    prow = din("prow", [1, DEPTH * 2 * 1024])
    lnw = din("lnw", [DEPTH, 2, D])
    wsp = din("wsp", [DEPTH, 2, 128, 1024])
    cst = din("cst", [7, 128, 128])
    cst16 = din("cst16", [10, 128, 128])
    cbd = din("cbd", [128, 2064])

    NSCR = 2 * 41
    wscr = nc.dram_tensor("wscr", [NSCR, 128, 4096], BF16, kind="Internal").ap()
    wscr_b = bufs("wscr", NSCR)
    y_p = dout("y_p", [2048, D])
    y_s = dout("y_s", [128, D])
    sd_p = dout("sd_p", [DEPTH, H, 128, 128])
    sc_p = dout("sc_p", [DEPTH, 3, 3072])
    cv_p = dout("cv_p", [DEPTH, 128, D])
    sd_s = dout("sd_s", [DEPTH, 16, H, 128, 128])
    sc_s = dout("sc_s", [DEPTH, 16, 3, 3072])
    cv_s = dout("cv_s", [DEPTH, 128, D])

    E = Em()
    es = ExitStack()
    with es:
        def sb(name, shape, dt):
            return es.enter_context(nc.sbuf_tensor(name, shape, dt))

        xT = sb("xT", [128, 8, 512], F32); xT_b = bufs("xT", 8)
        hT = sb("hT", [128, 8, 512], BF16); hT_b = bufs("hT", 8)
        R1 = sb("R1", [128, 24, 512], BF16); R1_b = bufs("R1", 24)
        sz = sb("sz", [128, 8, 512], BF16); sz_b = bufs("sz", 8)
        oa = sb("oa", [128, 8, 512], BF16); oa_b = bufs("oa", 8)
        uT_b = R1_b[0:8]; vT_b = R1_b[8:16]
        t32 = sb("t32", [128, 8, 512], F32); t32_b = bufs("t32", 8)
        pre = [sb(f"pre{i}", [128, 515], F32) for i in range(2)]; pre_b = bufs("pre", 2)
        acc = [sb(f"acc{i}", [128, 512], F32) for i in range(2)]; acc_b = bufs("acc", 2)
        sq = [sb(f"sq{i}", [128, 1024], BF16) for i in range(2)]; sq_b = bufs("sq", 2)
        rstd = sb("rstd", [128, 1024], F32); rstd_b = Buf("rstd")
        carry = sb("carry", [128, DEPTH, 24, 3], F32); carry_b = [bufs(f"car{l}_", 24) for l in range(DEPTH)]
        wsl = [sb(f"wsl{i}", [128, 8, 512], BF16) for i in range(NSLOT)]; wsl_b = bufs("wsl", NSLOT)
        ppt = sb("ppt", [128, 262], F32); ppt_b = Buf("ppt")
        nea = sb("nea", [128, 2], F32); nea_b = Buf("nea")
        dnw = sb("dnw", [128, 2], F32); dnw_b = Buf("dnw")
        c32 = sb("c32", [128, 7, 128], F32); c32_b = Buf("c32")
        c16 = sb("c16", [128, 10, 128], BF16); c16_b = Buf("c16")
        seq32 = sb("seq32", [128, 16], F32); cb32_b = Buf("cb32")
        mskc = sb("mskc", [128, 16, 128], BF16); seqs = sb("seqs", [128, 16], BF16); cb16_b = Buf("cb16")
        selT = sb("selT", [16, 16, 128], BF16); selT_b = Buf("selT")
        prw16 = sb("prw16", [1, 1024], BF16); prw_b = Buf("prw")
        t32f = t32[:].rearrange("p a b -> p (a b)")
        lnt = t32f[:, 0:2048].rearrange("p (a b) -> p a b", a=2)
        xin = t32f[:, 0:1024]
        yout = [t32f[:, 1024:2048], t32f[:, 2048:3072]]
        scrow = t32f[0:3, 0:3072]
        scin = sb("scin", [128, 24, 48], F32); scin_b = Buf("scin")
        wsT = sb("wsT", [128, 8, 128], BF16); wsT_b = Buf("wsT")
        gT = pre[0][0:8, 0:512]; gT_b = pre_b[0]
        bT = pre[1][0:8, 0:512]; bT_b = pre_b[1]
        rn = acc[1][0:16, :]; rn_b = acc_b[1]
        rnb = acc[0][0:16, 0:256].bitcast(BF16); rnb_b = acc_b[0]
        rnl = acc[0][0:16, 256:512].bitcast(BF16)
        S32 = sb("S32", [128, DEPTH, H, 128], F32); S32_b = bufs("S32_", DEPTH)
        S16 = sb("S16", [128, DEPTH, H, 128], BF16); S16_b = bufs("S16_", DEPTH)
        s0f_0 = S32[:].rearrange("p a b c -> p (a b) c"); s0f_b0 = S32_b
        s0b_0 = S16[:].rearrange("p a b c -> p (a b) c"); s0b_b0 = S16_b
        gbt = sb("gbt", [128, 16], F32); gbt_b = Buf("gbt")
        es24 = sb("es24", [128, 24], F32); es_b = Buf("es24")
        kbs = sb("kbs", [128, 8], F32); kbs_b = Buf("kbs")
        Gm1 = sb("Gm1", [128, 8, 128], F32); Gm1_b = Buf("Gm1")
        Gm2 = sb("Gm2", [128, 8, 128], F32); Gm2_b = Buf("Gm2")
        ws32 = Gm1[:].rearrange("p a b -> p (a b)"); ws32_b = Gm1_b
        Kd = sb("Kd", [128, 8, 128], F32); Kd_b = Buf("Kd")
        Vb = sb("Vb", [128, 8, 128], BF16); Vb_b = Buf("Vb")
        Bo = sb("Bo", [128, 8, 128], BF16); Bo_b = Buf("Bo")
        r16 = sb("r16", [128, 8, 128], BF16); r16_b = Buf("r16")
        neg = sb("neg", [128, 8], F32); neg_b = Buf("neg")
        dsm = Gm1; dsm_b = Gm1_b
        dTm = Gm2; dTm_b = Gm2_b
        reg = sb("reg", [128, 8, 128], BF16); reg_b = Buf("reg")
        AmT = sb("AmT", [128, 2, 8, 128], BF16); Am = [AmT[:, 0, :, :], AmT[:, 1, :, :]]; Am_b = bufs("Am", 2)
        BmT = sb("BmT", [128, 2, 8, 128], BF16); Bm = [BmT[:, 0, :, :], BmT[:, 1, :, :]]; Bm_b = bufs("Bm", 2)
        vn32 = AmT[:].rearrange("p a b c -> p (a b c)").bitcast(F32).rearrange("p (b c) -> p b c", c=128)
        r32 = BmT[:].rearrange("p a b c -> p (a b c)").bitcast(F32).rearrange("p (b c) -> p b c", c=128)
        Pm0 = sb("Pm0", [128, 8, 128], BF16); Pm_b0 = Buf("Pm")
        Pf = rstd[:].rearrange("p (b c) -> p b c", c=128)
        PTm = sb("PTm", [128, 8, 128], BF16); PTm_b = Buf("PTm")
        vn = sb("vn", [128, 8, 128], BF16); vn_b = Buf("vn")
        qt = sb("qt", [128, 8, 128], BF16); qt_b = Buf("qt")
        on32 = sb("on32", [128, 8, 128], F32); on32_b = Buf("on32")
        lnv32 = on32[:].rearrange("p a b -> p (a b)"); lnv32_b = on32_b
        vnb = sb("vnb", [128, D], BF16); vnb_b = Buf("vnb")
        bnst = sb("bnst", [128, 2, 6], F32); mv = sb("mv", [128, 4], F32); bn_b = Buf("bn")
        t16 = t32f.bitcast(BF16)

        def t16tile(c):
            return t16[:, c * 1024:(c + 1) * 1024].rearrange("p (a b) -> p a b", a=8)
        KdV = Kd[:].rearrange("p a b -> p (a b)").bitcast(BF16).rearrange("p (s a b) -> p s a b", s=2, a=8)
        Kd2 = [KdV[:, 0, :, :], KdV[:, 1, :, :]]; Kd2_b = bufs("Kd2_", 2)
        Vb2 = [Vb[:], t16tile(0)]; Vb2_b = [Vb_b, t32_b[0]]
        qt2 = [qt[:], t16tile(1)]; qt2_b = [qt_b, t32_b[1]]
        PTm2 = [PTm[:], t16tile(2)]; PTm2_b = [PTm_b, t32_b[2]]
        Pm2 = [Pm0[:], t16tile(3)]; Pm2_b = [Pm_b0, t32_b[3]]
        Bo2 = [Bo[:], t16tile(4)]; Bo2_b = [Bo_b, t32_b[4]]
        neg1 = sb("neg1", [128, 8], F32); neg2 = [neg, neg1]; neg2_b = [neg_b, Buf("neg1")]
        es1 = sb("es1", [128, 24], F32); es2 = [es24, es1]; es2_b = [es_b, Buf("es1")]
        rstd9 = t32f[:, 3072:4096]
        msk16 = sb("msk16", [128, 16, 128], BF16); msk16_b = Buf("msk16")
        egl = sb("egl", [128, 16, 8], F32); egl_b = Buf("egl")
        gsel = sb("gsel", [128, 16, 8], F32); gsel_b = Buf("gsel")
        snew = [sb(f"snew{i}", [128, 4, 128], F32) for i in range(2)]; snew_b = bufs("snew", 2)

        ps = es.enter_context(nc.psum_tensor("ps", [128, 8, 512], F32))
        ps_b = bufs("ps", 8)
        bptr = [0]

        def bank():
            b = bptr[0] % 8
            bptr[0] = (b + 1) % 8
            return b

        def pair():
            b = bptr[0] % 8
            if b % 2:
                b = (b + 1) % 8
            bptr[0] = (b + 2) % 8
            return b

        def P32(i, n=1):
            return ps[:, i:i + n, :].rearrange("p a b -> p (a b)") if n > 1 else ps[:, i, :]

        def P16(i):
            return ps[:, i, :].bitcast(BF16)

        def V(fn, r, w): return E.op("dve", fn, r, w)
        def A(fn, r, w): return E.op("act", fn, r, w)
        def T(fn, r, w): return E.op("pe", fn, r, w)
        def G(fn, r, w): return E.op("pool", fn, r, w)

        def dma(eng, chan, out, in_, r, w):
            return E.op(eng, lambda e: e.dma_start(out=out, in_=in_), r, w, chan=chan, ndma=1)

        slot_i = [0]
        issued = [0]
        WT = {"w_in": w_in, "w_pa": w_pa, "w_pb": w_pb, "w_o": w_o, "w_f1": w_f1, "w_f2": w_f2}

        scr_idx = {}

        def issue_w(k, desc):
            wn, l_, r0, nr, c0, ncol = desc
            s_ = k % NSLOT
            kc_ = nr // 128
            dst = wsl[s_][:, 0:kc_, 0:ncol]
            if desc not in scr_idx:
                i_ = len(scr_idx)
                scr_idx[desc] = i_
                src = WT[wn][l_, r0:r0 + nr, c0:c0 + ncol].rearrange("(c p) n -> p c n", p=128)
                dma("pool", f"w{s_}", dst, src, [], [wsl_b[s_]])
                sv = wscr[i_, :, 0:kc_ * ncol].rearrange("p (c n) -> p c n", c=kc_)
                dma("sp", f"scrw{i_ % 2}", sv, dst, [wsl_b[s_]], [wscr_b[i_]])
            else:
                i_ = scr_idx[desc]
                sv = wscr[i_, :, 0:kc_ * ncol].rearrange("p (c n) -> p c n", c=kc_)
                dma("sp", f"v{s_}", dst, sv, [wscr_b[i_]], [wsl_b[s_]])

        def load_w(desc, kc=None, ncols=None):
            k = slot_i[0]
            slot_i[0] += 1
            rec_reqs.append(desc)
            if all_reqs is None:
                issue_w(k, desc)
            else:
                assert all_reqs[k] == desc
                pump()
                assert issued[0] > k, "weight slot ring exhausted"
            return k % NSLOT

        wdone = [0]

        def pump():
            while issued[0] < len(all_reqs) and issued[0] - NSLOT < wdone[0]:
                issue_w(issued[0], all_reqs[issued[0]])
                issued[0] += 1

        def w_done(n=1):
            wdone[0] += n
            if all_reqs is not None:
                pump()

        def wview(wt, l, r0, nr, c0, ncol):
            return (wt, l, r0, nr, c0, ncol)

        I32 = c32[:, 0, :]; I16 = c16[:, 0, :]; ONES16 = c16[:, 1, :]; ONES32 = c32[:, 1, :]

        def mm_group(bk, col0, ncol, lhs_fn, rhs_fn, nk, rbufs, extra_r=(), mp=128):
            def fn(e):
                ins = None
                for k in range(nk):
                    ins = e.matmul(ps[0:mp, bk, col0:col0 + ncol], lhs_fn(k), rhs_fn(k),
                                   start=(k == 0), stop=(k == nk - 1))
                return ins
            return T(fn, list(rbufs) + list(extra_r), [ps_b[bk]])

        dma("sp", "par", ppt[:], pp, [], [ppt_b])
        dma("sp", "par", c32[:], cst.rearrange("k p n -> p k n"), [], [c32_b])
        dma("sp", "par", seq32[:], cbd[:, 2048:2064], [], [cb32_b])
        for hf in range(2):
            dma("pool", "cst16", mskc[:, hf * 8:(hf + 1) * 8, :].rearrange("p a b -> p (a b)"), cbd[:, hf * 1024:(hf + 1) * 1024], [], [cb16_b])
        dma("pool", "cst16", c16[:], cst16.rearrange("k p n -> p k n"), [], [c16_b])
        V(lambda e: e.tensor_copy(out=seqs[:], in_=seq32[:]), [cb32_b, cb16_b], [cb16_b])
        V(lambda e: e.tensor_copy(out=selT[:], in_=c32[0:16, 0, 0:16].unsqueeze(2).to_broadcast([16, 16, 128])),
          [c32_b], [selT_b])
        def NW(l, n): return ppt[:, (l * 4 + n) * 8:(l * 4 + n) * 8 + 8]
        def CW(l, tap, ch): return ppt[:, 64 + l * 96 + tap * 24 + ch:64 + l * 96 + tap * 24 + ch + 1]
        A(lambda e: e.activation(out=nea[0:8, :], in_=ppt[0:8, 258:260], func=AF.Exp), [ppt_b], [nea_b])
        V(lambda e: e.tensor_scalar(out=nea[0:8, :], in0=nea[0:8, :], scalar1=-1.0, scalar2=None, op0=ALU.mult), [nea_b], [nea_b])
        V(lambda e: e.tensor_scalar(out=dnw[:], in0=ppt[:, 256:258], scalar1=float(128 ** -0.5), scalar2=None, op0=ALU.mult),
          [ppt_b], [dnw_b])
        sel16 = c16[:, 6:8, :].rearrange("p a b -> p (a b)")

        def rsqrt_from_psum(out_ap, in_ap, scale, eps, r, w):
            A(lambda e: e.activation(out=out_ap, in_=in_ap, func=AF.Ln, scale=scale, bias=eps), r, w)
            A(lambda e: e.activation(out=out_ap, in_=out_ap, func=AF.Exp, scale=-0.5), w, w)

        def rmsnorm(src, src_b, W, gain, mode, l):
            bk = bank()
            for c in range(8):
                s = c % 2
                A(lambda e, c=c, s=s: e.activation(out=sq[s][:, 0:W], in_=src[:, c, 0:W], func=AF.Square),
                  [src_b[c]], [sq_b[s]])
                T(lambda e, c=c, s=s: e.matmul(ps[:, bk, 0:W], ONES16, sq[s][:, 0:W], start=(c == 0), stop=(c == 7)),
                  [sq_b[s], c16_b], [ps_b[bk]])
            rsqrt_from_psum(rstd[:, 0:W], ps[:, bk, 0:W], 1.0 / D, 1e-6, [ps_b[bk]], [rstd_b])
            for c in range(8):
                if mode == "h":
                    V(lambda e, c=c: e.scalar_tensor_tensor(out=hT[:, c, 0:W], in0=src[:, c, 0:W], scalar=gain[:, c:c + 1],
                                                            in1=rstd[:, 0:W], op0=ALU.mult, op1=ALU.mult),
                      [src_b[c], rstd_b, ppt_b], [hT_b[c]])
                else:
                    V(lambda e, c=c: e.scalar_tensor_tensor(out=src[:, c, 0:W], in0=src[:, c, 0:W], scalar=gain[:, c:c + 1],
                                                            in1=rstd[:, 0:W], op0=ALU.mult, op1=ALU.mult),
                      [src_b[c], rstd_b, ppt_b], [src_b[c]])
                    G(lambda e, c=c: e.tensor_tensor(out=xT[:, c, 0:W], in0=xT[:, c, 0:W], in1=src[:, c, 0:W], op=ALU.add),
                      [src_b[c], xT_b[c]], [xT_b[c]])

        def proj_chunks(wt, l, c0, nchunks, W, rhs, rhs_b, consume, kc=8, r0=0):
            j = 0
            pend = [None]
            while j < nchunks:
                nj = min(4, nchunks - j)
                s = load_w(wview(wt, l, r0, kc * 128, c0 + j * 128, nj * 128), kc, nj * 128)
                for jj in range(nj):
                    bk = bank()
                    mm_group(bk, 0, W, lambda k, s=s, jj=jj: wsl[s][:, k, jj * 128:(jj + 1) * 128],
                             lambda k: rhs[:, k, 0:W], kc, rhs_b, [wsl_b[s]])
                    d_ = consume(j + jj, bk)
                    if pend[0] is not None:
                        pend[0]()
                    pend[0] = d_ if callable(d_) else None
                w_done()
                j += nj
            if pend[0] is not None:
                pend[0]()

        def layer(l, W, samp, last_blk):
            NT = W // 128
            if stop <= 0:
                return
            rmsnorm(xT, xT_b, W, NW(l, 0), "h", l)
            if stop <= 1:
                return
            if samp:
                tin = t32f
                dma("sp", "scin", tin[0:48, 0:3072], sc[l], [], t32_b[0:6])
                for q4 in range(6):
                    bk = bank()
                    for cc in range(4):
                        ch = q4 * 4 + cc
                        T(lambda e, ch=ch, cc=cc, bk=bk: e.transpose(out=ps[:, bk, cc * 48:(cc + 1) * 48], in_=tin[0:48, ch * 128:(ch + 1) * 128],
                                                                    identity=c32[0:48, 0, 0:48]), t32_b[0:6] + [c32_b], [ps_b[bk]])
                    V(lambda e, q4=q4, bk=bk: e.tensor_copy(out=scin[:, q4 * 4:(q4 + 1) * 4, :],
                                                            in_=ps[:, bk, 0:192].rearrange("p (a b) -> p a b", a=4)),
                      [ps_b[bk]], [scin_b])
            tok_b = t32_b
            if samp:
                tokm = t32f[:, 0:3072]

            def qkv_consume(ch, bk):
                s = ch % 2
                if samp:
                    pv = pre[s][:, 0:176].rearrange("p (a b) -> p a b", a=16)
                    G(lambda e: e.tensor_copy(out=pv[:, :, 0:3], in_=scin[:, ch, :].rearrange("p (a b) -> p a b", a=16)),
                      [scin_b], [pre_b[s]])
                    A(lambda e: e.activation(out=pv[:, :, 3:11], in_=ps[:, bk, 0:128].rearrange("p (a b) -> p a b", a=16),
                                             func=AF.Copy), [ps_b[bk]], [pre_b[s]])
                    av = acc[s][:, 0:128].rearrange("p (a b) -> p a b", a=16)
                    A(lambda e: e.activation(out=acc[s][:, 0:128], in_=ps[:, bk, 0:128], func=AF.Copy, scale=CW(l, 3, ch)),
                      [ps_b[bk], ppt_b], [acc_b[s]])
                    for tap in (2, 1, 0):
                        V(lambda e, tap=tap: e.scalar_tensor_tensor(out=av, in0=pv[:, :, tap:tap + 8], scalar=CW(l, tap, ch),
                                                                    in1=av, op0=ALU.mult, op1=ALU.add),
                          [pre_b[s], acc_b[s], ppt_b], [acc_b[s]])
                    G(lambda e: e.tensor_copy(out=sq[s][:, 0:256].bitcast(F32).rearrange("p (a b) -> p a b", a=16),
                                              in_=pv[:, :, 3:11]), [pre_b[s]], [sq_b[s]])
                    b2 = bank()
                    T(lambda e: e.transpose(out=ps[:, b2, 0:128], in_=sq[s][:, 0:256].bitcast(F32), identity=I32),
                      [sq_b[s], c32_b], [ps_b[b2]])
                    V(lambda e: e.tensor_copy(out=tokm[:, ch * 128:(ch + 1) * 128], in_=ps[:, b2, 0:128]),
                      [ps_b[b2]], [tok_b[ch // 4]])
                else:
                    G(lambda e: e.tensor_copy(out=pre[s][:, 0:3], in_=carry[:, l, ch, :]), [carry_b[l][ch]], [pre_b[s]])
                    A(lambda e: e.activation(out=pre[s][:, 3:515], in_=ps[:, bk, :], func=AF.Copy), [ps_b[bk]], [pre_b[s]])
                    A(lambda e: e.activation(out=acc[s][:], in_=ps[:, bk, :], func=AF.Copy, scale=CW(l, 3, ch)),
                      [ps_b[bk], ppt_b], [acc_b[s]])
                    G(lambda e: e.tensor_copy(out=carry[:, l, ch, :], in_=pre[s][:, 512:515]), [pre_b[s]], [carry_b[l][ch]])
                    for tap in (2, 1, 0):
                        V(lambda e, tap=tap: e.scalar_tensor_tensor(out=acc[s][:], in0=pre[s][:, tap:tap + 512],
                                                                    scalar=CW(l, tap, ch), in1=acc[s][:],
                                                                    op0=ALU.mult, op1=ALU.add),
                          [pre_b[s], acc_b[s], ppt_b], [acc_b[s]])
                return lambda: A(lambda e: e.activation(out=R1[:, ch, 0:W], in_=acc[s][:, 0:W], func=AF.Silu), [acc_b[s]], [R1_b[ch]])

            proj_chunks("w_in", l, 0, 24, W, hT, hT_b, qkv_consume)
            if samp:
                for j in range(3):
                    dma("sp", "o_sc", sc_s[l, :, j, :], tokm[5 + j:128:8, :], tok_b[0:6], [])
            elif last_blk:
                for q4 in range(6):
                    bk = bank()
                    for cc in range(4):
                        ch = q4 * 4 + cc
                        T(lambda e, ch=ch, cc=cc: e.transpose(out=ps[0:3, bk, cc * 128:(cc + 1) * 128], in_=carry[:, l, ch, :],
                                                              identity=I32), [carry_b[l][ch], c32_b], [ps_b[bk]])
                    V(lambda e, q4=q4: e.tensor_copy(out=scrow[:, q4 * 512:(q4 + 1) * 512], in_=ps[0:3, bk, :]),
                      [ps_b[bk]], t32_b[0:6])
                dma("sp", "o_sc", sc_p[l], scrow, t32_b[0:6], [])

            if stop <= 2:
                return
            proj_chunks("w_in", l, 3072, 8, W, hT, hT_b,
                        lambda ch, bk: A(lambda e: e.activation(out=sz[:, ch, 0:W], in_=ps[:, bk, 0:W], func=AF.Silu),
                                         [ps_b[bk]], [sz_b[ch]]))
            s = load_w(wview("w_in", l, 0, 1024, 4096, 16), 8, 16)
            ba = bank()
            mm_group(ba, 0, W, lambda k: wsl[s][:, k, 0:8], lambda k: hT[:, k, 0:W], 8, hT_b, [wsl_b[s]], mp=8)
            bb = bank()
            mm_group(bb, 0, W, lambda k: wsl[s][:, k, 8:16], lambda k: hT[:, k, 0:W], 8, hT_b, [wsl_b[s]], mp=8)
            w_done()
            A(lambda e: e.activation(out=gT[:, 0:W], in_=ps[0:8, ba, 0:W], func=AF.Exp, bias=ppt[0:8, 260 + l:261 + l]),
              [ps_b[ba], ppt_b], [gT_b])
            A(lambda e: e.activation(out=gT[:, 0:W], in_=gT[:, 0:W], func=AF.Ln, bias=1.0), [gT_b], [gT_b])
            V(lambda e: e.tensor_scalar(out=gT[:, 0:W], in0=gT[:, 0:W], scalar1=nea[0:8, l:l + 1], scalar2=None, op0=ALU.mult),
              [gT_b, nea_b], [gT_b])
            A(lambda e: e.activation(out=bT[:, 0:W], in_=ps[0:8, bb, 0:W], func=AF.Sigmoid), [ps_b[bb]], [bT_b])
            bn_ = bank()
            for j in range(16):
                s2 = j % 2
                A(lambda e, j=j, s2=s2: e.activation(out=sq[s2][:, 0:W], in_=R1[:, j, 0:W], func=AF.Square),
                  [R1_b[j]], [sq_b[s2]])
                T(lambda e, j=j, s2=s2: e.matmul(ps[0:16, bn_, 0:W], sel16[:, j * 16:(j + 1) * 16], sq[s2][:, 0:W],
                                                 start=(j == 0), stop=(j == 15)), [sq_b[s2], c16_b], [ps_b[bn_]])
            rsqrt_from_psum(rn[:, 0:W], ps[0:16, bn_, 0:W], 1.0, 1e-6, [ps_b[bn_]], [rn_b])
            V(lambda e: e.tensor_copy(out=rnb[:, 0:W], in_=rn[:, 0:W]), [rn_b], [rnb_b])
            V(lambda e: e.tensor_tensor(out=rnl[:, 0:W], in0=rn[:, 0:W], in1=rnb[:, 0:W], op=ALU.subtract), [rn_b, rnb_b], [rnb_b])
            for j in range(16):
                bk = bank()
                def fbc(e, j=j, bk=bk):
                    e.matmul(ps[:, bk, 0:W], selT[:, j, :], rnb[:, 0:W], start=True, stop=False)
                    return e.matmul(ps[:, bk, 0:W], selT[:, j, :], rnl[:, 0:W], start=False, stop=True)
                T(fbc, [rnb_b, selT_b], [ps_b[bk]])
                V(lambda e, j=j, bk=bk: e.tensor_tensor(out=R1[:, j, 0:W], in0=R1[:, j, 0:W], in1=ps[:, bk, 0:W], op=ALU.mult),
                  [R1_b[j], ps_b[bk]], [R1_b[j]])

            if stop <= 3:
                return
            cU, cSL, cN1, cN2 = (4, 5, 4, 5) if samp else (2, 3, 2, 3)
            nlev = 3 if samp else 6

            def bc8(ap):
                return ap.unsqueeze(2).to_broadcast([128, 8, 128])

            def bcm(ap):
                return ap.unsqueeze(1).to_broadcast([128, 8, 128])

            def PV(i):
                return ps[:, i:i + 2, :].rearrange("p a (b c) -> p (a b) c", c=128)

            class Alloc:
                def __init__(self, lo, n):
                    self.lo, self.n, self.p = lo, n, 0

                def bank(self):
                    b_ = self.p % self.n
                    self.p = (b_ + 1) % self.n
                    return self.lo + b_

                def pair(self):
                    b_ = self.p % self.n
                    if b_ % 2:
                        b_ = (b_ + 1) % self.n
                    self.p = (b_ + 2) % self.n
                    return self.lo + b_

            a1 = Alloc(0, 4)
            a2 = Alloc(0, 8)
            qb = R1_b[0:8]

            def chunk_gen(t):
                cs = slice(t * 128, (t + 1) * 128)
                st = 0 if samp else t % 2
                Kd_, Kd_b_ = Kd2[st], Kd2_b[st]
                Vb_, Vb_b_ = Vb2[st], Vb2_b[st]
                qt_, qt_b_ = qt2[st], qt2_b[st]
                PTm_, PTm_b_ = PTm2[st], PTm2_b[st]
                TT, TT_b = Pm2[st], Pm2_b[st]
                Bo_, Bo_b_ = Bo2[st], Bo2_b[st]
                neg_, neg_b_ = neg2[st], neg2_b[st]
                es_, es_b_ = es2[st], es2_b[st]
                bK = a1.bank()
                for h in range(H):
                    T(lambda e, h=h: e.transpose(out=P16(bK)[:, h * 128:(h + 1) * 128], in_=R1[:, 8 + h, cs], identity=I16),
                      [R1_b[8 + h], c16_b], [ps_b[bK]])
                bV = a1.bank()
                for h in range(H):
                    T(lambda e, h=h: e.transpose(out=P16(bV)[:, h * 128:(h + 1) * 128], in_=R1[:, 16 + h, cs], identity=I16),
                      [R1_b[16 + h], c16_b], [ps_b[bV]])
                bG = a1.bank()
                T(lambda e: e.transpose(out=ps[:, bG, 0:8], in_=gT[:, cs], identity=c32[0:8, 0, 0:8]), [gT_b, c32_b], [ps_b[bG]])
                T(lambda e: e.transpose(out=ps[:, bG, 8:16], in_=bT[:, cs], identity=c32[0:8, 0, 0:8]), [bT_b, c32_b], [ps_b[bG]])
                V(lambda e: e.tensor_copy(out=gbt[:], in_=ps[:, bG, 0:16]), [ps_b[bG]], [gbt_b])
                yield 1
                bS = a1.bank()
                cLast = 6 if samp else 1
                for i3, ci in enumerate((cU, cSL, cLast)):
                    T(lambda e, i3=i3, ci=ci: e.matmul(ps[:, bS, i3 * 8:(i3 + 1) * 8], c32[:, ci, :], gbt[:, 0:8],
                                                       start=True, stop=True), [gbt_b, c32_b], [ps_b[bS]])
                A(lambda e: e.activation(out=es_[:], in_=ps[:, bS, 0:24], func=AF.Exp), [ps_b[bS]], [es_b_])
                V(lambda e: e.tensor_tensor(out=kbs[:], in0=gbt[:, 8:16], in1=es_[:, 0:8], op=ALU.mult), [gbt_b, es_b_], [kbs_b])
                yield 1
                KP = P16(bK).rearrange("p (a b) -> p a b", a=8)
                VP = P16(bV).rearrange("p (a b) -> p a b", a=8)
                V(lambda e: e.tensor_tensor(out=Kd_, in0=KP, in1=bc8(es_[:, 8:16]), op=ALU.mult), [ps_b[bK], es_b_], [Kd_b_])
                V(lambda e: e.tensor_tensor(out=Vb_, in0=VP, in1=bc8(gbt[:, 8:16]), op=ALU.mult), [ps_b[bV], gbt_b], [Vb_b_])
                V(lambda e: e.tensor_scalar(out=neg_[:], in0=kbs[:], scalar1=-1.0, scalar2=None, op0=ALU.mult), [kbs_b], [neg_b_])
                yield 1
                G(lambda e: e.tensor_tensor(out=Gm1[:], in0=bcm(c32[:, cSL, :]), in1=bc8(gbt[:, 0:8]), op=ALU.mult),
                  [c32_b, gbt_b], [Gm1_b])
                V(lambda e: e.tensor_tensor(out=Gm2[:], in0=bcm(c32[:, cU, :]), in1=bc8(gbt[:, 0:8]), op=ALU.mult),
                  [c32_b, gbt_b], [Gm2_b])
                yield 1
                pD = a1.pair()
                pT = a1.pair()
                for hh in range(2):
                    g1 = Gm1[:, hh * 4:(hh + 1) * 4, :].rearrange("p a b -> p (a b)")
                    g2 = Gm2[:, hh * 4:(hh + 1) * 4, :].rearrange("p a b -> p (a b)")

                    def fD(e, hh=hh, g1=g1):
                        e.matmul(ps[:, pD + hh, :], c32[:, cU, :], g1, start=True, stop=False)
                        ins = None
                        for q in range(4):
                            ins = e.matmul(ps[:, pD + hh, q * 128:(q + 1) * 128], I16, c16[:, cN1, :], start=False, stop=(q == 3))
                        return ins
                    T(fD, [Gm1_b, c32_b, c16_b], [ps_b[pD + hh]])

                    def fT(e, hh=hh, g2=g2):
                        e.matmul(ps[:, pT + hh, :], c32[:, cSL, :], g2, start=True, stop=False)
                        ins = None
                        for q in range(4):
                            ins = e.matmul(ps[:, pT + hh, q * 128:(q + 1) * 128], I16, c16[:, cN2, :], start=False, stop=(q == 3))
                        return ins
                    T(fT, [Gm2_b, c32_b, c16_b], [ps_b[pT + hh]])
                yield 1
                A(lambda e: e.activation(out=dsm[:].rearrange("p a b -> p (a b)"), in_=P32(pD, 2), func=AF.Exp),
                  [ps_b[pD], ps_b[pD + 1]], [dsm_b])
                pR = a1.pair()
                for hh in range(2):
                    g2 = Gm2[:, hh * 4:(hh + 1) * 4, :].rearrange("p a b -> p (a b)")
                    T(lambda e, hh=hh, g2=g2: e.matmul(ps[:, pR + hh, :], ONES32, g2, start=True, stop=True),
                      [Gm2_b, c32_b], [ps_b[pR + hh]])
                A(lambda e: e.activation(out=dTm[:].rearrange("p a b -> p (a b)"), in_=P32(pT, 2), func=AF.Exp),
                  [ps_b[pT], ps_b[pT + 1]], [dTm_b])
                A(lambda e: e.activation(out=reg[:].rearrange("p a b -> p (a b)"), in_=P32(pR, 2), func=AF.Exp),
                  [ps_b[pR], ps_b[pR + 1]], [reg_b])
                yield 1
                V(lambda e: e.tensor_tensor(out=qt_, in0=R1[:, 0:8, cs], in1=reg[:], op=ALU.mult), qb + [reg_b], [qt_b_])
                pG = a1.pair()
                pP = a1.pair()
                for h in range(H):
                    T(lambda e, h=h: e.matmul(PV(pG)[:, h, :], R1[:, 8 + h, cs], R1[:, 8 + h, cs], start=True, stop=True),
                      [R1_b[8 + h]], [ps_b[pG + h // 4]])
                for h in range(H):
                    T(lambda e, h=h: e.matmul(PV(pP)[:, h, :], R1[:, 8 + h, cs], R1[:, h, cs], start=True, stop=True),
                      [R1_b[8 + h], R1_b[h]], [ps_b[pP + h // 4]])
                yield 1
                for h in range(H):
                    V(lambda e, h=h: e.scalar_tensor_tensor(out=Am[0][:, h, :], in0=PV(pG)[:, h, :], scalar=gbt[:, 8 + h:9 + h],
                                                            in1=dsm[:, h, :], op0=ALU.mult, op1=ALU.mult),
                      [ps_b[pG + h // 4], gbt_b, dsm_b], [Am_b[0]])
                V(lambda e: e.tensor_tensor(out=PTm_, in0=PV(pP), in1=dTm[:], op=ALU.mult),
                  [ps_b[pP], ps_b[pP + 1], dTm_b], [PTm_b_])
                yield 1
                bB = a1.bank()
                for h in range(H):
                    T(lambda e, h=h: e.transpose(out=P16(bB)[:, h * 128:(h + 1) * 128], in_=Am[0][:, h, :], identity=I16),
                      [Am_b[0], c16_b], [ps_b[bB]])
                BP = P16(bB).rearrange("p (a b) -> p a b", a=8)
                if samp:
                    A(lambda e: e.activation(out=Bm[0], in_=BP, func=AF.Copy), [ps_b[bB]], [Bm_b[0]])
                else:
                    A(lambda e: e.activation(out=Bm[1], in_=BP, func=AF.Copy), [ps_b[bB]], [Bm_b[1]])
                    yield 1
                    G(lambda e: e.tensor_tensor(out=Bm[0], in0=Bm[1], in1=bcm(c16[:, 8, :]), op=ALU.mult), [Bm_b[1], c16_b], [Bm_b[0]])
                    G(lambda e: e.tensor_tensor(out=Am[0], in0=Am[0], in1=bcm(c16[:, 8, :]), op=ALU.mult), [Am_b[0], c16_b], [Am_b[0]])
                    G(lambda e: e.tensor_tensor(out=Bo_, in0=Bm[1], in1=bcm(c16[:, 9, :]), op=ALU.mult), [Bm_b[1], c16_b], [Bo_b_])
                yield 1
                V(lambda e: e.tensor_tensor(out=TT, in0=bcm(I32), in1=Bm[0], op=ALU.subtract), [Bm_b[0], c32_b], [TT_b])
                V(lambda e: e.tensor_tensor(out=Pf, in0=bcm(I32), in1=Bm[0], op=ALU.subtract), [Bm_b[0], c32_b], [rstd_b])
                ca = 0
                for lev in range(1, nlev):
                    na = 1 - ca
                    pA = a1.pair()
                    for h in range(H):
                        T(lambda e, h=h: e.matmul(PV(pA)[:, h, :], Bm[ca][:, h, :], Am[ca][:, h, :], start=True, stop=True),
                          [Am_b[ca], Bm_b[ca]], [ps_b[pA + h // 4]])
                    A(lambda e: e.activation(out=Am[na], in_=PV(pA), func=AF.Copy), [ps_b[pA], ps_b[pA + 1]], [Am_b[na]])
                    yield 1
                    if lev < nlev - 1:
                        pB = a1.pair()
                        for h in range(H):
                            T(lambda e, h=h: e.matmul(PV(pB)[:, h, :], Am[ca][:, h, :], Bm[ca][:, h, :], start=True, stop=True),
                              [Am_b[ca], Bm_b[ca]], [ps_b[pB + h // 4]])
                        A(lambda e: e.activation(out=Bm[na], in_=PV(pB), func=AF.Copy), [ps_b[pB], ps_b[pB + 1]], [Bm_b[na]])
                        yield 1
                    pU = a1.pair()
                    for h in range(H):
                        T(lambda e, h=h: e.matmul(PV(pU)[:, h, :], Am[na][:, h, :], TT[:, h, :], start=True, stop=True),
                          [Am_b[na], TT_b], [ps_b[pU + h // 4]])
                    V(lambda e: e.tensor_tensor(out=TT, in0=PV(pU), in1=Pf, op=ALU.add), [ps_b[pU], ps_b[pU + 1], rstd_b], [TT_b])
                    if lev < nlev - 1:
                        V(lambda e: e.tensor_tensor(out=Pf, in0=PV(pU), in1=Pf, op=ALU.add), [ps_b[pU], ps_b[pU + 1], rstd_b], [rstd_b])
                    yield 1
                    ca = na
                yield "S2"
                r32 = on32[:]
                R32b = [on32_b]
                if samp:
                    pK = a2.pair(); pN = a2.pair(); pO = a2.pair()
                    other = [b_ for b_ in range(8) if b_ not in (pK, pK + 1, pN, pN + 1, pO, pO + 1)]
                    V(lambda e: e.tensor_tensor(out=gsel[:], in0=gbt[:, 0:8].unsqueeze(1).to_broadcast([128, 16, 8]),
                                                in1=seq32[:].unsqueeze(2).to_broadcast([128, 16, 8]), op=ALU.mult),
                      [gbt_b, cb32_b], [gsel_b])
                    bE = other[0]
                    T(lambda e: e.matmul(ps[:, bE, 0:128], ONES32, gsel[:].rearrange("p a b -> p (a b)"), start=True, stop=True),
                      [gsel_b, c32_b], [ps_b[bE]])
                    A(lambda e: e.activation(out=egl[:].rearrange("p a b -> p (a b)"), in_=ps[:, bE, 0:128], func=AF.Exp),
                      [ps_b[bE]], [egl_b])
                    for h in range(H):
                        if h % 2 == 0:
                            s0f, s0f_b, s0b, s0b_b = s0f_0, s0f_b0, s0b_0, s0b_b0
                        else:
                            s0f = t32f[:, 0:2048].rearrange("p (a b) -> p a b", a=16); s0f_b = t32_b[0:4]
                            s0b = t16[:, 4096:6144].rearrange("p (a b) -> p a b", a=16); s0b_b = t32_b[4:6]
                        dma("sp", f"s0in{h % 2}", s0f, sd[l, :, h, :, :].rearrange("s k v -> k s v"), [], s0f_b)
                        A(lambda e: e.activation(out=s0b, in_=s0f, func=AF.Copy), s0f_b, s0b_b)
                        V(lambda e, h=h: e.tensor_tensor(out=msk16[:], in0=R1[:, 8 + h, cs].unsqueeze(1).to_broadcast([128, 16, 128]),
                                                         in1=mskc[:], op=ALU.mult), [R1_b[8 + h], cb16_b], [msk16_b])

                        def fks(e, h=h):
                            ins = None
                            for s_ in range(16):
                                ins = e.matmul(PV(pK)[:, h, :], msk16[:, s_, :], s0b[:, s_, :], start=(s_ == 0), stop=(s_ == 15))
                            return ins
                        T(fks, [msk16_b] + s0b_b, [ps_b[pK + h // 4]])
                        V(lambda e, h=h: e.scalar_tensor_tensor(out=r32[:, h, :], in0=PV(pK)[:, h, :], scalar=neg_[:, h:h + 1],
                                                                in1=Vb_[:, h, :], op0=ALU.mult, op1=ALU.add),
                          [ps_b[pK + h // 4], neg_b_, Vb_b_], R32b)
                        V(lambda e, h=h: e.tensor_copy(out=r16[:, h, :], in_=r32[:, h, :]), R32b, [r16_b])
                        T(lambda e, h=h: e.matmul(PV(pN)[:, h, :], TT[:, h, :], r16[:, h, :], start=True, stop=True),
                          [TT_b, r16_b], [ps_b[pN + h // 4]])
                        A(lambda e, h=h: e.activation(out=vn[:, h, :], in_=PV(pN)[:, h, :], func=AF.Copy), [ps_b[pN + h // 4]], [vn_b])
                        V(lambda e, h=h: e.tensor_tensor(out=msk16[:], in0=qt_[:, h, :].unsqueeze(1).to_broadcast([128, 16, 128]),
                                                         in1=mskc[:], op=ALU.mult), [qt_b_, cb16_b], [msk16_b])

                        def fo(e, h=h):
                            for s_ in range(16):
                                e.matmul(PV(pO)[:, h, :], s0b[:, s_, :], msk16[:, s_, :], start=(s_ == 0), stop=False)
                            return e.matmul(PV(pO)[:, h, :], vn[:, h, :], PTm_[:, h, :], start=False, stop=True)
                        T(fo, s0b_b + [msk16_b, vn_b, PTm_b_], [ps_b[pO + h // 4]])
                        V(lambda e, h=h: e.tensor_tensor(out=msk16[:], in0=Kd_[:, h, :].unsqueeze(1).to_broadcast([128, 16, 128]),
                                                         in1=seqs[:].unsqueeze(2).to_broadcast([128, 16, 128]), op=ALU.mult),
                          [Kd_b_, cb16_b], [msk16_b])
                        for q in range(4):
                            bq = other[1 + (h * 4 + q) % (len(other) - 1)]

                            def fs(e, h=h, q=q, bq=bq):
                                ins = None
                                for s4 in range(4):
                                    ins = e.matmul(ps[:, bq, s4 * 128:(s4 + 1) * 128], msk16[:, q * 4 + s4, :], vn[:, h, :],
                                                   start=True, stop=True)
                                return ins
                            T(fs, [msk16_b, vn_b], [ps_b[bq]])
                            for s4 in range(4):
                                s_ = q * 4 + s4
                                V(lambda e, h=h, s_=s_, s4=s4, bq=bq, q=q: e.scalar_tensor_tensor(
                                    out=snew[q % 2][:, s4, :], in0=s0f[:, s_, :], scalar=egl[:, s_, h:h + 1],
                                    in1=ps[:, bq, s4 * 128:(s4 + 1) * 128], op0=ALU.mult, op1=ALU.add),
                                  s0f_b + [egl_b, ps_b[bq]], [snew_b[q % 2]])
                            dma("sp", f"s0out{q % 2}", sd_s[l, q * 4:(q + 1) * 4, h, :, :].rearrange("s k v -> k s v"), snew[q % 2][:],
                                [snew_b[q % 2]], [])
                    pQ = a2.pair()
                else:
                    pK, pN, pO, pS, pQ = 4, 6, 4, 6, 6
                    pKv = PV(pK); pNv = PV(pN)
                    for h in range(H):
                        T(lambda e, h=h: e.matmul(PV(pK)[:, h, :], R1[:, 8 + h, cs], S16[:, l, h, :], start=True, stop=True),
                          [R1_b[8 + h], S16_b[l]], [ps_b[pK + h // 4]])
                    V(lambda e: e.tensor_tensor(out=r32, in0=pKv, in1=bc8(neg_[:]), op=ALU.mult), [ps_b[pK], ps_b[pK + 1], neg_b_], R32b)
                    yield 1
                    V(lambda e: e.tensor_tensor(out=r32, in0=r32, in1=Vb_, op=ALU.add), R32b + [Vb_b_], R32b)
                    A(lambda e: e.activation(out=r16[:], in_=r32, func=AF.Copy), R32b, [r16_b])
                    yield 1
                    for h in range(H):
                        T(lambda e, h=h: e.matmul(PV(pN)[:, h, :], TT[:, h, :], r16[:, h, :], start=True, stop=True),
                          [TT_b, r16_b], [ps_b[pN + h // 4]])
                    A(lambda e: e.activation(out=vn[:], in_=pNv, func=AF.Copy), [ps_b[pN], ps_b[pN + 1]], [vn_b])
                    yield 1
                    for h in range(H):
                        T(lambda e, h=h: e.matmul(PV(pK)[:, h, :], Bo_[:, h, :], vn[:, h, :], start=True, stop=True),
                          [Bo_b_, vn_b], [ps_b[pK + h // 4]])
                    V(lambda e: e.tensor_tensor(out=r16[:], in0=r32, in1=pKv, op=ALU.subtract), R32b + [ps_b[pK], ps_b[pK + 1]], [r16_b])
                    yield 1
                    for h in range(H):
                        T(lambda e, h=h: e.matmul(PV(pN)[:, h, :], TT[:, h, :], r16[:, h, :], start=True, stop=True),
                          [TT_b, r16_b], [ps_b[pN + h // 4]])
                    A(lambda e: e.activation(out=vn[:], in_=pNv, func=AF.Copy), [ps_b[pN], ps_b[pN + 1]], [vn_b])
                    yield 1
                    for h in range(H):
                        def fo(e, h=h):
                            e.matmul(PV(pO)[:, h, :], S16[:, l, h, :], qt_[:, h, :], start=True, stop=False)
                            return e.matmul(PV(pO)[:, h, :], vn[:, h, :], PTm_[:, h, :], start=False, stop=True)
                        T(fo, [S16_b[l], qt_b_, vn_b, PTm_b_], [ps_b[pO + h // 4]])
                    yield 1
                    for h in range(H):
                        T(lambda e, h=h: e.matmul(PV(pS)[:, h, :], Kd_[:, h, :], vn[:, h, :], start=True, stop=True),
                          [Kd_b_, vn_b], [ps_b[pS + h // 4]])
                    A(lambda e: e.activation(out=sq[0][:], in_=P32(pO, 2), func=AF.Square, scale=float(128 ** -0.5)),
                      [ps_b[pO], ps_b[pO + 1]], [sq_b[0]])
                    yield 1
                    for h in range(H):
                        V(lambda e, h=h: e.scalar_tensor_tensor(out=S32[:, l, h, :], in0=S32[:, l, h, :], scalar=es_[:, 16 + h:17 + h],
                                                                in1=PV(pS)[:, h, :], op0=ALU.mult, op1=ALU.add),
                          [S32_b[l], es_b_, ps_b[pS + h // 4]], [S32_b[l]])
                    A(lambda e: e.activation(out=S16[:, l, :, :], in_=S32[:, l, :, :], func=AF.Copy), [S32_b[l]], [S16_b[l]])
                    yield 1
                if samp:
                    A(lambda e: e.activation(out=sq[0][:], in_=P32(pO, 2), func=AF.Square, scale=float(128 ** -0.5)),
                      [ps_b[pO], ps_b[pO + 1]], [sq_b[0]])
                for hh in range(2):
                    T(lambda e, hh=hh: e.matmul(ps[:, pQ + hh, :], ONES16, sq[0][:, hh * 512:(hh + 1) * 512], start=True, stop=True),
                      [sq_b[0], c16_b], [ps_b[pQ + hh]])
                rsqrt_from_psum(rstd9, P32(pQ, 2), 1.0 / 128, 1e-6, [ps_b[pQ], ps_b[pQ + 1]], t32_b[6:8])
                yield 1
                V(lambda e: e.scalar_tensor_tensor(out=on32[:].rearrange("p a b -> p (a b)"), in0=P32(pO, 2), scalar=dnw[:, l:l + 1],
                                                   in1=rstd9, op0=ALU.mult, op1=ALU.mult),
                  [ps_b[pO], ps_b[pO + 1], dnw_b] + t32_b[6:8], [on32_b])
                G(lambda e: e.tensor_tensor(out=oa[:, :, cs], in0=on32[:], in1=sz[:, :, cs], op=ALU.mult),
                  [on32_b] + sz_b, oa_b)
                yield 1

            cur = None
            for g in [chunk_gen(t) for t in range(NT)] + [None]:
                g_done = g is None
                c_done = cur is None
                while not (g_done and c_done):
                    if not c_done:
                        try:
                            next(cur)
                        except StopIteration:
                            c_done = True
                    if not g_done:
                        if next(g) == "S2":
                            g_done = True
                cur = g

            if (not samp) and last_blk:
                dma("sp", "o_sd", sd_p[l].rearrange("h k v -> k h v"), S32[:, l, :, :], [S32_b[l]], [])

            if stop <= 4:
                return
            def uv_consume(ch, bk):
                A(lambda e: e.activation(out=R1[:, ch, 0:W], in_=ps[:, bk, 0:W], func=AF.Gelu), [ps_b[bk]], [R1_b[ch]])
            proj_chunks("w_in", l, 4112, 16, W, hT, hT_b, uv_consume)
            boff = (l * 2 + (1 if samp else 0)) * 1024
            dma("sp", "lnw", t32f[:, 0:2048], lnw[l].rearrange("b d -> (b d)").partition_broadcast(128), [], t32_b[0:4])
            dma("pool", "prw", prw16[:], prow[:, boff:boff + 1024], [], [prw_b])
            lnt16 = t16[:, 4096:6144].rearrange("p (a b) -> p a b", a=2)
            A(lambda e: e.activation(out=t16[:, 4096:6144], in_=t32f[:, 0:2048], func=AF.Copy), t32_b[0:4], t32_b[4:6])
            dma("sp", "wsp", ws32[:], wsp[l, 1 if samp else 0], [], [ws32_b])
            V(lambda e: e.tensor_tensor(out=wsT[:], in0=ws32[:].rearrange("p (a b) -> p a b", a=8), in1=bcm(c32[:, cU, :]),
                                        op=ALU.mult), [ws32_b, c32_b], [wsT_b])
            for t in range(NT):
                cs = slice(t * 128, (t + 1) * 128)
                bk = bank()
                for g in range(8):
                    T(lambda e, g=g: e.transpose(out=P16(bk)[:, g * 128:(g + 1) * 128], in_=R1[:, 8 + g, cs], identity=I16),
                      [vT_b[g], c16_b], [ps_b[bk]])
                for hh in range(2):
                    V(lambda e, hh=hh: e.bn_stats(out=bnst[:, hh, :], in_=P16(bk)[:, hh * 512:(hh + 1) * 512]), [ps_b[bk]], [bn_b])
                V(lambda e: e.bn_aggr(out=mv[:, 0:2], in_=bnst[:].rearrange("p a b -> p (a b)")), [bn_b], [bn_b])
                A(lambda e: e.activation(out=mv[:, 2:3], in_=mv[:, 1:2], func=AF.Ln, bias=1e-5), [bn_b], [bn_b])
                A(lambda e: e.activation(out=mv[:, 2:3], in_=mv[:, 2:3], func=AF.Exp, scale=-0.5), [bn_b], [bn_b])
                V(lambda e: e.tensor_scalar(out=vnb[:], in0=P16(bk), scalar1=mv[:, 0:1], scalar2=mv[:, 2:3],
                                            op0=ALU.subtract, op1=ALU.mult), [ps_b[bk], bn_b], [vnb_b])
                V(lambda e: e.tensor_tensor(out=vnb[:], in0=vnb[:], in1=lnt16[:, 0, :], op=ALU.mult), [vnb_b] + t32_b[4:6], [vnb_b])
                V(lambda e: e.tensor_tensor(out=vnb[:], in0=vnb[:], in1=lnt16[:, 1, :], op=ALU.add), [vnb_b] + t32_b[4:6], [vnb_b])
                if samp or (last_blk and t == NT - 1):
                    A(lambda e: e.activation(out=lnv32, in_=vnb[:], func=AF.Copy), [vnb_b], [lnv32_b])
                    dma("sp", "o_cv", cv_s[l] if samp else cv_p[l], lnv32, [lnv32_b], [])
                pM = pair()
                for g in range(8):
                    def fm(e, g=g):
                        e.matmul(PV(pM)[:, g, :], vnb[:, g * 128:(g + 1) * 128], wsT[:, g, :], start=True, stop=False)
                        return e.matmul(PV(pM)[:, g, :], c16[0:1, 1, :], prw16[0:1, g * 128:(g + 1) * 128],
                                        start=False, stop=True)
                    T(fm, [vnb_b, wsT_b, c16_b, prw_b], [ps_b[pM + g // 4]])
                V(lambda e: e.tensor_tensor(out=R1[:, 0:8, cs], in0=R1[:, 0:8, cs], in1=PV(pM), op=ALU.mult),
                  uT_b + [ps_b[pM], ps_b[pM + 1]], uT_b)

            if stop <= 5:
                return
            mg, mg_b = sz, sz_b
            for half in range(2):
                sA = load_w(wview("w_in", l, 0, 1024, 6160 + half * 512, 512), 8, 512)
                sPA = load_w(wview("w_pa", l, 0, 1024, half * 512, 512), 8, 512)
                for jj in range(4):
                    d = half * 4 + jj
                    b1 = bank()
                    mm_group(b1, 0, W, lambda k, jj=jj: wsl[sA][:, k, jj * 128:(jj + 1) * 128], lambda k: hT[:, k, 0:W], 8,
                             hT_b, [wsl_b[sA]])
                    b2 = bank()
                    mm_group(b2, 0, W, lambda k, jj=jj: wsl[sPA][:, k, jj * 128:(jj + 1) * 128], lambda k: oa[:, k, 0:W], 8,
                             oa_b, [wsl_b[sPA]])
                    A(lambda e, b1=b1: e.activation(out=acc[0][:, 0:W], in_=ps[:, b1, 0:W], func=AF.Sigmoid), [ps_b[b1]], [acc_b[0]])
                    V(lambda e, d=d, b2=b2: e.tensor_tensor(out=t32[:, d, 0:W], in0=acc[0][:, 0:W], in1=ps[:, b2, 0:W], op=ALU.mult),
                      [acc_b[0], ps_b[b2]], [t32_b[d]])
                w_done(2)
            for half in range(2):
                sB = load_w(wview("w_in", l, 0, 1024, 7184 + half * 512, 512), 8, 512)
                sPB = load_w(wview("w_pb", l, 0, 1024, half * 512, 512), 8, 512)
                for jj in range(4):
                    d = half * 4 + jj
                    b1 = bank()
                    mm_group(b1, 0, W, lambda k, jj=jj: wsl[sB][:, k, jj * 128:(jj + 1) * 128], lambda k: hT[:, k, 0:W], 8,
                             hT_b, [wsl_b[sB]])
                    b2 = bank()
                    mm_group(b2, 0, W, lambda k, jj=jj: wsl[sPB][:, k, jj * 128:(jj + 1) * 128], lambda k: R1[:, k, 0:W], 8,
                             uT_b, [wsl_b[sPB]])
                    A(lambda e, b1=b1: e.activation(out=acc[1][:, 0:W], in_=ps[:, b1, 0:W], func=AF.Sigmoid), [ps_b[b1]], [acc_b[1]])
                    V(lambda e, b2=b2: e.tensor_tensor(out=acc[1][:, 0:W], in0=acc[1][:, 0:W], in1=ps[:, b2, 0:W], op=ALU.mult),
                      [acc_b[1], ps_b[b2]], [acc_b[1]])
                    G(lambda e, d=d: e.tensor_tensor(out=mg[:, d, 0:W], in0=acc[1][:, 0:W], in1=t32[:, d, 0:W], op=ALU.add),
                      [acc_b[1], t32_b[d]], [mg_b[d]])
                w_done(2)
            proj_chunks("w_o", l, 0, 8, W, mg, mg_b,
                        lambda ch, bk: A(lambda e: e.activation(out=t32[:, ch, 0:W], in_=ps[:, bk, 0:W], func=AF.Copy),
                                         [ps_b[bk]], [t32_b[ch]]))
            rmsnorm(t32, t32_b, W, NW(l, 1), "x", l)
            if stop <= 6:
                return
            rmsnorm(xT, xT_b, W, NW(l, 2), "h", l)
            hid, hid_b = R1, R1_b
            proj_chunks("w_f1", l, 0, 22, W, hT, hT_b,
                        lambda ch, bk: A(lambda e: e.activation(out=hid[:, ch, 0:W], in_=ps[:, bk, 0:W], func=AF.Silu),
                                         [ps_b[bk]], [hid_b[ch]]))
            proj_chunks("w_f1", l, DFF, 22, W, hT, hT_b,
                        lambda ch, bk: V(lambda e: e.tensor_tensor(out=hid[:, ch, 0:W], in0=hid[:, ch, 0:W], in1=ps[:, bk, 0:W],
                                                                   op=ALU.mult), [hid_b[ch], ps_b[bk]], [hid_b[ch]]))
            for half in range(2):
                bks = [bank() for _ in range(4)]
                for kg, (k0, nk) in enumerate(((0, 8), (8, 8), (16, 6))):
                    s = load_w(wview("w_f2", l, k0 * 128, nk * 128, half * 512, 512), nk, 512)
                    for jj in range(4):
                        def ff(e, s=s, jj=jj, k0=k0, nk=nk, kg=kg):
                            ins = None
                            for k in range(nk):
                                ins = e.matmul(ps[:, bks[jj], 0:W], wsl[s][:, k, jj * 128:(jj + 1) * 128], hid[:, k0 + k, 0:W],
                                               start=(kg == 0 and k == 0), stop=(kg == 2 and k == nk - 1))
                            return ins
                        T(ff, hid_b[k0:k0 + nk] + [wsl_b[s]], [ps_b[bks[jj]]])
                    w_done()
                for jj in range(4):
                    d = half * 4 + jj
                    A(lambda e, d=d, jj=jj: e.activation(out=t32[:, d, 0:W], in_=ps[:, bks[jj], 0:W], func=AF.Copy),
                      [ps_b[bks[jj]]], [t32_b[d]])
            rmsnorm(t32, t32_b, W, NW(l, 3), "x", l)

        def load_x(src, W):
            for t in range(W // 128):
                dma("sp", "xin", xin, src[t * 128:(t + 1) * 128, :], [], t32_b[0:2])
                for hh in range(2):
                    bk = bank()
                    for c4 in range(4):
                        c = hh * 4 + c4
                        T(lambda e, c=c, c4=c4, bk=bk: e.transpose(out=ps[:, bk, c4 * 128:(c4 + 1) * 128], in_=xin[:, c * 128:(c + 1) * 128],
                                                                  identity=I32), t32_b[0:2] + [c32_b], [ps_b[bk]])
                    A(lambda e, hh=hh, bk=bk, t=t: e.activation(out=xT[:, hh * 4:(hh + 1) * 4, t * 128:(t + 1) * 128],
                                                               in_=ps[:, bk, :].rearrange("p (a b) -> p a b", a=4), func=AF.Copy),
                      [ps_b[bk]], xT_b[hh * 4:(hh + 1) * 4])

        def store_y(dst, W):
            for t in range(W // 128):
                yo = t % 2
                for hh in range(2):
                    bk = bank()
                    for c4 in range(4):
                        c = hh * 4 + c4
                        T(lambda e, c=c, c4=c4, bk=bk, t=t: e.transpose(out=ps[:, bk, c4 * 128:(c4 + 1) * 128],
                                                                       in_=xT[:, c, t * 128:(t + 1) * 128], identity=I32),
                          [xT_b[c], c32_b], [ps_b[bk]])
                    A(lambda e, hh=hh, bk=bk, yo=yo: e.activation(out=yout[yo][:, hh * 512:(hh + 1) * 512], in_=ps[:, bk, :],
                                                                  func=AF.Copy), [ps_b[bk]], t32_b[2 + 2 * yo:4 + 2 * yo])
                dma("sp", f"yo{yo}", dst[t * 128:(t + 1) * 128, :], yout[yo], t32_b[2 + 2 * yo:4 + 2 * yo], [])

        G(lambda e: e.memset(S32[:].rearrange("p a b c -> p (a b c)"), 0.0), [], S32_b)
        G(lambda e: e.memset(S16[:].rearrange("p a b c -> p (a b c)"), 0.0), [], S16_b)
        G(lambda e: e.memset(carry[:].rearrange("p a b c -> p (a b c)"), 0.0), [], carry_b[0] + carry_b[1])

        for blk in range(nblk):
            load_x(xp[blk * 512:(blk + 1) * 512, :], 512)
            for l in range(DEPTH):
                E.epoch += 1
                layer(l, 512, False, blk == nblk - 1)
            store_y(y_p[blk * 512:(blk + 1) * 512, :], 512)
        E.epoch += 1
        if do_samp:
            load_x(xs, 128)
            for l in range(DEPTH):
                E.epoch += 1
                layer(l, 128, True, False)
            store_y(y_s, 128)

        E.finalize()
        if os.environ.get("KDBG"):
            print("est_span_us", getattr(E, "est_span", None), "n_ops", len(E.ops), flush=True)
        sems = {}
        for e_ in ENGS:
            for ep in range(E.epoch + 1):
                sems[(e_, ep)] = es.enter_context(nc.semaphore(f"s_{e_}_{ep}"))
        chansems = {ch: es.enter_context(nc.semaphore(f"c_{ch}")) for ch in E.chans}
        block = es.enter_context(nc.Block())
        E.emit(block, sems, chansems)
    return nc


def _pack(inp):
    f = lambda a: np.ascontiguousarray(np.asarray(a, dtype=np.float32))
    pp = np.zeros((128, 262), np.float32)
    names = ["norm_pre_mix", "norm_post_mix", "norm_pre_ffn", "norm_post_ffn"]
    for l in range(DEPTH):
        for n, nm in enumerate(names):
            pp[:, (l * 4 + n) * 8:(l * 4 + n) * 8 + 8] = f(inp[nm])[l].reshape(8, 128).T
        cw = f(inp["conv_w"])[l]
        pp[:, 64 + l * 96:64 + (l + 1) * 96] = cw.reshape(4, 24, 128).transpose(2, 0, 1).reshape(128, 96)
        pp[:, 256 + l] = f(inp["delta_norm_w"])[l]
        pp[0:8, 258 + l] = f(inp["a_log"])[l]
        pp[0:8, 260 + l] = f(inp["dt_bias"])[l]
    bs = f(inp["b_spatial"])
    prow = np.zeros((DEPTH, 2, 8, 128), np.float32)
    prow[:, 0] = bs
    prow[:, 1] = np.tile(bs[:, :, :8], (1, 1, 16))
    prow = prow.reshape(1, -1)
    lnw = np.stack([f(inp["sgu_ln_w"]), f(inp["sgu_ln_b"])], axis=1)
    ws = f(inp["w_spatial"])
    wsT = ws.transpose(0, 3, 1, 2)
    wss = np.tile(ws[:, :, :8, :8], (1, 1, 16, 16)).transpose(0, 3, 1, 2)
    wsp = np.ascontiguousarray(np.stack([wsT, wss], axis=1).reshape(DEPTH, 2, 128, 1024))
    shared = dict(
        w_in=f(inp["w_in"]), w_pa=f(inp["w_proj_a"]), w_pb=f(inp["w_proj_b"]), w_o=f(inp["w_out"]),
        w_f1=f(inp["w_ffn_in"]), w_f2=f(inp["w_ffn_out"]), pp=pp, prow=prow, lnw=np.ascontiguousarray(lnw), wsp=wsp,
        cst=_consts()[0], cst16=_consts()[1], cbd=_cb(),
    )
    xpr = f(inp["x_prompt"]); xsm = f(inp["x_sample"]); sdl = f(inp["state_delta"]); scv = f(inp["state_conv"])
    maps = []
    for c in range(NCORES):
        m = dict(shared)
        m["xp"] = xpr[c]
        m["xs"] = np.ascontiguousarray(xsm[16 * c:16 * (c + 1)].reshape(128, D))
        m["sd"] = np.ascontiguousarray(sdl[:, 16 * c:16 * (c + 1)])
        m["sc"] = np.ascontiguousarray(scv[:, 16 * c:16 * (c + 1)].reshape(DEPTH, 48, 3072))
        maps.append(m)
    return maps


def kernel(**inputs):
    maps = _pack(inputs)
    nc = build()
    res = run_bass_kernel_spmd(nc, maps, core_ids=list(range(NCORES)))
    r = res.results
    y_p = np.stack([r[c]["y_p"] for c in range(NCORES)], 0).reshape(8, 2048, D)
    y_s = np.concatenate([r[c]["y_s"].reshape(16, 8, D) for c in range(NCORES)], 0)
    sd_p = np.stack([r[c]["sd_p"] for c in range(NCORES)], 1)
    sc_p = np.stack([r[c]["sc_p"] for c in range(NCORES)], 1)
    cv_p = np.stack([r[c]["cv_p"] for c in range(NCORES)], 1)
    sd_s = np.concatenate([r[c]["sd_s"] for c in range(NCORES)], 1)
    sc_s = np.concatenate([r[c]["sc_s"] for c in range(NCORES)], 1)
    cv_s = np.concatenate([r[c]["cv_s"].reshape(DEPTH, 16, 8, D) for c in range(NCORES)], 1)
    return tuple(np.ascontiguousarray(a.astype(np.float32)) for a in (y_p, y_s, sd_p, sc_p, cv_p, sd_s, sc_s, cv_s))
```

```python
import os
import numpy as np
from contextlib import ExitStack
import concourse.bass as bass
import concourse.mybir as mybir
from concourse.bass_utils import run_bass_kernel_spmd

F32 = mybir.dt.float32
BF16 = mybir.dt.bfloat16
AF = mybir.ActivationFunctionType
ALU = mybir.AluOpType

NCORES = 8
D = 1024
DEPTH = 2
H = 8
DIN = 8208
DFF = 2816
BIG = 30000.0
NSLOT = 3
ENGS = ["pe", "act", "dve", "pool", "sp"]
BNAME = {"pe": "tensor", "act": "scalar", "dve": "vector", "pool": "gpsimd", "sp": "sync"}
NEPOCH = 12
SCHED = True


class Buf:
    __slots__ = ("name", "lw", "rd")

    def __init__(self, name):
        self.name = name
        self.lw = None
        self.rd = []


def bufs(name, n):
    return [Buf(f"{name}{i}") for i in range(n)]


class Rec:
    def __init__(self):
        self.calls = []

    def __getattr__(self, name):
        def f(*a, **k):
            self.calls.append((name, a, k))
            return self
        return f


class Em:
    def __init__(self):
        self.ops = []
        self.chans = {}
        self.epoch = 0

    def op(self, eng, fn, reads=(), writes=(), chan=None, ndma=0):
        idx = len(self.ops)
        deps = set()
        soft = set()
        for b in reads:
            if b.lw is not None:
                deps.add(b.lw)
        for b in writes:
            if b.lw is not None:
                soft.add(b.lw)
            soft.update(b.rd)
        deps |= soft
        val = None
        if chan:
            c = self.chans.setdefault(chan, {"count": 0, "last": None, "eng": eng})
            assert c["eng"] == eng
            if c["last"] is not None:
                deps.add(c["last"])
            c["count"] += 16 * ndma
            c["last"] = idx
            val = c["count"]
        key = chan if chan else eng
        for b in writes:
            b.lw = idx
            b.rd = []
        for b in reads:
            b.rd.append(idx)
        rec = Rec()
        fn(rec)
        assert rec.calls
        self.ops.append(dict(eng=eng, calls=rec.calls, deps=deps, chan=chan, val=val, inc=False, ep=self.epoch))
        return idx

    @staticmethod
    def _est(o):
        eng = o["eng"]
        tot = 0.0
        for name, a, k in o["calls"]:
            out = k.get("out", a[0] if a else None)
            try:
                fs = float(out.free_size())
            except Exception:
                fs = 128.0
            if o["chan"]:
                try:
                    nb = float(out.nbytes())
                except Exception:
                    nb = 1e5
                tot += 2.0 + nb / 1.8e5
            elif eng == "pe":
                lhs = k.get("lhsT", a[1] if len(a) > 1 else None) if name == "matmul" else k.get("in_")
                f32 = getattr(lhs, "dtype", None) == F32
                tot += 0.045 + max(fs, 64.0) * (4.0 if (f32 and name == "matmul") else 1.0) / 2400.0 + (0.05 if fs <= 128 else 0.0)
            elif eng == "dve":
                tot += 0.12 + fs * 1.1e-3
            elif eng == "act":
                tot += 0.17 + fs * 0.9e-3
            else:
                tot += 0.25 + fs * 2.2e-3
        return tot

    def schedule(self):
        import heapq
        ops = self.ops
        n = len(ops)
        succ = [[] for _ in range(n)]
        indeg = [0] * n
        for i, o in enumerate(ops):
            for d in o["deps"]:
                succ[d].append(i)
                indeg[i] += 1
        dur = [self._est(o) for o in ops]
        bl = [0.0] * n
        for i in range(n - 1, -1, -1):
            m_ = 0.0
            for j in succ[i]:
                if bl[j] > m_:
                    m_ = bl[j]
            bl[i] = dur[i] + 0.3 + m_
        PRIO = os.environ.get("KPRIO", "bl")
        ready = [0.0] * n
        fin = [0.0] * n
        free = {e: 0.0 for e in ENGS}
        heaps = {e: [] for e in ENGS}
        for i in range(n):
            if indeg[i] == 0:
                heapq.heappush(heaps[ops[i]["eng"]], (0.0, i))
        order = []
        LAT = 0.3
        while len(order) < n:
            best = None
            for e in ENGS:
                hp = heaps[e]
                if not hp:
                    continue
                cand = hp[0]
                st = max(free[e], cand[0])
                key = (st, cand[1])
                if best is None or key < best[0]:
                    best = (key, e)
            (st, i), e = best
            hp = heaps[e]
            pool_ = []
            while hp and hp[0][0] <= st:
                pool_.append(heapq.heappop(hp))
            if PRIO == "bl":
                pool_.sort(key=lambda x: (-bl[x[1]], x[1]))
            else:
                pool_.sort(key=lambda x: x[1])
            _, i = pool_[0]
            for x in pool_[1:]:
                heapq.heappush(hp, x)
            o = ops[i]
            fin[i] = st + dur[i]
            free[e] = st + (min(dur[i], 1.0) if o["chan"] else dur[i])
            order.append(i)
            for j in succ[i]:
                r = fin[i] + (LAT if ops[j]["eng"] != e or o["chan"] else 0.1)
                if r > ready[j]:
                    ready[j] = r
                indeg[j] -= 1
                if indeg[j] == 0:
                    heapq.heappush(heaps[ops[j]["eng"]], (ready[j], j))
        pos = {old: new for new, old in enumerate(order)}
        new_ops = []
        for old in order:
            o = ops[old]
            o["deps"] = {pos[d] for d in o["deps"]}
            new_ops.append(o)
        self.ops = new_ops
        self.est_span = max(fin) if fin else 0.0
        if os.environ.get("KDBG"):
            cp = [0.0] * n
            for old in order:
                o = ops[old]
            newdur = [dur[old] for old in order]
            for i_, o in enumerate(new_ops):
                st_ = 0.0
                for d in o["deps"]:
                    st_ = max(st_, cp[d] + LAT)
                cp[i_] = st_ + newdur[i_]
            busy = {}
            for i_, o in enumerate(new_ops):
                busy[o["eng"]] = busy.get(o["eng"], 0.0) + (min(newdur[i_], 1.0) if o["chan"] else newdur[i_])
            print("critical_path_us", max(cp), "busy", {k: round(v) for k, v in busy.items()}, flush=True)
            i_ = max(range(n), key=lambda q: cp[q])
            agg = {}
            seq = []
            while True:
                o = new_ops[i_]
                nm = o["calls"][0][0] + ("/dma" if o["chan"] else "")
                outap = o["calls"][0][2].get("out", o["calls"][0][1][0] if o["calls"][0][1] else None)
                tn = getattr(getattr(outap, "tensor", None), "name", "?")
                key = (o["eng"], nm, tn)
                a_ = agg.setdefault(key, [0, 0.0])
                a_[0] += 1
                a_[1] += newdur[i_] + LAT
                seq.append(key)
                prev = None
                for d in o["deps"]:
                    if prev is None or cp[d] > cp[prev]:
                        prev = d
                if prev is None:
                    break
                i_ = prev
            for k_, v_ in sorted(agg.items(), key=lambda kv: -kv[1][1])[:28]:
                print("   CP", k_, v_[0], round(v_[1], 1), flush=True)

    def prune(self):
        for o in self.ops:
            best = {}
            for d in o["deps"]:
                p = self.ops[d]
                key = ("c", p["chan"]) if p["chan"] else ("e", p["eng"])
                if key not in best or d > best[key]:
                    best[key] = d
            o["deps"] = set(best.values())

    def finalize(self):
        if SCHED:
            self.schedule()
        self.prune()
        for o in self.ops:
            for d in o["deps"]:
                p = self.ops[d]
                if p["chan"] is None:
                    if p["eng"] == "pe" and o["eng"] == "pe" and o["chan"] is None:
                        continue
                    p["inc"] = True
        cnt = {}
        for o in self.ops:
            if o["chan"] is None and o["inc"]:
                k = (o["eng"], o["ep"])
                cnt[k] = cnt.get(k, 0) + 1
                o["val"] = cnt[k]

    def emit(self, block, sems, chansems):
        for e in ENGS:
            ops_e = [o for o in self.ops if o["eng"] == e]

            def body(eng, ops_e=ops_e, e=e):
                seen = {}
                for o in ops_e:
                    for d in sorted(o["deps"]):
                        p = self.ops[d]
                        if p["chan"] is None:
                            if not p["inc"]:
                                continue
                            if p["eng"] == "pe" and e == "pe" and o["chan"] is None:
                                continue
                            key = (p["eng"], p["ep"])
                            sem = sems[key]
                        else:
                            key = p["chan"]
                            sem = chansems[key]
                        v = p["val"]
                        if seen.get(key, 0) >= v:
                            continue
                        seen[key] = v
                        eng.wait_ge(sem, v)
                    ins = None
                    for name, a, k in o["calls"]:
                        ins = getattr(eng, name)(*a, **k)
                        if o["chan"]:
                            ins.then_inc(chansems[o["chan"]], 16)
                    if (not o["chan"]) and o["inc"]:
                        ins.then_inc(sems[(e, o["ep"])], 1)
                for ch, c in self.chans.items():
                    if c["eng"] == e and c["count"] > 0:
                        eng.wait_ge(chansems[ch], c["count"])

            getattr(block, BNAME[e])(body)


def _consts():
    i = np.arange(128)
    m = i[:, None]
    p = i[None, :]
    blk = i // 8
    same = blk[:, None] == blk[None, :]
    c = np.zeros((13, 128, 128), np.float32)
    c[0] = np.eye(128)
    c[1] = 1.0
    c[2] = m <= p
    c[3] = m > p
    c[4] = -BIG * (p >= m)
    c[5] = -BIG * (p < m)
    c[6] = (m <= p) & same
    c[7] = (m > p) & same
    c[8] = -BIG * (~((p < m) & same))
    c[9] = -BIG * (~((p >= m) & same))
    c[10] = same
    s = np.zeros((128, 16, 16), np.float32)
    for j in range(16):
        s[:, j, j] = 1.0
    c[11] = s.reshape(128, 256)[:, :128]
    c[12] = s.reshape(128, 256)[:, 128:]
    c32 = np.ascontiguousarray(c[[0, 1, 2, 3, 6, 7, 10]])
    md = ((m // 64) == (p // 64)).astype(np.float32)
    mo = ((m < 64) & (p >= 64)).astype(np.float32)
    c16 = np.ascontiguousarray(np.concatenate([c[[0, 1, 4, 5, 8, 9, 11, 12]], md[None], mo[None]], 0))
    return c32, c16


def _cb():
    i = np.arange(128)
    maskc = (i[None, :] // 8 == np.arange(16)[:, None]).astype(np.float32)
    maskc = np.broadcast_to(maskc[None], (128, 16, 128)).reshape(128, 2048)
    seqsel = (i[:, None] // 8 == np.arange(16)[None, :]).astype(np.float32)
    return np.concatenate([maskc, seqsel], axis=1).astype(np.float32)


def build(nblk=4, do_samp=True, stop=99):
    reqs = []
    _build(nblk, do_samp, stop, reqs, None)
    return _build(nblk, do_samp, stop, [], reqs)


LOOKAHEAD = 2
SAMP_K0 = [None]


def _build(nblk, do_samp, stop, rec_reqs, all_reqs):
    nc = bass.Bass("TRN2", target_bir_lowering=False)

    def din(name, shape):
        return nc.dram_tensor(name, list(shape), F32, kind="ExternalInput").ap()

    def dout(name, shape):
        return nc.dram_tensor(name, list(shape), F32, kind="ExternalOutput").ap()

    xp = din("xp", [2048, D])
    xs = din("xs", [128, D])
    sd = din("sd", [DEPTH, 16, H, 128, 128])
    sc = din("sc", [DEPTH, 48, 3072])
    w_in = din("w_in", [DEPTH, D, DIN])
    w_pa = din("w_pa", [DEPTH, D, D])
    w_pb = din("w_pb", [DEPTH, D, D])
    w_o = din("w_o", [DEPTH, D, D])
    w_f1 = din("w_f1", [DEPTH, D, 2 * DFF])
    w_f2 = din("w_f2", [DEPTH, DFF, D])
    pp = din("pp", [128, 2 * 32 + 2 * 96 + 2 + 4])
    prow = din("prow", [1, DEPTH * 2 * 1024])
    lnw = din("lnw", [DEPTH, 2, D])
    wsp = din("wsp", [DEPTH, 2, 128, 1024])
    cst = din("cst", [7, 128, 128])
    cst16 = din("cst16", [10, 128, 128])
    cbd = din("cbd", [128, 2064])

    NSCR = 2 * 41
    wscr = nc.dram_tensor("wscr", [NSCR, 128, 4096], BF16, kind="Internal").ap()
    wscr_b = bufs("wscr", NSCR)
    y_p = dout("y_p", [2048, D])
    y_s = dout("y_s", [128, D])
    sd_p = dout("sd_p", [DEPTH, H, 128, 128])
    sc_p = dout("sc_p", [DEPTH, 3, 3072])
    cv_p = dout("cv_p", [DEPTH, 128, D])
    sd_s = dout("sd_s", [DEPTH, 16, H, 128, 128])
    sc_s = dout("sc_s", [DEPTH, 16, 3, 3072])
    cv_s = dout("cv_s", [DEPTH, 128, D])

    E = Em()
    es = ExitStack()
    with es:
        def sb(name, shape, dt):
            return es.enter_context(nc.sbuf_tensor(name, shape, dt))

        xT = sb("xT", [128, 8, 512], F32); xT_b = bufs("xT", 8)
        hT = sb("hT", [128, 8, 512], BF16); hT_b = bufs("hT", 8)
        R1 = sb("R1", [128, 24, 512], BF16); R1_b = bufs("R1", 24)
        sz = sb("sz", [128, 8, 512], BF16); sz_b = bufs("sz", 8)
        oa = sb("oa", [128, 8, 512], BF16); oa_b = bufs("oa", 8)
        uT_b = R1_b[0:8]; vT_b = R1_b[8:16]
        t32 = sb("t32", [128, 8, 512], F32); t32_b = bufs("t32", 8)
        pre = [sb(f"pre{i}", [128, 515], F32) for i in range(2)]; pre_b = bufs("pre", 2)
        acc = [sb(f"acc{i}", [128, 512], F32) for i in range(2)]; acc_b = bufs("acc", 2)
        sq = [sb(f"sq{i}", [128, 1024], BF16) for i in range(2)]; sq_b = bufs("sq", 2)
        rstd = sb("rstd", [128, 1024], F32); rstd_b = Buf("rstd")
        carry = sb("carry", [128, DEPTH, 24, 3], F32); carry_b = [bufs(f"car{l}_", 24) for l in range(DEPTH)]
        wsl = [sb(f"wsl{i}", [128, 8, 512], BF16) for i in range(5)]; wsl_b = bufs("wsl", 5)
        w3f = wsl[3][:].rearrange("p a b -> p (a b)")
        w4f = wsl[4][:].rearrange("p a b -> p (a b)")
        ppt = sb("ppt", [128, 262], F32); ppt_b = Buf("ppt")
        nea = sb("nea", [128, 2], F32); nea_b = Buf("nea")
        dnw = sb("dnw", [128, 2], F32); dnw_b = Buf("dnw")
        c32 = sb("c32", [128, 7, 128], F32); c32_b = Buf("c32")
        c16 = sb("c16", [128, 10, 128], BF16); c16_b = Buf("c16")
        seq32 = sb("seq32", [128, 16], F32); cb32_b = Buf("cb32")
        mskc = w3f[:, 0:2048].rearrange("p (a b) -> p a b", a=16)
        seqs = sb("seqs", [128, 16], BF16); cb16_b = Buf("cb16")
        selT = sb("selT", [16, 16, 128], BF16); selT_b = Buf("selT")
        t32f = t32[:].rearrange("p a b -> p (a b)")
        prw16 = t32f.bitcast(BF16)[0:1, 6144:7168]; prw_b = t32_b[6]
        lnt = t32f[:, 0:2048].rearrange("p (a b) -> p a b", a=2)
        xin = t32f[:, 0:1024]
        yout = [t32f[:, 1024:2048], t32f[:, 2048:3072]]
        scrow = t32f[0:3, 0:3072]
        scin = w4f[:, 0:2304].bitcast(F32).rearrange("p (a b) -> p a b", a=24); scin_b = Buf("scin")
        wsT = sb("wsT", [128, 8, 128], BF16); wsT_b = Buf("wsT")
        gT = pre[0][0:8, 0:512]; gT_b = pre_b[0]
        bT = pre[1][0:8, 0:512]; bT_b = pre_b[1]
        rn = acc[1][0:16, :]; rn_b = acc_b[1]
        rnb = acc[0][0:16, 0:256].bitcast(BF16); rnb_b = acc_b[0]
        rnl = acc[0][0:16, 256:512].bitcast(BF16)
        S32 = sb("S32", [128, DEPTH, H, 128], F32); S32_b = bufs("S32_", DEPTH)
        S16 = sb("S16", [128, DEPTH, H, 128], BF16); S16_b = bufs("S16_", DEPTH)
        s0f_0 = S32[:].rearrange("p a b c -> p (a b) c"); s0f_b0 = S32_b
        s0b_0 = S16[:].rearrange("p a b c -> p (a b) c"); s0b_b0 = S16_b
        gbt = sb("gbt", [128, 16], F32); gbt_b = Buf("gbt")
        es24 = sb("es24", [128, 24], F32); es_b = Buf("es24")
        kbs = sb("kbs", [128, 8], F32); kbs_b = Buf("kbs")
        Gm1 = sb("Gm1", [128, 8, 128], F32); Gm1_b = Buf("Gm1")
        Gm2 = sb("Gm2", [128, 8, 128], F32); Gm2_b = Buf("Gm2")
        ws32 = Gm1[:].rearrange("p a b -> p (a b)"); ws32_b = Gm1_b
        Kd = sb("Kd", [128, 8, 128], F32); Kd_b = Buf("Kd")
        Vb = sb("Vb", [128, 8, 128], BF16); Vb_b = Buf("Vb")
        Bo = sb("Bo", [128, 8, 128], BF16); Bo_b = Buf("Bo")
        r16 = sb("r16", [128, 8, 128], BF16); r16_b = Buf("r16")
        neg = sb("neg", [128, 8], F32); neg_b = Buf("neg")
        dsm = Gm1; dsm_b = Gm1_b
        dTm = Gm2; dTm_b = Gm2_b
        reg = sb("reg", [128, 8, 128], BF16); reg_b = Buf("reg")
        AmT = sb("AmT", [128, 2, 8, 128], BF16); Am = [AmT[:, 0, :, :], AmT[:, 1, :, :]]; Am_b = bufs("Am", 2)
        BmT = sb("BmT", [128, 2, 8, 128], BF16); Bm = [BmT[:, 0, :, :], BmT[:, 1, :, :]]; Bm_b = bufs("Bm", 2)
        vn32 = AmT[:].rearrange("p a b c -> p (a b c)").bitcast(F32).rearrange("p (b c) -> p b c", c=128)
        r32 = BmT[:].rearrange("p a b c -> p (a b c)").bitcast(F32).rearrange("p (b c) -> p b c", c=128)
        Pm0 = sb("Pm0", [128, 8, 128], BF16); Pm_b0 = Buf("Pm")
        Pf = rstd[:].rearrange("p (b c) -> p b c", c=128)
        PTm = sb("PTm", [128, 8, 128], BF16); PTm_b = Buf("PTm")
        vn = sb("vn", [128, 8, 128], BF16); vn_b = Buf("vn")
        qt = sb("qt", [128, 8, 128], BF16); qt_b = Buf("qt")
        on32 = sb("on32", [128, 8, 128], F32); on32_b = Buf("on32")
        lnv32 = on32[:].rearrange("p a b -> p (a b)"); lnv32_b = on32_b
        vnb = sb("vnb", [128, D], BF16); vnb_b = Buf("vnb")
        bnst = sb("bnst", [128, 2, 6], F32); mv = sb("mv", [128, 4], F32); bn_b = Buf("bn")
        t16 = t32f.bitcast(BF16)

        def t16tile(c):
            return t16[:, c * 1024:(c + 1) * 1024].rearrange("p (a b) -> p a b", a=8)
        KdV = Kd[:].rearrange("p a b -> p (a b)").bitcast(BF16).rearrange("p (s a b) -> p s a b", s=2, a=8)
        Kd2 = [KdV[:, 0, :, :], KdV[:, 1, :, :]]; Kd2_b = bufs("Kd2_", 2)
        Vb2 = [Vb[:], t16tile(0)]; Vb2_b = [Vb_b, t32_b[0]]
        qt2 = [qt[:], t16tile(1)]; qt2_b = [qt_b, t32_b[1]]
        PTm2 = [PTm[:], t16tile(2)]; PTm2_b = [PTm_b, t32_b[2]]
        Pm2 = [Pm0[:], t16tile(3)]; Pm2_b = [Pm_b0, t32_b[3]]
        Bo2 = [Bo[:], t16tile(4)]; Bo2_b = [Bo_b, t32_b[4]]
        neg1 = sb("neg1", [128, 8], F32); neg2 = [neg, neg1]; neg2_b = [neg_b, Buf("neg1")]
        es1 = sb("es1", [128, 24], F32); es2 = [es24, es1]; es2_b = [es_b, Buf("es1")]
        rstd9 = t32f[:, 3072:4096]
        msk16 = w3f[:, 2048:4096].rearrange("p (a b) -> p a b", a=16); msk16_b = Buf("msk16")
        egl = sb("egl", [128, 16, 8], F32); egl_b = Buf("egl")
        gsel = sb("gsel", [128, 16, 8], F32); gsel_b = Buf("gsel")
        snew = [w4f[:, 2304:3328].bitcast(F32).rearrange("p (a b) -> p a b", a=4), sb("snew1", [128, 4, 128], F32)[:]]
        snew_b = bufs("snew", 2)

        ps = es.enter_context(nc.psum_tensor("ps", [128, 8, 512], F32))
        ps_b = bufs("ps", 8)
        bptr = [0]

        def bank():
            b = bptr[0] % 8
            bptr[0] = (b + 1) % 8
            return b

        def pair():
            b = bptr[0] % 8
            if b % 2:
                b = (b + 1) % 8
            bptr[0] = (b + 2) % 8
            return b

        def P32(i, n=1):
            return ps[:, i:i + n, :].rearrange("p a b -> p (a b)") if n > 1 else ps[:, i, :]

        def P16(i):
            return ps[:, i, :].bitcast(BF16)

        def V(fn, r, w): return E.op("dve", fn, r, w)
        def A(fn, r, w): return E.op("act", fn, r, w)
        def T(fn, r, w): return E.op("pe", fn, r, w)
        def G(fn, r, w): return E.op("pool", fn, r, w)

        def dma(eng, chan, out, in_, r, w):
            return E.op(eng, lambda e: e.dma_start(out=out, in_=in_), r, w, chan=chan, ndma=1)

        slot_i = [0]
        issued = [0]
        mask_loaded = [False]
        WT = {"w_in": w_in, "w_pa": w_pa, "w_pb": w_pb, "w_o": w_o, "w_f1": w_f1, "w_f2": w_f2}

        scr_idx = {}

        def issue_w(k, desc):
            wn, l_, r0, nr, c0, ncol = desc
            s_ = slot_of(k)
            kc_ = nr // 128
            dst = wsl[s_][:, 0:kc_, 0:ncol]
            if desc not in scr_idx:
                i_ = len(scr_idx)
                scr_idx[desc] = i_
                src = WT[wn][l_, r0:r0 + nr, c0:c0 + ncol].rearrange("(c p) n -> p c n", p=128)
                dma("pool", f"w{s_}", dst, src, [], [wsl_b[s_]])
                sv = wscr[i_, :, 0:kc_ * ncol].rearrange("p (c n) -> p c n", c=kc_)
                dma("sp", f"scrw{i_ % 2}", sv, dst, [wsl_b[s_]], [wscr_b[i_]])
            else:
                i_ = scr_idx[desc]
                sv = wscr[i_, :, 0:kc_ * ncol].rearrange("p (c n) -> p c n", c=kc_)
                dma("sp", f"v{s_}", dst, sv, [wscr_b[i_]], [wsl_b[s_]])

        def load_w(desc, kc=None, ncols=None):
            k = slot_i[0]
            slot_i[0] += 1
            rec_reqs.append(desc)
            if all_reqs is None:
                issue_w(k, desc)
            else:
                assert all_reqs[k] == desc
                pump()
                assert issued[0] > k, "weight slot ring exhausted"
            return slot_of(k)

        samp_k0 = [None]
        if all_reqs is not None:
            samp_k0[0] = SAMP_K0[0]

        def slot_of(k):
            k0 = samp_k0[0]
            if k0 is None or k < k0:
                return k % 5
            return (k - k0) % 3

        def prev_user(k):
            s_ = slot_of(k)
            j = k - 1
            while j >= 0:
                if slot_of(j) == s_:
                    return j
                j -= 1
            return -1

        wdone = [0]

        def pump():
            while issued[0] < len(all_reqs) and prev_user(issued[0]) < wdone[0]:
                issue_w(issued[0], all_reqs[issued[0]])
                issued[0] += 1

        def w_done(n=1):
            wdone[0] += n
            if all_reqs is not None:
                pump()

        def wview(wt, l, r0, nr, c0, ncol):
            return (wt, l, r0, nr, c0, ncol)

        I32 = c32[:, 0, :]; I16 = c16[:, 0, :]; ONES16 = c16[:, 1, :]; ONES32 = c32[:, 1, :]

        def mm_group(bk, col0, ncol, lhs_fn, rhs_fn, nk, rbufs, extra_r=(), mp=128):
            def fn(e):
                ins = None
                for k in range(nk):
                    ins = e.matmul(ps[0:mp, bk, col0:col0 + ncol], lhs_fn(k), rhs_fn(k),
                                   start=(k == 0), stop=(k == nk - 1))
                return ins
            return T(fn, list(rbufs) + list(extra_r), [ps_b[bk]])

        dma("sp", "par", ppt[:], pp, [], [ppt_b])
        dma("sp", "par", c32[:], cst.rearrange("k p n -> p k n"), [], [c32_b])
        dma("sp", "par", seq32[:], cbd[:, 2048:2064], [], [cb32_b])
        dma("pool", "cst16", c16[:], cst16.rearrange("k p n -> p k n"), [], [c16_b])
        V(lambda e: e.tensor_copy(out=seqs[:], in_=seq32[:]), [cb32_b, cb16_b], [cb16_b])
        V(lambda e: e.tensor_copy(out=selT[:], in_=c32[0:16, 0, 0:16].unsqueeze(2).to_broadcast([16, 16, 128])),
          [c32_b], [selT_b])
        def NW(l, n): return ppt[:, (l * 4 + n) * 8:(l * 4 + n) * 8 + 8]
        def CW(l, tap, ch): return ppt[:, 64 + l * 96 + tap * 24 + ch:64 + l * 96 + tap * 24 + ch + 1]
        A(lambda e: e.activation(out=nea[0:8, :], in_=ppt[0:8, 258:260], func=AF.Exp), [ppt_b], [nea_b])
        V(lambda e: e.tensor_scalar(out=nea[0:8, :], in0=nea[0:8, :], scalar1=-1.0, scalar2=None, op0=ALU.mult), [nea_b], [nea_b])
        V(lambda e: e.tensor_scalar(out=dnw[:], in0=ppt[:, 256:258], scalar1=float(128 ** -0.5), scalar2=None, op0=ALU.mult),
          [ppt_b], [dnw_b])
        sel16 = c16[:, 6:8, :].rearrange("p a b -> p (a b)")

        def rsqrt_from_psum(out_ap, in_ap, scale, eps, r, w):
            A(lambda e: e.activation(out=out_ap, in_=in_ap, func=AF.Ln, scale=scale, bias=eps), r, w)
            A(lambda e: e.activation(out=out_ap, in_=out_ap, func=AF.Exp, scale=-0.5), w, w)

        def rmsnorm(src, src_b, W, gain, mode, l):
            bk = bank()
            for c in range(8):
                s = c % 2
                A(lambda e, c=c, s=s: e.activation(out=sq[s][:, 0:W], in_=src[:, c, 0:W], func=AF.Square),
                  [src_b[c]], [sq_b[s]])
                T(lambda e, c=c, s=s: e.matmul(ps[:, bk, 0:W], ONES16, sq[s][:, 0:W], start=(c == 0), stop=(c == 7)),
                  [sq_b[s], c16_b], [ps_b[bk]])
            rsqrt_from_psum(rstd[:, 0:W], ps[:, bk, 0:W], 1.0 / D, 1e-6, [ps_b[bk]], [rstd_b])
            for c in range(8):
                if mode == "h":
                    V(lambda e, c=c: e.scalar_tensor_tensor(out=hT[:, c, 0:W], in0=src[:, c, 0:W], scalar=gain[:, c:c + 1],
                                                            in1=rstd[:, 0:W], op0=ALU.mult, op1=ALU.mult),
                      [src_b[c], rstd_b, ppt_b], [hT_b[c]])
                else:
                    V(lambda e, c=c: e.scalar_tensor_tensor(out=src[:, c, 0:W], in0=src[:, c, 0:W], scalar=gain[:, c:c + 1],
                                                            in1=rstd[:, 0:W], op0=ALU.mult, op1=ALU.mult),
                      [src_b[c], rstd_b, ppt_b], [src_b[c]])
                    G(lambda e, c=c: e.tensor_tensor(out=xT[:, c, 0:W], in0=xT[:, c, 0:W], in1=src[:, c, 0:W], op=ALU.add),
                      [src_b[c], xT_b[c]], [xT_b[c]])

        def proj_chunks(wt, l, c0, nchunks, W, rhs, rhs_b, consume, kc=8, r0=0):
            j = 0
            pend = [None]
            while j < nchunks:
                nj = min(4, nchunks - j)
                s = load_w(wview(wt, l, r0, kc * 128, c0 + j * 128, nj * 128), kc, nj * 128)
                for jj in range(nj):
                    bk = bank()
                    mm_group(bk, 0, W, lambda k, s=s, jj=jj: wsl[s][:, k, jj * 128:(jj + 1) * 128],
                             lambda k: rhs[:, k, 0:W], kc, rhs_b, [wsl_b[s]])
                    d_ = consume(j + jj, bk)
                    if pend[0] is not None:
                        pend[0]()
                    pend[0] = d_ if callable(d_) else None
                w_done()
                j += nj
            if pend[0] is not None:
                pend[0]()

        def layer(l, W, samp, last_blk):
            NT = W // 128
            if stop <= 0:
                return
            rmsnorm(xT, xT_b, W, NW(l, 0), "h", l)
            if stop <= 1:
                return
            if samp:
                tin = t32f
                dma("sp", "scin", tin[0:48, 0:3072], sc[l], [], t32_b[0:6])
                for q4 in range(6):
                    bk = bank()
                    for cc in range(4):
                        ch = q4 * 4 + cc
                        T(lambda e, ch=ch, cc=cc, bk=bk: e.transpose(out=ps[:, bk, cc * 48:(cc + 1) * 48], in_=tin[0:48, ch * 128:(ch + 1) * 128],
                                                                    identity=c32[0:48, 0, 0:48]), t32_b[0:6] + [c32_b], [ps_b[bk]])
                    V(lambda e, q4=q4, bk=bk: e.tensor_copy(out=scin[:, q4 * 4:(q4 + 1) * 4, :],
                                                            in_=ps[:, bk, 0:192].rearrange("p (a b) -> p a b", a=4)),
                      [ps_b[bk]], [scin_b])
            tok_b = t32_b
            if samp:
                tokm = t32f[:, 0:3072]

            def qkv_consume(ch, bk):
                s = ch % 2
                if samp:
                    pv = pre[s][:, 0:176].rearrange("p (a b) -> p a b", a=16)
                    G(lambda e: e.tensor_copy(out=pv[:, :, 0:3], in_=scin[:, ch, :].rearrange("p (a b) -> p a b", a=16)),
                      [scin_b], [pre_b[s]])
                    A(lambda e: e.activation(out=pv[:, :, 3:11], in_=ps[:, bk, 0:128].rearrange("p (a b) -> p a b", a=16),
                                             func=AF.Copy), [ps_b[bk]], [pre_b[s]])
                    av = acc[s][:, 0:128].rearrange("p (a b) -> p a b", a=16)
                    A(lambda e: e.activation(out=acc[s][:, 0:128], in_=ps[:, bk, 0:128], func=AF.Copy, scale=CW(l, 3, ch)),
                      [ps_b[bk], ppt_b], [acc_b[s]])
                    for tap in (2, 1, 0):
                        V(lambda e, tap=tap: e.scalar_tensor_tensor(out=av, in0=pv[:, :, tap:tap + 8], scalar=CW(l, tap, ch),
                                                                    in1=av, op0=ALU.mult, op1=ALU.add),
                          [pre_b[s], acc_b[s], ppt_b], [acc_b[s]])
                    G(lambda e: e.tensor_copy(out=sq[s][:, 0:256].bitcast(F32).rearrange("p (a b) -> p a b", a=16),
                                              in_=pv[:, :, 3:11]), [pre_b[s]], [sq_b[s]])
                    b2 = bank()
                    T(lambda e: e.transpose(out=ps[:, b2, 0:128], in_=sq[s][:, 0:256].bitcast(F32), identity=I32),
                      [sq_b[s], c32_b], [ps_b[b2]])
                    V(lambda e: e.tensor_copy(out=tokm[:, ch * 128:(ch + 1) * 128], in_=ps[:, b2, 0:128]),
                      [ps_b[b2]], [tok_b[ch // 4]])
                else:
                    G(lambda e: e.tensor_copy(out=pre[s][:, 0:3], in_=carry[:, l, ch, :]), [carry_b[l][ch]], [pre_b[s]])
                    A(lambda e: e.activation(out=pre[s][:, 3:515], in_=ps[:, bk, :], func=AF.Copy), [ps_b[bk]], [pre_b[s]])
                    A(lambda e: e.activation(out=acc[s][:], in_=ps[:, bk, :], func=AF.Copy, scale=CW(l, 3, ch)),
                      [ps_b[bk], ppt_b], [acc_b[s]])
                    G(lambda e: e.tensor_copy(out=carry[:, l, ch, :], in_=pre[s][:, 512:515]), [pre_b[s]], [carry_b[l][ch]])
                    for tap in (2, 1, 0):
                        V(lambda e, tap=tap: e.scalar_tensor_tensor(out=acc[s][:], in0=pre[s][:, tap:tap + 512],
                                                                    scalar=CW(l, tap, ch), in1=acc[s][:],
                                                                    op0=ALU.mult, op1=ALU.add),
                          [pre_b[s], acc_b[s], ppt_b], [acc_b[s]])
                return lambda: A(lambda e: e.activation(out=R1[:, ch, 0:W], in_=acc[s][:, 0:W], func=AF.Silu), [acc_b[s]], [R1_b[ch]])

            proj_chunks("w_in", l, 0, 24, W, hT, hT_b, qkv_consume)
            if samp:
                for j in range(3):
                    dma("sp", "o_sc", sc_s[l, :, j, :], tokm[5 + j:128:8, :], tok_b[0:6], [])
            elif last_blk:
                for q4 in range(6):
                    bk = bank()
                    for cc in range(4):
                        ch = q4 * 4 + cc
                        T(lambda e, ch=ch, cc=cc: e.transpose(out=ps[0:3, bk, cc * 128:(cc + 1) * 128], in_=carry[:, l, ch, :],
                                                              identity=I32), [carry_b[l][ch], c32_b], [ps_b[bk]])
                    V(lambda e, q4=q4: e.tensor_copy(out=scrow[:, q4 * 512:(q4 + 1) * 512], in_=ps[0:3, bk, :]),
                      [ps_b[bk]], t32_b[0:6])
                dma("sp", "o_sc", sc_p[l], scrow, t32_b[0:6], [])

            if stop <= 2:
                return
            proj_chunks("w_in", l, 3072, 8, W, hT, hT_b,
                        lambda ch, bk: A(lambda e: e.activation(out=sz[:, ch, 0:W], in_=ps[:, bk, 0:W], func=AF.Silu),
                                         [ps_b[bk]], [sz_b[ch]]))
            s = load_w(wview("w_in", l, 0, 1024, 4096, 16), 8, 16)
            ba = bank()
            mm_group(ba, 0, W, lambda k: wsl[s][:, k, 0:8], lambda k: hT[:, k, 0:W], 8, hT_b, [wsl_b[s]], mp=8)
            bb = bank()
            mm_group(bb, 0, W, lambda k: wsl[s][:, k, 8:16], lambda k: hT[:, k, 0:W], 8, hT_b, [wsl_b[s]], mp=8)
            w_done()
            A(lambda e: e.activation(out=gT[:, 0:W], in_=ps[0:8, ba, 0:W], func=AF.Exp, bias=ppt[0:8, 260 + l:261 + l]),
              [ps_b[ba], ppt_b], [gT_b])
            A(lambda e: e.activation(out=gT[:, 0:W], in_=gT[:, 0:W], func=AF.Ln, bias=1.0), [gT_b], [gT_b])
            V(lambda e: e.tensor_scalar(out=gT[:, 0:W], in0=gT[:, 0:W], scalar1=nea[0:8, l:l + 1], scalar2=None, op0=ALU.mult),
              [gT_b, nea_b], [gT_b])
            A(lambda e: e.activation(out=bT[:, 0:W], in_=ps[0:8, bb, 0:W], func=AF.Sigmoid), [ps_b[bb]], [bT_b])
            bn_ = bank()
            for j in range(16):
                s2 = j % 2
                A(lambda e, j=j, s2=s2: e.activation(out=sq[s2][:, 0:W], in_=R1[:, j, 0:W], func=AF.Square),
                  [R1_b[j]], [sq_b[s2]])
                T(lambda e, j=j, s2=s2: e.matmul(ps[0:16, bn_, 0:W], sel16[:, j * 16:(j + 1) * 16], sq[s2][:, 0:W],
                                                 start=(j == 0), stop=(j == 15)), [sq_b[s2], c16_b], [ps_b[bn_]])
            rsqrt_from_psum(rn[:, 0:W], ps[0:16, bn_, 0:W], 1.0, 1e-6, [ps_b[bn_]], [rn_b])
            V(lambda e: e.tensor_copy(out=rnb[:, 0:W], in_=rn[:, 0:W]), [rn_b], [rnb_b])
            V(lambda e: e.tensor_tensor(out=rnl[:, 0:W], in0=rn[:, 0:W], in1=rnb[:, 0:W], op=ALU.subtract), [rn_b, rnb_b], [rnb_b])
            for j in range(16):
                bk = bank()
                def fbc(e, j=j, bk=bk):
                    e.matmul(ps[:, bk, 0:W], selT[:, j, :], rnb[:, 0:W], start=True, stop=False)
                    return e.matmul(ps[:, bk, 0:W], selT[:, j, :], rnl[:, 0:W], start=False, stop=True)
                T(fbc, [rnb_b, selT_b], [ps_b[bk]])
                V(lambda e, j=j, bk=bk: e.tensor_tensor(out=R1[:, j, 0:W], in0=R1[:, j, 0:W], in1=ps[:, bk, 0:W], op=ALU.mult),
                  [R1_b[j], ps_b[bk]], [R1_b[j]])

            if stop <= 3:
                return
            cU, cSL, cN1, cN2 = (4, 5, 4, 5) if samp else (2, 3, 2, 3)
            nlev = 3 if samp else 6

            def bc8(ap):
                return ap.unsqueeze(2).to_broadcast([128, 8, 128])

            def bcm(ap):
                return ap.unsqueeze(1).to_broadcast([128, 8, 128])

            def PV(i):
                return ps[:, i:i + 2, :].rearrange("p a (b c) -> p (a b) c", c=128)

            class Alloc:
                def __init__(self, lo, n):
                    self.lo, self.n, self.p = lo, n, 0

                def bank(self):
                    b_ = self.p % self.n
                    self.p = (b_ + 1) % self.n
                    return self.lo + b_

                def pair(self):
                    b_ = self.p % self.n
                    if b_ % 2:
                        b_ = (b_ + 1) % self.n
                    self.p = (b_ + 2) % self.n
                    return self.lo + b_

            a1 = Alloc(0, 4)
            a2 = Alloc(0, 8)
            qb = R1_b[0:8]

            def chunk_gen(t):
                cs = slice(t * 128, (t + 1) * 128)
                st = 0 if samp else t % 2
                Kd_, Kd_b_ = Kd2[st], Kd2_b[st]
                Vb_, Vb_b_ = Vb2[st], Vb2_b[st]
                qt_, qt_b_ = qt2[st], qt2_b[st]
                PTm_, PTm_b_ = PTm2[st], PTm2_b[st]
                TT, TT_b = Pm2[st], Pm2_b[st]
                Bo_, Bo_b_ = Bo2[st], Bo2_b[st]
                neg_, neg_b_ = neg2[st], neg2_b[st]
                es_, es_b_ = es2[st], es2_b[st]
                bK = a1.bank()
                for h in range(H):
                    T(lambda e, h=h: e.transpose(out=P16(bK)[:, h * 128:(h + 1) * 128], in_=R1[:, 8 + h, cs], identity=I16),
                      [R1_b[8 + h], c16_b], [ps_b[bK]])
                bV = a1.bank()
                for h in range(H):
                    T(lambda e, h=h: e.transpose(out=P16(bV)[:, h * 128:(h + 1) * 128], in_=R1[:, 16 + h, cs], identity=I16),
                      [R1_b[16 + h], c16_b], [ps_b[bV]])
                bG = a1.bank()
                T(lambda e: e.transpose(out=ps[:, bG, 0:8], in_=gT[:, cs], identity=c32[0:8, 0, 0:8]), [gT_b, c32_b], [ps_b[bG]])
                T(lambda e: e.transpose(out=ps[:, bG, 8:16], in_=bT[:, cs], identity=c32[0:8, 0, 0:8]), [bT_b, c32_b], [ps_b[bG]])
                V(lambda e: e.tensor_copy(out=gbt[:], in_=ps[:, bG, 0:16]), [ps_b[bG]], [gbt_b])
                yield 1
                bS = a1.bank()
                cLast = 6 if samp else 1
                for i3, ci in enumerate((cU, cSL, cLast)):
                    T(lambda e, i3=i3, ci=ci: e.matmul(ps[:, bS, i3 * 8:(i3 + 1) * 8], c32[:, ci, :], gbt[:, 0:8],
                                                       start=True, stop=True), [gbt_b, c32_b], [ps_b[bS]])
                A(lambda e: e.activation(out=es_[:], in_=ps[:, bS, 0:24], func=AF.Exp), [ps_b[bS]], [es_b_])
                V(lambda e: e.tensor_tensor(out=kbs[:], in0=gbt[:, 8:16], in1=es_[:, 0:8], op=ALU.mult), [gbt_b, es_b_], [kbs_b])
                yield 1
                KP = P16(bK).rearrange("p (a b) -> p a b", a=8)
                VP = P16(bV).rearrange("p (a b) -> p a b", a=8)
                V(lambda e: e.tensor_tensor(out=Kd_, in0=KP, in1=bc8(es_[:, 8:16]), op=ALU.mult), [ps_b[bK], es_b_], [Kd_b_])
                V(lambda e: e.tensor_tensor(out=Vb_, in0=VP, in1=bc8(gbt[:, 8:16]), op=ALU.mult), [ps_b[bV], gbt_b], [Vb_b_])
                V(lambda e: e.tensor_scalar(out=neg_[:], in0=kbs[:], scalar1=-1.0, scalar2=None, op0=ALU.mult), [kbs_b], [neg_b_])
                yield 1
                G(lambda e: e.tensor_tensor(out=Gm1[:], in0=bcm(c32[:, cSL, :]), in1=bc8(gbt[:, 0:8]), op=ALU.mult),
                  [c32_b, gbt_b], [Gm1_b])
                V(lambda e: e.tensor_tensor(out=Gm2[:], in0=bcm(c32[:, cU, :]), in1=bc8(gbt[:, 0:8]), op=ALU.mult),
                  [c32_b, gbt_b], [Gm2_b])
                yield 1
                pD = a1.pair()
                pT = a1.pair()
                for hh in range(2):
                    g1 = Gm1[:, hh * 4:(hh + 1) * 4, :].rearrange("p a b -> p (a b)")
                    g2 = Gm2[:, hh * 4:(hh + 1) * 4, :].rearrange("p a b -> p (a b)")

                    def fD(e, hh=hh, g1=g1):
                        e.matmul(ps[:, pD + hh, :], c32[:, cU, :], g1, start=True, stop=False)
                        ins = None
                        for q in range(4):
                            ins = e.matmul(ps[:, pD + hh, q * 128:(q + 1) * 128], I16, c16[:, cN1, :], start=False, stop=(q == 3))
                        return ins
                    T(fD, [Gm1_b, c32_b, c16_b], [ps_b[pD + hh]])

                    def fT(e, hh=hh, g2=g2):
                        e.matmul(ps[:, pT + hh, :], c32[:, cSL, :], g2, start=True, stop=False)
                        ins = None
                        for q in range(4):
                            ins = e.matmul(ps[:, pT + hh, q * 128:(q + 1) * 128], I16, c16[:, cN2, :], start=False, stop=(q == 3))
                        return ins
                    T(fT, [Gm2_b, c32_b, c16_b], [ps_b[pT + hh]])
                yield 1
                A(lambda e: e.activation(out=dsm[:].rearrange("p a b -> p (a b)"), in_=P32(pD, 2), func=AF.Exp),
                  [ps_b[pD], ps_b[pD + 1]], [dsm_b])
                pR = a1.pair()
                for hh in range(2):
                    g2 = Gm2[:, hh * 4:(hh + 1) * 4, :].rearrange("p a b -> p (a b)")
                    T(lambda e, hh=hh, g2=g2: e.matmul(ps[:, pR + hh, :], ONES32, g2, start=True, stop=True),
                      [Gm2_b, c32_b], [ps_b[pR + hh]])
                A(lambda e: e.activation(out=dTm[:].rearrange("p a b -> p (a b)"), in_=P32(pT, 2), func=AF.Exp),
                  [ps_b[pT], ps_b[pT + 1]], [dTm_b])
                A(lambda e: e.activation(out=reg[:].rearrange("p a b -> p (a b)"), in_=P32(pR, 2), func=AF.Exp),
                  [ps_b[pR], ps_b[pR + 1]], [reg_b])
                yield 1
                V(lambda e: e.tensor_tensor(out=qt_, in0=R1[:, 0:8, cs], in1=reg[:], op=ALU.mult), qb + [reg_b], [qt_b_])
                pG = a1.pair()
                pP = a1.pair()
                for h in range(H):
                    T(lambda e, h=h: e.matmul(PV(pG)[:, h, :], R1[:, 8 + h, cs], R1[:, 8 + h, cs], start=True, stop=True),
                      [R1_b[8 + h]], [ps_b[pG + h // 4]])
                for h in range(H):
                    T(lambda e, h=h: e.matmul(PV(pP)[:, h, :], R1[:, 8 + h, cs], R1[:, h, cs], start=True, stop=True),
                      [R1_b[8 + h], R1_b[h]], [ps_b[pP + h // 4]])
                yield 1
                for h in range(H):
                    V(lambda e, h=h: e.scalar_tensor_tensor(out=Am[0][:, h, :], in0=PV(pG)[:, h, :], scalar=gbt[:, 8 + h:9 + h],
                                                            in1=dsm[:, h, :], op0=ALU.mult, op1=ALU.mult),
                      [ps_b[pG + h // 4], gbt_b, dsm_b], [Am_b[0]])
                V(lambda e: e.tensor_tensor(out=PTm_, in0=PV(pP), in1=dTm[:], op=ALU.mult),
                  [ps_b[pP], ps_b[pP + 1], dTm_b], [PTm_b_])
                yield 1
                bB = a1.bank()
                for h in range(H):
                    T(lambda e, h=h: e.transpose(out=P16(bB)[:, h * 128:(h + 1) * 128], in_=Am[0][:, h, :], identity=I16),
                      [Am_b[0], c16_b], [ps_b[bB]])
                BP = P16(bB).rearrange("p (a b) -> p a b", a=8)
                if samp:
                    A(lambda e: e.activation(out=Bm[0], in_=BP, func=AF.Copy), [ps_b[bB]], [Bm_b[0]])
                else:
                    A(lambda e: e.activation(out=Bm[1], in_=BP, func=AF.Copy), [ps_b[bB]], [Bm_b[1]])
                    yield 1
                    G(lambda e: e.tensor_tensor(out=Bm[0], in0=Bm[1], in1=bcm(c16[:, 8, :]), op=ALU.mult), [Bm_b[1], c16_b], [Bm_b[0]])
                    G(lambda e: e.tensor_tensor(out=Am[0], in0=Am[0], in1=bcm(c16[:, 8, :]), op=ALU.mult), [Am_b[0], c16_b], [Am_b[0]])
                    G(lambda e: e.tensor_tensor(out=Bo_, in0=Bm[1], in1=bcm(c16[:, 9, :]), op=ALU.mult), [Bm_b[1], c16_b], [Bo_b_])
                yield 1
                V(lambda e: e.tensor_tensor(out=TT, in0=bcm(I32), in1=Bm[0], op=ALU.subtract), [Bm_b[0], c32_b], [TT_b])
                V(lambda e: e.tensor_tensor(out=Pf, in0=bcm(I32), in1=Bm[0], op=ALU.subtract), [Bm_b[0], c32_b], [rstd_b])
                ca = 0
                for lev in range(1, nlev):
                    na = 1 - ca
                    pA = a1.pair()
                    for h in range(H):
                        T(lambda e, h=h: e.matmul(PV(pA)[:, h, :], Bm[ca][:, h, :], Am[ca][:, h, :], start=True, stop=True),
                          [Am_b[ca], Bm_b[ca]], [ps_b[pA + h // 4]])
                    A(lambda e: e.activation(out=Am[na], in_=PV(pA), func=AF.Copy), [ps_b[pA], ps_b[pA + 1]], [Am_b[na]])
                    yield 1
                    if lev < nlev - 1:
                        pB = a1.pair()
                        for h in range(H):
                            T(lambda e, h=h: e.matmul(PV(pB)[:, h, :], Am[ca][:, h, :], Bm[ca][:, h, :], start=True, stop=True),
                              [Am_b[ca], Bm_b[ca]], [ps_b[pB + h // 4]])
                        A(lambda e: e.activation(out=Bm[na], in_=PV(pB), func=AF.Copy), [ps_b[pB], ps_b[pB + 1]], [Bm_b[na]])
                        yield 1
                    pU = a1.pair()
                    for h in range(H):
                        T(lambda e, h=h: e.matmul(PV(pU)[:, h, :], Am[na][:, h, :], TT[:, h, :], start=True, stop=True),
                          [Am_b[na], TT_b], [ps_b[pU + h // 4]])
                    V(lambda e: e.tensor_tensor(out=TT, in0=PV(pU), in1=Pf, op=ALU.add), [ps_b[pU], ps_b[pU + 1], rstd_b], [TT_b])
                    if lev < nlev - 1:
                        V(lambda e: e.tensor_tensor(out=Pf, in0=PV(pU), in1=Pf, op=ALU.add), [ps_b[pU], ps_b[pU + 1], rstd_b], [rstd_b])
                    yield 1
                    ca = na
                yield "S2"
                r32 = on32[:]
                R32b = [on32_b]
                if samp:
                    pK = a2.pair(); pN = a2.pair(); pO = a2.pair()
                    other = [b_ for b_ in range(8) if b_ not in (pK, pK + 1, pN, pN + 1, pO, pO + 1)]
                    V(lambda e: e.tensor_tensor(out=gsel[:], in0=gbt[:, 0:8].unsqueeze(1).to_broadcast([128, 16, 8]),
                                                in1=seq32[:].unsqueeze(2).to_broadcast([128, 16, 8]), op=ALU.mult),
                      [gbt_b, cb32_b], [gsel_b])
                    bE = other[0]
                    T(lambda e: e.matmul(ps[:, bE, 0:128], ONES32, gsel[:].rearrange("p a b -> p (a b)"), start=True, stop=True),
                      [gsel_b, c32_b], [ps_b[bE]])
                    A(lambda e: e.activation(out=egl[:].rearrange("p a b -> p (a b)"), in_=ps[:, bE, 0:128], func=AF.Exp),
                      [ps_b[bE]], [egl_b])
                    for h in range(H):
                        if h % 2 == 0:
                            s0f, s0f_b, s0b, s0b_b = s0f_0, s0f_b0, s0b_0, s0b_b0
                        else:
                            s0f = t32f[:, 0:2048].rearrange("p (a b) -> p a b", a=16); s0f_b = t32_b[0:4]
                            s0b = t16[:, 4096:6144].rearrange("p (a b) -> p a b", a=16); s0b_b = t32_b[4:6]
                        dma("sp", f"s0in{h % 2}", s0f, sd[l, :, h, :, :].rearrange("s k v -> k s v"), [], s0f_b)
                        A(lambda e: e.activation(out=s0b, in_=s0f, func=AF.Copy), s0f_b, s0b_b)
                        V(lambda e, h=h: e.tensor_tensor(out=msk16[:], in0=R1[:, 8 + h, cs].unsqueeze(1).to_broadcast([128, 16, 128]),
                                                         in1=mskc[:], op=ALU.mult), [R1_b[8 + h], cb16_b], [msk16_b])

                        def fks(e, h=h):
                            ins = None
                            for s_ in range(16):
                                ins = e.matmul(PV(pK)[:, h, :], msk16[:, s_, :], s0b[:, s_, :], start=(s_ == 0), stop=(s_ == 15))
                            return ins
                        T(fks, [msk16_b] + s0b_b, [ps_b[pK + h // 4]])
                        V(lambda e, h=h: e.scalar_tensor_tensor(out=r32[:, h, :], in0=PV(pK)[:, h, :], scalar=neg_[:, h:h + 1],
                                                                in1=Vb_[:, h, :], op0=ALU.mult, op1=ALU.add),
                          [ps_b[pK + h // 4], neg_b_, Vb_b_], R32b)
                        V(lambda e, h=h: e.tensor_copy(out=r16[:, h, :], in_=r32[:, h, :]), R32b, [r16_b])
                        T(lambda e, h=h: e.matmul(PV(pN)[:, h, :], TT[:, h, :], r16[:, h, :], start=True, stop=True),
                          [TT_b, r16_b], [ps_b[pN + h // 4]])
                        A(lambda e, h=h: e.activation(out=vn[:, h, :], in_=PV(pN)[:, h, :], func=AF.Copy), [ps_b[pN + h // 4]], [vn_b])
                        V(lambda e, h=h: e.tensor_tensor(out=msk16[:], in0=qt_[:, h, :].unsqueeze(1).to_broadcast([128, 16, 128]),
                                                         in1=mskc[:], op=ALU.mult), [qt_b_, cb16_b], [msk16_b])

                        def fo(e, h=h):
                            for s_ in range(16):
                                e.matmul(PV(pO)[:, h, :], s0b[:, s_, :], msk16[:, s_, :], start=(s_ == 0), stop=False)
                            return e.matmul(PV(pO)[:, h, :], vn[:, h, :], PTm_[:, h, :], start=False, stop=True)
                        T(fo, s0b_b + [msk16_b, vn_b, PTm_b_], [ps_b[pO + h // 4]])
                        V(lambda e, h=h: e.tensor_tensor(out=msk16[:], in0=Kd_[:, h, :].unsqueeze(1).to_broadcast([128, 16, 128]),
                                                         in1=seqs[:].unsqueeze(2).to_broadcast([128, 16, 128]), op=ALU.mult),
                          [Kd_b_, cb16_b], [msk16_b])
                        for q in range(4):
                            bq = other[1 + (h * 4 + q) % (len(other) - 1)]

                            def fs(e, h=h, q=q, bq=bq):
                                ins = None
                                for s4 in range(4):
                                    ins = e.matmul(ps[:, bq, s4 * 128:(s4 + 1) * 128], msk16[:, q * 4 + s4, :], vn[:, h, :],
                                                   start=True, stop=True)
                                return ins
                            T(fs, [msk16_b, vn_b], [ps_b[bq]])
                            for s4 in range(4):
                                s_ = q * 4 + s4
                                V(lambda e, h=h, s_=s_, s4=s4, bq=bq, q=q: e.scalar_tensor_tensor(
                                    out=snew[q % 2][:, s4, :], in0=s0f[:, s_, :], scalar=egl[:, s_, h:h + 1],
                                    in1=ps[:, bq, s4 * 128:(s4 + 1) * 128], op0=ALU.mult, op1=ALU.add),
                                  s0f_b + [egl_b, ps_b[bq]], [snew_b[q % 2]])
                            dma("sp", f"s0out{q % 2}", sd_s[l, q * 4:(q + 1) * 4, h, :, :].rearrange("s k v -> k s v"), snew[q % 2],
                                [snew_b[q % 2]], [])
                    pQ = a2.pair()
                else:
                    pK, pN, pO, pS, pQ = 4, 6, 4, 6, 6
                    pKv = PV(pK); pNv = PV(pN)
                    for h in range(H):
                        T(lambda e, h=h: e.matmul(PV(pK)[:, h, :], R1[:, 8 + h, cs], S16[:, l, h, :], start=True, stop=True),
                          [R1_b[8 + h], S16_b[l]], [ps_b[pK + h // 4]])
                    V(lambda e: e.tensor_tensor(out=r32, in0=pKv, in1=bc8(neg_[:]), op=ALU.mult), [ps_b[pK], ps_b[pK + 1], neg_b_], R32b)
                    yield 1
                    V(lambda e: e.tensor_tensor(out=r32, in0=r32, in1=Vb_, op=ALU.add), R32b + [Vb_b_], R32b)
                    A(lambda e: e.activation(out=r16[:], in_=r32, func=AF.Copy), R32b, [r16_b])
                    yield 1
                    for h in range(H):
                        T(lambda e, h=h: e.matmul(PV(pN)[:, h, :], TT[:, h, :], r16[:, h, :], start=True, stop=True),
                          [TT_b, r16_b], [ps_b[pN + h // 4]])
                    A(lambda e: e.activation(out=vn[:], in_=pNv, func=AF.Copy), [ps_b[pN], ps_b[pN + 1]], [vn_b])
                    yield 1
                    for h in range(H):
                        T(lambda e, h=h: e.matmul(PV(pK)[:, h, :], Bo_[:, h, :], vn[:, h, :], start=True, stop=True),
                          [Bo_b_, vn_b], [ps_b[pK + h // 4]])
                    V(lambda e: e.tensor_tensor(out=r16[:], in0=r32, in1=pKv, op=ALU.subtract), R32b + [ps_b[pK], ps_b[pK + 1]], [r16_b])
                    yield 1
                    for h in range(H):
                        T(lambda e, h=h: e.matmul(PV(pN)[:, h, :], TT[:, h, :], r16[:, h, :], start=True, stop=True),
                          [TT_b, r16_b], [ps_b[pN + h // 4]])
                    A(lambda e: e.activation(out=vn[:], in_=pNv, func=AF.Copy), [ps_b[pN], ps_b[pN + 1]], [vn_b])
                    yield 1
                    for h in range(H):
                        def fo(e, h=h):
                            e.matmul(PV(pO)[:, h, :], S16[:, l, h, :], qt_[:, h, :], start=True, stop=False)
                            return e.matmul(PV(pO)[:, h, :], vn[:, h, :], PTm_[:, h, :], start=False, stop=True)
                        T(fo, [S16_b[l], qt_b_, vn_b, PTm_b_], [ps_b[pO + h // 4]])
                    yield 1
                    for h in range(H):
                        T(lambda e, h=h: e.matmul(PV(pS)[:, h, :], Kd_[:, h, :], vn[:, h, :], start=True, stop=True),
                          [Kd_b_, vn_b], [ps_b[pS + h // 4]])
                    A(lambda e: e.activation(out=sq[0][:], in_=P32(pO, 2), func=AF.Square, scale=float(128 ** -0.5)),
                      [ps_b[pO], ps_b[pO + 1]], [sq_b[0]])
                    yield 1
                    for h in range(H):
                        V(lambda e, h=h: e.scalar_tensor_tensor(out=S32[:, l, h, :], in0=S32[:, l, h, :], scalar=es_[:, 16 + h:17 + h],
                                                                in1=PV(pS)[:, h, :], op0=ALU.mult, op1=ALU.add),
                          [S32_b[l], es_b_, ps_b[pS + h // 4]], [S32_b[l]])
                    A(lambda e: e.activation(out=S16[:, l, :, :], in_=S32[:, l, :, :], func=AF.Copy), [S32_b[l]], [S16_b[l]])
                    yield 1
                if samp:
                    A(lambda e: e.activation(out=sq[0][:], in_=P32(pO, 2), func=AF.Square, scale=float(128 ** -0.5)),
                      [ps_b[pO], ps_b[pO + 1]], [sq_b[0]])
                for hh in range(2):
                    T(lambda e, hh=hh: e.matmul(ps[:, pQ + hh, :], ONES16, sq[0][:, hh * 512:(hh + 1) * 512], start=True, stop=True),
                      [sq_b[0], c16_b], [ps_b[pQ + hh]])
                rsqrt_from_psum(rstd9, P32(pQ, 2), 1.0 / 128, 1e-6, [ps_b[pQ], ps_b[pQ + 1]], t32_b[6:8])
                yield 1
                V(lambda e: e.scalar_tensor_tensor(out=on32[:].rearrange("p a b -> p (a b)"), in0=P32(pO, 2), scalar=dnw[:, l:l + 1],
                                                   in1=rstd9, op0=ALU.mult, op1=ALU.mult),
                  [ps_b[pO], ps_b[pO + 1], dnw_b] + t32_b[6:8], [on32_b])
                G(lambda e: e.tensor_tensor(out=oa[:, :, cs], in0=on32[:], in1=sz[:, :, cs], op=ALU.mult),
                  [on32_b] + sz_b, oa_b)
                yield 1

            cur = None
            for g in [chunk_gen(t) for t in range(NT)] + [None]:
                g_done = g is None
                c_done = cur is None
                while not (g_done and c_done):
                    if not c_done:
                        try:
                            next(cur)
                        except StopIteration:
                            c_done = True
                    if not g_done:
                        if next(g) == "S2":
                            g_done = True
                cur = g

            if (not samp) and last_blk:
                dma("sp", "o_sd", sd_p[l].rearrange("h k v -> k h v"), S32[:, l, :, :], [S32_b[l]], [])

            if stop <= 4:
                return
            def uv_consume(ch, bk):
                A(lambda e: e.activation(out=R1[:, ch, 0:W], in_=ps[:, bk, 0:W], func=AF.Gelu), [ps_b[bk]], [R1_b[ch]])
            proj_chunks("w_in", l, 4112, 16, W, hT, hT_b, uv_consume)
            boff = (l * 2 + (1 if samp else 0)) * 1024
            dma("sp", "lnw", t32f[:, 0:2048], lnw[l].rearrange("b d -> (b d)").partition_broadcast(128), [], t32_b[0:4])
            dma("pool", "prw", prw16[:], prow[:, boff:boff + 1024], [], [prw_b])
            lnt16 = t16[:, 4096:6144].rearrange("p (a b) -> p a b", a=2)
            A(lambda e: e.activation(out=t16[:, 4096:6144], in_=t32f[:, 0:2048], func=AF.Copy), t32_b[0:4], t32_b[4:6])
            dma("sp", "wsp", ws32[:], wsp[l, 1 if samp else 0], [], [ws32_b])
            V(lambda e: e.tensor_tensor(out=wsT[:], in0=ws32[:].rearrange("p (a b) -> p a b", a=8), in1=bcm(c32[:, cU, :]),
                                        op=ALU.mult), [ws32_b, c32_b], [wsT_b])
            for t in range(NT):
                cs = slice(t * 128, (t + 1) * 128)
                bk = bank()
                for g in range(8):
                    T(lambda e, g=g: e.transpose(out=P16(bk)[:, g * 128:(g + 1) * 128], in_=R1[:, 8 + g, cs], identity=I16),
                      [vT_b[g], c16_b], [ps_b[bk]])
                for hh in range(2):
                    V(lambda e, hh=hh: e.bn_stats(out=bnst[:, hh, :], in_=P16(bk)[:, hh * 512:(hh + 1) * 512]), [ps_b[bk]], [bn_b])
                V(lambda e: e.bn_aggr(out=mv[:, 0:2], in_=bnst[:].rearrange("p a b -> p (a b)")), [bn_b], [bn_b])
                A(lambda e: e.activation(out=mv[:, 2:3], in_=mv[:, 1:2], func=AF.Ln, bias=1e-5), [bn_b], [bn_b])
                A(lambda e: e.activation(out=mv[:, 2:3], in_=mv[:, 2:3], func=AF.Exp, scale=-0.5), [bn_b], [bn_b])
                V(lambda e: e.tensor_scalar(out=vnb[:], in0=P16(bk), scalar1=mv[:, 0:1], scalar2=mv[:, 2:3],
                                            op0=ALU.subtract, op1=ALU.mult), [ps_b[bk], bn_b], [vnb_b])
                V(lambda e: e.tensor_tensor(out=vnb[:], in0=vnb[:], in1=lnt16[:, 0, :], op=ALU.mult), [vnb_b] + t32_b[4:6], [vnb_b])
                V(lambda e: e.tensor_tensor(out=vnb[:], in0=vnb[:], in1=lnt16[:, 1, :], op=ALU.add), [vnb_b] + t32_b[4:6], [vnb_b])
                if samp or (last_blk and t == NT - 1):
                    A(lambda e: e.activation(out=lnv32, in_=vnb[:], func=AF.Copy), [vnb_b], [lnv32_b])
                    dma("sp", "o_cv", cv_s[l] if samp else cv_p[l], lnv32, [lnv32_b], [])
                pM = pair()
                for g in range(8):
                    def fm(e, g=g):
                        e.matmul(PV(pM)[:, g, :], vnb[:, g * 128:(g + 1) * 128], wsT[:, g, :], start=True, stop=False)
                        return e.matmul(PV(pM)[:, g, :], c16[0:1, 1, :], prw16[0:1, g * 128:(g + 1) * 128],
                                        start=False, stop=True)
                    T(fm, [vnb_b, wsT_b, c16_b, prw_b], [ps_b[pM + g // 4]])
                V(lambda e: e.tensor_tensor(out=R1[:, 0:8, cs], in0=R1[:, 0:8, cs], in1=PV(pM), op=ALU.mult),
                  uT_b + [ps_b[pM], ps_b[pM + 1]], uT_b)

            if stop <= 5:
                return
            mg, mg_b = sz, sz_b
            for half in range(2):
                sA = load_w(wview("w_in", l, 0, 1024, 6160 + half * 512, 512), 8, 512)
                sPA = load_w(wview("w_pa", l, 0, 1024, half * 512, 512), 8, 512)
                for jj in range(4):
                    d = half * 4 + jj
                    b1 = bank()
                    mm_group(b1, 0, W, lambda k, jj=jj: wsl[sA][:, k, jj * 128:(jj + 1) * 128], lambda k: hT[:, k, 0:W], 8,
                             hT_b, [wsl_b[sA]])
                    b2 = bank()
                    mm_group(b2, 0, W, lambda k, jj=jj: wsl[sPA][:, k, jj * 128:(jj + 1) * 128], lambda k: oa[:, k, 0:W], 8,
                             oa_b, [wsl_b[sPA]])
                    A(lambda e, b1=b1: e.activation(out=acc[0][:, 0:W], in_=ps[:, b1, 0:W], func=AF.Sigmoid), [ps_b[b1]], [acc_b[0]])
                    V(lambda e, d=d, b2=b2: e.tensor_tensor(out=t32[:, d, 0:W], in0=acc[0][:, 0:W], in1=ps[:, b2, 0:W], op=ALU.mult),
                      [acc_b[0], ps_b[b2]], [t32_b[d]])
                w_done(2)
            for half in range(2):
                sB = load_w(wview("w_in", l, 0, 1024, 7184 + half * 512, 512), 8, 512)
                sPB = load_w(wview("w_pb", l, 0, 1024, half * 512, 512), 8, 512)
                for jj in range(4):
                    d = half * 4 + jj
                    b1 = bank()
                    mm_group(b1, 0, W, lambda k, jj=jj: wsl[sB][:, k, jj * 128:(jj + 1) * 128], lambda k: hT[:, k, 0:W], 8,
                             hT_b, [wsl_b[sB]])
                    b2 = bank()
                    mm_group(b2, 0, W, lambda k, jj=jj: wsl[sPB][:, k, jj * 128:(jj + 1) * 128], lambda k: R1[:, k, 0:W], 8,
                             uT_b, [wsl_b[sPB]])
                    A(lambda e, b1=b1: e.activation(out=acc[1][:, 0:W], in_=ps[:, b1, 0:W], func=AF.Sigmoid), [ps_b[b1]], [acc_b[1]])
                    V(lambda e, b2=b2: e.tensor_tensor(out=acc[1][:, 0:W], in0=acc[1][:, 0:W], in1=ps[:, b2, 0:W], op=ALU.mult),
                      [acc_b[1], ps_b[b2]], [acc_b[1]])
                    G(lambda e, d=d: e.tensor_tensor(out=mg[:, d, 0:W], in0=acc[1][:, 0:W], in1=t32[:, d, 0:W], op=ALU.add),
                      [acc_b[1], t32_b[d]], [mg_b[d]])
                w_done(2)
            proj_chunks("w_o", l, 0, 8, W, mg, mg_b,
                        lambda ch, bk: A(lambda e: e.activation(out=t32[:, ch, 0:W], in_=ps[:, bk, 0:W], func=AF.Copy),
                                         [ps_b[bk]], [t32_b[ch]]))
            rmsnorm(t32, t32_b, W, NW(l, 1), "x", l)
            if stop <= 6:
                return
            rmsnorm(xT, xT_b, W, NW(l, 2), "h", l)
            hid, hid_b = R1, R1_b
            proj_chunks("w_f1", l, 0, 22, W, hT, hT_b,
                        lambda ch, bk: A(lambda e: e.activation(out=hid[:, ch, 0:W], in_=ps[:, bk, 0:W], func=AF.Silu),
                                         [ps_b[bk]], [hid_b[ch]]))
            proj_chunks("w_f1", l, DFF, 22, W, hT, hT_b,
                        lambda ch, bk: V(lambda e: e.tensor_tensor(out=hid[:, ch, 0:W], in0=hid[:, ch, 0:W], in1=ps[:, bk, 0:W],
                                                                   op=ALU.mult), [hid_b[ch], ps_b[bk]], [hid_b[ch]]))
            for half in range(2):
                bks = [bank() for _ in range(4)]
                for kg, (k0, nk) in enumerate(((0, 8), (8, 8), (16, 6))):
                    s = load_w(wview("w_f2", l, k0 * 128, nk * 128, half * 512, 512), nk, 512)
                    for jj in range(4):
                        def ff(e, s=s, jj=jj, k0=k0, nk=nk, kg=kg):
                            ins = None
                            for k in range(nk):
                                ins = e.matmul(ps[:, bks[jj], 0:W], wsl[s][:, k, jj * 128:(jj + 1) * 128], hid[:, k0 + k, 0:W],
                                               start=(kg == 0 and k == 0), stop=(kg == 2 and k == nk - 1))
                            return ins
                        T(ff, hid_b[k0:k0 + nk] + [wsl_b[s]], [ps_b[bks[jj]]])
                    w_done()
                for jj in range(4):
                    d = half * 4 + jj
                    A(lambda e, d=d, jj=jj: e.activation(out=t32[:, d, 0:W], in_=ps[:, bks[jj], 0:W], func=AF.Copy),
                      [ps_b[bks[jj]]], [t32_b[d]])
            rmsnorm(t32, t32_b, W, NW(l, 3), "x", l)

        def load_x(src, W):
            for t in range(W // 128):
                dma("sp", "xin", xin, src[t * 128:(t + 1) * 128, :], [], t32_b[0:2])
                for hh in range(2):
                    bk = bank()
                    for c4 in range(4):
                        c = hh * 4 + c4
                        T(lambda e, c=c, c4=c4, bk=bk: e.transpose(out=ps[:, bk, c4 * 128:(c4 + 1) * 128], in_=xin[:, c * 128:(c + 1) * 128],
                                                                  identity=I32), t32_b[0:2] + [c32_b], [ps_b[bk]])
                    A(lambda e, hh=hh, bk=bk, t=t: e.activation(out=xT[:, hh * 4:(hh + 1) * 4, t * 128:(t + 1) * 128],
                                                               in_=ps[:, bk, :].rearrange("p (a b) -> p a b", a=4), func=AF.Copy),
                      [ps_b[bk]], xT_b[hh * 4:(hh + 1) * 4])

        def store_y(dst, W):
            for t in range(W // 128):
                yo = t % 2
                for hh in range(2):
                    bk = bank()
                    for c4 in range(4):
                        c = hh * 4 + c4
                        T(lambda e, c=c, c4=c4, bk=bk, t=t: e.transpose(out=ps[:, bk, c4 * 128:(c4 + 1) * 128],
                                                                       in_=xT[:, c, t * 128:(t + 1) * 128], identity=I32),
                          [xT_b[c], c32_b], [ps_b[bk]])
                    A(lambda e, hh=hh, bk=bk, yo=yo: e.activation(out=yout[yo][:, hh * 512:(hh + 1) * 512], in_=ps[:, bk, :],
                                                                  func=AF.Copy), [ps_b[bk]], t32_b[2 + 2 * yo:4 + 2 * yo])
                dma("sp", f"yo{yo}", dst[t * 128:(t + 1) * 128, :], yout[yo], t32_b[2 + 2 * yo:4 + 2 * yo], [])

        G(lambda e: e.memset(S32[:].rearrange("p a b c -> p (a b c)"), 0.0), [], S32_b)
        G(lambda e: e.memset(S16[:].rearrange("p a b c -> p (a b c)"), 0.0), [], S16_b)
        G(lambda e: e.memset(carry[:].rearrange("p a b c -> p (a b c)"), 0.0), [], carry_b[0] + carry_b[1])

        for blk in range(nblk):
            load_x(xp[blk * 512:(blk + 1) * 512, :], 512)
            for l in range(DEPTH):
                E.epoch += 1
                layer(l, 512, False, blk == nblk - 1)
            store_y(y_p[blk * 512:(blk + 1) * 512, :], 512)
        E.epoch += 1
        if do_samp:
            if all_reqs is None:
                SAMP_K0[0] = slot_i[0]
            samp_k0[0] = slot_i[0]
            for hf in range(2):
                dma("pool", "cst16", mskc[:, hf * 8:(hf + 1) * 8, :].rearrange("p a b -> p (a b)"), cbd[:, hf * 1024:(hf + 1) * 1024],
                    [], [cb16_b, wsl_b[3], wsl_b[4], msk16_b, scin_b, snew_b[0]])
            mask_loaded[0] = True
            if all_reqs is not None:
                pump()
            load_x(xs, 128)
            for l in range(DEPTH):
                E.epoch += 1
                layer(l, 128, True, False)
            store_y(y_s, 128)

        E.finalize()
        if os.environ.get("KDBG"):
            print("est_span_us", getattr(E, "est_span", None), "n_ops", len(E.ops), flush=True)
        sems = {}
        for e_ in ENGS:
            for ep in range(E.epoch + 1):
                sems[(e_, ep)] = es.enter_context(nc.semaphore(f"s_{e_}_{ep}"))
        chansems = {ch: es.enter_context(nc.semaphore(f"c_{ch}")) for ch in E.chans}
        block = es.enter_context(nc.Block())
        E.emit(block, sems, chansems)
    return nc


def _pack(inp):
    f = lambda a: np.ascontiguousarray(np.asarray(a, dtype=np.float32))
    pp = np.zeros((128, 262), np.float32)
    names = ["norm_pre_mix", "norm_post_mix", "norm_pre_ffn", "norm_post_ffn"]
    for l in range(DEPTH):
        for n, nm in enumerate(names):
            pp[:, (l * 4 + n) * 8:(l * 4 + n) * 8 + 8] = f(inp[nm])[l].reshape(8, 128).T
        cw = f(inp["conv_w"])[l]
        pp[:, 64 + l * 96:64 + (l + 1) * 96] = cw.reshape(4, 24, 128).transpose(2, 0, 1).reshape(128, 96)
        pp[:, 256 + l] = f(inp["delta_norm_w"])[l]
        pp[0:8, 258 + l] = f(inp["a_log"])[l]
        pp[0:8, 260 + l] = f(inp["dt_bias"])[l]
    bs = f(inp["b_spatial"])
    prow = np.zeros((DEPTH, 2, 8, 128), np.float32)
    prow[:, 0] = bs
    prow[:, 1] = np.tile(bs[:, :, :8], (1, 1, 16))
    prow = prow.reshape(1, -1)
    lnw = np.stack([f(inp["sgu_ln_w"]), f(inp["sgu_ln_b"])], axis=1)
    ws = f(inp["w_spatial"])
    wsT = ws.transpose(0, 3, 1, 2)
    wss = np.tile(ws[:, :, :8, :8], (1, 1, 16, 16)).transpose(0, 3, 1, 2)
    wsp = np.ascontiguousarray(np.stack([wsT, wss], axis=1).reshape(DEPTH, 2, 128, 1024))
    shared = dict(
        w_in=f(inp["w_in"]), w_pa=f(inp["w_proj_a"]), w_pb=f(inp["w_proj_b"]), w_o=f(inp["w_out"]),
        w_f1=f(inp["w_ffn_in"]), w_f2=f(inp["w_ffn_out"]), pp=pp, prow=prow, lnw=np.ascontiguousarray(lnw), wsp=wsp,
        cst=_consts()[0], cst16=_consts()[1], cbd=_cb(),
    )
    xpr = f(inp["x_prompt"]); xsm = f(inp["x_sample"]); sdl = f(inp["state_delta"]); scv = f(inp["state_conv"])
    maps = []
    for c in range(NCORES):
        m = dict(shared)
        m["xp"] = xpr[c]
        m["xs"] = np.ascontiguousarray(xsm[16 * c:16 * (c + 1)].reshape(128, D))
        m["sd"] = np.ascontiguousarray(sdl[:, 16 * c:16 * (c + 1)])
        m["sc"] = np.ascontiguousarray(scv[:, 16 * c:16 * (c + 1)].reshape(DEPTH, 48, 3072))
        maps.append(m)
    return maps


def kernel(**inputs):
    maps = _pack(inputs)
    nc = build()
    res = run_bass_kernel_spmd(nc, maps, core_ids=list(range(NCORES)))
    r = res.results
    y_p = np.stack([r[c]["y_p"] for c in range(NCORES)], 0).reshape(8, 2048, D)
    y_s = np.concatenate([r[c]["y_s"].reshape(16, 8, D) for c in range(NCORES)], 0)
    sd_p = np.stack([r[c]["sd_p"] for c in range(NCORES)], 1)
    sc_p = np.stack([r[c]["sc_p"] for c in range(NCORES)], 1)
    cv_p = np.stack([r[c]["cv_p"] for c in range(NCORES)], 1)
    sd_s = np.concatenate([r[c]["sd_s"] for c in range(NCORES)], 1)
    sc_s = np.concatenate([r[c]["sc_s"] for c in range(NCORES)], 1)
    cv_s = np.concatenate([r[c]["cv_s"].reshape(DEPTH, 16, 8, D) for c in range(NCORES)], 1)
    return tuple(np.ascontiguousarray(a.astype(np.float32)) for a in (y_p, y_s, sd_p, sc_p, cv_p, sd_s, sc_s, cv_s))
```

```python
import os
import numpy as np
from contextlib import ExitStack
import concourse.bass as bass
import concourse.mybir as mybir
from concourse.bass_utils import run_bass_kernel_spmd

F32 = mybir.dt.float32
BF16 = mybir.dt.bfloat16
AF = mybir.ActivationFunctionType
ALU = mybir.AluOpType

NCORES = 8
D = 1024
DEPTH = 2
H = 8
DIN = 8208
DFF = 2816
BIG = 30000.0
NSLOT = 3
ENGS = ["pe", "act", "dve", "pool", "sp"]
BNAME = {"pe": "tensor", "act": "scalar", "dve": "vector", "pool": "gpsimd", "sp": "sync"}
NEPOCH = 12
SCHED = True


class Buf:
    __slots__ = ("name", "lw", "rd")

    def __init__(self, name):
        self.name = name
        self.lw = None
        self.rd = []


def bufs(name, n):
    return [Buf(f"{name}{i}") for i in range(n)]


class Rec:
    def __init__(self):
        self.calls = []

    def __getattr__(self, name):
        def f(*a, **k):
            self.calls.append((name, a, k))
            return self
        return f


class Em:
    def __init__(self):
        self.ops = []
        self.chans = {}
        self.epoch = 0

    def op(self, eng, fn, reads=(), writes=(), chan=None, ndma=0):
        idx = len(self.ops)
        deps = set()
        soft = set()
        for b in reads:
            if b.lw is not None:
                deps.add(b.lw)
        for b in writes:
            if b.lw is not None:
                soft.add(b.lw)
            soft.update(b.rd)
        deps |= soft
        val = None
        if chan:
            c = self.chans.setdefault(chan, {"count": 0, "last": None, "eng": eng})
            assert c["eng"] == eng
            if c["last"] is not None:
                deps.add(c["last"])
            c["count"] += 16 * ndma
            c["last"] = idx
            val = c["count"]
        key = chan if chan else eng
        for b in writes:
            b.lw = idx
            b.rd = []
        for b in reads:
            b.rd.append(idx)
        rec = Rec()
        fn(rec)
        assert rec.calls
        self.ops.append(dict(eng=eng, calls=rec.calls, deps=deps, chan=chan, val=val, inc=False, ep=self.epoch))
        return idx

    @staticmethod
    def _est(o):
        eng = o["eng"]
        tot = 0.0
        for name, a, k in o["calls"]:
            out = k.get("out", a[0] if a else None)
            try:
                fs = float(out.free_size())
            except Exception:
                fs = 128.0
            if o["chan"]:
                try:
                    nb = float(out.nbytes())
                except Exception:
                    nb = 1e5
                tot += 2.0 + nb / 1.8e5
            elif eng == "pe":
                lhs = k.get("lhsT", a[1] if len(a) > 1 else None) if name == "matmul" else k.get("in_")
                f32 = getattr(lhs, "dtype", None) == F32
                tot += 0.045 + max(fs, 64.0) * (4.0 if (f32 and name == "matmul") else 1.0) / 2400.0 + (0.05 if fs <= 128 else 0.0)
            elif eng == "dve":
                tot += 0.12 + fs * 1.1e-3
            elif eng == "act":
                tot += 0.17 + fs * 0.9e-3
            else:
                tot += 0.25 + fs * 2.2e-3
        return tot

    def schedule(self):
        import heapq
        ops = self.ops
        n = len(ops)
        succ = [[] for _ in range(n)]
        indeg = [0] * n
        for i, o in enumerate(ops):
            for d in o["deps"]:
                succ[d].append(i)
                indeg[i] += 1
        dur = [self._est(o) for o in ops]
        bl = [0.0] * n
        for i in range(n - 1, -1, -1):
            m_ = 0.0
            for j in succ[i]:
                if bl[j] > m_:
                    m_ = bl[j]
            bl[i] = dur[i] + 0.3 + m_
        PRIO = os.environ.get("KPRIO", "bl")
        ready = [0.0] * n
        fin = [0.0] * n
        free = {e: 0.0 for e in ENGS}
        heaps = {e: [] for e in ENGS}
        for i in range(n):
            if indeg[i] == 0:
                heapq.heappush(heaps[ops[i]["eng"]], (0.0, i))
        order = []
        LAT = 0.15
        while len(order) < n:
            best = None
            for e in ENGS:
                hp = heaps[e]
                if not hp:
                    continue
                cand = hp[0]
                st = max(free[e], cand[0])
                key = (st, cand[1])
                if best is None or key < best[0]:
                    best = (key, e)
            (st, i), e = best
            hp = heaps[e]
            pool_ = []
            while hp and hp[0][0] <= st:
                pool_.append(heapq.heappop(hp))
            if PRIO == "bl":
                pool_.sort(key=lambda x: (-bl[x[1]], x[1]))
            else:
                pool_.sort(key=lambda x: x[1])
            _, i = pool_[0]
            for x in pool_[1:]:
                heapq.heappush(hp, x)
            o = ops[i]
            fin[i] = st + dur[i]
            free[e] = st + (min(dur[i], 1.0) if o["chan"] else dur[i])
            order.append(i)
            for j in succ[i]:
                r = fin[i] + (LAT if ops[j]["eng"] != e or o["chan"] else 0.1)
                if r > ready[j]:
                    ready[j] = r
                indeg[j] -= 1
                if indeg[j] == 0:
                    heapq.heappush(heaps[ops[j]["eng"]], (ready[j], j))
        pos = {old: new for new, old in enumerate(order)}
        new_ops = []
        for old in order:
            o = ops[old]
            o["deps"] = {pos[d] for d in o["deps"]}
            new_ops.append(o)
        self.ops = new_ops
        self.est_span = max(fin) if fin else 0.0
        if os.environ.get("KDBG"):
            cp = [0.0] * n
            for old in order:
                o = ops[old]
            newdur = [dur[old] for old in order]
            for i_, o in enumerate(new_ops):
                st_ = 0.0
                for d in o["deps"]:
                    st_ = max(st_, cp[d] + LAT)
                cp[i_] = st_ + newdur[i_]
            busy = {}
            for i_, o in enumerate(new_ops):
                busy[o["eng"]] = busy.get(o["eng"], 0.0) + (min(newdur[i_], 1.0) if o["chan"] else newdur[i_])
            print("critical_path_us", max(cp), "busy", {k: round(v) for k, v in busy.items()}, flush=True)
            i_ = max(range(n), key=lambda q: cp[q])
            agg = {}
            seq = []
            while True:
                o = new_ops[i_]
                nm = o["calls"][0][0] + ("/dma" if o["chan"] else "")
                outap = o["calls"][0][2].get("out", o["calls"][0][1][0] if o["calls"][0][1] else None)
                tn = getattr(getattr(outap, "tensor", None), "name", "?")
                key = (o["eng"], nm, tn)
                a_ = agg.setdefault(key, [0, 0.0])
                a_[0] += 1
                a_[1] += newdur[i_] + LAT
                seq.append(key)
                prev = None
                for d in o["deps"]:
                    if prev is None or cp[d] > cp[prev]:
                        prev = d
                if prev is None:
                    break
                i_ = prev
            for k_, v_ in sorted(agg.items(), key=lambda kv: -kv[1][1])[:28]:
                print("   CP", k_, v_[0], round(v_[1], 1), flush=True)

    def prune(self):
        for o in self.ops:
            best = {}
            for d in o["deps"]:
                p = self.ops[d]
                key = ("c", p["chan"]) if p["chan"] else ("e", p["eng"])
                if key not in best or d > best[key]:
                    best[key] = d
            o["deps"] = set(best.values())

    def finalize(self):
        if SCHED:
            self.schedule()
        self.prune()
        for o in self.ops:
            for d in o["deps"]:
                p = self.ops[d]
                if p["chan"] is None:
                    if p["eng"] == "pe" and o["eng"] == "pe" and o["chan"] is None:
                        continue
                    p["inc"] = True
        cnt = {}
        for o in self.ops:
            if o["chan"] is None and o["inc"]:
                k = (o["eng"], o["ep"])
                cnt[k] = cnt.get(k, 0) + 1
                o["val"] = cnt[k]

    def emit(self, block, sems, chansems):
        for e in ENGS:
            ops_e = [o for o in self.ops if o["eng"] == e]

            def body(eng, ops_e=ops_e, e=e):
                seen = {}
                for o in ops_e:
                    for d in sorted(o["deps"]):
                        p = self.ops[d]
                        if p["chan"] is None:
                            if not p["inc"]:
                                continue
                            if p["eng"] == "pe" and e == "pe" and o["chan"] is None:
                                continue
                            key = (p["eng"], p["ep"])
                            sem = sems[key]
                        else:
                            key = p["chan"]
                            sem = chansems[key]
                        v = p["val"]
                        if seen.get(key, 0) >= v:
                            continue
                        seen[key] = v
                        eng.wait_ge(sem, v)
                    ins = None
                    for name, a, k in o["calls"]:
                        ins = getattr(eng, name)(*a, **k)
                        if o["chan"]:
                            ins.then_inc(chansems[o["chan"]], 16)
                    if (not o["chan"]) and o["inc"]:
                        ins.then_inc(sems[(e, o["ep"])], 1)
                for ch, c in self.chans.items():
                    if c["eng"] == e and c["count"] > 0:
                        eng.wait_ge(chansems[ch], c["count"])

            getattr(block, BNAME[e])(body)


def _consts():
    i = np.arange(128)
    m = i[:, None]
    p = i[None, :]
    blk = i // 8
    same = blk[:, None] == blk[None, :]
    c = np.zeros((13, 128, 128), np.float32)
    c[0] = np.eye(128)
    c[1] = 1.0
    c[2] = m <= p
    c[3] = m > p
    c[4] = -BIG * (p >= m)
    c[5] = -BIG * (p < m)
    c[6] = (m <= p) & same
    c[7] = (m > p) & same
    c[8] = -BIG * (~((p < m) & same))
    c[9] = -BIG * (~((p >= m) & same))
    c[10] = same
    s = np.zeros((128, 16, 16), np.float32)
    for j in range(16):
        s[:, j, j] = 1.0
    c[11] = s.reshape(128, 256)[:, :128]
    c[12] = s.reshape(128, 256)[:, 128:]
    c32 = np.ascontiguousarray(c[[0, 1, 2, 3, 6, 7, 10]])
    md = ((m // 64) == (p // 64)).astype(np.float32)
    mo = ((m < 64) & (p >= 64)).astype(np.float32)
    c16 = np.ascontiguousarray(np.concatenate([c[[0, 1, 4, 5, 8, 9, 11, 12]], md[None], mo[None]], 0))
    return c32, c16


def _cb():
    i = np.arange(128)
    maskc = (i[None, :] // 8 == np.arange(16)[:, None]).astype(np.float32)
    maskc = np.broadcast_to(maskc[None], (128, 16, 128)).reshape(128, 2048)
    seqsel = (i[:, None] // 8 == np.arange(16)[None, :]).astype(np.float32)
    return np.concatenate([maskc, seqsel], axis=1).astype(np.float32)


def build(nblk=4, do_samp=True, stop=99):
    reqs = []
    _build(nblk, do_samp, stop, reqs, None)
    return _build(nblk, do_samp, stop, [], reqs)


LOOKAHEAD = 2
SAMP_K0 = [None]


def _build(nblk, do_samp, stop, rec_reqs, all_reqs):
    nc = bass.Bass("TRN2", target_bir_lowering=False)

    def din(name, shape):
        return nc.dram_tensor(name, list(shape), F32, kind="ExternalInput").ap()

    def dout(name, shape):
        return nc.dram_tensor(name, list(shape), F32, kind="ExternalOutput").ap()

    xp = din("xp", [2048, D])
    xs = din("xs", [128, D])
    sd = din("sd", [DEPTH, 16, H, 128, 128])
    sc = din("sc", [DEPTH, 48, 3072])
    w_in = din("w_in", [DEPTH, D, DIN])
    w_pa = din("w_pa", [DEPTH, D, D])
    w_pb = din("w_pb", [DEPTH, D, D])
    w_o = din("w_o", [DEPTH, D, D])
    w_f1 = din("w_f1", [DEPTH, D, 2 * DFF])
    w_f2 = din("w_f2", [DEPTH, DFF, D])
    pp = din("pp", [128, 2 * 32 + 2 * 96 + 2 + 4])
    prow = din("prow", [1, DEPTH * 2 * 1024])
    lnw = din("lnw", [DEPTH, 2, D])
    wsp = din("wsp", [DEPTH, 2, 128, 1024])
    cst = din("cst", [7, 128, 128])
    cst16 = din("cst16", [10, 128, 128])
    cbd = din("cbd", [128, 2064])

    NSCR = 2 * 41
    wscr = nc.dram_tensor("wscr", [NSCR, 128, 4096], BF16, kind="Internal").ap()
    wscr_b = bufs("wscr", NSCR)
    y_p = dout("y_p", [2048, D])
    y_s = dout("y_s", [128, D])
    sd_p = dout("sd_p", [DEPTH, H, 128, 128])
    sc_p = dout("sc_p", [DEPTH, 3, 3072])
    cv_p = dout("cv_p", [DEPTH, 128, D])
    sd_s = dout("sd_s", [DEPTH, 16, H, 128, 128])
    sc_s = dout("sc_s", [DEPTH, 16, 3, 3072])
    cv_s = dout("cv_s", [DEPTH, 128, D])

    E = Em()
    es = ExitStack()
    with es:
        def sb(name, shape, dt):
            return es.enter_context(nc.sbuf_tensor(name, shape, dt))

        xT = sb("xT", [128, 8, 512], F32); xT_b = bufs("xT", 8)
        hT = sb("hT", [128, 8, 512], BF16); hT_b = bufs("hT", 8)
        R1 = sb("R1", [128, 24, 512], BF16); R1_b = bufs("R1", 24)
        sz = sb("sz", [128, 8, 512], BF16); sz_b = bufs("sz", 8)
        oa = sb("oa", [128, 8, 512], BF16); oa_b = bufs("oa", 8)
        uT_b = R1_b[0:8]; vT_b = R1_b[8:16]
        t32 = sb("t32", [128, 8, 512], F32); t32_b = bufs("t32", 8)
        pre = [sb(f"pre{i}", [128, 515], F32) for i in range(2)]; pre_b = bufs("pre", 2)
        acc = [sb(f"acc{i}", [128, 512], F32) for i in range(2)]; acc_b = bufs("acc", 2)
        sq = [sb(f"sq{i}", [128, 1024], BF16) for i in range(2)]; sq_b = bufs("sq", 2)
        rstd = sb("rstd", [128, 1024], F32); rstd_b = Buf("rstd")
        carry = sb("carry", [128, DEPTH, 24, 3], F32); carry_b = [bufs(f"car{l}_", 24) for l in range(DEPTH)]
        wsl = [sb(f"wsl{i}", [128, 8, 512], BF16) for i in range(5)]; wsl_b = bufs("wsl", 5)
        w3f = wsl[3][:].rearrange("p a b -> p (a b)")
        w4f = wsl[4][:].rearrange("p a b -> p (a b)")
        ppt = sb("ppt", [128, 262], F32); ppt_b = Buf("ppt")
        nea = sb("nea", [128, 2], F32); nea_b = Buf("nea")
        dnw = sb("dnw", [128, 2], F32); dnw_b = Buf("dnw")
        c32 = sb("c32", [128, 7, 128], F32); c32_b = Buf("c32")
        c16 = sb("c16", [128, 10, 128], BF16); c16_b = Buf("c16")
        seq32 = sb("seq32", [128, 16], F32); cb32_b = Buf("cb32")
        mskc = w3f[:, 0:2048].rearrange("p (a b) -> p a b", a=16)
        seqs = sb("seqs", [128, 16], BF16); cb16_b = Buf("cb16")
        selT = sb("selT", [16, 16, 128], BF16); selT_b = Buf("selT")
        t32f = t32[:].rearrange("p a b -> p (a b)")
        prw16 = t32f.bitcast(BF16)[0:1, 6144:7168]; prw_b = t32_b[6]
        lnt = t32f[:, 0:2048].rearrange("p (a b) -> p a b", a=2)
        xin = t32f[:, 0:1024]
        yout = [t32f[:, 1024:2048], t32f[:, 2048:3072]]
        scrow = t32f[0:3, 0:3072]
        scin = w4f[:, 0:2304].bitcast(F32).rearrange("p (a b) -> p a b", a=24); scin_b = Buf("scin")
        wsT = sb("wsT", [128, 8, 128], BF16); wsT_b = Buf("wsT")
        gT = pre[0][0:8, 0:512]; gT_b = pre_b[0]
        bT = pre[1][0:8, 0:512]; bT_b = pre_b[1]
        rn = acc[1][0:16, :]; rn_b = acc_b[1]
        rnb = acc[0][0:16, 0:256].bitcast(BF16); rnb_b = acc_b[0]
        rnl = acc[0][0:16, 256:512].bitcast(BF16)
        S32 = sb("S32", [128, DEPTH, H, 128], F32); S32_b = bufs("S32_", DEPTH)
        S16 = sb("S16", [128, DEPTH, H, 128], BF16); S16_b = bufs("S16_", DEPTH)
        s0f_0 = S32[:].rearrange("p a b c -> p (a b) c"); s0f_b0 = S32_b
        s0b_0 = S16[:].rearrange("p a b c -> p (a b) c"); s0b_b0 = S16_b
        gbt = sb("gbt", [128, 16], F32); gbt_b = Buf("gbt")
        es24 = sb("es24", [128, 24], F32); es_b = Buf("es24")
        kbs = sb("kbs", [128, 8], F32); kbs_b = Buf("kbs")
        Gm1 = sb("Gm1", [128, 8, 128], F32); Gm1_b = Buf("Gm1")
        Gm2 = sb("Gm2", [128, 8, 128], F32); Gm2_b = Buf("Gm2")
        ws32 = Gm1[:].rearrange("p a b -> p (a b)"); ws32_b = Gm1_b
        Kd = sb("Kd", [128, 8, 128], F32); Kd_b = Buf("Kd")
        Vb = sb("Vb", [128, 8, 128], BF16); Vb_b = Buf("Vb")
        Bo = sb("Bo", [128, 8, 128], BF16); Bo_b = Buf("Bo")
        r16 = sb("r16", [128, 8, 128], BF16); r16_b = Buf("r16")
        neg = sb("neg", [128, 8], F32); neg_b = Buf("neg")
        dsm = Gm1; dsm_b = Gm1_b
        dTm = Gm2; dTm_b = Gm2_b
        reg = sb("reg", [128, 8, 128], BF16); reg_b = Buf("reg")
        AmT = sb("AmT", [128, 2, 8, 128], BF16); Am = [AmT[:, 0, :, :], AmT[:, 1, :, :]]; Am_b = bufs("Am", 2)
        BmT = sb("BmT", [128, 2, 8, 128], BF16); Bm = [BmT[:, 0, :, :], BmT[:, 1, :, :]]; Bm_b = bufs("Bm", 2)
        vn32 = AmT[:].rearrange("p a b c -> p (a b c)").bitcast(F32).rearrange("p (b c) -> p b c", c=128)
        r32 = BmT[:].rearrange("p a b c -> p (a b c)").bitcast(F32).rearrange("p (b c) -> p b c", c=128)
        Pm0 = sb("Pm0", [128, 8, 128], BF16); Pm_b0 = Buf("Pm")
        Pf = rstd[:].rearrange("p (b c) -> p b c", c=128)
        PTm = sb("PTm", [128, 8, 128], BF16); PTm_b = Buf("PTm")
        vn = sb("vn", [128, 8, 128], BF16); vn_b = Buf("vn")
        qt = sb("qt", [128, 8, 128], BF16); qt_b = Buf("qt")
        on32 = sb("on32", [128, 8, 128], F32); on32_b = Buf("on32")
        lnv32 = on32[:].rearrange("p a b -> p (a b)"); lnv32_b = on32_b
        vnb = sb("vnb", [128, D], BF16); vnb_b = Buf("vnb")
        bnst = sb("bnst", [128, 2, 6], F32); mv = sb("mv", [128, 4], F32); bn_b = Buf("bn")
        t16 = t32f.bitcast(BF16)

        def t16tile(c):
            return t16[:, c * 1024:(c + 1) * 1024].rearrange("p (a b) -> p a b", a=8)
        KdV = Kd[:].rearrange("p a b -> p (a b)").bitcast(BF16).rearrange("p (s a b) -> p s a b", s=2, a=8)
        Kd2 = [KdV[:, 0, :, :], KdV[:, 1, :, :]]; Kd2_b = bufs("Kd2_", 2)
        Vb2 = [Vb[:], t16tile(0)]; Vb2_b = [Vb_b, t32_b[0]]
        qt2 = [qt[:], t16tile(1)]; qt2_b = [qt_b, t32_b[1]]
        PTm2 = [PTm[:], t16tile(2)]; PTm2_b = [PTm_b, t32_b[2]]
        Pm2 = [Pm0[:], t16tile(3)]; Pm2_b = [Pm_b0, t32_b[3]]
        Bo2 = [Bo[:], t16tile(4)]; Bo2_b = [Bo_b, t32_b[4]]
        neg1 = sb("neg1", [128, 8], F32); neg2 = [neg, neg1]; neg2_b = [neg_b, Buf("neg1")]
        es1 = sb("es1", [128, 24], F32); es2 = [es24, es1]; es2_b = [es_b, Buf("es1")]
        rstd9 = t32f[:, 3072:4096]
        msk16 = w3f[:, 2048:4096].rearrange("p (a b) -> p a b", a=16); msk16_b = Buf("msk16")
        egl = sb("egl", [128, 16, 8], F32); egl_b = Buf("egl")
        gsel = sb("gsel", [128, 16, 8], F32); gsel_b = Buf("gsel")
        snew = [w4f[:, 2304:3328].bitcast(F32).rearrange("p (a b) -> p a b", a=4), sb("snew1", [128, 4, 128], F32)[:]]
        snew_b = bufs("snew", 2)

        ps = es.enter_context(nc.psum_tensor("ps", [128, 8, 512], F32))
        ps_b = bufs("ps", 8)
        bptr = [0]

        def bank():
            b = bptr[0] % 8
            bptr[0] = (b + 1) % 8
            return b

        def pair():
            b = bptr[0] % 8
            if b % 2:
                b = (b + 1) % 8
            bptr[0] = (b + 2) % 8
            return b

        def P32(i, n=1):
            return ps[:, i:i + n, :].rearrange("p a b -> p (a b)") if n > 1 else ps[:, i, :]

        def P16(i):
            return ps[:, i, :].bitcast(BF16)

        def V(fn, r, w): return E.op("dve", fn, r, w)
        def A(fn, r, w): return E.op("act", fn, r, w)
        def T(fn, r, w): return E.op("pe", fn, r, w)
        def G(fn, r, w): return E.op("pool", fn, r, w)

        def dma(eng, chan, out, in_, r, w):
            return E.op(eng, lambda e: e.dma_start(out=out, in_=in_), r, w, chan=chan, ndma=1)

        slot_i = [0]
        issued = [0]
        mask_loaded = [False]
        WT = {"w_in": w_in, "w_pa": w_pa, "w_pb": w_pb, "w_o": w_o, "w_f1": w_f1, "w_f2": w_f2}

        scr_idx = {}

        def issue_w(k, desc):
            wn, l_, r0, nr, c0, ncol = desc
            s_ = slot_of(k)
            kc_ = nr // 128
            dst = wsl[s_][:, 0:kc_, 0:ncol]
            if desc not in scr_idx:
                i_ = len(scr_idx)
                scr_idx[desc] = i_
                src = WT[wn][l_, r0:r0 + nr, c0:c0 + ncol].rearrange("(c p) n -> p c n", p=128)
                dma("pool", f"w{s_}", dst, src, [], [wsl_b[s_]])
                sv = wscr[i_, :, 0:kc_ * ncol].rearrange("p (c n) -> p c n", c=kc_)
                dma("sp", f"scrw{i_ % 2}", sv, dst, [wsl_b[s_]], [wscr_b[i_]])
            else:
                i_ = scr_idx[desc]
                sv = wscr[i_, :, 0:kc_ * ncol].rearrange("p (c n) -> p c n", c=kc_)
                dma("sp", f"v{s_}", dst, sv, [wscr_b[i_]], [wsl_b[s_]])

        def load_w(desc, kc=None, ncols=None):
            k = slot_i[0]
            slot_i[0] += 1
            rec_reqs.append(desc)
            if all_reqs is None:
                issue_w(k, desc)
            else:
                assert all_reqs[k] == desc
                pump()
                assert issued[0] > k, "weight slot ring exhausted"
            return slot_of(k)

        samp_k0 = [None]
        if all_reqs is not None:
            samp_k0[0] = SAMP_K0[0]

        def slot_of(k):
            k0 = samp_k0[0]
            if k0 is None or k < k0:
                return k % 5
            return (k - k0) % 3

        def prev_user(k):
            s_ = slot_of(k)
            j = k - 1
            while j >= 0:
                if slot_of(j) == s_:
                    return j
                j -= 1
            return -1

        wdone = [0]

        def pump():
            while issued[0] < len(all_reqs) and prev_user(issued[0]) < wdone[0]:
                issue_w(issued[0], all_reqs[issued[0]])
                issued[0] += 1

        def w_done(n=1):
            wdone[0] += n
            if all_reqs is not None:
                pump()

        def wview(wt, l, r0, nr, c0, ncol):
            return (wt, l, r0, nr, c0, ncol)

        I32 = c32[:, 0, :]; I16 = c16[:, 0, :]; ONES16 = c16[:, 1, :]; ONES32 = c32[:, 1, :]

        def mm_group(bk, col0, ncol, lhs_fn, rhs_fn, nk, rbufs, extra_r=(), mp=128):
            def fn(e):
                ins = None
                for k in range(nk):
                    ins = e.matmul(ps[0:mp, bk, col0:col0 + ncol], lhs_fn(k), rhs_fn(k),
                                   start=(k == 0), stop=(k == nk - 1))
                return ins
            return T(fn, list(rbufs) + list(extra_r), [ps_b[bk]])

        dma("sp", "par", ppt[:], pp, [], [ppt_b])
        dma("sp", "par", c32[:], cst.rearrange("k p n -> p k n"), [], [c32_b])
        dma("sp", "par", seq32[:], cbd[:, 2048:2064], [], [cb32_b])
        dma("pool", "cst16", c16[:], cst16.rearrange("k p n -> p k n"), [], [c16_b])
        V(lambda e: e.tensor_copy(out=seqs[:], in_=seq32[:]), [cb32_b, cb16_b], [cb16_b])
        V(lambda e: e.tensor_copy(out=selT[:], in_=c32[0:16, 0, 0:16].unsqueeze(2).to_broadcast([16, 16, 128])),
          [c32_b], [selT_b])
        def NW(l, n): return ppt[:, (l * 4 + n) * 8:(l * 4 + n) * 8 + 8]
        def CW(l, tap, ch): return ppt[:, 64 + l * 96 + tap * 24 + ch:64 + l * 96 + tap * 24 + ch + 1]
        A(lambda e: e.activation(out=nea[0:8, :], in_=ppt[0:8, 258:260], func=AF.Exp), [ppt_b], [nea_b])
        V(lambda e: e.tensor_scalar(out=nea[0:8, :], in0=nea[0:8, :], scalar1=-1.0, scalar2=None, op0=ALU.mult), [nea_b], [nea_b])
        V(lambda e: e.tensor_scalar(out=dnw[:], in0=ppt[:, 256:258], scalar1=float(128 ** -0.5), scalar2=None, op0=ALU.mult),
          [ppt_b], [dnw_b])
        sel16 = c16[:, 6:8, :].rearrange("p a b -> p (a b)")

        def rsqrt_from_psum(out_ap, in_ap, scale, eps, r, w):
            A(lambda e: e.activation(out=out_ap, in_=in_ap, func=AF.Ln, scale=scale, bias=eps), r, w)
            A(lambda e: e.activation(out=out_ap, in_=out_ap, func=AF.Exp, scale=-0.5), w, w)

        def rmsnorm(src, src_b, W, gain, mode, l):
            bk = bank()
            for c in range(8):
                s = c % 2
                A(lambda e, c=c, s=s: e.activation(out=sq[s][:, 0:W], in_=src[:, c, 0:W], func=AF.Square),
                  [src_b[c]], [sq_b[s]])
                T(lambda e, c=c, s=s: e.matmul(ps[:, bk, 0:W], ONES16, sq[s][:, 0:W], start=(c == 0), stop=(c == 7)),
                  [sq_b[s], c16_b], [ps_b[bk]])
            rsqrt_from_psum(rstd[:, 0:W], ps[:, bk, 0:W], 1.0 / D, 1e-6, [ps_b[bk]], [rstd_b])
            for c in range(8):
                if mode == "h":
                    V(lambda e, c=c: e.scalar_tensor_tensor(out=hT[:, c, 0:W], in0=src[:, c, 0:W], scalar=gain[:, c:c + 1],
                                                            in1=rstd[:, 0:W], op0=ALU.mult, op1=ALU.mult),
                      [src_b[c], rstd_b, ppt_b], [hT_b[c]])
                else:
                    V(lambda e, c=c: e.scalar_tensor_tensor(out=src[:, c, 0:W], in0=src[:, c, 0:W], scalar=gain[:, c:c + 1],
                                                            in1=rstd[:, 0:W], op0=ALU.mult, op1=ALU.mult),
                      [src_b[c], rstd_b, ppt_b], [src_b[c]])
                    G(lambda e, c=c: e.tensor_tensor(out=xT[:, c, 0:W], in0=xT[:, c, 0:W], in1=src[:, c, 0:W], op=ALU.add),
                      [src_b[c], xT_b[c]], [xT_b[c]])

        def proj_chunks(wt, l, c0, nchunks, W, rhs, rhs_b, consume, kc=8, r0=0):
            j = 0
            pend = [None]
            while j < nchunks:
                nj = min(4, nchunks - j)
                s = load_w(wview(wt, l, r0, kc * 128, c0 + j * 128, nj * 128), kc, nj * 128)
                for jj in range(nj):
                    bk = bank()
                    mm_group(bk, 0, W, lambda k, s=s, jj=jj: wsl[s][:, k, jj * 128:(jj + 1) * 128],
                             lambda k: rhs[:, k, 0:W], kc, rhs_b, [wsl_b[s]])
                    d_ = consume(j + jj, bk)
                    if pend[0] is not None:
                        pend[0]()
                    pend[0] = d_ if callable(d_) else None
                w_done()
                j += nj
            if pend[0] is not None:
                pend[0]()

        def layer(l, W, samp, last_blk):
            NT = W // 128
            if stop <= 0:
                return
            rmsnorm(xT, xT_b, W, NW(l, 0), "h", l)
            if stop <= 1:
                return
            if samp:
                tin = t32f
                dma("sp", "scin", tin[0:48, 0:3072], sc[l], [], t32_b[0:6])
                for q4 in range(6):
                    bk = bank()
                    for cc in range(4):
                        ch = q4 * 4 + cc
                        T(lambda e, ch=ch, cc=cc, bk=bk: e.transpose(out=ps[:, bk, cc * 48:(cc + 1) * 48], in_=tin[0:48, ch * 128:(ch + 1) * 128],
                                                                    identity=c32[0:48, 0, 0:48]), t32_b[0:6] + [c32_b], [ps_b[bk]])
                    V(lambda e, q4=q4, bk=bk: e.tensor_copy(out=scin[:, q4 * 4:(q4 + 1) * 4, :],
                                                            in_=ps[:, bk, 0:192].rearrange("p (a b) -> p a b", a=4)),
                      [ps_b[bk]], [scin_b])
            tok_b = t32_b
            if samp:
                tokm = t32f[:, 0:3072]

            def qkv_consume(ch, bk):
                s = ch % 2
                if samp:
                    pv = pre[s][:, 0:176].rearrange("p (a b) -> p a b", a=16)
                    G(lambda e: e.tensor_copy(out=pv[:, :, 0:3], in_=scin[:, ch, :].rearrange("p (a b) -> p a b", a=16)),
                      [scin_b], [pre_b[s]])
                    A(lambda e: e.activation(out=pv[:, :, 3:11], in_=ps[:, bk, 0:128].rearrange("p (a b) -> p a b", a=16),
                                             func=AF.Copy), [ps_b[bk]], [pre_b[s]])
                    av = acc[s][:, 0:128].rearrange("p (a b) -> p a b", a=16)
                    A(lambda e: e.activation(out=acc[s][:, 0:128], in_=ps[:, bk, 0:128], func=AF.Copy, scale=CW(l, 3, ch)),
                      [ps_b[bk], ppt_b], [acc_b[s]])
                    for tap in (2, 1, 0):
                        V(lambda e, tap=tap: e.scalar_tensor_tensor(out=av, in0=pv[:, :, tap:tap + 8], scalar=CW(l, tap, ch),
                                                                    in1=av, op0=ALU.mult, op1=ALU.add),
                          [pre_b[s], acc_b[s], ppt_b], [acc_b[s]])
                    G(lambda e: e.tensor_copy(out=sq[s][:, 0:256].bitcast(F32).rearrange("p (a b) -> p a b", a=16),
                                              in_=pv[:, :, 3:11]), [pre_b[s]], [sq_b[s]])
                    b2 = bank()
                    T(lambda e: e.transpose(out=ps[:, b2, 0:128], in_=sq[s][:, 0:256].bitcast(F32), identity=I32),
                      [sq_b[s], c32_b], [ps_b[b2]])
                    V(lambda e: e.tensor_copy(out=tokm[:, ch * 128:(ch + 1) * 128], in_=ps[:, b2, 0:128]),
                      [ps_b[b2]], [tok_b[ch // 4]])
                else:
                    G(lambda e: e.tensor_copy(out=pre[s][:, 0:3], in_=carry[:, l, ch, :]), [carry_b[l][ch]], [pre_b[s]])
                    A(lambda e: e.activation(out=pre[s][:, 3:515], in_=ps[:, bk, :], func=AF.Copy), [ps_b[bk]], [pre_b[s]])
                    A(lambda e: e.activation(out=acc[s][:], in_=ps[:, bk, :], func=AF.Copy, scale=CW(l, 3, ch)),
                      [ps_b[bk], ppt_b], [acc_b[s]])
                    G(lambda e: e.tensor_copy(out=carry[:, l, ch, :], in_=pre[s][:, 512:515]), [pre_b[s]], [carry_b[l][ch]])
                    for tap in (2, 1, 0):
                        V(lambda e, tap=tap: e.scalar_tensor_tensor(out=acc[s][:], in0=pre[s][:, tap:tap + 512],
                                                                    scalar=CW(l, tap, ch), in1=acc[s][:],
                                                                    op0=ALU.mult, op1=ALU.add),
                          [pre_b[s], acc_b[s], ppt_b], [acc_b[s]])
                return lambda: A(lambda e: e.activation(out=R1[:, ch, 0:W], in_=acc[s][:, 0:W], func=AF.Silu), [acc_b[s]], [R1_b[ch]])

            proj_chunks("w_in", l, 0, 24, W, hT, hT_b, qkv_consume)
            if samp:
                for j in range(3):
                    dma("sp", "o_sc", sc_s[l, :, j, :], tokm[5 + j:128:8, :], tok_b[0:6], [])
            elif last_blk:
                for q4 in range(6):
                    bk = bank()
                    for cc in range(4):
                        ch = q4 * 4 + cc
                        T(lambda e, ch=ch, cc=cc: e.transpose(out=ps[0:3, bk, cc * 128:(cc + 1) * 128], in_=carry[:, l, ch, :],
                                                              identity=I32), [carry_b[l][ch], c32_b], [ps_b[bk]])
                    V(lambda e, q4=q4: e.tensor_copy(out=scrow[:, q4 * 512:(q4 + 1) * 512], in_=ps[0:3, bk, :]),
                      [ps_b[bk]], t32_b[0:6])
                dma("sp", "o_sc", sc_p[l], scrow, t32_b[0:6], [])

            if stop <= 2:
                return
            proj_chunks("w_in", l, 3072, 8, W, hT, hT_b,
                        lambda ch, bk: A(lambda e: e.activation(out=sz[:, ch, 0:W], in_=ps[:, bk, 0:W], func=AF.Silu),
                                         [ps_b[bk]], [sz_b[ch]]))
            s = load_w(wview("w_in", l, 0, 1024, 4096, 16), 8, 16)
            ba = bank()
            mm_group(ba, 0, W, lambda k: wsl[s][:, k, 0:8], lambda k: hT[:, k, 0:W], 8, hT_b, [wsl_b[s]], mp=8)
            bb = bank()
            mm_group(bb, 0, W, lambda k: wsl[s][:, k, 8:16], lambda k: hT[:, k, 0:W], 8, hT_b, [wsl_b[s]], mp=8)
            w_done()
            A(lambda e: e.activation(out=gT[:, 0:W], in_=ps[0:8, ba, 0:W], func=AF.Exp, bias=ppt[0:8, 260 + l:261 + l]),
              [ps_b[ba], ppt_b], [gT_b])
            A(lambda e: e.activation(out=gT[:, 0:W], in_=gT[:, 0:W], func=AF.Ln, bias=1.0), [gT_b], [gT_b])
            V(lambda e: e.tensor_scalar(out=gT[:, 0:W], in0=gT[:, 0:W], scalar1=nea[0:8, l:l + 1], scalar2=None, op0=ALU.mult),
              [gT_b, nea_b], [gT_b])
            A(lambda e: e.activation(out=bT[:, 0:W], in_=ps[0:8, bb, 0:W], func=AF.Sigmoid), [ps_b[bb]], [bT_b])
            bn_ = bank()
            for j in range(16):
                s2 = j % 2
                A(lambda e, j=j, s2=s2: e.activation(out=sq[s2][:, 0:W], in_=R1[:, j, 0:W], func=AF.Square),
                  [R1_b[j]], [sq_b[s2]])
                T(lambda e, j=j, s2=s2: e.matmul(ps[0:16, bn_, 0:W], sel16[:, j * 16:(j + 1) * 16], sq[s2][:, 0:W],
                                                 start=(j == 0), stop=(j == 15)), [sq_b[s2], c16_b], [ps_b[bn_]])
            rsqrt_from_psum(rn[:, 0:W], ps[0:16, bn_, 0:W], 1.0, 1e-6, [ps_b[bn_]], [rn_b])
            V(lambda e: e.tensor_copy(out=rnb[:, 0:W], in_=rn[:, 0:W]), [rn_b], [rnb_b])
            V(lambda e: e.tensor_tensor(out=rnl[:, 0:W], in0=rn[:, 0:W], in1=rnb[:, 0:W], op=ALU.subtract), [rn_b, rnb_b], [rnb_b])
            for j in range(16):
                bk = bank()
                def fbc(e, j=j, bk=bk):
                    e.matmul(ps[:, bk, 0:W], selT[:, j, :], rnb[:, 0:W], start=True, stop=False)
                    return e.matmul(ps[:, bk, 0:W], selT[:, j, :], rnl[:, 0:W], start=False, stop=True)
                T(fbc, [rnb_b, selT_b], [ps_b[bk]])
                V(lambda e, j=j, bk=bk: e.tensor_tensor(out=R1[:, j, 0:W], in0=R1[:, j, 0:W], in1=ps[:, bk, 0:W], op=ALU.mult),
                  [R1_b[j], ps_b[bk]], [R1_b[j]])

            if stop <= 3:
                return
            cU, cSL, cN1, cN2 = (4, 5, 4, 5) if samp else (2, 3, 2, 3)
            nlev = 3 if samp else 6

            def bc8(ap):
                return ap.unsqueeze(2).to_broadcast([128, 8, 128])

            def bcm(ap):
                return ap.unsqueeze(1).to_broadcast([128, 8, 128])

            def PV(i):
                return ps[:, i:i + 2, :].rearrange("p a (b c) -> p (a b) c", c=128)

            class Alloc:
                def __init__(self, lo, n):
                    self.lo, self.n, self.p = lo, n, 0

                def bank(self):
                    b_ = self.p % self.n
                    self.p = (b_ + 1) % self.n
                    return self.lo + b_

                def pair(self):
                    b_ = self.p % self.n
                    if b_ % 2:
                        b_ = (b_ + 1) % self.n
                    self.p = (b_ + 2) % self.n
                    return self.lo + b_

            a1 = Alloc(0, 4)
            a2 = Alloc(0, 8)
            qb = R1_b[0:8]

            def chunk_gen(t):
                cs = slice(t * 128, (t + 1) * 128)
                st = 0 if samp else t % 2
                Kd_, Kd_b_ = Kd2[st], Kd2_b[st]
                Vb_, Vb_b_ = Vb2[st], Vb2_b[st]
                qt_, qt_b_ = qt2[st], qt2_b[st]
                PTm_, PTm_b_ = PTm2[st], PTm2_b[st]
                TT, TT_b = Pm2[st], Pm2_b[st]
                Bo_, Bo_b_ = Bo2[st], Bo2_b[st]
                neg_, neg_b_ = neg2[st], neg2_b[st]
                es_, es_b_ = es2[st], es2_b[st]
                bK = a1.bank()
                for h in range(H):
                    T(lambda e, h=h: e.transpose(out=P16(bK)[:, h * 128:(h + 1) * 128], in_=R1[:, 8 + h, cs], identity=I16),
                      [R1_b[8 + h], c16_b], [ps_b[bK]])
                bV = a1.bank()
                for h in range(H):
                    T(lambda e, h=h: e.transpose(out=P16(bV)[:, h * 128:(h + 1) * 128], in_=R1[:, 16 + h, cs], identity=I16),
                      [R1_b[16 + h], c16_b], [ps_b[bV]])
                bG = a1.bank()
                T(lambda e: e.transpose(out=ps[:, bG, 0:8], in_=gT[:, cs], identity=c32[0:8, 0, 0:8]), [gT_b, c32_b], [ps_b[bG]])
                T(lambda e: e.transpose(out=ps[:, bG, 8:16], in_=bT[:, cs], identity=c32[0:8, 0, 0:8]), [bT_b, c32_b], [ps_b[bG]])
                V(lambda e: e.tensor_copy(out=gbt[:], in_=ps[:, bG, 0:16]), [ps_b[bG]], [gbt_b])
                yield 1
                bS = a1.bank()
                cLast = 6 if samp else 1
                for i3, ci in enumerate((cU, cSL, cLast)):
                    T(lambda e, i3=i3, ci=ci: e.matmul(ps[:, bS, i3 * 8:(i3 + 1) * 8], c32[:, ci, :], gbt[:, 0:8],
                                                       start=True, stop=True), [gbt_b, c32_b], [ps_b[bS]])
                A(lambda e: e.activation(out=es_[:], in_=ps[:, bS, 0:24], func=AF.Exp), [ps_b[bS]], [es_b_])
                V(lambda e: e.tensor_tensor(out=kbs[:], in0=gbt[:, 8:16], in1=es_[:, 0:8], op=ALU.mult), [gbt_b, es_b_], [kbs_b])
                yield 1
                KP = P16(bK).rearrange("p (a b) -> p a b", a=8)
                VP = P16(bV).rearrange("p (a b) -> p a b", a=8)
                V(lambda e: e.tensor_tensor(out=Kd_, in0=KP, in1=bc8(es_[:, 8:16]), op=ALU.mult), [ps_b[bK], es_b_], [Kd_b_])
                V(lambda e: e.tensor_tensor(out=Vb_, in0=VP, in1=bc8(gbt[:, 8:16]), op=ALU.mult), [ps_b[bV], gbt_b], [Vb_b_])
                V(lambda e: e.tensor_scalar(out=neg_[:], in0=kbs[:], scalar1=-1.0, scalar2=None, op0=ALU.mult), [kbs_b], [neg_b_])
                yield 1
                G(lambda e: e.tensor_tensor(out=Gm1[:], in0=bcm(c32[:, cSL, :]), in1=bc8(gbt[:, 0:8]), op=ALU.mult),
                  [c32_b, gbt_b], [Gm1_b])
                V(lambda e: e.tensor_tensor(out=Gm2[:], in0=bcm(c32[:, cU, :]), in1=bc8(gbt[:, 0:8]), op=ALU.mult),
                  [c32_b, gbt_b], [Gm2_b])
                yield 1
                pD = a1.pair()
                pT = a1.pair()
                for hh in range(2):
                    g1 = Gm1[:, hh * 4:(hh + 1) * 4, :].rearrange("p a b -> p (a b)")
                    g2 = Gm2[:, hh * 4:(hh + 1) * 4, :].rearrange("p a b -> p (a b)")

                    def fD(e, hh=hh, g1=g1):
                        e.matmul(ps[:, pD + hh, :], c32[:, cU, :], g1, start=True, stop=False)
                        ins = None
                        for q in range(4):
                            ins = e.matmul(ps[:, pD + hh, q * 128:(q + 1) * 128], I16, c16[:, cN1, :], start=False, stop=(q == 3))
                        return ins
                    T(fD, [Gm1_b, c32_b, c16_b], [ps_b[pD + hh]])

                    def fT(e, hh=hh, g2=g2):
                        e.matmul(ps[:, pT + hh, :], c32[:, cSL, :], g2, start=True, stop=False)
                        ins = None
                        for q in range(4):
                            ins = e.matmul(ps[:, pT + hh, q * 128:(q + 1) * 128], I16, c16[:, cN2, :], start=False, stop=(q == 3))
                        return ins
                    T(fT, [Gm2_b, c32_b, c16_b], [ps_b[pT + hh]])
                yield 1
                A(lambda e: e.activation(out=dsm[:].rearrange("p a b -> p (a b)"), in_=P32(pD, 2), func=AF.Exp),
                  [ps_b[pD], ps_b[pD + 1]], [dsm_b])
                pR = a1.pair()
                for hh in range(2):
                    g2 = Gm2[:, hh * 4:(hh + 1) * 4, :].rearrange("p a b -> p (a b)")
                    T(lambda e, hh=hh, g2=g2: e.matmul(ps[:, pR + hh, :], ONES32, g2, start=True, stop=True),
                      [Gm2_b, c32_b], [ps_b[pR + hh]])
                A(lambda e: e.activation(out=dTm[:].rearrange("p a b -> p (a b)"), in_=P32(pT, 2), func=AF.Exp),
                  [ps_b[pT], ps_b[pT + 1]], [dTm_b])
                A(lambda e: e.activation(out=reg[:].rearrange("p a b -> p (a b)"), in_=P32(pR, 2), func=AF.Exp),
                  [ps_b[pR], ps_b[pR + 1]], [reg_b])
                yield 1
                V(lambda e: e.tensor_tensor(out=qt_, in0=R1[:, 0:8, cs], in1=reg[:], op=ALU.mult), qb + [reg_b], [qt_b_])
                pG = a1.pair()
                pP = a1.pair()
                for h in range(H):
                    T(lambda e, h=h: e.matmul(PV(pG)[:, h, :], R1[:, 8 + h, cs], R1[:, 8 + h, cs], start=True, stop=True),
                      [R1_b[8 + h]], [ps_b[pG + h // 4]])
                for h in range(H):
                    T(lambda e, h=h: e.matmul(PV(pP)[:, h, :], R1[:, 8 + h, cs], R1[:, h, cs], start=True, stop=True),
                      [R1_b[8 + h], R1_b[h]], [ps_b[pP + h // 4]])
                yield 1
                for h in range(H):
                    V(lambda e, h=h: e.scalar_tensor_tensor(out=Am[0][:, h, :], in0=PV(pG)[:, h, :], scalar=gbt[:, 8 + h:9 + h],
                                                            in1=dsm[:, h, :], op0=ALU.mult, op1=ALU.mult),
                      [ps_b[pG + h // 4], gbt_b, dsm_b], [Am_b[0]])
                V(lambda e: e.tensor_tensor(out=PTm_, in0=PV(pP), in1=dTm[:], op=ALU.mult),
                  [ps_b[pP], ps_b[pP + 1], dTm_b], [PTm_b_])
                yield 1
                bB = a1.bank()
                for h in range(H):
                    T(lambda e, h=h: e.transpose(out=P16(bB)[:, h * 128:(h + 1) * 128], in_=Am[0][:, h, :], identity=I16),
                      [Am_b[0], c16_b], [ps_b[bB]])
                BP = P16(bB).rearrange("p (a b) -> p a b", a=8)
                if samp:
                    A(lambda e: e.activation(out=Bm[0], in_=BP, func=AF.Copy), [ps_b[bB]], [Bm_b[0]])
                else:
                    A(lambda e: e.activation(out=Bm[1], in_=BP, func=AF.Copy), [ps_b[bB]], [Bm_b[1]])
                    yield 1
                    G(lambda e: e.tensor_tensor(out=Bm[0], in0=Bm[1], in1=bcm(c16[:, 8, :]), op=ALU.mult), [Bm_b[1], c16_b], [Bm_b[0]])
                    G(lambda e: e.tensor_tensor(out=Am[0], in0=Am[0], in1=bcm(c16[:, 8, :]), op=ALU.mult), [Am_b[0], c16_b], [Am_b[0]])
                    G(lambda e: e.tensor_tensor(out=Bo_, in0=Bm[1], in1=bcm(c16[:, 9, :]), op=ALU.mult), [Bm_b[1], c16_b], [Bo_b_])
                yield 1
                V(lambda e: e.tensor_tensor(out=TT, in0=bcm(I32), in1=Bm[0], op=ALU.subtract), [Bm_b[0], c32_b], [TT_b])
                V(lambda e: e.tensor_tensor(out=Pf, in0=bcm(I32), in1=Bm[0], op=ALU.subtract), [Bm_b[0], c32_b], [rstd_b])
                ca = 0
                for lev in range(1, nlev):
                    na = 1 - ca
                    pA = a1.pair()
                    for h in range(H):
                        T(lambda e, h=h: e.matmul(PV(pA)[:, h, :], Bm[ca][:, h, :], Am[ca][:, h, :], start=True, stop=True),
                          [Am_b[ca], Bm_b[ca]], [ps_b[pA + h // 4]])
                    A(lambda e: e.activation(out=Am[na], in_=PV(pA), func=AF.Copy), [ps_b[pA], ps_b[pA + 1]], [Am_b[na]])
                    yield 1
                    if lev < nlev - 1:
                        pB = a1.pair()
                        for h in range(H):
                            T(lambda e, h=h: e.matmul(PV(pB)[:, h, :], Am[ca][:, h, :], Bm[ca][:, h, :], start=True, stop=True),
                              [Am_b[ca], Bm_b[ca]], [ps_b[pB + h // 4]])
                        A(lambda e: e.activation(out=Bm[na], in_=PV(pB), func=AF.Copy), [ps_b[pB], ps_b[pB + 1]], [Bm_b[na]])
                        yield 1
                    pU = a1.pair()
                    for h in range(H):
                        T(lambda e, h=h: e.matmul(PV(pU)[:, h, :], Am[na][:, h, :], TT[:, h, :], start=True, stop=True),
                          [Am_b[na], TT_b], [ps_b[pU + h // 4]])
                    V(lambda e: e.tensor_tensor(out=TT, in0=PV(pU), in1=Pf, op=ALU.add), [ps_b[pU], ps_b[pU + 1], rstd_b], [TT_b])
                    if lev < nlev - 1:
                        V(lambda e: e.tensor_tensor(out=Pf, in0=PV(pU), in1=Pf, op=ALU.add), [ps_b[pU], ps_b[pU + 1], rstd_b], [rstd_b])
                    yield 1
                    ca = na
                yield "S2"
                r32 = on32[:]
                R32b = [on32_b]
                if samp:
                    pK = a2.pair(); pN = a2.pair(); pO = a2.pair()
                    other = [b_ for b_ in range(8) if b_ not in (pK, pK + 1, pN, pN + 1, pO, pO + 1)]
                    V(lambda e: e.tensor_tensor(out=gsel[:], in0=gbt[:, 0:8].unsqueeze(1).to_broadcast([128, 16, 8]),
                                                in1=seq32[:].unsqueeze(2).to_broadcast([128, 16, 8]), op=ALU.mult),
                      [gbt_b, cb32_b], [gsel_b])
                    bE = other[0]
                    T(lambda e: e.matmul(ps[:, bE, 0:128], ONES32, gsel[:].rearrange("p a b -> p (a b)"), start=True, stop=True),
                      [gsel_b, c32_b], [ps_b[bE]])
                    A(lambda e: e.activation(out=egl[:].rearrange("p a b -> p (a b)"), in_=ps[:, bE, 0:128], func=AF.Exp),
                      [ps_b[bE]], [egl_b])
                    for h in range(H):
                        if h % 2 == 0:
                            s0f, s0f_b, s0b, s0b_b = s0f_0, s0f_b0, s0b_0, s0b_b0
                        else:
                            s0f = t32f[:, 0:2048].rearrange("p (a b) -> p a b", a=16); s0f_b = t32_b[0:4]
                            s0b = t16[:, 4096:6144].rearrange("p (a b) -> p a b", a=16); s0b_b = t32_b[4:6]
                        dma("sp", f"s0in{h % 2}", s0f, sd[l, :, h, :, :].rearrange("s k v -> k s v"), [], s0f_b)
                        A(lambda e: e.activation(out=s0b, in_=s0f, func=AF.Copy), s0f_b, s0b_b)
                        V(lambda e, h=h: e.tensor_tensor(out=msk16[:], in0=R1[:, 8 + h, cs].unsqueeze(1).to_broadcast([128, 16, 128]),
                                                         in1=mskc[:], op=ALU.mult), [R1_b[8 + h], cb16_b], [msk16_b])

                        def fks(e, h=h):
                            ins = None
                            for s_ in range(16):
                                ins = e.matmul(PV(pK)[:, h, :], msk16[:, s_, :], s0b[:, s_, :], start=(s_ == 0), stop=(s_ == 15))
                            return ins
                        T(fks, [msk16_b] + s0b_b, [ps_b[pK + h // 4]])
                        V(lambda e, h=h: e.scalar_tensor_tensor(out=r32[:, h, :], in0=PV(pK)[:, h, :], scalar=neg_[:, h:h + 1],
                                                                in1=Vb_[:, h, :], op0=ALU.mult, op1=ALU.add),
                          [ps_b[pK + h // 4], neg_b_, Vb_b_], R32b)
                        V(lambda e, h=h: e.tensor_copy(out=r16[:, h, :], in_=r32[:, h, :]), R32b, [r16_b])
                        T(lambda e, h=h: e.matmul(PV(pN)[:, h, :], TT[:, h, :], r16[:, h, :], start=True, stop=True),
                          [TT_b, r16_b], [ps_b[pN + h // 4]])
                        A(lambda e, h=h: e.activation(out=vn[:, h, :], in_=PV(pN)[:, h, :], func=AF.Copy), [ps_b[pN + h // 4]], [vn_b])
                        V(lambda e, h=h: e.tensor_tensor(out=msk16[:], in0=qt_[:, h, :].unsqueeze(1).to_broadcast([128, 16, 128]),
                                                         in1=mskc[:], op=ALU.mult), [qt_b_, cb16_b], [msk16_b])

                        def fo(e, h=h):
                            for s_ in range(16):
                                e.matmul(PV(pO)[:, h, :], s0b[:, s_, :], msk16[:, s_, :], start=(s_ == 0), stop=False)
                            return e.matmul(PV(pO)[:, h, :], vn[:, h, :], PTm_[:, h, :], start=False, stop=True)
                        T(fo, s0b_b + [msk16_b, vn_b, PTm_b_], [ps_b[pO + h // 4]])
                        V(lambda e, h=h: e.tensor_tensor(out=msk16[:], in0=Kd_[:, h, :].unsqueeze(1).to_broadcast([128, 16, 128]),
                                                         in1=seqs[:].unsqueeze(2).to_broadcast([128, 16, 128]), op=ALU.mult),
                          [Kd_b_, cb16_b], [msk16_b])
                        for q in range(4):
                            bq = other[1 + (h * 4 + q) % (len(other) - 1)]

                            def fs(e, h=h, q=q, bq=bq):
                                ins = None
                                for s4 in range(4):
                                    ins = e.matmul(ps[:, bq, s4 * 128:(s4 + 1) * 128], msk16[:, q * 4 + s4, :], vn[:, h, :],
                                                   start=True, stop=True)
                                return ins
                            T(fs, [msk16_b, vn_b], [ps_b[bq]])
                            for s4 in range(4):
                                s_ = q * 4 + s4
                                V(lambda e, h=h, s_=s_, s4=s4, bq=bq, q=q: e.scalar_tensor_tensor(
                                    out=snew[q % 2][:, s4, :], in0=s0f[:, s_, :], scalar=egl[:, s_, h:h + 1],
                                    in1=ps[:, bq, s4 * 128:(s4 + 1) * 128], op0=ALU.mult, op1=ALU.add),
                                  s0f_b + [egl_b, ps_b[bq]], [snew_b[q % 2]])
                            dma("sp", f"s0out{q % 2}", sd_s[l, q * 4:(q + 1) * 4, h, :, :].rearrange("s k v -> k s v"), snew[q % 2],
                                [snew_b[q % 2]], [])
                    pQ = a2.pair()
                else:
                    pK, pN, pO, pS, pQ = 4, 6, 4, 6, 6
                    pKv = PV(pK); pNv = PV(pN)
                    for h in range(H):
                        T(lambda e, h=h: e.matmul(PV(pK)[:, h, :], R1[:, 8 + h, cs], S16[:, l, h, :], start=True, stop=True),
                          [R1_b[8 + h], S16_b[l]], [ps_b[pK + h // 4]])
                    V(lambda e: e.tensor_tensor(out=r32, in0=pKv, in1=bc8(neg_[:]), op=ALU.mult), [ps_b[pK], ps_b[pK + 1], neg_b_], R32b)
                    yield 1
                    V(lambda e: e.tensor_tensor(out=r32, in0=r32, in1=Vb_, op=ALU.add), R32b + [Vb_b_], R32b)
                    A(lambda e: e.activation(out=r16[:], in_=r32, func=AF.Copy), R32b, [r16_b])
                    yield 1
                    for h in range(H):
                        T(lambda e, h=h: e.matmul(PV(pN)[:, h, :], TT[:, h, :], r16[:, h, :], start=True, stop=True),
                          [TT_b, r16_b], [ps_b[pN + h // 4]])
                    A(lambda e: e.activation(out=vn[:], in_=pNv, func=AF.Copy), [ps_b[pN], ps_b[pN + 1]], [vn_b])
                    yield 1
                    for h in range(H):
                        T(lambda e, h=h: e.matmul(PV(pK)[:, h, :], Bo_[:, h, :], vn[:, h, :], start=True, stop=True),
                          [Bo_b_, vn_b], [ps_b[pK + h // 4]])
                    V(lambda e: e.tensor_tensor(out=r16[:], in0=r32, in1=pKv, op=ALU.subtract), R32b + [ps_b[pK], ps_b[pK + 1]], [r16_b])
                    yield 1
                    for h in range(H):
                        T(lambda e, h=h: e.matmul(PV(pN)[:, h, :], TT[:, h, :], r16[:, h, :], start=True, stop=True),
                          [TT_b, r16_b], [ps_b[pN + h // 4]])
                    A(lambda e: e.activation(out=vn[:], in_=pNv, func=AF.Copy), [ps_b[pN], ps_b[pN + 1]], [vn_b])
                    yield 1
                    for h in range(H):
                        def fo(e, h=h):
                            e.matmul(PV(pO)[:, h, :], S16[:, l, h, :], qt_[:, h, :], start=True, stop=False)
                            return e.matmul(PV(pO)[:, h, :], vn[:, h, :], PTm_[:, h, :], start=False, stop=True)
                        T(fo, [S16_b[l], qt_b_, vn_b, PTm_b_], [ps_b[pO + h // 4]])
                    yield 1
                    for h in range(H):
                        T(lambda e, h=h: e.matmul(PV(pS)[:, h, :], Kd_[:, h, :], vn[:, h, :], start=True, stop=True),
                          [Kd_b_, vn_b], [ps_b[pS + h // 4]])
                    A(lambda e: e.activation(out=sq[0][:], in_=P32(pO, 2), func=AF.Square, scale=float(128 ** -0.5)),
                      [ps_b[pO], ps_b[pO + 1]], [sq_b[0]])
                    yield 1
                    for h in range(H):
                        V(lambda e, h=h: e.scalar_tensor_tensor(out=S32[:, l, h, :], in0=S32[:, l, h, :], scalar=es_[:, 16 + h:17 + h],
                                                                in1=PV(pS)[:, h, :], op0=ALU.mult, op1=ALU.add),
                          [S32_b[l], es_b_, ps_b[pS + h // 4]], [S32_b[l]])
                    A(lambda e: e.activation(out=S16[:, l, :, :], in_=S32[:, l, :, :], func=AF.Copy), [S32_b[l]], [S16_b[l]])
                    yield 1
                if samp:
                    A(lambda e: e.activation(out=sq[0][:], in_=P32(pO, 2), func=AF.Square, scale=float(128 ** -0.5)),
                      [ps_b[pO], ps_b[pO + 1]], [sq_b[0]])
                for hh in range(2):
                    T(lambda e, hh=hh: e.matmul(ps[:, pQ + hh, :], ONES16, sq[0][:, hh * 512:(hh + 1) * 512], start=True, stop=True),
                      [sq_b[0], c16_b], [ps_b[pQ + hh]])
                rsqrt_from_psum(rstd9, P32(pQ, 2), 1.0 / 128, 1e-6, [ps_b[pQ], ps_b[pQ + 1]], t32_b[6:8])
                yield 1
                V(lambda e: e.scalar_tensor_tensor(out=on32[:].rearrange("p a b -> p (a b)"), in0=P32(pO, 2), scalar=dnw[:, l:l + 1],
                                                   in1=rstd9, op0=ALU.mult, op1=ALU.mult),
                  [ps_b[pO], ps_b[pO + 1], dnw_b] + t32_b[6:8], [on32_b])
                G(lambda e: e.tensor_tensor(out=oa[:, :, cs], in0=on32[:], in1=sz[:, :, cs], op=ALU.mult),
                  [on32_b] + sz_b, oa_b)
                yield 1

            cur = None
            for g in [chunk_gen(t) for t in range(NT)] + [None]:
                g_done = g is None
                c_done = cur is None
                while not (g_done and c_done):
                    if not c_done:
                        try:
                            next(cur)
                        except StopIteration:
                            c_done = True
                    if not g_done:
                        if next(g) == "S2":
                            g_done = True
                cur = g

            if (not samp) and last_blk:
                dma("sp", "o_sd", sd_p[l].rearrange("h k v -> k h v"), S32[:, l, :, :], [S32_b[l]], [])

            if stop <= 4:
                return
            def uv_consume(ch, bk):
                A(lambda e: e.activation(out=R1[:, ch, 0:W], in_=ps[:, bk, 0:W], func=AF.Gelu), [ps_b[bk]], [R1_b[ch]])
            proj_chunks("w_in", l, 4112, 16, W, hT, hT_b, uv_consume)
            boff = (l * 2 + (1 if samp else 0)) * 1024
            dma("sp", "lnw", t32f[:, 0:2048], lnw[l].rearrange("b d -> (b d)").partition_broadcast(128), [], t32_b[0:4])
            dma("pool", "prw", prw16[:], prow[:, boff:boff + 1024], [], [prw_b])
            lnt16 = t16[:, 4096:6144].rearrange("p (a b) -> p a b", a=2)
            A(lambda e: e.activation(out=t16[:, 4096:6144], in_=t32f[:, 0:2048], func=AF.Copy), t32_b[0:4], t32_b[4:6])
            dma("sp", "wsp", ws32[:], wsp[l, 1 if samp else 0], [], [ws32_b])
            V(lambda e: e.tensor_tensor(out=wsT[:], in0=ws32[:].rearrange("p (a b) -> p a b", a=8), in1=bcm(c32[:, cU, :]),
                                        op=ALU.mult), [ws32_b, c32_b], [wsT_b])
            for t in range(NT):
                cs = slice(t * 128, (t + 1) * 128)
                bk = bank()
                for g in range(8):
                    T(lambda e, g=g: e.transpose(out=P16(bk)[:, g * 128:(g + 1) * 128], in_=R1[:, 8 + g, cs], identity=I16),
                      [vT_b[g], c16_b], [ps_b[bk]])
                for hh in range(2):
                    V(lambda e, hh=hh: e.bn_stats(out=bnst[:, hh, :], in_=P16(bk)[:, hh * 512:(hh + 1) * 512]), [ps_b[bk]], [bn_b])
                V(lambda e: e.bn_aggr(out=mv[:, 0:2], in_=bnst[:].rearrange("p a b -> p (a b)")), [bn_b], [bn_b])
                A(lambda e: e.activation(out=mv[:, 2:3], in_=mv[:, 1:2], func=AF.Ln, bias=1e-5), [bn_b], [bn_b])
                A(lambda e: e.activation(out=mv[:, 2:3], in_=mv[:, 2:3], func=AF.Exp, scale=-0.5), [bn_b], [bn_b])
                V(lambda e: e.tensor_scalar(out=vnb[:], in0=P16(bk), scalar1=mv[:, 0:1], scalar2=mv[:, 2:3],
                                            op0=ALU.subtract, op1=ALU.mult), [ps_b[bk], bn_b], [vnb_b])
                V(lambda e: e.tensor_tensor(out=vnb[:], in0=vnb[:], in1=lnt16[:, 0, :], op=ALU.mult), [vnb_b] + t32_b[4:6], [vnb_b])
                V(lambda e: e.tensor_tensor(out=vnb[:], in0=vnb[:], in1=lnt16[:, 1, :], op=ALU.add), [vnb_b] + t32_b[4:6], [vnb_b])
                if samp or (last_blk and t == NT - 1):
                    A(lambda e: e.activation(out=lnv32, in_=vnb[:], func=AF.Copy), [vnb_b], [lnv32_b])
                    dma("sp", "o_cv", cv_s[l] if samp else cv_p[l], lnv32, [lnv32_b], [])
                pM = pair()
                for g in range(8):
                    def fm(e, g=g):
                        e.matmul(PV(pM)[:, g, :], vnb[:, g * 128:(g + 1) * 128], wsT[:, g, :], start=True, stop=False)
                        return e.matmul(PV(pM)[:, g, :], c16[0:1, 1, :], prw16[0:1, g * 128:(g + 1) * 128],
                                        start=False, stop=True)
                    T(fm, [vnb_b, wsT_b, c16_b, prw_b], [ps_b[pM + g // 4]])
                V(lambda e: e.tensor_tensor(out=R1[:, 0:8, cs], in0=R1[:, 0:8, cs], in1=PV(pM), op=ALU.mult),
                  uT_b + [ps_b[pM], ps_b[pM + 1]], uT_b)

            if stop <= 5:
                return
            mg, mg_b = sz, sz_b
            for half in range(2):
                sA = load_w(wview("w_in", l, 0, 1024, 6160 + half * 512, 512), 8, 512)
                sPA = load_w(wview("w_pa", l, 0, 1024, half * 512, 512), 8, 512)
                for jj in range(4):
                    d = half * 4 + jj
                    b1 = bank()
                    mm_group(b1, 0, W, lambda k, jj=jj: wsl[sA][:, k, jj * 128:(jj + 1) * 128], lambda k: hT[:, k, 0:W], 8,
                             hT_b, [wsl_b[sA]])
                    b2 = bank()
                    mm_group(b2, 0, W, lambda k, jj=jj: wsl[sPA][:, k, jj * 128:(jj + 1) * 128], lambda k: oa[:, k, 0:W], 8,
                             oa_b, [wsl_b[sPA]])
                    A(lambda e, b1=b1: e.activation(out=acc[0][:, 0:W], in_=ps[:, b1, 0:W], func=AF.Sigmoid), [ps_b[b1]], [acc_b[0]])
                    V(lambda e, d=d, b2=b2: e.tensor_tensor(out=t32[:, d, 0:W], in0=acc[0][:, 0:W], in1=ps[:, b2, 0:W], op=ALU.mult),
                      [acc_b[0], ps_b[b2]], [t32_b[d]])
                w_done(2)
            for half in range(2):
                sB = load_w(wview("w_in", l, 0, 1024, 7184 + half * 512, 512), 8, 512)
                sPB = load_w(wview("w_pb", l, 0, 1024, half * 512, 512), 8, 512)
                for jj in range(4):
                    d = half * 4 + jj
                    b1 = bank()
                    mm_group(b1, 0, W, lambda k, jj=jj: wsl[sB][:, k, jj * 128:(jj + 1) * 128], lambda k: hT[:, k, 0:W], 8,
                             hT_b, [wsl_b[sB]])
                    b2 = bank()
                    mm_group(b2, 0, W, lambda k, jj=jj: wsl[sPB][:, k, jj * 128:(jj + 1) * 128], lambda k: R1[:, k, 0:W], 8,
                             uT_b, [wsl_b[sPB]])
                    A(lambda e, b1=b1: e.activation(out=acc[1][:, 0:W], in_=ps[:, b1, 0:W], func=AF.Sigmoid), [ps_b[b1]], [acc_b[1]])
                    V(lambda e, b2=b2: e.tensor_tensor(out=acc[1][:, 0:W], in0=acc[1][:, 0:W], in1=ps[:, b2, 0:W], op=ALU.mult),
                      [acc_b[1], ps_b[b2]], [acc_b[1]])
                    G(lambda e, d=d: e.tensor_tensor(out=mg[:, d, 0:W], in0=acc[1][:, 0:W], in1=t32[:, d, 0:W], op=ALU.add),
                      [acc_b[1], t32_b[d]], [mg_b[d]])
                w_done(2)
            proj_chunks("w_o", l, 0, 8, W, mg, mg_b,
                        lambda ch, bk: A(lambda e: e.activation(out=t32[:, ch, 0:W], in_=ps[:, bk, 0:W], func=AF.Copy),
                                         [ps_b[bk]], [t32_b[ch]]))
            rmsnorm(t32, t32_b, W, NW(l, 1), "x", l)
            if stop <= 6:
                return
            rmsnorm(xT, xT_b, W, NW(l, 2), "h", l)
            hid, hid_b = R1, R1_b
            proj_chunks("w_f1", l, 0, 22, W, hT, hT_b,
                        lambda ch, bk: A(lambda e: e.activation(out=hid[:, ch, 0:W], in_=ps[:, bk, 0:W], func=AF.Silu),
                                         [ps_b[bk]], [hid_b[ch]]))
            proj_chunks("w_f1", l, DFF, 22, W, hT, hT_b,
                        lambda ch, bk: V(lambda e: e.tensor_tensor(out=hid[:, ch, 0:W], in0=hid[:, ch, 0:W], in1=ps[:, bk, 0:W],
                                                                   op=ALU.mult), [hid_b[ch], ps_b[bk]], [hid_b[ch]]))
            for half in range(2):
                bks = [bank() for _ in range(4)]
                for kg, (k0, nk) in enumerate(((0, 8), (8, 8), (16, 6))):
                    s = load_w(wview("w_f2", l, k0 * 128, nk * 128, half * 512, 512), nk, 512)
                    for jj in range(4):
                        def ff(e, s=s, jj=jj, k0=k0, nk=nk, kg=kg):
                            ins = None
                            for k in range(nk):
                                ins = e.matmul(ps[:, bks[jj], 0:W], wsl[s][:, k, jj * 128:(jj + 1) * 128], hid[:, k0 + k, 0:W],
                                               start=(kg == 0 and k == 0), stop=(kg == 2 and k == nk - 1))
                            return ins
                        T(ff, hid_b[k0:k0 + nk] + [wsl_b[s]], [ps_b[bks[jj]]])
                    w_done()
                for jj in range(4):
                    d = half * 4 + jj
                    A(lambda e, d=d, jj=jj: e.activation(out=t32[:, d, 0:W], in_=ps[:, bks[jj], 0:W], func=AF.Copy),
                      [ps_b[bks[jj]]], [t32_b[d]])
            rmsnorm(t32, t32_b, W, NW(l, 3), "x", l)

        def load_x(src, W):
            for t in range(W // 128):
                dma("sp", "xin", xin, src[t * 128:(t + 1) * 128, :], [], t32_b[0:2])
                for hh in range(2):
                    bk = bank()
                    for c4 in range(4):
                        c = hh * 4 + c4
                        T(lambda e, c=c, c4=c4, bk=bk: e.transpose(out=ps[:, bk, c4 * 128:(c4 + 1) * 128], in_=xin[:, c * 128:(c + 1) * 128],
                                                                  identity=I32), t32_b[0:2] + [c32_b], [ps_b[bk]])
                    A(lambda e, hh=hh, bk=bk, t=t: e.activation(out=xT[:, hh * 4:(hh + 1) * 4, t * 128:(t + 1) * 128],
                                                               in_=ps[:, bk, :].rearrange("p (a b) -> p a b", a=4), func=AF.Copy),
                      [ps_b[bk]], xT_b[hh * 4:(hh + 1) * 4])

        def store_y(dst, W):
            for t in range(W // 128):
                yo = t % 2
                for hh in range(2):
                    bk = bank()
                    for c4 in range(4):
                        c = hh * 4 + c4
                        T(lambda e, c=c, c4=c4, bk=bk, t=t: e.transpose(out=ps[:, bk, c4 * 128:(c4 + 1) * 128],
                                                                       in_=xT[:, c, t * 128:(t + 1) * 128], identity=I32),
                          [xT_b[c], c32_b], [ps_b[bk]])
                    A(lambda e, hh=hh, bk=bk, yo=yo: e.activation(out=yout[yo][:, hh * 512:(hh + 1) * 512], in_=ps[:, bk, :],
                                                                  func=AF.Copy), [ps_b[bk]], t32_b[2 + 2 * yo:4 + 2 * yo])
                dma("sp", f"yo{yo}", dst[t * 128:(t + 1) * 128, :], yout[yo], t32_b[2 + 2 * yo:4 + 2 * yo], [])

        G(lambda e: e.memset(S32[:].rearrange("p a b c -> p (a b c)"), 0.0), [], S32_b)
        G(lambda e: e.memset(S16[:].rearrange("p a b c -> p (a b c)"), 0.0), [], S16_b)
        G(lambda e: e.memset(carry[:].rearrange("p a b c -> p (a b c)"), 0.0), [], carry_b[0] + carry_b[1])

        for blk in range(nblk):
            load_x(xp[blk * 512:(blk + 1) * 512, :], 512)
            for l in range(DEPTH):
                E.epoch += 1
                layer(l, 512, False, blk == nblk - 1)
            store_y(y_p[blk * 512:(blk + 1) * 512, :], 512)
        E.epoch += 1
        if do_samp:
            if all_reqs is None:
                SAMP_K0[0] = slot_i[0]
            samp_k0[0] = slot_i[0]
            for hf in range(2):
                dma("pool", "cst16", mskc[:, hf * 8:(hf + 1) * 8, :].rearrange("p a b -> p (a b)"), cbd[:, hf * 1024:(hf + 1) * 1024],
                    [], [cb16_b, wsl_b[3], wsl_b[4], msk16_b, scin_b, snew_b[0]])
            mask_loaded[0] = True
            if all_reqs is not None:
                pump()
            load_x(xs, 128)
            for l in range(DEPTH):
                E.epoch += 1
                layer(l, 128, True, False)
            store_y(y_s, 128)

        E.finalize()
        if os.environ.get("KDBG"):
            print("est_span_us", getattr(E, "est_span", None), "n_ops", len(E.ops), flush=True)
        sems = {}
        for e_ in ENGS:
            for ep in range(E.epoch + 1):
                sems[(e_, ep)] = es.enter_context(nc.semaphore(f"s_{e_}_{ep}"))
        chansems = {ch: es.enter_context(nc.semaphore(f"c_{ch}")) for ch in E.chans}
        block = es.enter_context(nc.Block())
        E.emit(block, sems, chansems)
    return nc


def _pack(inp):
    f = lambda a: np.ascontiguousarray(np.asarray(a, dtype=np.float32))
    pp = np.zeros((128, 262), np.float32)
    names = ["norm_pre_mix", "norm_post_mix", "norm_pre_ffn", "norm_post_ffn"]
    for l in range(DEPTH):
        for n, nm in enumerate(names):
            pp[:, (l * 4 + n) * 8:(l * 4 + n) * 8 + 8] = f(inp[nm])[l].reshape(8, 128).T
        cw = f(inp["conv_w"])[l]
        pp[:, 64 + l * 96:64 + (l + 1) * 96] = cw.reshape(4, 24, 128).transpose(2, 0, 1).reshape(128, 96)
        pp[:, 256 + l] = f(inp["delta_norm_w"])[l]
        pp[0:8, 258 + l] = f(inp["a_log"])[l]
        pp[0:8, 260 + l] = f(inp["dt_bias"])[l]
    bs = f(inp["b_spatial"])
    prow = np.zeros((DEPTH, 2, 8, 128), np.float32)
    prow[:, 0] = bs
    prow[:, 1] = np.tile(bs[:, :, :8], (1, 1, 16))
    prow = prow.reshape(1, -1)
    lnw = np.stack([f(inp["sgu_ln_w"]), f(inp["sgu_ln_b"])], axis=1)
    ws = f(inp["w_spatial"])
    wsT = ws.transpose(0, 3, 1, 2)
    wss = np.tile(ws[:, :, :8, :8], (1, 1, 16, 16)).transpose(0, 3, 1, 2)
    wsp = np.ascontiguousarray(np.stack([wsT, wss], axis=1).reshape(DEPTH, 2, 128, 1024))
    shared = dict(
        w_in=f(inp["w_in"]), w_pa=f(inp["w_proj_a"]), w_pb=f(inp["w_proj_b"]), w_o=f(inp["w_out"]),
        w_f1=f(inp["w_ffn_in"]), w_f2=f(inp["w_ffn_out"]), pp=pp, prow=prow, lnw=np.ascontiguousarray(lnw), wsp=wsp,
        cst=_consts()[0], cst16=_consts()[1], cbd=_cb(),
    )
    xpr = f(inp["x_prompt"]); xsm = f(inp["x_sample"]); sdl = f(inp["state_delta"]); scv = f(inp["state_conv"])
    maps = []
    for c in range(NCORES):
        m = dict(shared)
        m["xp"] = xpr[c]
        m["xs"] = np.ascontiguousarray(xsm[16 * c:16 * (c + 1)].reshape(128, D))
        m["sd"] = np.ascontiguousarray(sdl[:, 16 * c:16 * (c + 1)])
        m["sc"] = np.ascontiguousarray(scv[:, 16 * c:16 * (c + 1)].reshape(DEPTH, 48, 3072))
        maps.append(m)
    return maps


def kernel(**inputs):
    maps = _pack(inputs)
    nc = build()
    res = run_bass_kernel_spmd(nc, maps, core_ids=list(range(NCORES)))
    r = res.results
    y_p = np.stack([r[c]["y_p"] for c in range(NCORES)], 0).reshape(8, 2048, D)
    y_s = np.concatenate([r[c]["y_s"].reshape(16, 8, D) for c in range(NCORES)], 0)
    sd_p = np.stack([r[c]["sd_p"] for c in range(NCORES)], 1)
    sc_p = np.stack([r[c]["sc_p"] for c in range(NCORES)], 1)
    cv_p = np.stack([r[c]["cv_p"] for c in range(NCORES)], 1)
    sd_s = np.concatenate([r[c]["sd_s"] for c in range(NCORES)], 1)
    sc_s = np.concatenate([r[c]["sc_s"] for c in range(NCORES)], 1)
    cv_s = np.concatenate([r[c]["cv_s"].reshape(DEPTH, 16, 8, D) for c in range(NCORES)], 1)
    return tuple(np.ascontiguousarray(a.astype(np.float32)) for a in (y_p, y_s, sd_p, sc_p, cv_p, sd_s, sc_s, cv_s))
```

```python
import os
import numpy as np
from contextlib import ExitStack
import concourse.bass as bass
import concourse.mybir as mybir
from concourse.bass_utils import run_bass_kernel_spmd

F32 = mybir.dt.float32
BF16 = mybir.dt.bfloat16
AF = mybir.ActivationFunctionType
ALU = mybir.AluOpType

NCORES = 8
D = 1024
DEPTH = 2
H = 8
DIN = 8208
DFF = 2816
BIG = 30000.0
NSLOT = 3
ENGS = ["pe", "act", "dve", "pool", "sp"]
BNAME = {"pe": "tensor", "act": "scalar", "dve": "vector", "pool": "gpsimd", "sp": "sync"}
NEPOCH = 12
SCHED = True


class Buf:
    __slots__ = ("name", "lw", "rd")

    def __init__(self, name):
        self.name = name
        self.lw = None
        self.rd = []


def bufs(name, n):
    return [Buf(f"{name}{i}") for i in range(n)]


class Rec:
    def __init__(self):
        self.calls = []

    def __getattr__(self, name):
        def f(*a, **k):
            self.calls.append((name, a, k))
            return self
        return f


class Em:
    def __init__(self):
        self.ops = []
        self.chans = {}
        self.epoch = 0

    def op(self, eng, fn, reads=(), writes=(), chan=None, ndma=0):
        idx = len(self.ops)
        deps = set()
        soft = set()
        for b in reads:
            if b.lw is not None:
                deps.add(b.lw)
        for b in writes:
            if b.lw is not None:
                soft.add(b.lw)
            soft.update(b.rd)
        deps |= soft
        val = None
        if chan:
            c = self.chans.setdefault(chan, {"count": 0, "last": None, "eng": eng})
            assert c["eng"] == eng
            if c["last"] is not None:
                deps.add(c["last"])
            c["count"] += 16 * ndma
            c["last"] = idx
            val = c["count"]
        key = chan if chan else eng
        for b in writes:
            b.lw = idx
            b.rd = []
        for b in reads:
            b.rd.append(idx)
        rec = Rec()
        fn(rec)
        assert rec.calls
        self.ops.append(dict(eng=eng, calls=rec.calls, deps=deps, chan=chan, val=val, inc=False, ep=self.epoch))
        return idx

    @staticmethod
    def _est(o):
        eng = o["eng"]
        tot = 0.0
        for name, a, k in o["calls"]:
            out = k.get("out", a[0] if a else None)
            try:
                fs = float(out.free_size())
            except Exception:
                fs = 128.0
            if o["chan"]:
                try:
                    nb = float(out.nbytes())
                except Exception:
                    nb = 1e5
                tot += 2.0 + nb / 1.8e5
            elif eng == "pe":
                lhs = k.get("lhsT", a[1] if len(a) > 1 else None) if name == "matmul" else k.get("in_")
                f32 = getattr(lhs, "dtype", None) == F32
                tot += 0.045 + max(fs, 64.0) * (4.0 if (f32 and name == "matmul") else 1.0) / 2400.0 + (0.05 if fs <= 128 else 0.0)
            elif eng == "dve":
                tot += 0.12 + fs * 1.1e-3
            elif eng == "act":
                tot += 0.17 + fs * 0.9e-3
            else:
                tot += 0.25 + fs * 2.2e-3
        return tot

    def schedule(self):
        import heapq
        ops = self.ops
        n = len(ops)
        succ = [[] for _ in range(n)]
        indeg = [0] * n
        for i, o in enumerate(ops):
            for d in o["deps"]:
                succ[d].append(i)
                indeg[i] += 1
        dur = [self._est(o) for o in ops]
        bl = [0.0] * n
        for i in range(n - 1, -1, -1):
            m_ = 0.0
            for j in succ[i]:
                if bl[j] > m_:
                    m_ = bl[j]
            bl[i] = dur[i] + 0.3 + m_
        PRIO = os.environ.get("KPRIO", "bl")
        ready = [0.0] * n
        fin = [0.0] * n
        free = {e: 0.0 for e in ENGS}
        heaps = {e: [] for e in ENGS}
        for i in range(n):
            if indeg[i] == 0:
                heapq.heappush(heaps[ops[i]["eng"]], (0.0, i))
        order = []
        LAT = 0.15
        while len(order) < n:
            best = None
            for e in ENGS:
                hp = heaps[e]
                if not hp:
                    continue
                cand = hp[0]
                st = max(free[e], cand[0])
                key = (st, cand[1])
                if best is None or key < best[0]:
                    best = (key, e)
            (st, i), e = best
            hp = heaps[e]
            pool_ = []
            while hp and hp[0][0] <= st:
                pool_.append(heapq.heappop(hp))
            if PRIO == "bl":
                pool_.sort(key=lambda x: (-bl[x[1]], x[1]))
            else:
                pool_.sort(key=lambda x: x[1])
            _, i = pool_[0]
            for x in pool_[1:]:
                heapq.heappush(hp, x)
            o = ops[i]
            fin[i] = st + dur[i]
            free[e] = st + (min(dur[i], 1.0) if o["chan"] else dur[i])
            order.append(i)
            for j in succ[i]:
                r = fin[i] + (LAT if ops[j]["eng"] != e or o["chan"] else 0.1)
                if r > ready[j]:
                    ready[j] = r
                indeg[j] -= 1
                if indeg[j] == 0:
                    heapq.heappush(heaps[ops[j]["eng"]], (ready[j], j))
        pos = {old: new for new, old in enumerate(order)}
        new_ops = []
        for old in order:
            o = ops[old]
            o["deps"] = {pos[d] for d in o["deps"]}
            new_ops.append(o)
        self.ops = new_ops
        self.est_span = max(fin) if fin else 0.0
        if os.environ.get("KDBG"):
            cp = [0.0] * n
            for old in order:
                o = ops[old]
            newdur = [dur[old] for old in order]
            for i_, o in enumerate(new_ops):
                st_ = 0.0
                for d in o["deps"]:
                    st_ = max(st_, cp[d] + LAT)
                cp[i_] = st_ + newdur[i_]
            busy = {}
            for i_, o in enumerate(new_ops):
                busy[o["eng"]] = busy.get(o["eng"], 0.0) + (min(newdur[i_], 1.0) if o["chan"] else newdur[i_])
            print("critical_path_us", max(cp), "busy", {k: round(v) for k, v in busy.items()}, flush=True)
            i_ = max(range(n), key=lambda q: cp[q])
            agg = {}
            seq = []
            while True:
                o = new_ops[i_]
                nm = o["calls"][0][0] + ("/dma" if o["chan"] else "")
                outap = o["calls"][0][2].get("out", o["calls"][0][1][0] if o["calls"][0][1] else None)
                tn = getattr(getattr(outap, "tensor", None), "name", "?")
                key = (o["eng"], nm, tn)
                a_ = agg.setdefault(key, [0, 0.0])
                a_[0] += 1
                a_[1] += newdur[i_] + LAT
                seq.append(key)
                prev = None
                for d in o["deps"]:
                    if prev is None or cp[d] > cp[prev]:
                        prev = d
                if prev is None:
                    break
                i_ = prev
            for k_, v_ in sorted(agg.items(), key=lambda kv: -kv[1][1])[:28]:
                print("   CP", k_, v_[0], round(v_[1], 1), flush=True)

    def prune(self):
        for o in self.ops:
            best = {}
            for d in o["deps"]:
                p = self.ops[d]
                key = ("c", p["chan"]) if p["chan"] else ("e", p["eng"])
                if key not in best or d > best[key]:
                    best[key] = d
            o["deps"] = set(best.values())

    def finalize(self):
        if SCHED:
            self.schedule()
        self.prune()
        for o in self.ops:
            for d in o["deps"]:
                p = self.ops[d]
                if p["chan"] is None:
                    if p["eng"] == "pe" and o["eng"] == "pe" and o["chan"] is None:
                        continue
                    p["inc"] = True
        cnt = {}
        for o in self.ops:
            if o["chan"] is None and o["inc"]:
                k = (o["eng"], o["ep"])
                cnt[k] = cnt.get(k, 0) + 1
                o["val"] = cnt[k]

    def emit(self, block, sems, chansems):
        for e in ENGS:
            ops_e = [o for o in self.ops if o["eng"] == e]

            def body(eng, ops_e=ops_e, e=e):
                seen = {}
                for o in ops_e:
                    for d in sorted(o["deps"]):
                        p = self.ops[d]
                        if p["chan"] is None:
                            if not p["inc"]:
                                continue
                            if p["eng"] == "pe" and e == "pe" and o["chan"] is None:
                                continue
                            key = (p["eng"], p["ep"])
                            sem = sems[key]
                        else:
                            key = p["chan"]
                            sem = chansems[key]
                        v = p["val"]
                        if seen.get(key, 0) >= v:
                            continue
                        seen[key] = v
                        eng.wait_ge(sem, v)
                    ins = None
                    for name, a, k in o["calls"]:
                        ins = getattr(eng, name)(*a, **k)
                        if o["chan"]:
                            ins.then_inc(chansems[o["chan"]], 16)
                    if (not o["chan"]) and o["inc"]:
                        ins.then_inc(sems[(e, o["ep"])], 1)
                for ch, c in self.chans.items():
                    if c["eng"] == e and c["count"] > 0:
                        eng.wait_ge(chansems[ch], c["count"])

            getattr(block, BNAME[e])(body)


def _consts():
    i = np.arange(128)
    m = i[:, None]
    p = i[None, :]
    blk = i // 8
    same = blk[:, None] == blk[None, :]
    c = np.zeros((13, 128, 128), np.float32)
    c[0] = np.eye(128)
    c[1] = 1.0
    c[2] = m <= p
    c[3] = m > p
    c[4] = -BIG * (p >= m)
    c[5] = -BIG * (p < m)
    c[6] = (m <= p) & same
    c[7] = (m > p) & same
    c[8] = -BIG * (~((p < m) & same))
    c[9] = -BIG * (~((p >= m) & same))
    c[10] = same
    s = np.zeros((128, 16, 16), np.float32)
    for j in range(16):
        s[:, j, j] = 1.0
    c[11] = s.reshape(128, 256)[:, :128]
    c[12] = s.reshape(128, 256)[:, 128:]
    c32 = np.ascontiguousarray(c[[0, 1, 2, 3, 6, 7, 10]])
    md = ((m // 64) == (p // 64)).astype(np.float32)
    mo = ((m < 64) & (p >= 64)).astype(np.float32)
    c16 = np.ascontiguousarray(np.concatenate([c[[0, 1, 4, 5, 8, 9, 11, 12]], md[None], mo[None]], 0))
    return c32, c16


def _cb():
    i = np.arange(128)
    maskc = (i[None, :] // 8 == np.arange(16)[:, None]).astype(np.float32)
    maskc = np.broadcast_to(maskc[None], (128, 16, 128)).reshape(128, 2048)
    seqsel = (i[:, None] // 8 == np.arange(16)[None, :]).astype(np.float32)
    return np.concatenate([maskc, seqsel], axis=1).astype(np.float32)


def build(nblk=4, do_samp=True, stop=99):
    reqs = []
    _build(nblk, do_samp, stop, reqs, None)
    return _build(nblk, do_samp, stop, [], reqs)


LOOKAHEAD = 2
SAMP_K0 = [None]


def _build(nblk, do_samp, stop, rec_reqs, all_reqs):
    nc = bass.Bass("TRN2", target_bir_lowering=False)

    def din(name, shape):
        return nc.dram_tensor(name, list(shape), F32, kind="ExternalInput").ap()

    def dout(name, shape):
        return nc.dram_tensor(name, list(shape), F32, kind="ExternalOutput").ap()

    xp = din("xp", [2048, D])
    xs = din("xs", [128, D])
    sd = din("sd", [DEPTH, 16, H, 128, 128])
    sc = din("sc", [DEPTH, 48, 3072])
    w_in = din("w_in", [DEPTH, D, DIN])
    w_pa = din("w_pa", [DEPTH, D, D])
    w_pb = din("w_pb", [DEPTH, D, D])
    w_o = din("w_o", [DEPTH, D, D])
    w_f1 = din("w_f1", [DEPTH, D, 2 * DFF])
    w_f2 = din("w_f2", [DEPTH, DFF, D])
    pp = din("pp", [128, 2 * 32 + 2 * 96 + 2 + 4])
    prow = din("prow", [1, DEPTH * 2 * 1024])
    lnw = din("lnw", [DEPTH, 2, D])
    wsp = din("wsp", [DEPTH, 2, 128, 1024])
    cst = din("cst", [7, 128, 128])
    cst16 = din("cst16", [10, 128, 128])
    cbd = din("cbd", [128, 2064])

    NSCR = 2 * 41
    wscr = nc.dram_tensor("wscr", [NSCR, 128, 4096], BF16, kind="Internal").ap()
    wscr_b = bufs("wscr", NSCR)
    y_p = dout("y_p", [2048, D])
    y_s = dout("y_s", [128, D])
    sd_p = dout("sd_p", [DEPTH, H, 128, 128])
    sc_p = dout("sc_p", [DEPTH, 3, 3072])
    cv_p = dout("cv_p", [DEPTH, 128, D])
    sd_s = dout("sd_s", [DEPTH, 16, H, 128, 128])
    sc_s = dout("sc_s", [DEPTH, 16, 3, 3072])
    cv_s = dout("cv_s", [DEPTH, 128, D])

    E = Em()
    es = ExitStack()
    with es:
        def sb(name, shape, dt):
            return es.enter_context(nc.sbuf_tensor(name, shape, dt))

        xT = sb("xT", [128, 8, 512], F32); xT_b = bufs("xT", 8)
        hT = sb("hT", [128, 8, 512], BF16); hT_b = bufs("hT", 8)
        R1 = sb("R1", [128, 24, 512], BF16); R1_b = bufs("R1", 24)
        sz = sb("sz", [128, 8, 512], BF16); sz_b = bufs("sz", 8)
        oa = sb("oa", [128, 8, 512], BF16); oa_b = bufs("oa", 8)
        uT_b = R1_b[0:8]; vT_b = R1_b[8:16]
        t32 = sb("t32", [128, 8, 512], F32); t32_b = bufs("t32", 8)
        pre = [sb(f"pre{i}", [128, 515], F32) for i in range(2)]; pre_b = bufs("pre", 2)
        acc = [sb(f"acc{i}", [128, 512], F32) for i in range(2)]; acc_b = bufs("acc", 2)
        sq = [sb(f"sq{i}", [128, 1024], BF16) for i in range(2)]; sq_b = bufs("sq", 2)
        rstd = sb("rstd", [128, 1024], F32); rstd_b = Buf("rstd")
        carry = sb("carry", [128, DEPTH, 24, 3], F32); carry_b = [bufs(f"car{l}_", 24) for l in range(DEPTH)]
        wsl = [sb(f"wsl{i}", [128, 8, 512], BF16) for i in range(5)]; wsl_b = bufs("wsl", 5)
        w3f = wsl[3][:].rearrange("p a b -> p (a b)")
        w4f = wsl[4][:].rearrange("p a b -> p (a b)")
        ppt = sb("ppt", [128, 262], F32); ppt_b = Buf("ppt")
        nea = sb("nea", [128, 2], F32); nea_b = Buf("nea")
        dnw = sb("dnw", [128, 2], F32); dnw_b = Buf("dnw")
        c32 = sb("c32", [128, 7, 128], F32); c32_b = Buf("c32")
        c16 = sb("c16", [128, 10, 128], BF16); c16_b = Buf("c16")
        seq32 = sb("seq32", [128, 16], F32); cb32_b = Buf("cb32")
        mskc = w3f[:, 0:2048].rearrange("p (a b) -> p a b", a=16)
        seqs = sb("seqs", [128, 16], BF16); cb16_b = Buf("cb16")
        selT = sb("selT", [16, 16, 128], BF16); selT_b = Buf("selT")
        t32f = t32[:].rearrange("p a b -> p (a b)")
        prw16 = t32f.bitcast(BF16)[0:1, 6144:7168]; prw_b = t32_b[6]
        lnt = t32f[:, 0:2048].rearrange("p (a b) -> p a b", a=2)
        xin = t32f[:, 0:1024]
        yout = [t32f[:, 1024:2048], t32f[:, 2048:3072]]
        scrow = t32f[0:3, 0:3072]
        scin = w4f[:, 0:2304].bitcast(F32).rearrange("p (a b) -> p a b", a=24); scin_b = Buf("scin")
        wsT = sb("wsT", [128, 8, 128], BF16); wsT_b = Buf("wsT")
        gT = pre[0][0:8, 0:512]; gT_b = pre_b[0]
        bT = pre[1][0:8, 0:512]; bT_b = pre_b[1]
        rn = acc[1][0:16, :]; rn_b = acc_b[1]
        rnb = acc[0][0:16, 0:256].bitcast(BF16); rnb_b = acc_b[0]
        rnl = acc[0][0:16, 256:512].bitcast(BF16)
        S32 = sb("S32", [128, DEPTH, H, 128], F32); S32_b = bufs("S32_", DEPTH)
        S16 = sb("S16", [128, DEPTH, H, 128], BF16); S16_b = bufs("S16_", DEPTH)
        s0f_0 = S32[:].rearrange("p a b c -> p (a b) c"); s0f_b0 = S32_b
        s0b_0 = S16[:].rearrange("p a b c -> p (a b) c"); s0b_b0 = S16_b
        gbt = sb("gbt", [128, 16], F32); gbt_b = Buf("gbt")
        es24 = sb("es24", [128, 24], F32); es_b = Buf("es24")
        kbs = sb("kbs", [128, 8], F32); kbs_b = Buf("kbs")
        Gm1 = sb("Gm1", [128, 8, 128], F32); Gm1_b = Buf("Gm1")
        Gm2 = sb("Gm2", [128, 8, 128], F32); Gm2_b = Buf("Gm2")
        ws32 = Gm1[:].rearrange("p a b -> p (a b)"); ws32_b = Gm1_b
        Kd = sb("Kd", [128, 8, 128], F32); Kd_b = Buf("Kd")
        Vb = sb("Vb", [128, 8, 128], BF16); Vb_b = Buf("Vb")
        Bo = sb("Bo", [128, 8, 128], BF16); Bo_b = Buf("Bo")
        r16 = sb("r16", [128, 8, 128], BF16); r16_b = Buf("r16")
        neg = sb("neg", [128, 8], F32); neg_b = Buf("neg")
        dsm = Gm1; dsm_b = Gm1_b
        dTm = Gm2; dTm_b = Gm2_b
        reg = sb("reg", [128, 8, 128], BF16); reg_b = Buf("reg")
        AmT = sb("AmT", [128, 2, 8, 128], BF16); Am = [AmT[:, 0, :, :], AmT[:, 1, :, :]]; Am_b = bufs("Am", 2)
        BmT = sb("BmT", [128, 2, 8, 128], BF16); Bm = [BmT[:, 0, :, :], BmT[:, 1, :, :]]; Bm_b = bufs("Bm", 2)
        vn32 = AmT[:].rearrange("p a b c -> p (a b c)").bitcast(F32).rearrange("p (b c) -> p b c", c=128)
        r32 = BmT[:].rearrange("p a b c -> p (a b c)").bitcast(F32).rearrange("p (b c) -> p b c", c=128)
        Pm0 = sb("Pm0", [128, 8, 128], BF16); Pm_b0 = Buf("Pm")
        Pf = rstd[:].rearrange("p (b c) -> p b c", c=128)
        PTm = sb("PTm", [128, 8, 128], BF16); PTm_b = Buf("PTm")
        vn = sb("vn", [128, 8, 128], BF16); vn_b = Buf("vn")
        qt = sb("qt", [128, 8, 128], BF16); qt_b = Buf("qt")
        on32 = sb("on32", [128, 8, 128], F32); on32_b = Buf("on32")
        lnv32 = on32[:].rearrange("p a b -> p (a b)"); lnv32_b = on32_b
        vnb = sb("vnb", [128, D], BF16); vnb_b = Buf("vnb")
        bnst = sb("bnst", [128, 2, 6], F32); mv = sb("mv", [128, 4], F32); bn_b = Buf("bn")
        t16 = t32f.bitcast(BF16)

        def t16tile(c):
            return t16[:, c * 1024:(c + 1) * 1024].rearrange("p (a b) -> p a b", a=8)
        KdV = Kd[:].rearrange("p a b -> p (a b)").bitcast(BF16).rearrange("p (s a b) -> p s a b", s=2, a=8)
        Kd2 = [KdV[:, 0, :, :], KdV[:, 1, :, :]]; Kd2_b = bufs("Kd2_", 2)
        Vb2 = [Vb[:], t16tile(0)]; Vb2_b = [Vb_b, t32_b[0]]
        qt2 = [qt[:], t16tile(1)]; qt2_b = [qt_b, t32_b[1]]
        PTm2 = [PTm[:], t16tile(2)]; PTm2_b = [PTm_b, t32_b[2]]
        Pm2 = [Pm0[:], t16tile(3)]; Pm2_b = [Pm_b0, t32_b[3]]
        Bo2 = [Bo[:], t16tile(4)]; Bo2_b = [Bo_b, t32_b[4]]
        neg1 = sb("neg1", [128, 8], F32); neg2 = [neg, neg1]; neg2_b = [neg_b, Buf("neg1")]
        es1 = sb("es1", [128, 24], F32); es2 = [es24, es1]; es2_b = [es_b, Buf("es1")]
        rstd9 = t32f[:, 3072:4096]
        msk16 = w3f[:, 2048:4096].rearrange("p (a b) -> p a b", a=16); msk16_b = Buf("msk16")
        egl = sb("egl", [128, 16, 8], F32); egl_b = Buf("egl")
        gsel = sb("gsel", [128, 16, 8], F32); gsel_b = Buf("gsel")
        snew = [w4f[:, 2304:3328].bitcast(F32).rearrange("p (a b) -> p a b", a=4), sb("snew1", [128, 4, 128], F32)[:]]
        snew_b = bufs("snew", 2)

        ps = es.enter_context(nc.psum_tensor("ps", [128, 8, 512], F32))
        ps_b = bufs("ps", 8)
        bptr = [0]

        def bank():
            b = bptr[0] % 8
            bptr[0] = (b + 1) % 8
            return b

        def pair():
            b = bptr[0] % 8
            if b % 2:
                b = (b + 1) % 8
            bptr[0] = (b + 2) % 8
            return b

        def P32(i, n=1):
            return ps[:, i:i + n, :].rearrange("p a b -> p (a b)") if n > 1 else ps[:, i, :]

        def P16(i):
            return ps[:, i, :].bitcast(BF16)

        def V(fn, r, w): return E.op("dve", fn, r, w)
        def A(fn, r, w): return E.op("act", fn, r, w)
        def T(fn, r, w): return E.op("pe", fn, r, w)
        def G(fn, r, w): return E.op("pool", fn, r, w)

        def dma(eng, chan, out, in_, r, w):
            return E.op(eng, lambda e: e.dma_start(out=out, in_=in_), r, w, chan=chan, ndma=1)

        slot_i = [0]
        issued = [0]
        mask_loaded = [False]
        WT = {"w_in": w_in, "w_pa": w_pa, "w_pb": w_pb, "w_o": w_o, "w_f1": w_f1, "w_f2": w_f2}

        scr_idx = {}

        def issue_w(k, desc):
            wn, l_, r0, nr, c0, ncol = desc
            s_ = slot_of(k)
            kc_ = nr // 128
            dst = wsl[s_][:, 0:kc_, 0:ncol]
            if desc not in scr_idx:
                i_ = len(scr_idx)
                scr_idx[desc] = i_
                src = WT[wn][l_, r0:r0 + nr, c0:c0 + ncol].rearrange("(c p) n -> p c n", p=128)
                dma("pool", f"w{s_}", dst, src, [], [wsl_b[s_]])
                sv = wscr[i_, :, 0:kc_ * ncol].rearrange("p (c n) -> p c n", c=kc_)
                dma("sp", f"scrw{i_ % 2}", sv, dst, [wsl_b[s_]], [wscr_b[i_]])
            else:
                i_ = scr_idx[desc]
                sv = wscr[i_, :, 0:kc_ * ncol].rearrange("p (c n) -> p c n", c=kc_)
                dma("sp", f"v{s_}", dst, sv, [wscr_b[i_]], [wsl_b[s_]])

        def load_w(desc, kc=None, ncols=None):
            k = slot_i[0]
            slot_i[0] += 1
            rec_reqs.append(desc)
            if all_reqs is None:
                issue_w(k, desc)
            else:
                assert all_reqs[k] == desc
                pump()
                assert issued[0] > k, "weight slot ring exhausted"
            return slot_of(k)

        samp_k0 = [None]
        if all_reqs is not None:
            samp_k0[0] = SAMP_K0[0]

        def slot_of(k):
            k0 = samp_k0[0]
            if k0 is None or k < k0:
                return k % 5
            return (k - k0) % 3

        def prev_user(k):
            s_ = slot_of(k)
            j = k - 1
            while j >= 0:
                if slot_of(j) == s_:
                    return j
                j -= 1
            return -1

        wdone = [0]

        def pump():
            while issued[0] < len(all_reqs) and prev_user(issued[0]) < wdone[0]:
                issue_w(issued[0], all_reqs[issued[0]])
                issued[0] += 1

        def w_done(n=1):
            wdone[0] += n
            if all_reqs is not None:
                pump()

        def wview(wt, l, r0, nr, c0, ncol):
            return (wt, l, r0, nr, c0, ncol)

        I32 = c32[:, 0, :]; I16 = c16[:, 0, :]; ONES16 = c16[:, 1, :]; ONES32 = c32[:, 1, :]

        def mm_group(bk, col0, ncol, lhs_fn, rhs_fn, nk, rbufs, extra_r=(), mp=128):
            def fn(e):
                ins = None
                for k in range(nk):
                    ins = e.matmul(ps[0:mp, bk, col0:col0 + ncol], lhs_fn(k), rhs_fn(k),
                                   start=(k == 0), stop=(k == nk - 1))
                return ins
            return T(fn, list(rbufs) + list(extra_r), [ps_b[bk]])

        dma("sp", "par", ppt[:], pp, [], [ppt_b])
        dma("sp", "par", c32[:], cst.rearrange("k p n -> p k n"), [], [c32_b])
        dma("sp", "par", seq32[:], cbd[:, 2048:2064], [], [cb32_b])
        dma("pool", "cst16", c16[:], cst16.rearrange("k p n -> p k n"), [], [c16_b])
        V(lambda e: e.tensor_copy(out=seqs[:], in_=seq32[:]), [cb32_b, cb16_b], [cb16_b])
        V(lambda e: e.tensor_copy(out=selT[:], in_=c32[0:16, 0, 0:16].unsqueeze(2).to_broadcast([16, 16, 128])),
          [c32_b], [selT_b])
        def NW(l, n): return ppt[:, (l * 4 + n) * 8:(l * 4 + n) * 8 + 8]
        def CW(l, tap, ch): return ppt[:, 64 + l * 96 + tap * 24 + ch:64 + l * 96 + tap * 24 + ch + 1]
        A(lambda e: e.activation(out=nea[0:8, :], in_=ppt[0:8, 258:260], func=AF.Exp), [ppt_b], [nea_b])
        V(lambda e: e.tensor_scalar(out=nea[0:8, :], in0=nea[0:8, :], scalar1=-1.0, scalar2=None, op0=ALU.mult), [nea_b], [nea_b])
        V(lambda e: e.tensor_scalar(out=dnw[:], in0=ppt[:, 256:258], scalar1=float(128 ** -0.5), scalar2=None, op0=ALU.mult),
          [ppt_b], [dnw_b])
        sel16 = c16[:, 6:8, :].rearrange("p a b -> p (a b)")

        def rsqrt_from_psum(out_ap, in_ap, scale, eps, r, w):
            A(lambda e: e.activation(out=out_ap, in_=in_ap, func=AF.Ln, scale=scale, bias=eps), r, w)
            A(lambda e: e.activation(out=out_ap, in_=out_ap, func=AF.Exp, scale=-0.5), w, w)

        def rmsnorm(src, src_b, W, gain, mode, l):
            bk = bank()
            for c in range(8):
                s = c % 2
                A(lambda e, c=c, s=s: e.activation(out=sq[s][:, 0:W], in_=src[:, c, 0:W], func=AF.Square),
                  [src_b[c]], [sq_b[s]])
                T(lambda e, c=c, s=s: e.matmul(ps[:, bk, 0:W], ONES16, sq[s][:, 0:W], start=(c == 0), stop=(c == 7)),
                  [sq_b[s], c16_b], [ps_b[bk]])
            rsqrt_from_psum(rstd[:, 0:W], ps[:, bk, 0:W], 1.0 / D, 1e-6, [ps_b[bk]], [rstd_b])
            for c in range(8):
                if mode == "h":
                    V(lambda e, c=c: e.scalar_tensor_tensor(out=hT[:, c, 0:W], in0=src[:, c, 0:W], scalar=gain[:, c:c + 1],
                                                            in1=rstd[:, 0:W], op0=ALU.mult, op1=ALU.mult),
                      [src_b[c], rstd_b, ppt_b], [hT_b[c]])
                else:
                    V(lambda e, c=c: e.scalar_tensor_tensor(out=src[:, c, 0:W], in0=src[:, c, 0:W], scalar=gain[:, c:c + 1],
                                                            in1=rstd[:, 0:W], op0=ALU.mult, op1=ALU.mult),
                      [src_b[c], rstd_b, ppt_b], [src_b[c]])
                    G(lambda e, c=c: e.tensor_tensor(out=xT[:, c, 0:W], in0=xT[:, c, 0:W], in1=src[:, c, 0:W], op=ALU.add),
                      [src_b[c], xT_b[c]], [xT_b[c]])

        def proj_chunks(wt, l, c0, nchunks, W, rhs, rhs_b, consume, kc=8, r0=0):
            j = 0
            pend = [None]
            while j < nchunks:
                nj = min(4, nchunks - j)
                s = load_w(wview(wt, l, r0, kc * 128, c0 + j * 128, nj * 128), kc, nj * 128)
                for jj in range(nj):
                    bk = bank()
                    mm_group(bk, 0, W, lambda k, s=s, jj=jj: wsl[s][:, k, jj * 128:(jj + 1) * 128],
                             lambda k: rhs[:, k, 0:W], kc, rhs_b, [wsl_b[s]])
                    d_ = consume(j + jj, bk)
                    if pend[0] is not None:
                        pend[0]()
                    pend[0] = d_ if callable(d_) else None
                w_done()
                j += nj
            if pend[0] is not None:
                pend[0]()

        def layer(l, W, samp, last_blk):
            NT = W // 128
            if stop <= 0:
                return
            rmsnorm(xT, xT_b, W, NW(l, 0), "h", l)
            if stop <= 1:
                return
            if samp:
                tin = t32f
                dma("sp", "scin", tin[0:48, 0:3072], sc[l], [], t32_b[0:6])
                for q4 in range(6):
                    bk = bank()
                    for cc in range(4):
                        ch = q4 * 4 + cc
                        T(lambda e, ch=ch, cc=cc, bk=bk: e.transpose(out=ps[:, bk, cc * 48:(cc + 1) * 48], in_=tin[0:48, ch * 128:(ch + 1) * 128],
                                                                    identity=c32[0:48, 0, 0:48]), t32_b[0:6] + [c32_b], [ps_b[bk]])
                    V(lambda e, q4=q4, bk=bk: e.tensor_copy(out=scin[:, q4 * 4:(q4 + 1) * 4, :],
                                                            in_=ps[:, bk, 0:192].rearrange("p (a b) -> p a b", a=4)),
                      [ps_b[bk]], [scin_b])
            tok_b = t32_b
            if samp:
                tokm = t32f[:, 0:3072]

            def qkv_consume(ch, bk):
                s = ch % 2
                if samp:
                    pv = pre[s][:, 0:176].rearrange("p (a b) -> p a b", a=16)
                    G(lambda e: e.tensor_copy(out=pv[:, :, 0:3], in_=scin[:, ch, :].rearrange("p (a b) -> p a b", a=16)),
                      [scin_b], [pre_b[s]])
                    A(lambda e: e.activation(out=pv[:, :, 3:11], in_=ps[:, bk, 0:128].rearrange("p (a b) -> p a b", a=16),
                                             func=AF.Copy), [ps_b[bk]], [pre_b[s]])
                    av = acc[s][:, 0:128].rearrange("p (a b) -> p a b", a=16)
                    A(lambda e: e.activation(out=acc[s][:, 0:128], in_=ps[:, bk, 0:128], func=AF.Copy, scale=CW(l, 3, ch)),
                      [ps_b[bk], ppt_b], [acc_b[s]])
                    for tap in (2, 1, 0):
                        V(lambda e, tap=tap: e.scalar_tensor_tensor(out=av, in0=pv[:, :, tap:tap + 8], scalar=CW(l, tap, ch),
                                                                    in1=av, op0=ALU.mult, op1=ALU.add),
                          [pre_b[s], acc_b[s], ppt_b], [acc_b[s]])
                    G(lambda e: e.tensor_copy(out=sq[s][:, 0:256].bitcast(F32).rearrange("p (a b) -> p a b", a=16),
                                              in_=pv[:, :, 3:11]), [pre_b[s]], [sq_b[s]])
                    b2 = bank()
                    T(lambda e: e.transpose(out=ps[:, b2, 0:128], in_=sq[s][:, 0:256].bitcast(F32), identity=I32),
                      [sq_b[s], c32_b], [ps_b[b2]])
                    V(lambda e: e.tensor_copy(out=tokm[:, ch * 128:(ch + 1) * 128], in_=ps[:, b2, 0:128]),
                      [ps_b[b2]], [tok_b[ch // 4]])
                else:
                    G(lambda e: e.tensor_copy(out=pre[s][:, 0:3], in_=carry[:, l, ch, :]), [carry_b[l][ch]], [pre_b[s]])
                    A(lambda e: e.activation(out=pre[s][:, 3:515], in_=ps[:, bk, :], func=AF.Copy), [ps_b[bk]], [pre_b[s]])
                    A(lambda e: e.activation(out=acc[s][:], in_=ps[:, bk, :], func=AF.Copy, scale=CW(l, 3, ch)),
                      [ps_b[bk], ppt_b], [acc_b[s]])
                    G(lambda e: e.tensor_copy(out=carry[:, l, ch, :], in_=pre[s][:, 512:515]), [pre_b[s]], [carry_b[l][ch]])
                    for tap in (2, 1, 0):
                        V(lambda e, tap=tap: e.scalar_tensor_tensor(out=acc[s][:], in0=pre[s][:, tap:tap + 512],
                                                                    scalar=CW(l, tap, ch), in1=acc[s][:],
                                                                    op0=ALU.mult, op1=ALU.add),
                          [pre_b[s], acc_b[s], ppt_b], [acc_b[s]])
                return lambda: A(lambda e: e.activation(out=R1[:, ch, 0:W], in_=acc[s][:, 0:W], func=AF.Silu), [acc_b[s]], [R1_b[ch]])

            proj_chunks("w_in", l, 0, 24, W, hT, hT_b, qkv_consume)
            if samp:
                for j in range(3):
                    dma("sp", "o_sc", sc_s[l, :, j, :], tokm[5 + j:128:8, :], tok_b[0:6], [])
            elif last_blk:
                for q4 in range(6):
                    bk = bank()
                    for cc in range(4):
                        ch = q4 * 4 + cc
                        T(lambda e, ch=ch, cc=cc: e.transpose(out=ps[0:3, bk, cc * 128:(cc + 1) * 128], in_=carry[:, l, ch, :],
                                                              identity=I32), [carry_b[l][ch], c32_b], [ps_b[bk]])
                    V(lambda e, q4=q4: e.tensor_copy(out=scrow[:, q4 * 512:(q4 + 1) * 512], in_=ps[0:3, bk, :]),
                      [ps_b[bk]], t32_b[0:6])
                dma("sp", "o_sc", sc_p[l], scrow, t32_b[0:6], [])

            if stop <= 2:
                return
            proj_chunks("w_in", l, 3072, 8, W, hT, hT_b,
                        lambda ch, bk: A(lambda e: e.activation(out=sz[:, ch, 0:W], in_=ps[:, bk, 0:W], func=AF.Silu),
                                         [ps_b[bk]], [sz_b[ch]]))
            s = load_w(wview("w_in", l, 0, 1024, 4096, 16), 8, 16)
            ba = bank()
            mm_group(ba, 0, W, lambda k: wsl[s][:, k, 0:8], lambda k: hT[:, k, 0:W], 8, hT_b, [wsl_b[s]], mp=8)
            bb = bank()
            mm_group(bb, 0, W, lambda k: wsl[s][:, k, 8:16], lambda k: hT[:, k, 0:W], 8, hT_b, [wsl_b[s]], mp=8)
            w_done()
            A(lambda e: e.activation(out=gT[:, 0:W], in_=ps[0:8, ba, 0:W], func=AF.Exp, bias=ppt[0:8, 260 + l:261 + l]),
              [ps_b[ba], ppt_b], [gT_b])
            A(lambda e: e.activation(out=gT[:, 0:W], in_=gT[:, 0:W], func=AF.Ln, bias=1.0), [gT_b], [gT_b])
            V(lambda e: e.tensor_scalar(out=gT[:, 0:W], in0=gT[:, 0:W], scalar1=nea[0:8, l:l + 1], scalar2=None, op0=ALU.mult),
              [gT_b, nea_b], [gT_b])
            A(lambda e: e.activation(out=bT[:, 0:W], in_=ps[0:8, bb, 0:W], func=AF.Sigmoid), [ps_b[bb]], [bT_b])
            bn_ = bank()
            for j in range(16):
                s2 = j % 2
                A(lambda e, j=j, s2=s2: e.activation(out=sq[s2][:, 0:W], in_=R1[:, j, 0:W], func=AF.Square),
                  [R1_b[j]], [sq_b[s2]])
                T(lambda e, j=j, s2=s2: e.matmul(ps[0:16, bn_, 0:W], sel16[:, j * 16:(j + 1) * 16], sq[s2][:, 0:W],
                                                 start=(j == 0), stop=(j == 15)), [sq_b[s2], c16_b], [ps_b[bn_]])
            rsqrt_from_psum(rn[:, 0:W], ps[0:16, bn_, 0:W], 1.0, 1e-6, [ps_b[bn_]], [rn_b])
            V(lambda e: e.tensor_copy(out=rnb[:, 0:W], in_=rn[:, 0:W]), [rn_b], [rnb_b])
            V(lambda e: e.tensor_tensor(out=rnl[:, 0:W], in0=rn[:, 0:W], in1=rnb[:, 0:W], op=ALU.subtract), [rn_b, rnb_b], [rnb_b])
            for j in range(16):
                bk = bank()
                def fbc(e, j=j, bk=bk):
                    e.matmul(ps[:, bk, 0:W], selT[:, j, :], rnb[:, 0:W], start=True, stop=False)
                    return e.matmul(ps[:, bk, 0:W], selT[:, j, :], rnl[:, 0:W], start=False, stop=True)
                T(fbc, [rnb_b, selT_b], [ps_b[bk]])
                V(lambda e, j=j, bk=bk: e.tensor_tensor(out=R1[:, j, 0:W], in0=R1[:, j, 0:W], in1=ps[:, bk, 0:W], op=ALU.mult),
                  [R1_b[j], ps_b[bk]], [R1_b[j]])

            if stop <= 3:
                return
            cU, cSL, cN1, cN2 = (4, 5, 4, 5) if samp else (2, 3, 2, 3)
            nlev = 3 if samp else 6

            def bc8(ap):
                return ap.unsqueeze(2).to_broadcast([128, 8, 128])

            def bcm(ap):
                return ap.unsqueeze(1).to_broadcast([128, 8, 128])

            def PV(i):
                return ps[:, i:i + 2, :].rearrange("p a (b c) -> p (a b) c", c=128)

            class Alloc:
                def __init__(self, lo, n):
                    self.lo, self.n, self.p = lo, n, 0

                def bank(self):
                    b_ = self.p % self.n
                    self.p = (b_ + 1) % self.n
                    return self.lo + b_

                def pair(self):
                    b_ = self.p % self.n
                    if b_ % 2:
                        b_ = (b_ + 1) % self.n
                    self.p = (b_ + 2) % self.n
                    return self.lo + b_

            a1 = Alloc(0, 4)
            a2 = Alloc(0, 8)
            qb = R1_b[0:8]

            def chunk_gen(t):
                cs = slice(t * 128, (t + 1) * 128)
                st = 0 if samp else t % 2
                Kd_, Kd_b_ = Kd2[st], Kd2_b[st]
                Vb_, Vb_b_ = Vb2[st], Vb2_b[st]
                qt_, qt_b_ = qt2[st], qt2_b[st]
                PTm_, PTm_b_ = PTm2[st], PTm2_b[st]
                TT, TT_b = Pm2[st], Pm2_b[st]
                Bo_, Bo_b_ = Bo2[st], Bo2_b[st]
                neg_, neg_b_ = neg2[st], neg2_b[st]
                es_, es_b_ = es2[st], es2_b[st]
                bK = a1.bank()
                for h in range(H):
                    T(lambda e, h=h: e.transpose(out=P16(bK)[:, h * 128:(h + 1) * 128], in_=R1[:, 8 + h, cs], identity=I16),
                      [R1_b[8 + h], c16_b], [ps_b[bK]])
                bV = a1.bank()
                for h in range(H):
                    T(lambda e, h=h: e.transpose(out=P16(bV)[:, h * 128:(h + 1) * 128], in_=R1[:, 16 + h, cs], identity=I16),
                      [R1_b[16 + h], c16_b], [ps_b[bV]])
                bG = a1.bank()
                T(lambda e: e.transpose(out=ps[:, bG, 0:8], in_=gT[:, cs], identity=c32[0:8, 0, 0:8]), [gT_b, c32_b], [ps_b[bG]])
                T(lambda e: e.transpose(out=ps[:, bG, 8:16], in_=bT[:, cs], identity=c32[0:8, 0, 0:8]), [bT_b, c32_b], [ps_b[bG]])
                V(lambda e: e.tensor_copy(out=gbt[:], in_=ps[:, bG, 0:16]), [ps_b[bG]], [gbt_b])
                yield 1
                bS = a1.bank()
                cLast = 6 if samp else 1
                for i3, ci in enumerate((cU, cSL, cLast)):
                    T(lambda e, i3=i3, ci=ci: e.matmul(ps[:, bS, i3 * 8:(i3 + 1) * 8], c32[:, ci, :], gbt[:, 0:8],
                                                       start=True, stop=True), [gbt_b, c32_b], [ps_b[bS]])
                A(lambda e: e.activation(out=es_[:], in_=ps[:, bS, 0:24], func=AF.Exp), [ps_b[bS]], [es_b_])
                V(lambda e: e.tensor_tensor(out=kbs[:], in0=gbt[:, 8:16], in1=es_[:, 0:8], op=ALU.mult), [gbt_b, es_b_], [kbs_b])
                yield 1
                KP = P16(bK).rearrange("p (a b) -> p a b", a=8)
                VP = P16(bV).rearrange("p (a b) -> p a b", a=8)
                V(lambda e: e.tensor_tensor(out=Kd_, in0=KP, in1=bc8(es_[:, 8:16]), op=ALU.mult), [ps_b[bK], es_b_], [Kd_b_])
                V(lambda e: e.tensor_tensor(out=Vb_, in0=VP, in1=bc8(gbt[:, 8:16]), op=ALU.mult), [ps_b[bV], gbt_b], [Vb_b_])
                V(lambda e: e.tensor_scalar(out=neg_[:], in0=kbs[:], scalar1=-1.0, scalar2=None, op0=ALU.mult), [kbs_b], [neg_b_])
                yield 1
                G(lambda e: e.tensor_tensor(out=Gm1[:], in0=bcm(c32[:, cSL, :]), in1=bc8(gbt[:, 0:8]), op=ALU.mult),
                  [c32_b, gbt_b], [Gm1_b])
                V(lambda e: e.tensor_tensor(out=Gm2[:], in0=bcm(c32[:, cU, :]), in1=bc8(gbt[:, 0:8]), op=ALU.mult),
                  [c32_b, gbt_b], [Gm2_b])
                yield 1
                pD = a1.pair()
                pT = a1.pair()
                for hh in range(2):
                    g1 = Gm1[:, hh * 4:(hh + 1) * 4, :].rearrange("p a b -> p (a b)")
                    g2 = Gm2[:, hh * 4:(hh + 1) * 4, :].rearrange("p a b -> p (a b)")

                    def fD(e, hh=hh, g1=g1):
                        e.matmul(ps[:, pD + hh, :], c32[:, cU, :], g1, start=True, stop=False)
                        ins = None
                        for q in range(4):
                            ins = e.matmul(ps[:, pD + hh, q * 128:(q + 1) * 128], I16, c16[:, cN1, :], start=False, stop=(q == 3))
                        return ins
                    T(fD, [Gm1_b, c32_b, c16_b], [ps_b[pD + hh]])

                    def fT(e, hh=hh, g2=g2):
                        e.matmul(ps[:, pT + hh, :], c32[:, cSL, :], g2, start=True, stop=False)
                        ins = None
                        for q in range(4):
                            ins = e.matmul(ps[:, pT + hh, q * 128:(q + 1) * 128], I16, c16[:, cN2, :], start=False, stop=(q == 3))
                        return ins
                    T(fT, [Gm2_b, c32_b, c16_b], [ps_b[pT + hh]])
                yield 1
                A(lambda e: e.activation(out=dsm[:].rearrange("p a b -> p (a b)"), in_=P32(pD, 2), func=AF.Exp),
                  [ps_b[pD], ps_b[pD + 1]], [dsm_b])
                pR = a1.pair()
                for hh in range(2):
                    g2 = Gm2[:, hh * 4:(hh + 1) * 4, :].rearrange("p a b -> p (a b)")
                    T(lambda e, hh=hh, g2=g2: e.matmul(ps[:, pR + hh, :], ONES32, g2, start=True, stop=True),
                      [Gm2_b, c32_b], [ps_b[pR + hh]])
                A(lambda e: e.activation(out=dTm[:].rearrange("p a b -> p (a b)"), in_=P32(pT, 2), func=AF.Exp),
                  [ps_b[pT], ps_b[pT + 1]], [dTm_b])
                A(lambda e: e.activation(out=reg[:].rearrange("p a b -> p (a b)"), in_=P32(pR, 2), func=AF.Exp),
                  [ps_b[pR], ps_b[pR + 1]], [reg_b])
                yield 1
                V(lambda e: e.tensor_tensor(out=qt_, in0=R1[:, 0:8, cs], in1=reg[:], op=ALU.mult), qb + [reg_b], [qt_b_])
                pG = a1.pair()
                pP = a1.pair()
                for h in range(H):
                    T(lambda e, h=h: e.matmul(PV(pG)[:, h, :], R1[:, 8 + h, cs], R1[:, 8 + h, cs], start=True, stop=True),
                      [R1_b[8 + h]], [ps_b[pG + h // 4]])
                for h in range(H):
                    T(lambda e, h=h: e.matmul(PV(pP)[:, h, :], R1[:, 8 + h, cs], R1[:, h, cs], start=True, stop=True),
                      [R1_b[8 + h], R1_b[h]], [ps_b[pP + h // 4]])
                yield 1
                for h in range(H):
                    V(lambda e, h=h: e.scalar_tensor_tensor(out=Am[0][:, h, :], in0=PV(pG)[:, h, :], scalar=gbt[:, 8 + h:9 + h],
                                                            in1=dsm[:, h, :], op0=ALU.mult, op1=ALU.mult),
                      [ps_b[pG + h // 4], gbt_b, dsm_b], [Am_b[0]])
                V(lambda e: e.tensor_tensor(out=PTm_, in0=PV(pP), in1=dTm[:], op=ALU.mult),
                  [ps_b[pP], ps_b[pP + 1], dTm_b], [PTm_b_])
                yield 1
                bB = a1.bank()
                for h in range(H):
                    T(lambda e, h=h: e.transpose(out=P16(bB)[:, h * 128:(h + 1) * 128], in_=Am[0][:, h, :], identity=I16),
                      [Am_b[0], c16_b], [ps_b[bB]])
                BP = P16(bB).rearrange("p (a b) -> p a b", a=8)
                if samp:
                    A(lambda e: e.activation(out=Bm[0], in_=BP, func=AF.Copy), [ps_b[bB]], [Bm_b[0]])
                else:
                    A(lambda e: e.activation(out=Bm[1], in_=BP, func=AF.Copy), [ps_b[bB]], [Bm_b[1]])
                    yield 1
                    V(lambda e: e.tensor_tensor(out=Bm[0], in0=Bm[1], in1=bcm(c16[:, 8, :]), op=ALU.mult), [Bm_b[1], c16_b], [Bm_b[0]])
                    V(lambda e: e.tensor_tensor(out=Am[0], in0=Am[0], in1=bcm(c16[:, 8, :]), op=ALU.mult), [Am_b[0], c16_b], [Am_b[0]])
                    G(lambda e: e.tensor_tensor(out=Bo_, in0=Bm[1], in1=bcm(c16[:, 9, :]), op=ALU.mult), [Bm_b[1], c16_b], [Bo_b_])
                yield 1
                V(lambda e: e.tensor_tensor(out=TT, in0=bcm(I32), in1=Bm[0], op=ALU.subtract), [Bm_b[0], c32_b], [TT_b])
                V(lambda e: e.tensor_tensor(out=Pf, in0=bcm(I32), in1=Bm[0], op=ALU.subtract), [Bm_b[0], c32_b], [rstd_b])
                ca = 0
                for lev in range(1, nlev):
                    na = 1 - ca
                    pA = a1.pair()
                    for h in range(H):
                        T(lambda e, h=h: e.matmul(PV(pA)[:, h, :], Bm[ca][:, h, :], Am[ca][:, h, :], start=True, stop=True),
                          [Am_b[ca], Bm_b[ca]], [ps_b[pA + h // 4]])
                    A(lambda e: e.activation(out=Am[na], in_=PV(pA), func=AF.Copy), [ps_b[pA], ps_b[pA + 1]], [Am_b[na]])
                    yield 1
                    if lev < nlev - 1:
                        pB = a1.pair()
                        for h in range(H):
                            T(lambda e, h=h: e.matmul(PV(pB)[:, h, :], Am[ca][:, h, :], Bm[ca][:, h, :], start=True, stop=True),
                              [Am_b[ca], Bm_b[ca]], [ps_b[pB + h // 4]])
                        A(lambda e: e.activation(out=Bm[na], in_=PV(pB), func=AF.Copy), [ps_b[pB], ps_b[pB + 1]], [Bm_b[na]])
                        yield 1
                    pU = a1.pair()
                    for h in range(H):
                        T(lambda e, h=h: e.matmul(PV(pU)[:, h, :], Am[na][:, h, :], TT[:, h, :], start=True, stop=True),
                          [Am_b[na], TT_b], [ps_b[pU + h // 4]])
                    V(lambda e: e.tensor_tensor(out=TT, in0=PV(pU), in1=Pf, op=ALU.add), [ps_b[pU], ps_b[pU + 1], rstd_b], [TT_b])
                    if lev < nlev - 1:
                        V(lambda e: e.tensor_tensor(out=Pf, in0=PV(pU), in1=Pf, op=ALU.add), [ps_b[pU], ps_b[pU + 1], rstd_b], [rstd_b])
                    yield 1
                    ca = na
                yield "S2"
                r32 = on32[:]
                R32b = [on32_b]
                if samp:
                    pK = a2.pair(); pN = a2.pair(); pO = a2.pair()
                    other = [b_ for b_ in range(8) if b_ not in (pK, pK + 1, pN, pN + 1, pO, pO + 1)]
                    V(lambda e: e.tensor_tensor(out=gsel[:], in0=gbt[:, 0:8].unsqueeze(1).to_broadcast([128, 16, 8]),
                                                in1=seq32[:].unsqueeze(2).to_broadcast([128, 16, 8]), op=ALU.mult),
                      [gbt_b, cb32_b], [gsel_b])
                    bE = other[0]
                    T(lambda e: e.matmul(ps[:, bE, 0:128], ONES32, gsel[:].rearrange("p a b -> p (a b)"), start=True, stop=True),
                      [gsel_b, c32_b], [ps_b[bE]])
                    A(lambda e: e.activation(out=egl[:].rearrange("p a b -> p (a b)"), in_=ps[:, bE, 0:128], func=AF.Exp),
                      [ps_b[bE]], [egl_b])
                    for h in range(H):
                        if h % 2 == 0:
                            s0f, s0f_b, s0b, s0b_b = s0f_0, s0f_b0, s0b_0, s0b_b0
                        else:
                            s0f = t32f[:, 0:2048].rearrange("p (a b) -> p a b", a=16); s0f_b = t32_b[0:4]
                            s0b = t16[:, 4096:6144].rearrange("p (a b) -> p a b", a=16); s0b_b = t32_b[4:6]
                        dma("sp", f"s0in{h % 2}", s0f, sd[l, :, h, :, :].rearrange("s k v -> k s v"), [], s0f_b)
                        A(lambda e: e.activation(out=s0b, in_=s0f, func=AF.Copy), s0f_b, s0b_b)
                        V(lambda e, h=h: e.tensor_tensor(out=msk16[:], in0=R1[:, 8 + h, cs].unsqueeze(1).to_broadcast([128, 16, 128]),
                                                         in1=mskc[:], op=ALU.mult), [R1_b[8 + h], cb16_b], [msk16_b])

                        def fks(e, h=h):
                            ins = None
                            for s_ in range(16):
                                ins = e.matmul(PV(pK)[:, h, :], msk16[:, s_, :], s0b[:, s_, :], start=(s_ == 0), stop=(s_ == 15))
                            return ins
                        T(fks, [msk16_b] + s0b_b, [ps_b[pK + h // 4]])
                        V(lambda e, h=h: e.scalar_tensor_tensor(out=r32[:, h, :], in0=PV(pK)[:, h, :], scalar=neg_[:, h:h + 1],
                                                                in1=Vb_[:, h, :], op0=ALU.mult, op1=ALU.add),
                          [ps_b[pK + h // 4], neg_b_, Vb_b_], R32b)
                        V(lambda e, h=h: e.tensor_copy(out=r16[:, h, :], in_=r32[:, h, :]), R32b, [r16_b])
                        T(lambda e, h=h: e.matmul(PV(pN)[:, h, :], TT[:, h, :], r16[:, h, :], start=True, stop=True),
                          [TT_b, r16_b], [ps_b[pN + h // 4]])
                        A(lambda e, h=h: e.activation(out=vn[:, h, :], in_=PV(pN)[:, h, :], func=AF.Copy), [ps_b[pN + h // 4]], [vn_b])
                        V(lambda e, h=h: e.tensor_tensor(out=msk16[:], in0=qt_[:, h, :].unsqueeze(1).to_broadcast([128, 16, 128]),
                                                         in1=mskc[:], op=ALU.mult), [qt_b_, cb16_b], [msk16_b])

                        def fo(e, h=h):
                            for s_ in range(16):
                                e.matmul(PV(pO)[:, h, :], s0b[:, s_, :], msk16[:, s_, :], start=(s_ == 0), stop=False)
                            return e.matmul(PV(pO)[:, h, :], vn[:, h, :], PTm_[:, h, :], start=False, stop=True)
                        T(fo, s0b_b + [msk16_b, vn_b, PTm_b_], [ps_b[pO + h // 4]])
                        V(lambda e, h=h: e.tensor_tensor(out=msk16[:], in0=Kd_[:, h, :].unsqueeze(1).to_broadcast([128, 16, 128]),
                                                         in1=seqs[:].unsqueeze(2).to_broadcast([128, 16, 128]), op=ALU.mult),
                          [Kd_b_, cb16_b], [msk16_b])
                        for q in range(4):
                            bq = other[1 + (h * 4 + q) % (len(other) - 1)]

                            def fs(e, h=h, q=q, bq=bq):
                                ins = None
                                for s4 in range(4):
                                    ins = e.matmul(ps[:, bq, s4 * 128:(s4 + 1) * 128], msk16[:, q * 4 + s4, :], vn[:, h, :],
                                                   start=True, stop=True)
                                return ins
                            T(fs, [msk16_b, vn_b], [ps_b[bq]])
                            for s4 in range(4):
                                s_ = q * 4 + s4
                                V(lambda e, h=h, s_=s_, s4=s4, bq=bq, q=q: e.scalar_tensor_tensor(
                                    out=snew[q % 2][:, s4, :], in0=s0f[:, s_, :], scalar=egl[:, s_, h:h + 1],
                                    in1=ps[:, bq, s4 * 128:(s4 + 1) * 128], op0=ALU.mult, op1=ALU.add),
                                  s0f_b + [egl_b, ps_b[bq]], [snew_b[q % 2]])
                            dma("sp", f"s0out{q % 2}", sd_s[l, q * 4:(q + 1) * 4, h, :, :].rearrange("s k v -> k s v"), snew[q % 2],
                                [snew_b[q % 2]], [])
                    pQ = a2.pair()
                else:
                    pK, pN, pO, pS, pQ = 4, 6, 4, 6, 6
                    pKv = PV(pK); pNv = PV(pN)
                    for h in range(H):
                        T(lambda e, h=h: e.matmul(PV(pK)[:, h, :], R1[:, 8 + h, cs], S16[:, l, h, :], start=True, stop=True),
                          [R1_b[8 + h], S16_b[l]], [ps_b[pK + h // 4]])
                    V(lambda e: e.tensor_tensor(out=r32, in0=pKv, in1=bc8(neg_[:]), op=ALU.mult), [ps_b[pK], ps_b[pK + 1], neg_b_], R32b)
                    yield 1
                    V(lambda e: e.tensor_tensor(out=r32, in0=r32, in1=Vb_, op=ALU.add), R32b + [Vb_b_], R32b)
                    A(lambda e: e.activation(out=r16[:], in_=r32, func=AF.Copy), R32b, [r16_b])
                    yield 1
                    for h in range(H):
                        T(lambda e, h=h: e.matmul(PV(pN)[:, h, :], TT[:, h, :], r16[:, h, :], start=True, stop=True),
                          [TT_b, r16_b], [ps_b[pN + h // 4]])
                    A(lambda e: e.activation(out=vn[:], in_=pNv, func=AF.Copy), [ps_b[pN], ps_b[pN + 1]], [vn_b])
                    yield 1
                    for h in range(H):
                        T(lambda e, h=h: e.matmul(PV(pK)[:, h, :], Bo_[:, h, :], vn[:, h, :], start=True, stop=True),
                          [Bo_b_, vn_b], [ps_b[pK + h // 4]])
                    V(lambda e: e.tensor_tensor(out=r16[:], in0=r32, in1=pKv, op=ALU.subtract), R32b + [ps_b[pK], ps_b[pK + 1]], [r16_b])
                    yield 1
                    for h in range(H):
                        T(lambda e, h=h: e.matmul(PV(pN)[:, h, :], TT[:, h, :], r16[:, h, :], start=True, stop=True),
                          [TT_b, r16_b], [ps_b[pN + h // 4]])
                    A(lambda e: e.activation(out=vn[:], in_=pNv, func=AF.Copy), [ps_b[pN], ps_b[pN + 1]], [vn_b])
                    yield 1
                    for h in range(H):
                        def fo(e, h=h):
                            e.matmul(PV(pO)[:, h, :], S16[:, l, h, :], qt_[:, h, :], start=True, stop=False)
                            return e.matmul(PV(pO)[:, h, :], vn[:, h, :], PTm_[:, h, :], start=False, stop=True)
                        T(fo, [S16_b[l], qt_b_, vn_b, PTm_b_], [ps_b[pO + h // 4]])
                    yield 1
                    for h in range(H):
                        T(lambda e, h=h: e.matmul(PV(pS)[:, h, :], Kd_[:, h, :], vn[:, h, :], start=True, stop=True),
                          [Kd_b_, vn_b], [ps_b[pS + h // 4]])
                    A(lambda e: e.activation(out=sq[0][:], in_=P32(pO, 2), func=AF.Square, scale=float(128 ** -0.5)),
                      [ps_b[pO], ps_b[pO + 1]], [sq_b[0]])
                    yield 1
                    for h in range(H):
                        V(lambda e, h=h: e.scalar_tensor_tensor(out=S32[:, l, h, :], in0=S32[:, l, h, :], scalar=es_[:, 16 + h:17 + h],
                                                                in1=PV(pS)[:, h, :], op0=ALU.mult, op1=ALU.add),
                          [S32_b[l], es_b_, ps_b[pS + h // 4]], [S32_b[l]])
                    A(lambda e: e.activation(out=S16[:, l, :, :], in_=S32[:, l, :, :], func=AF.Copy), [S32_b[l]], [S16_b[l]])
                    yield 1
                if samp:
                    A(lambda e: e.activation(out=sq[0][:], in_=P32(pO, 2), func=AF.Square, scale=float(128 ** -0.5)),
                      [ps_b[pO], ps_b[pO + 1]], [sq_b[0]])
                for hh in range(2):
                    T(lambda e, hh=hh: e.matmul(ps[:, pQ + hh, :], ONES16, sq[0][:, hh * 512:(hh + 1) * 512], start=True, stop=True),
                      [sq_b[0], c16_b], [ps_b[pQ + hh]])
                rsqrt_from_psum(rstd9, P32(pQ, 2), 1.0 / 128, 1e-6, [ps_b[pQ], ps_b[pQ + 1]], t32_b[6:8])
                yield 1
                V(lambda e: e.scalar_tensor_tensor(out=on32[:].rearrange("p a b -> p (a b)"), in0=P32(pO, 2), scalar=dnw[:, l:l + 1],
                                                   in1=rstd9, op0=ALU.mult, op1=ALU.mult),
                  [ps_b[pO], ps_b[pO + 1], dnw_b] + t32_b[6:8], [on32_b])
                G(lambda e: e.tensor_tensor(out=oa[:, :, cs], in0=on32[:], in1=sz[:, :, cs], op=ALU.mult),
                  [on32_b] + sz_b, oa_b)
                yield 1

            cur = None
            for g in [chunk_gen(t) for t in range(NT)] + [None]:
                g_done = g is None
                c_done = cur is None
                while not (g_done and c_done):
                    if not c_done:
                        try:
                            next(cur)
                        except StopIteration:
                            c_done = True
                    if not g_done:
                        if next(g) == "S2":
                            g_done = True
                cur = g

            if (not samp) and last_blk:
                dma("sp", "o_sd", sd_p[l].rearrange("h k v -> k h v"), S32[:, l, :, :], [S32_b[l]], [])

            if stop <= 4:
                return
            def uv_consume(ch, bk):
                A(lambda e: e.activation(out=R1[:, ch, 0:W], in_=ps[:, bk, 0:W], func=AF.Gelu), [ps_b[bk]], [R1_b[ch]])
            proj_chunks("w_in", l, 4112, 16, W, hT, hT_b, uv_consume)
            boff = (l * 2 + (1 if samp else 0)) * 1024
            dma("sp", "lnw", t32f[:, 0:2048], lnw[l].rearrange("b d -> (b d)").partition_broadcast(128), [], t32_b[0:4])
            dma("pool", "prw", prw16[:], prow[:, boff:boff + 1024], [], [prw_b])
            lnt16 = t16[:, 4096:6144].rearrange("p (a b) -> p a b", a=2)
            A(lambda e: e.activation(out=t16[:, 4096:6144], in_=t32f[:, 0:2048], func=AF.Copy), t32_b[0:4], t32_b[4:6])
            dma("sp", "wsp", ws32[:], wsp[l, 1 if samp else 0], [], [ws32_b])
            V(lambda e: e.tensor_tensor(out=wsT[:], in0=ws32[:].rearrange("p (a b) -> p a b", a=8), in1=bcm(c32[:, cU, :]),
                                        op=ALU.mult), [ws32_b, c32_b], [wsT_b])
            for t in range(NT):
                cs = slice(t * 128, (t + 1) * 128)
                bk = bank()
                for g in range(8):
                    T(lambda e, g=g: e.transpose(out=P16(bk)[:, g * 128:(g + 1) * 128], in_=R1[:, 8 + g, cs], identity=I16),
                      [vT_b[g], c16_b], [ps_b[bk]])
                for hh in range(2):
                    V(lambda e, hh=hh: e.bn_stats(out=bnst[:, hh, :], in_=P16(bk)[:, hh * 512:(hh + 1) * 512]), [ps_b[bk]], [bn_b])
                V(lambda e: e.bn_aggr(out=mv[:, 0:2], in_=bnst[:].rearrange("p a b -> p (a b)")), [bn_b], [bn_b])
                A(lambda e: e.activation(out=mv[:, 2:3], in_=mv[:, 1:2], func=AF.Ln, bias=1e-5), [bn_b], [bn_b])
                A(lambda e: e.activation(out=mv[:, 2:3], in_=mv[:, 2:3], func=AF.Exp, scale=-0.5), [bn_b], [bn_b])
                V(lambda e: e.tensor_scalar(out=vnb[:], in0=P16(bk), scalar1=mv[:, 0:1], scalar2=mv[:, 2:3],
                                            op0=ALU.subtract, op1=ALU.mult), [ps_b[bk], bn_b], [vnb_b])
                V(lambda e: e.tensor_tensor(out=vnb[:], in0=vnb[:], in1=lnt16[:, 0, :], op=ALU.mult), [vnb_b] + t32_b[4:6], [vnb_b])
                V(lambda e: e.tensor_tensor(out=vnb[:], in0=vnb[:], in1=lnt16[:, 1, :], op=ALU.add), [vnb_b] + t32_b[4:6], [vnb_b])
                if samp or (last_blk and t == NT - 1):
                    A(lambda e: e.activation(out=lnv32, in_=vnb[:], func=AF.Copy), [vnb_b], [lnv32_b])
                    dma("sp", "o_cv", cv_s[l] if samp else cv_p[l], lnv32, [lnv32_b], [])
                pM = pair()
                for g in range(8):
                    def fm(e, g=g):
                        e.matmul(PV(pM)[:, g, :], vnb[:, g * 128:(g + 1) * 128], wsT[:, g, :], start=True, stop=False)
                        return e.matmul(PV(pM)[:, g, :], c16[0:1, 1, :], prw16[0:1, g * 128:(g + 1) * 128],
                                        start=False, stop=True)
                    T(fm, [vnb_b, wsT_b, c16_b, prw_b], [ps_b[pM + g // 4]])
                V(lambda e: e.tensor_tensor(out=R1[:, 0:8, cs], in0=R1[:, 0:8, cs], in1=PV(pM), op=ALU.mult),
                  uT_b + [ps_b[pM], ps_b[pM + 1]], uT_b)

            if stop <= 5:
                return
            mg, mg_b = sz, sz_b
            for half in range(2):
                sA = load_w(wview("w_in", l, 0, 1024, 6160 + half * 512, 512), 8, 512)
                sPA = load_w(wview("w_pa", l, 0, 1024, half * 512, 512), 8, 512)
                for jj in range(4):
                    d = half * 4 + jj
                    b1 = bank()
                    mm_group(b1, 0, W, lambda k, jj=jj: wsl[sA][:, k, jj * 128:(jj + 1) * 128], lambda k: hT[:, k, 0:W], 8,
                             hT_b, [wsl_b[sA]])
                    b2 = bank()
                    mm_group(b2, 0, W, lambda k, jj=jj: wsl[sPA][:, k, jj * 128:(jj + 1) * 128], lambda k: oa[:, k, 0:W], 8,
                             oa_b, [wsl_b[sPA]])
                    A(lambda e, b1=b1: e.activation(out=acc[0][:, 0:W], in_=ps[:, b1, 0:W], func=AF.Sigmoid), [ps_b[b1]], [acc_b[0]])
                    V(lambda e, d=d, b2=b2: e.tensor_tensor(out=t32[:, d, 0:W], in0=acc[0][:, 0:W], in1=ps[:, b2, 0:W], op=ALU.mult),
                      [acc_b[0], ps_b[b2]], [t32_b[d]])
                w_done(2)
            for half in range(2):
                sB = load_w(wview("w_in", l, 0, 1024, 7184 + half * 512, 512), 8, 512)
                sPB = load_w(wview("w_pb", l, 0, 1024, half * 512, 512), 8, 512)
                for jj in range(4):
                    d = half * 4 + jj
                    b1 = bank()
                    mm_group(b1, 0, W, lambda k, jj=jj: wsl[sB][:, k, jj * 128:(jj + 1) * 128], lambda k: hT[:, k, 0:W], 8,
                             hT_b, [wsl_b[sB]])
                    b2 = bank()
                    mm_group(b2, 0, W, lambda k, jj=jj: wsl[sPB][:, k, jj * 128:(jj + 1) * 128], lambda k: R1[:, k, 0:W], 8,
                             uT_b, [wsl_b[sPB]])
                    A(lambda e, b1=b1: e.activation(out=acc[1][:, 0:W], in_=ps[:, b1, 0:W], func=AF.Sigmoid), [ps_b[b1]], [acc_b[1]])
                    V(lambda e, b2=b2: e.tensor_tensor(out=acc[1][:, 0:W], in0=acc[1][:, 0:W], in1=ps[:, b2, 0:W], op=ALU.mult),
                      [acc_b[1], ps_b[b2]], [acc_b[1]])
                    G(lambda e, d=d: e.tensor_tensor(out=mg[:, d, 0:W], in0=acc[1][:, 0:W], in1=t32[:, d, 0:W], op=ALU.add),
                      [acc_b[1], t32_b[d]], [mg_b[d]])
                w_done(2)
            proj_chunks("w_o", l, 0, 8, W, mg, mg_b,
                        lambda ch, bk: A(lambda e: e.activation(out=t32[:, ch, 0:W], in_=ps[:, bk, 0:W], func=AF.Copy),
                                         [ps_b[bk]], [t32_b[ch]]))
            rmsnorm(t32, t32_b, W, NW(l, 1), "x", l)
            if stop <= 6:
                return
            rmsnorm(xT, xT_b, W, NW(l, 2), "h", l)
            hid, hid_b = R1, R1_b
            proj_chunks("w_f1", l, 0, 22, W, hT, hT_b,
                        lambda ch, bk: A(lambda e: e.activation(out=hid[:, ch, 0:W], in_=ps[:, bk, 0:W], func=AF.Silu),
                                         [ps_b[bk]], [hid_b[ch]]))
            proj_chunks("w_f1", l, DFF, 22, W, hT, hT_b,
                        lambda ch, bk: V(lambda e: e.tensor_tensor(out=hid[:, ch, 0:W], in0=hid[:, ch, 0:W], in1=ps[:, bk, 0:W],
                                                                   op=ALU.mult), [hid_b[ch], ps_b[bk]], [hid_b[ch]]))
            for half in range(2):
                bks = [bank() for _ in range(4)]
                for kg, (k0, nk) in enumerate(((0, 8), (8, 8), (16, 6))):
                    s = load_w(wview("w_f2", l, k0 * 128, nk * 128, half * 512, 512), nk, 512)
                    for jj in range(4):
                        def ff(e, s=s, jj=jj, k0=k0, nk=nk, kg=kg):
                            ins = None
                            for k in range(nk):
                                ins = e.matmul(ps[:, bks[jj], 0:W], wsl[s][:, k, jj * 128:(jj + 1) * 128], hid[:, k0 + k, 0:W],
                                               start=(kg == 0 and k == 0), stop=(kg == 2 and k == nk - 1))
                            return ins
                        T(ff, hid_b[k0:k0 + nk] + [wsl_b[s]], [ps_b[bks[jj]]])
                    w_done()
                for jj in range(4):
                    d = half * 4 + jj
                    A(lambda e, d=d, jj=jj: e.activation(out=t32[:, d, 0:W], in_=ps[:, bks[jj], 0:W], func=AF.Copy),
                      [ps_b[bks[jj]]], [t32_b[d]])
            rmsnorm(t32, t32_b, W, NW(l, 3), "x", l)

        def load_x(src, W):
            for t in range(W // 128):
                dma("sp", "xin", xin, src[t * 128:(t + 1) * 128, :], [], t32_b[0:2])
                for hh in range(2):
                    bk = bank()
                    for c4 in range(4):
                        c = hh * 4 + c4
                        T(lambda e, c=c, c4=c4, bk=bk: e.transpose(out=ps[:, bk, c4 * 128:(c4 + 1) * 128], in_=xin[:, c * 128:(c + 1) * 128],
                                                                  identity=I32), t32_b[0:2] + [c32_b], [ps_b[bk]])
                    A(lambda e, hh=hh, bk=bk, t=t: e.activation(out=xT[:, hh * 4:(hh + 1) * 4, t * 128:(t + 1) * 128],
                                                               in_=ps[:, bk, :].rearrange("p (a b) -> p a b", a=4), func=AF.Copy),
                      [ps_b[bk]], xT_b[hh * 4:(hh + 1) * 4])

        def store_y(dst, W):
            for t in range(W // 128):
                yo = t % 2
                for hh in range(2):
                    bk = bank()
                    for c4 in range(4):
                        c = hh * 4 + c4
                        T(lambda e, c=c, c4=c4, bk=bk, t=t: e.transpose(out=ps[:, bk, c4 * 128:(c4 + 1) * 128],
                                                                       in_=xT[:, c, t * 128:(t + 1) * 128], identity=I32),
                          [xT_b[c], c32_b], [ps_b[bk]])
                    A(lambda e, hh=hh, bk=bk, yo=yo: e.activation(out=yout[yo][:, hh * 512:(hh + 1) * 512], in_=ps[:, bk, :],
                                                                  func=AF.Copy), [ps_b[bk]], t32_b[2 + 2 * yo:4 + 2 * yo])
                dma("sp", f"yo{yo}", dst[t * 128:(t + 1) * 128, :], yout[yo], t32_b[2 + 2 * yo:4 + 2 * yo], [])

        G(lambda e: e.memset(S32[:].rearrange("p a b c -> p (a b c)"), 0.0), [], S32_b)
        G(lambda e: e.memset(S16[:].rearrange("p a b c -> p (a b c)"), 0.0), [], S16_b)
        G(lambda e: e.memset(carry[:].rearrange("p a b c -> p (a b c)"), 0.0), [], carry_b[0] + carry_b[1])

        for blk in range(nblk):
            load_x(xp[blk * 512:(blk + 1) * 512, :], 512)
            for l in range(DEPTH):
                E.epoch += 1
                layer(l, 512, False, blk == nblk - 1)
            store_y(y_p[blk * 512:(blk + 1) * 512, :], 512)
        E.epoch += 1
        if do_samp:
            if all_reqs is None:
                SAMP_K0[0] = slot_i[0]
            samp_k0[0] = slot_i[0]
            for hf in range(2):
                dma("pool", "cst16", mskc[:, hf * 8:(hf + 1) * 8, :].rearrange("p a b -> p (a b)"), cbd[:, hf * 1024:(hf + 1) * 1024],
                    [], [cb16_b, wsl_b[3], wsl_b[4], msk16_b, scin_b, snew_b[0]])
            mask_loaded[0] = True
            if all_reqs is not None:
                pump()
            load_x(xs, 128)
            for l in range(DEPTH):
                E.epoch += 1
                layer(l, 128, True, False)
            store_y(y_s, 128)

        E.finalize()
        if os.environ.get("KDBG"):
            print("est_span_us", getattr(E, "est_span", None), "n_ops", len(E.ops), flush=True)
        sems = {}
        for e_ in ENGS:
            for ep in range(E.epoch + 1):
                sems[(e_, ep)] = es.enter_context(nc.semaphore(f"s_{e_}_{ep}"))
        chansems = {ch: es.enter_context(nc.semaphore(f"c_{ch}")) for ch in E.chans}
        block = es.enter_context(nc.Block())
        E.emit(block, sems, chansems)
    return nc


def _pack(inp):
    f = lambda a: np.ascontiguousarray(np.asarray(a, dtype=np.float32))
    pp = np.zeros((128, 262), np.float32)
    names = ["norm_pre_mix", "norm_post_mix", "norm_pre_ffn", "norm_post_ffn"]
    for l in range(DEPTH):
        for n, nm in enumerate(names):
            pp[:, (l * 4 + n) * 8:(l * 4 + n) * 8 + 8] = f(inp[nm])[l].reshape(8, 128).T
        cw = f(inp["conv_w"])[l]
        pp[:, 64 + l * 96:64 + (l + 1) * 96] = cw.reshape(4, 24, 128).transpose(2, 0, 1).reshape(128, 96)
        pp[:, 256 + l] = f(inp["delta_norm_w"])[l]
        pp[0:8, 258 + l] = f(inp["a_log"])[l]
        pp[0:8, 260 + l] = f(inp["dt_bias"])[l]
    bs = f(inp["b_spatial"])
    prow = np.zeros((DEPTH, 2, 8, 128), np.float32)
    prow[:, 0] = bs
    prow[:, 1] = np.tile(bs[:, :, :8], (1, 1, 16))
    prow = prow.reshape(1, -1)
    lnw = np.stack([f(inp["sgu_ln_w"]), f(inp["sgu_ln_b"])], axis=1)
    ws = f(inp["w_spatial"])
    wsT = ws.transpose(0, 3, 1, 2)
    wss = np.tile(ws[:, :, :8, :8], (1, 1, 16, 16)).transpose(0, 3, 1, 2)
    wsp = np.ascontiguousarray(np.stack([wsT, wss], axis=1).reshape(DEPTH, 2, 128, 1024))
    shared = dict(
        w_in=f(inp["w_in"]), w_pa=f(inp["w_proj_a"]), w_pb=f(inp["w_proj_b"]), w_o=f(inp["w_out"]),
        w_f1=f(inp["w_ffn_in"]), w_f2=f(inp["w_ffn_out"]), pp=pp, prow=prow, lnw=np.ascontiguousarray(lnw), wsp=wsp,
        cst=_consts()[0], cst16=_consts()[1], cbd=_cb(),
    )
    xpr = f(inp["x_prompt"]); xsm = f(inp["x_sample"]); sdl = f(inp["state_delta"]); scv = f(inp["state_conv"])
    maps = []
    for c in range(NCORES):
        m = dict(shared)
        m["xp"] = xpr[c]
        m["xs"] = np.ascontiguousarray(xsm[16 * c:16 * (c + 1)].reshape(128, D))
        m["sd"] = np.ascontiguousarray(sdl[:, 16 * c:16 * (c + 1)])
        m["sc"] = np.ascontiguousarray(scv[:, 16 * c:16 * (c + 1)].reshape(DEPTH, 48, 3072))
        maps.append(m)
    return maps


def kernel(**inputs):
    maps = _pack(inputs)
    nc = build()
    res = run_bass_kernel_spmd(nc, maps, core_ids=list(range(NCORES)))
    r = res.results
    y_p = np.stack([r[c]["y_p"] for c in range(NCORES)], 0).reshape(8, 2048, D)
    y_s = np.concatenate([r[c]["y_s"].reshape(16, 8, D) for c in range(NCORES)], 0)
    sd_p = np.stack([r[c]["sd_p"] for c in range(NCORES)], 1)
    sc_p = np.stack([r[c]["sc_p"] for c in range(NCORES)], 1)
    cv_p = np.stack([r[c]["cv_p"] for c in range(NCORES)], 1)
    sd_s = np.concatenate([r[c]["sd_s"] for c in range(NCORES)], 1)
    sc_s = np.concatenate([r[c]["sc_s"] for c in range(NCORES)], 1)
    cv_s = np.concatenate([r[c]["cv_s"].reshape(DEPTH, 16, 8, D) for c in range(NCORES)], 1)
    return tuple(np.ascontiguousarray(a.astype(np.float32)) for a in (y_p, y_s, sd_p, sc_p, cv_p, sd_s, sc_s, cv_s))
```

```python
import os
import numpy as np
from contextlib import ExitStack
import concourse.bass as bass
import concourse.mybir as mybir
from concourse.bass_utils import run_bass_kernel_spmd

F32 = mybir.dt.float32
BF16 = mybir.dt.bfloat16
AF = mybir.ActivationFunctionType
ALU = mybir.AluOpType

NCORES = 8
D = 1024
DEPTH = 2
H = 8
DIN = 8208
DFF = 2816
BIG = 30000.0
NSLOT = 3
ENGS = ["pe", "act", "dve", "pool", "sp"]
BNAME = {"pe": "tensor", "act": "scalar", "dve": "vector", "pool": "gpsimd", "sp": "sync"}
NEPOCH = 12
SCHED = True


class Buf:
    __slots__ = ("name", "lw", "rd")

    def __init__(self, name):
        self.name = name
        self.lw = None
        self.rd = []


def bufs(name, n):
    return [Buf(f"{name}{i}") for i in range(n)]


class Rec:
    def __init__(self):
        self.calls = []

    def __getattr__(self, name):
        def f(*a, **k):
            self.calls.append((name, a, k))
            return self
        return f


class Em:
    def __init__(self):
        self.ops = []
        self.chans = {}
        self.epoch = 0

    def op(self, eng, fn, reads=(), writes=(), chan=None, ndma=0):
        idx = len(self.ops)
        deps = set()
        soft = set()
        for b in reads:
            if b.lw is not None:
                deps.add(b.lw)
        for b in writes:
            if b.lw is not None:
                soft.add(b.lw)
            soft.update(b.rd)
        deps |= soft
        val = None
        if chan:
            c = self.chans.setdefault(chan, {"count": 0, "last": None, "eng": eng})
            assert c["eng"] == eng
            if c["last"] is not None:
                deps.add(c["last"])
            c["count"] += 16 * ndma
            c["last"] = idx
            val = c["count"]
        key = chan if chan else eng
        for b in writes:
            b.lw = idx
            b.rd = []
        for b in reads:
            b.rd.append(idx)
        rec = Rec()
        fn(rec)
        assert rec.calls
        self.ops.append(dict(eng=eng, calls=rec.calls, deps=deps, chan=chan, val=val, inc=False, ep=self.epoch))
        return idx

    @staticmethod
    def _est(o):
        eng = o["eng"]
        tot = 0.0
        for name, a, k in o["calls"]:
            out = k.get("out", a[0] if a else None)
            try:
                fs = float(out.free_size())
            except Exception:
                fs = 128.0
            if o["chan"]:
                try:
                    nb = float(out.nbytes())
                except Exception:
                    nb = 1e5
                tot += 2.0 + nb / 1.8e5
            elif eng == "pe":
                lhs = k.get("lhsT", a[1] if len(a) > 1 else None) if name == "matmul" else k.get("in_")
                f32 = getattr(lhs, "dtype", None) == F32
                tot += 0.045 + max(fs, 64.0) * (4.0 if (f32 and name == "matmul") else 1.0) / 2400.0 + (0.05 if fs <= 128 else 0.0)
            elif eng == "dve":
                tot += 0.12 + fs * 1.1e-3
            elif eng == "act":
                tot += 0.17 + fs * 0.9e-3
            else:
                tot += 0.25 + fs * 2.2e-3
        return tot

    def schedule(self):
        import heapq
        ops = self.ops
        n = len(ops)
        succ = [[] for _ in range(n)]
        indeg = [0] * n
        for i, o in enumerate(ops):
            for d in o["deps"]:
                succ[d].append(i)
                indeg[i] += 1
        dur = [self._est(o) for o in ops]
        bl = [0.0] * n
        for i in range(n - 1, -1, -1):
            m_ = 0.0
            for j in succ[i]:
                if bl[j] > m_:
                    m_ = bl[j]
            bl[i] = dur[i] + 0.3 + m_
        PRIO = os.environ.get("KPRIO", "bl")
        WIN = float(os.environ.get("KWIN", "0.1"))
        ready = [0.0] * n
        fin = [0.0] * n
        free = {e: 0.0 for e in ENGS}
        heaps = {e: [] for e in ENGS}
        for i in range(n):
            if indeg[i] == 0:
                heapq.heappush(heaps[ops[i]["eng"]], (0.0, i))
        order = []
        LAT = 0.15
        while len(order) < n:
            best = None
            for e in ENGS:
                hp = heaps[e]
                if not hp:
                    continue
                cand = hp[0]
                st = max(free[e], cand[0])
                key = (st, cand[1])
                if best is None or key < best[0]:
                    best = (key, e)
            (st, i), e = best
            hp = heaps[e]
            pool_ = []
            while hp and hp[0][0] <= st + WIN:
                pool_.append(heapq.heappop(hp))
            if PRIO == "bl":
                pool_.sort(key=lambda x: (-bl[x[1]], x[1]))
            else:
                pool_.sort(key=lambda x: x[1])
            _, i = pool_[0]
            for x in pool_[1:]:
                heapq.heappush(hp, x)
            o = ops[i]
            st = max(st, ready[i])
            fin[i] = st + dur[i]
            free[e] = st + (min(dur[i], 1.0) if o["chan"] else dur[i])
            order.append(i)
            for j in succ[i]:
                r = fin[i] + (LAT if ops[j]["eng"] != e or o["chan"] else 0.1)
                if r > ready[j]:
                    ready[j] = r
                indeg[j] -= 1
                if indeg[j] == 0:
                    heapq.heappush(heaps[ops[j]["eng"]], (ready[j], j))
        pos = {old: new for new, old in enumerate(order)}
        new_ops = []
        for old in order:
            o = ops[old]
            o["deps"] = {pos[d] for d in o["deps"]}
            new_ops.append(o)
        self.ops = new_ops
        self.est_span = max(fin) if fin else 0.0
        if os.environ.get("KDBG"):
            cp = [0.0] * n
            for old in order:
                o = ops[old]
            newdur = [dur[old] for old in order]
            for i_, o in enumerate(new_ops):
                st_ = 0.0
                for d in o["deps"]:
                    st_ = max(st_, cp[d] + LAT)
                cp[i_] = st_ + newdur[i_]
            busy = {}
            for i_, o in enumerate(new_ops):
                busy[o["eng"]] = busy.get(o["eng"], 0.0) + (min(newdur[i_], 1.0) if o["chan"] else newdur[i_])
            print("critical_path_us", max(cp), "busy", {k: round(v) for k, v in busy.items()}, flush=True)
            i_ = max(range(n), key=lambda q: cp[q])
            agg = {}
            seq = []
            while True:
                o = new_ops[i_]
                nm = o["calls"][0][0] + ("/dma" if o["chan"] else "")
                outap = o["calls"][0][2].get("out", o["calls"][0][1][0] if o["calls"][0][1] else None)
                tn = getattr(getattr(outap, "tensor", None), "name", "?")
                key = (o["eng"], nm, tn)
                a_ = agg.setdefault(key, [0, 0.0])
                a_[0] += 1
                a_[1] += newdur[i_] + LAT
                seq.append(key)
                prev = None
                for d in o["deps"]:
                    if prev is None or cp[d] > cp[prev]:
                        prev = d
                if prev is None:
                    break
                i_ = prev
            for k_, v_ in sorted(agg.items(), key=lambda kv: -kv[1][1])[:28]:
                print("   CP", k_, v_[0], round(v_[1], 1), flush=True)

    def prune(self):
        for o in self.ops:
            best = {}
            for d in o["deps"]:
                p = self.ops[d]
                key = ("c", p["chan"]) if p["chan"] else ("e", p["eng"])
                if key not in best or d > best[key]:
                    best[key] = d
            o["deps"] = set(best.values())

    def finalize(self):
        if SCHED:
            self.schedule()
        self.prune()
        for o in self.ops:
            for d in o["deps"]:
                p = self.ops[d]
                if p["chan"] is None:
                    if p["eng"] == "pe" and o["eng"] == "pe" and o["chan"] is None:
                        continue
                    p["inc"] = True
        cnt = {}
        for o in self.ops:
            if o["chan"] is None and o["inc"]:
                k = (o["eng"], o["ep"])
                cnt[k] = cnt.get(k, 0) + 1
                o["val"] = cnt[k]

    def emit(self, block, sems, chansems):
        for e in ENGS:
            ops_e = [o for o in self.ops if o["eng"] == e]

            def body(eng, ops_e=ops_e, e=e):
                seen = {}
                for o in ops_e:
                    for d in sorted(o["deps"]):
                        p = self.ops[d]
                        if p["chan"] is None:
                            if not p["inc"]:
                                continue
                            if p["eng"] == "pe" and e == "pe" and o["chan"] is None:
                                continue
                            key = (p["eng"], p["ep"])
                            sem = sems[key]
                        else:
                            key = p["chan"]
                            sem = chansems[key]
                        v = p["val"]
                        if seen.get(key, 0) >= v:
                            continue
                        seen[key] = v
                        eng.wait_ge(sem, v)
                    ins = None
                    for name, a, k in o["calls"]:
                        ins = getattr(eng, name)(*a, **k)
                        if o["chan"]:
                            ins.then_inc(chansems[o["chan"]], 16)
                    if (not o["chan"]) and o["inc"]:
                        ins.then_inc(sems[(e, o["ep"])], 1)
                for ch, c in self.chans.items():
                    if c["eng"] == e and c["count"] > 0:
                        eng.wait_ge(chansems[ch], c["count"])

            getattr(block, BNAME[e])(body)


def _consts():
    i = np.arange(128)
    m = i[:, None]
    p = i[None, :]
    blk = i // 8
    same = blk[:, None] == blk[None, :]
    c = np.zeros((13, 128, 128), np.float32)
    c[0] = np.eye(128)
    c[1] = 1.0
    c[2] = m <= p
    c[3] = m > p
    c[4] = -BIG * (p >= m)
    c[5] = -BIG * (p < m)
    c[6] = (m <= p) & same
    c[7] = (m > p) & same
    c[8] = -BIG * (~((p < m) & same))
    c[9] = -BIG * (~((p >= m) & same))
    c[10] = same
    s = np.zeros((128, 16, 16), np.float32)
    for j in range(16):
        s[:, j, j] = 1.0
    c[11] = s.reshape(128, 256)[:, :128]
    c[12] = s.reshape(128, 256)[:, 128:]
    c32 = np.ascontiguousarray(c[[0, 1, 2, 3, 6, 7, 10]])
    md = ((m // 64) == (p // 64)).astype(np.float32)
    mo = ((m < 64) & (p >= 64)).astype(np.float32)
    c16 = np.ascontiguousarray(np.concatenate([c[[0, 1, 4, 5, 8, 9, 11, 12]], md[None], mo[None]], 0))
    return c32, c16


def _cb():
    i = np.arange(128)
    maskc = (i[None, :] // 8 == np.arange(16)[:, None]).astype(np.float32)
    maskc = np.broadcast_to(maskc[None], (128, 16, 128)).reshape(128, 2048)
    seqsel = (i[:, None] // 8 == np.arange(16)[None, :]).astype(np.float32)
    return np.concatenate([maskc, seqsel], axis=1).astype(np.float32)


def build(nblk=4, do_samp=True, stop=99):
    reqs = []
    _build(nblk, do_samp, stop, reqs, None)
    return _build(nblk, do_samp, stop, [], reqs)


LOOKAHEAD = 2
SAMP_K0 = [None]


def _build(nblk, do_samp, stop, rec_reqs, all_reqs):
    nc = bass.Bass("TRN2", target_bir_lowering=False)

    def din(name, shape):
        return nc.dram_tensor(name, list(shape), F32, kind="ExternalInput").ap()

    def dout(name, shape):
        return nc.dram_tensor(name, list(shape), F32, kind="ExternalOutput").ap()

    xp = din("xp", [2048, D])
    xs = din("xs", [128, D])
    sd = din("sd", [DEPTH, 16, H, 128, 128])
    sc = din("sc", [DEPTH, 48, 3072])
    w_in = din("w_in", [DEPTH, D, DIN])
    w_pa = din("w_pa", [DEPTH, D, D])
    w_pb = din("w_pb", [DEPTH, D, D])
    w_o = din("w_o", [DEPTH, D, D])
    w_f1 = din("w_f1", [DEPTH, D, 2 * DFF])
    w_f2 = din("w_f2", [DEPTH, DFF, D])
    pp = din("pp", [128, 2 * 32 + 2 * 96 + 2 + 4])
    prow = din("prow", [1, DEPTH * 2 * 1024])
    lnw = din("lnw", [DEPTH, 2, D])
    wsp = din("wsp", [DEPTH, 2, 128, 1024])
    cst = din("cst", [7, 128, 128])
    cst16 = din("cst16", [10, 128, 128])
    cbd = din("cbd", [128, 2064])

    NSCR = 2 * 41
    wscr = nc.dram_tensor("wscr", [NSCR, 128, 4096], BF16, kind="Internal").ap()
    wscr_b = bufs("wscr", NSCR)
    y_p = dout("y_p", [2048, D])
    y_s = dout("y_s", [128, D])
    sd_p = dout("sd_p", [DEPTH, H, 128, 128])
    sc_p = dout("sc_p", [DEPTH, 3, 3072])
    cv_p = dout("cv_p", [DEPTH, 128, D])
    sd_s = dout("sd_s", [DEPTH, 16, H, 128, 128])
    sc_s = dout("sc_s", [DEPTH, 16, 3, 3072])
    cv_s = dout("cv_s", [DEPTH, 128, D])

    E = Em()
    es = ExitStack()
    with es:
        def sb(name, shape, dt):
            return es.enter_context(nc.sbuf_tensor(name, shape, dt))

        xT = sb("xT", [128, 8, 512], F32); xT_b = bufs("xT", 8)
        hT = sb("hT", [128, 8, 512], BF16); hT_b = bufs("hT", 8)
        R1 = sb("R1", [128, 24, 512], BF16); R1_b = bufs("R1", 24)
        sz = sb("sz", [128, 8, 512], BF16); sz_b = bufs("sz", 8)
        oa = sb("oa", [128, 8, 512], BF16); oa_b = bufs("oa", 8)
        uT_b = R1_b[0:8]; vT_b = R1_b[8:16]
        t32 = sb("t32", [128, 8, 512], F32); t32_b = bufs("t32", 8)
        pre = [sb(f"pre{i}", [128, 515], F32) for i in range(2)]; pre_b = bufs("pre", 2)
        acc = [sb(f"acc{i}", [128, 512], F32) for i in range(2)]; acc_b = bufs("acc", 2)
        sq = [sb(f"sq{i}", [128, 1024], BF16) for i in range(2)]; sq_b = bufs("sq", 2)
        rstd = sb("rstd", [128, 1024], F32); rstd_b = Buf("rstd")
        carry = sb("carry", [128, DEPTH, 24, 3], F32); carry_b = [bufs(f"car{l}_", 24) for l in range(DEPTH)]
        wsl = [sb(f"wsl{i}", [128, 8, 512], BF16) for i in range(5)]; wsl_b = bufs("wsl", 5)
        w3f = wsl[3][:].rearrange("p a b -> p (a b)")
        w4f = wsl[4][:].rearrange("p a b -> p (a b)")
        ppt = sb("ppt", [128, 262], F32); ppt_b = Buf("ppt")
        nea = sb("nea", [128, 2], F32); nea_b = Buf("nea")
        dnw = sb("dnw", [128, 2], F32); dnw_b = Buf("dnw")
        c32 = sb("c32", [128, 7, 128], F32); c32_b = Buf("c32")
        c16 = sb("c16", [128, 10, 128], BF16); c16_b = Buf("c16")
        seq32 = sb("seq32", [128, 16], F32); cb32_b = Buf("cb32")
        mskc = w3f[:, 0:2048].rearrange("p (a b) -> p a b", a=16)
        seqs = sb("seqs", [128, 16], BF16); cb16_b = Buf("cb16")
        selT = sb("selT", [16, 16, 128], BF16); selT_b = Buf("selT")
        t32f = t32[:].rearrange("p a b -> p (a b)")
        prw16 = t32f.bitcast(BF16)[0:1, 6144:7168]; prw_b = t32_b[6]
        lnt = t32f[:, 0:2048].rearrange("p (a b) -> p a b", a=2)
        xin = t32f[:, 0:1024]
        yout = [t32f[:, 1024:2048], t32f[:, 2048:3072]]
        scrow = t32f[0:3, 0:3072]
        scin = w4f[:, 0:2304].bitcast(F32).rearrange("p (a b) -> p a b", a=24); scin_b = Buf("scin")
        wsT = sb("wsT", [128, 8, 128], BF16); wsT_b = Buf("wsT")
        gT = pre[0][0:8, 0:512]; gT_b = pre_b[0]
        bT = pre[1][0:8, 0:512]; bT_b = pre_b[1]
        rn = acc[1][0:16, :]; rn_b = acc_b[1]
        rnb = acc[0][0:16, 0:256].bitcast(BF16); rnb_b = acc_b[0]
        rnl = acc[0][0:16, 256:512].bitcast(BF16)
        S32 = sb("S32", [128, DEPTH, H, 128], F32); S32_b = bufs("S32_", DEPTH)
        S16 = sb("S16", [128, DEPTH, H, 128], BF16); S16_b = bufs("S16_", DEPTH)
        s0f_0 = S32[:].rearrange("p a b c -> p (a b) c"); s0f_b0 = S32_b
        s0b_0 = S16[:].rearrange("p a b c -> p (a b) c"); s0b_b0 = S16_b
        gbt = sb("gbt", [128, 16], F32); gbt_b = Buf("gbt")
        es24 = sb("es24", [128, 24], F32); es_b = Buf("es24")
        kbs = sb("kbs", [128, 8], F32); kbs_b = Buf("kbs")
        Gm1 = sb("Gm1", [128, 8, 128], F32); Gm1_b = Buf("Gm1")
        Gm2 = sb("Gm2", [128, 8, 128], F32); Gm2_b = Buf("Gm2")
        ws32 = Gm1[:].rearrange("p a b -> p (a b)"); ws32_b = Gm1_b
        Kd = sb("Kd", [128, 8, 128], F32); Kd_b = Buf("Kd")
        Vb = sb("Vb", [128, 8, 128], BF16); Vb_b = Buf("Vb")
        Bo = sb("Bo", [128, 8, 128], BF16); Bo_b = Buf("Bo")
        r16 = sb("r16", [128, 8, 128], BF16); r16_b = Buf("r16")
        neg = sb("neg", [128, 8], F32); neg_b = Buf("neg")
        dsm = Gm1; dsm_b = Gm1_b
        dTm = Gm2; dTm_b = Gm2_b
        reg = sb("reg", [128, 8, 128], BF16); reg_b = Buf("reg")
        AmT = sb("AmT", [128, 2, 8, 128], BF16); Am = [AmT[:, 0, :, :], AmT[:, 1, :, :]]; Am_b = bufs("Am", 2)
        BmT = sb("BmT", [128, 2, 8, 128], BF16); Bm = [BmT[:, 0, :, :], BmT[:, 1, :, :]]; Bm_b = bufs("Bm", 2)
        vn32 = AmT[:].rearrange("p a b c -> p (a b c)").bitcast(F32).rearrange("p (b c) -> p b c", c=128)
        r32 = BmT[:].rearrange("p a b c -> p (a b c)").bitcast(F32).rearrange("p (b c) -> p b c", c=128)
        Pm0 = sb("Pm0", [128, 8, 128], BF16); Pm_b0 = Buf("Pm")
        Pf = rstd[:].rearrange("p (b c) -> p b c", c=128)
        PTm = sb("PTm", [128, 8, 128], BF16); PTm_b = Buf("PTm")
        vn = sb("vn", [128, 8, 128], BF16); vn_b = Buf("vn")
        qt = sb("qt", [128, 8, 128], BF16); qt_b = Buf("qt")
        on32 = sb("on32", [128, 8, 128], F32); on32_b = Buf("on32")
        lnv32 = on32[:].rearrange("p a b -> p (a b)"); lnv32_b = on32_b
        vnb = sb("vnb", [128, D], BF16); vnb_b = Buf("vnb")
        bnst = sb("bnst", [128, 2, 6], F32); mv = sb("mv", [128, 4], F32); bn_b = Buf("bn")
        t16 = t32f.bitcast(BF16)

        def t16tile(c):
            return t16[:, c * 1024:(c + 1) * 1024].rearrange("p (a b) -> p a b", a=8)
        KdV = Kd[:].rearrange("p a b -> p (a b)").bitcast(BF16).rearrange("p (s a b) -> p s a b", s=2, a=8)
        Kd2 = [KdV[:, 0, :, :], KdV[:, 1, :, :]]; Kd2_b = bufs("Kd2_", 2)
        Vb2 = [Vb[:], t16tile(0)]; Vb2_b = [Vb_b, t32_b[0]]
        qt2 = [qt[:], t16tile(1)]; qt2_b = [qt_b, t32_b[1]]
        PTm2 = [PTm[:], t16tile(2)]; PTm2_b = [PTm_b, t32_b[2]]
        Pm2 = [Pm0[:], t16tile(3)]; Pm2_b = [Pm_b0, t32_b[3]]
        Bo2 = [Bo[:], t16tile(4)]; Bo2_b = [Bo_b, t32_b[4]]
        neg1 = sb("neg1", [128, 8], F32); neg2 = [neg, neg1]; neg2_b = [neg_b, Buf("neg1")]
        es1 = sb("es1", [128, 24], F32); es2 = [es24, es1]; es2_b = [es_b, Buf("es1")]
        rstd9 = t32f[:, 3072:4096]
        msk16 = w3f[:, 2048:4096].rearrange("p (a b) -> p a b", a=16); msk16_b = Buf("msk16")
        egl = sb("egl", [128, 16, 8], F32); egl_b = Buf("egl")
        gsel = sb("gsel", [128, 16, 8], F32); gsel_b = Buf("gsel")
        snew = [w4f[:, 2304:3328].bitcast(F32).rearrange("p (a b) -> p a b", a=4), sb("snew1", [128, 4, 128], F32)[:]]
        snew_b = bufs("snew", 2)

        ps = es.enter_context(nc.psum_tensor("ps", [128, 8, 512], F32))
        ps_b = bufs("ps", 8)
        bptr = [0]

        def bank():
            b = bptr[0] % 8
            bptr[0] = (b + 1) % 8
            return b

        def pair():
            b = bptr[0] % 8
            if b % 2:
                b = (b + 1) % 8
            bptr[0] = (b + 2) % 8
            return b

        def P32(i, n=1):
            return ps[:, i:i + n, :].rearrange("p a b -> p (a b)") if n > 1 else ps[:, i, :]

        def P16(i):
            return ps[:, i, :].bitcast(BF16)

        def V(fn, r, w): return E.op("dve", fn, r, w)
        def A(fn, r, w): return E.op("act", fn, r, w)
        def T(fn, r, w): return E.op("pe", fn, r, w)
        def G(fn, r, w): return E.op("pool", fn, r, w)

        def dma(eng, chan, out, in_, r, w):
            return E.op(eng, lambda e: e.dma_start(out=out, in_=in_), r, w, chan=chan, ndma=1)

        slot_i = [0]
        issued = [0]
        mask_loaded = [False]
        WT = {"w_in": w_in, "w_pa": w_pa, "w_pb": w_pb, "w_o": w_o, "w_f1": w_f1, "w_f2": w_f2}

        scr_idx = {}

        def issue_w(k, desc):
            wn, l_, r0, nr, c0, ncol = desc
            s_ = slot_of(k)
            kc_ = nr // 128
            dst = wsl[s_][:, 0:kc_, 0:ncol]
            if desc not in scr_idx:
                i_ = len(scr_idx)
                scr_idx[desc] = i_
                src = WT[wn][l_, r0:r0 + nr, c0:c0 + ncol].rearrange("(c p) n -> p c n", p=128)
                dma("pool", f"w{s_}", dst, src, [], [wsl_b[s_]])
                sv = wscr[i_, :, 0:kc_ * ncol].rearrange("p (c n) -> p c n", c=kc_)
                dma("sp", f"scrw{i_ % 2}", sv, dst, [wsl_b[s_]], [wscr_b[i_]])
            else:
                i_ = scr_idx[desc]
                sv = wscr[i_, :, 0:kc_ * ncol].rearrange("p (c n) -> p c n", c=kc_)
                dma("sp", f"v{s_}", dst, sv, [wscr_b[i_]], [wsl_b[s_]])

        def load_w(desc, kc=None, ncols=None):
            k = slot_i[0]
            slot_i[0] += 1
            rec_reqs.append(desc)
            if all_reqs is None:
                issue_w(k, desc)
            else:
                assert all_reqs[k] == desc
                pump()
                assert issued[0] > k, "weight slot ring exhausted"
            return slot_of(k)

        samp_k0 = [None]
        if all_reqs is not None:
            samp_k0[0] = SAMP_K0[0]

        def slot_of(k):
            k0 = samp_k0[0]
            if k0 is None or k < k0:
                return k % 5
            return (k - k0) % 3

        def prev_user(k):
            s_ = slot_of(k)
            j = k - 1
            while j >= 0:
                if slot_of(j) == s_:
                    return j
                j -= 1
            return -1

        wdone = [0]

        def pump():
            while issued[0] < len(all_reqs) and prev_user(issued[0]) < wdone[0]:
                issue_w(issued[0], all_reqs[issued[0]])
                issued[0] += 1

        def w_done(n=1):
            wdone[0] += n
            if all_reqs is not None:
                pump()

        def wview(wt, l, r0, nr, c0, ncol):
            return (wt, l, r0, nr, c0, ncol)

        I32 = c32[:, 0, :]; I16 = c16[:, 0, :]; ONES16 = c16[:, 1, :]; ONES32 = c32[:, 1, :]

        def mm_group(bk, col0, ncol, lhs_fn, rhs_fn, nk, rbufs, extra_r=(), mp=128):
            def fn(e):
                ins = None
                for k in range(nk):
                    ins = e.matmul(ps[0:mp, bk, col0:col0 + ncol], lhs_fn(k), rhs_fn(k),
                                   start=(k == 0), stop=(k == nk - 1))
                return ins
            return T(fn, list(rbufs) + list(extra_r), [ps_b[bk]])

        dma("sp", "par", ppt[:], pp, [], [ppt_b])
        dma("sp", "par", c32[:], cst.rearrange("k p n -> p k n"), [], [c32_b])
        dma("sp", "par", seq32[:], cbd[:, 2048:2064], [], [cb32_b])
        dma("pool", "cst16", c16[:], cst16.rearrange("k p n -> p k n"), [], [c16_b])
        V(lambda e: e.tensor_copy(out=seqs[:], in_=seq32[:]), [cb32_b, cb16_b], [cb16_b])
        V(lambda e: e.tensor_copy(out=selT[:], in_=c32[0:16, 0, 0:16].unsqueeze(2).to_broadcast([16, 16, 128])),
          [c32_b], [selT_b])
        def NW(l, n): return ppt[:, (l * 4 + n) * 8:(l * 4 + n) * 8 + 8]
        def CW(l, tap, ch): return ppt[:, 64 + l * 96 + tap * 24 + ch:64 + l * 96 + tap * 24 + ch + 1]
        A(lambda e: e.activation(out=nea[0:8, :], in_=ppt[0:8, 258:260], func=AF.Exp), [ppt_b], [nea_b])
        V(lambda e: e.tensor_scalar(out=nea[0:8, :], in0=nea[0:8, :], scalar1=-1.0, scalar2=None, op0=ALU.mult), [nea_b], [nea_b])
        V(lambda e: e.tensor_scalar(out=dnw[:], in0=ppt[:, 256:258], scalar1=float(128 ** -0.5), scalar2=None, op0=ALU.mult),
          [ppt_b], [dnw_b])
        sel16 = c16[:, 6:8, :].rearrange("p a b -> p (a b)")

        def rsqrt_from_psum(out_ap, in_ap, scale, eps, r, w):
            A(lambda e: e.activation(out=out_ap, in_=in_ap, func=AF.Ln, scale=scale, bias=eps), r, w)
            A(lambda e: e.activation(out=out_ap, in_=out_ap, func=AF.Exp, scale=-0.5), w, w)

        def rmsnorm(src, src_b, W, gain, mode, l):
            bk = bank()
            for c in range(8):
                s = c % 2
                A(lambda e, c=c, s=s: e.activation(out=sq[s][:, 0:W], in_=src[:, c, 0:W], func=AF.Square),
                  [src_b[c]], [sq_b[s]])
                T(lambda e, c=c, s=s: e.matmul(ps[:, bk, 0:W], ONES16, sq[s][:, 0:W], start=(c == 0), stop=(c == 7)),
                  [sq_b[s], c16_b], [ps_b[bk]])
            rsqrt_from_psum(rstd[:, 0:W], ps[:, bk, 0:W], 1.0 / D, 1e-6, [ps_b[bk]], [rstd_b])
            for c in range(8):
                if mode == "h":
                    V(lambda e, c=c: e.scalar_tensor_tensor(out=hT[:, c, 0:W], in0=src[:, c, 0:W], scalar=gain[:, c:c + 1],
                                                            in1=rstd[:, 0:W], op0=ALU.mult, op1=ALU.mult),
                      [src_b[c], rstd_b, ppt_b], [hT_b[c]])
                else:
                    V(lambda e, c=c: e.scalar_tensor_tensor(out=src[:, c, 0:W], in0=src[:, c, 0:W], scalar=gain[:, c:c + 1],
                                                            in1=rstd[:, 0:W], op0=ALU.mult, op1=ALU.mult),
                      [src_b[c], rstd_b, ppt_b], [src_b[c]])
                    (V if c % 2 else G)(
                        lambda e, c=c: e.tensor_tensor(out=xT[:, c, 0:W], in0=xT[:, c, 0:W], in1=src[:, c, 0:W], op=ALU.add),
                        [src_b[c], xT_b[c]], [xT_b[c]])

        def proj_chunks(wt, l, c0, nchunks, W, rhs, rhs_b, consume, kc=8, r0=0):
            j = 0
            pend = [None]
            while j < nchunks:
                nj = min(4, nchunks - j)
                s = load_w(wview(wt, l, r0, kc * 128, c0 + j * 128, nj * 128), kc, nj * 128)
                for jj in range(nj):
                    bk = bank()
                    mm_group(bk, 0, W, lambda k, s=s, jj=jj: wsl[s][:, k, jj * 128:(jj + 1) * 128],
                             lambda k: rhs[:, k, 0:W], kc, rhs_b, [wsl_b[s]])
                    d_ = consume(j + jj, bk)
                    if pend[0] is not None:
                        pend[0]()
                    pend[0] = d_ if callable(d_) else None
                w_done()
                j += nj
            if pend[0] is not None:
                pend[0]()

        def layer(l, W, samp, last_blk):
            NT = W // 128
            if stop <= 0:
                return
            rmsnorm(xT, xT_b, W, NW(l, 0), "h", l)
            if stop <= 1:
                return
            if samp:
                tin = t32f
                dma("sp", "scin", tin[0:48, 0:3072], sc[l], [], t32_b[0:6])
                for q4 in range(6):
                    bk = bank()
                    for cc in range(4):
                        ch = q4 * 4 + cc
                        T(lambda e, ch=ch, cc=cc, bk=bk: e.transpose(out=ps[:, bk, cc * 48:(cc + 1) * 48], in_=tin[0:48, ch * 128:(ch + 1) * 128],
                                                                    identity=c32[0:48, 0, 0:48]), t32_b[0:6] + [c32_b], [ps_b[bk]])
                    V(lambda e, q4=q4, bk=bk: e.tensor_copy(out=scin[:, q4 * 4:(q4 + 1) * 4, :],
                                                            in_=ps[:, bk, 0:192].rearrange("p (a b) -> p a b", a=4)),
                      [ps_b[bk]], [scin_b])
            tok_b = t32_b
            if samp:
                tokm = t32f[:, 0:3072]

            def qkv_consume(ch, bk):
                s = ch % 2
                if samp:
                    pv = pre[s][:, 0:176].rearrange("p (a b) -> p a b", a=16)
                    G(lambda e: e.tensor_copy(out=pv[:, :, 0:3], in_=scin[:, ch, :].rearrange("p (a b) -> p a b", a=16)),
                      [scin_b], [pre_b[s]])
                    A(lambda e: e.activation(out=pv[:, :, 3:11], in_=ps[:, bk, 0:128].rearrange("p (a b) -> p a b", a=16),
                                             func=AF.Copy), [ps_b[bk]], [pre_b[s]])
                    av = acc[s][:, 0:128].rearrange("p (a b) -> p a b", a=16)
                    A(lambda e: e.activation(out=acc[s][:, 0:128], in_=ps[:, bk, 0:128], func=AF.Copy, scale=CW(l, 3, ch)),
                      [ps_b[bk], ppt_b], [acc_b[s]])
                    for tap in (2, 1, 0):
                        V(lambda e, tap=tap: e.scalar_tensor_tensor(out=av, in0=pv[:, :, tap:tap + 8], scalar=CW(l, tap, ch),
                                                                    in1=av, op0=ALU.mult, op1=ALU.add),
                          [pre_b[s], acc_b[s], ppt_b], [acc_b[s]])
                    G(lambda e: e.tensor_copy(out=sq[s][:, 0:256].bitcast(F32).rearrange("p (a b) -> p a b", a=16),
                                              in_=pv[:, :, 3:11]), [pre_b[s]], [sq_b[s]])
                    b2 = bank()
                    T(lambda e: e.transpose(out=ps[:, b2, 0:128], in_=sq[s][:, 0:256].bitcast(F32), identity=I32),
                      [sq_b[s], c32_b], [ps_b[b2]])
                    V(lambda e: e.tensor_copy(out=tokm[:, ch * 128:(ch + 1) * 128], in_=ps[:, b2, 0:128]),
                      [ps_b[b2]], [tok_b[ch // 4]])
                else:
                    G(lambda e: e.tensor_copy(out=pre[s][:, 0:3], in_=carry[:, l, ch, :]), [carry_b[l][ch]], [pre_b[s]])
                    A(lambda e: e.activation(out=pre[s][:, 3:515], in_=ps[:, bk, :], func=AF.Copy), [ps_b[bk]], [pre_b[s]])
                    A(lambda e: e.activation(out=acc[s][:], in_=ps[:, bk, :], func=AF.Copy, scale=CW(l, 3, ch)),
                      [ps_b[bk], ppt_b], [acc_b[s]])
                    G(lambda e: e.tensor_copy(out=carry[:, l, ch, :], in_=pre[s][:, 512:515]), [pre_b[s]], [carry_b[l][ch]])
                    for tap in (2, 1, 0):
                        V(lambda e, tap=tap: e.scalar_tensor_tensor(out=acc[s][:], in0=pre[s][:, tap:tap + 512],
                                                                    scalar=CW(l, tap, ch), in1=acc[s][:],
                                                                    op0=ALU.mult, op1=ALU.add),
                          [pre_b[s], acc_b[s], ppt_b], [acc_b[s]])
                return lambda: A(lambda e: e.activation(out=R1[:, ch, 0:W], in_=acc[s][:, 0:W], func=AF.Silu), [acc_b[s]], [R1_b[ch]])

            proj_chunks("w_in", l, 0, 24, W, hT, hT_b, qkv_consume)
            if samp:
                for j in range(3):
                    dma("sp", "o_sc", sc_s[l, :, j, :], tokm[5 + j:128:8, :], tok_b[0:6], [])
            elif last_blk:
                for q4 in range(6):
                    bk = bank()
                    for cc in range(4):
                        ch = q4 * 4 + cc
                        T(lambda e, ch=ch, cc=cc: e.transpose(out=ps[0:3, bk, cc * 128:(cc + 1) * 128], in_=carry[:, l, ch, :],
                                                              identity=I32), [carry_b[l][ch], c32_b], [ps_b[bk]])
                    V(lambda e, q4=q4: e.tensor_copy(out=scrow[:, q4 * 512:(q4 + 1) * 512], in_=ps[0:3, bk, :]),
                      [ps_b[bk]], t32_b[0:6])
                dma("sp", "o_sc", sc_p[l], scrow, t32_b[0:6], [])

            if stop <= 2:
                return
            proj_chunks("w_in", l, 3072, 8, W, hT, hT_b,
                        lambda ch, bk: A(lambda e: e.activation(out=sz[:, ch, 0:W], in_=ps[:, bk, 0:W], func=AF.Silu),
                                         [ps_b[bk]], [sz_b[ch]]))
            s = load_w(wview("w_in", l, 0, 1024, 4096, 16), 8, 16)
            ba = bank()
            mm_group(ba, 0, W, lambda k: wsl[s][:, k, 0:8], lambda k: hT[:, k, 0:W], 8, hT_b, [wsl_b[s]], mp=8)
            bb = bank()
            mm_group(bb, 0, W, lambda k: wsl[s][:, k, 8:16], lambda k: hT[:, k, 0:W], 8, hT_b, [wsl_b[s]], mp=8)
            w_done()
            A(lambda e: e.activation(out=gT[:, 0:W], in_=ps[0:8, ba, 0:W], func=AF.Exp, bias=ppt[0:8, 260 + l:261 + l]),
              [ps_b[ba], ppt_b], [gT_b])
            A(lambda e: e.activation(out=gT[:, 0:W], in_=gT[:, 0:W], func=AF.Ln, bias=1.0), [gT_b], [gT_b])
            V(lambda e: e.tensor_scalar(out=gT[:, 0:W], in0=gT[:, 0:W], scalar1=nea[0:8, l:l + 1], scalar2=None, op0=ALU.mult),
              [gT_b, nea_b], [gT_b])
            A(lambda e: e.activation(out=bT[:, 0:W], in_=ps[0:8, bb, 0:W], func=AF.Sigmoid), [ps_b[bb]], [bT_b])
            bn_ = bank()
            for j in range(16):
                s2 = j % 2
                A(lambda e, j=j, s2=s2: e.activation(out=sq[s2][:, 0:W], in_=R1[:, j, 0:W], func=AF.Square),
                  [R1_b[j]], [sq_b[s2]])
                T(lambda e, j=j, s2=s2: e.matmul(ps[0:16, bn_, 0:W], sel16[:, j * 16:(j + 1) * 16], sq[s2][:, 0:W],
                                                 start=(j == 0), stop=(j == 15)), [sq_b[s2], c16_b], [ps_b[bn_]])
            rsqrt_from_psum(rn[:, 0:W], ps[0:16, bn_, 0:W], 1.0, 1e-6, [ps_b[bn_]], [rn_b])
            V(lambda e: e.tensor_copy(out=rnb[:, 0:W], in_=rn[:, 0:W]), [rn_b], [rnb_b])
            V(lambda e: e.tensor_tensor(out=rnl[:, 0:W], in0=rn[:, 0:W], in1=rnb[:, 0:W], op=ALU.subtract), [rn_b, rnb_b], [rnb_b])
            for j in range(16):
                bk = bank()
                def fbc(e, j=j, bk=bk):
                    e.matmul(ps[:, bk, 0:W], selT[:, j, :], rnb[:, 0:W], start=True, stop=False)
                    return e.matmul(ps[:, bk, 0:W], selT[:, j, :], rnl[:, 0:W], start=False, stop=True)
                T(fbc, [rnb_b, selT_b], [ps_b[bk]])
                V(lambda e, j=j, bk=bk: e.tensor_tensor(out=R1[:, j, 0:W], in0=R1[:, j, 0:W], in1=ps[:, bk, 0:W], op=ALU.mult),
                  [R1_b[j], ps_b[bk]], [R1_b[j]])

            if stop <= 3:
                return
            cU, cSL, cN1, cN2 = (4, 5, 4, 5) if samp else (2, 3, 2, 3)
            nlev = 3 if samp else 6

            def bc8(ap):
                return ap.unsqueeze(2).to_broadcast([128, 8, 128])

            def bcm(ap):
                return ap.unsqueeze(1).to_broadcast([128, 8, 128])

            def PV(i):
                return ps[:, i:i + 2, :].rearrange("p a (b c) -> p (a b) c", c=128)

            class Alloc:
                def __init__(self, lo, n):
                    self.lo, self.n, self.p = lo, n, 0

                def bank(self):
                    b_ = self.p % self.n
                    self.p = (b_ + 1) % self.n
                    return self.lo + b_

                def pair(self):
                    b_ = self.p % self.n
                    if b_ % 2:
                        b_ = (b_ + 1) % self.n
                    self.p = (b_ + 2) % self.n
                    return self.lo + b_

            a1 = Alloc(0, 4)
            a2 = Alloc(0, 8)
            qb = R1_b[0:8]

            def chunk_gen(t):
                cs = slice(t * 128, (t + 1) * 128)
                st = 0 if samp else t % 2
                Kd_, Kd_b_ = Kd2[st], Kd2_b[st]
                Vb_, Vb_b_ = Vb2[st], Vb2_b[st]
                qt_, qt_b_ = qt2[st], qt2_b[st]
                PTm_, PTm_b_ = PTm2[st], PTm2_b[st]
                TT, TT_b = Pm2[st], Pm2_b[st]
                Bo_, Bo_b_ = Bo2[st], Bo2_b[st]
                neg_, neg_b_ = neg2[st], neg2_b[st]
                es_, es_b_ = es2[st], es2_b[st]
                bK = a1.bank()
                for h in range(H):
                    T(lambda e, h=h: e.transpose(out=P16(bK)[:, h * 128:(h + 1) * 128], in_=R1[:, 8 + h, cs], identity=I16),
                      [R1_b[8 + h], c16_b], [ps_b[bK]])
                bV = a1.bank()
                for h in range(H):
                    T(lambda e, h=h: e.transpose(out=P16(bV)[:, h * 128:(h + 1) * 128], in_=R1[:, 16 + h, cs], identity=I16),
                      [R1_b[16 + h], c16_b], [ps_b[bV]])
                bG = a1.bank()
                T(lambda e: e.transpose(out=ps[:, bG, 0:8], in_=gT[:, cs], identity=c32[0:8, 0, 0:8]), [gT_b, c32_b], [ps_b[bG]])
                T(lambda e: e.transpose(out=ps[:, bG, 8:16], in_=bT[:, cs], identity=c32[0:8, 0, 0:8]), [bT_b, c32_b], [ps_b[bG]])
                V(lambda e: e.tensor_copy(out=gbt[:], in_=ps[:, bG, 0:16]), [ps_b[bG]], [gbt_b])
                yield 1
                bS = a1.bank()
                cLast = 6 if samp else 1
                for i3, ci in enumerate((cU, cSL, cLast)):
                    T(lambda e, i3=i3, ci=ci: e.matmul(ps[:, bS, i3 * 8:(i3 + 1) * 8], c32[:, ci, :], gbt[:, 0:8],
                                                       start=True, stop=True), [gbt_b, c32_b], [ps_b[bS]])
                A(lambda e: e.activation(out=es_[:], in_=ps[:, bS, 0:24], func=AF.Exp), [ps_b[bS]], [es_b_])
                V(lambda e: e.tensor_tensor(out=kbs[:], in0=gbt[:, 8:16], in1=es_[:, 0:8], op=ALU.mult), [gbt_b, es_b_], [kbs_b])
                yield 1
                KP = P16(bK).rearrange("p (a b) -> p a b", a=8)
                VP = P16(bV).rearrange("p (a b) -> p a b", a=8)
                V(lambda e: e.tensor_tensor(out=Kd_, in0=KP, in1=bc8(es_[:, 8:16]), op=ALU.mult), [ps_b[bK], es_b_], [Kd_b_])
                V(lambda e: e.tensor_tensor(out=Vb_, in0=VP, in1=bc8(gbt[:, 8:16]), op=ALU.mult), [ps_b[bV], gbt_b], [Vb_b_])
                V(lambda e: e.tensor_scalar(out=neg_[:], in0=kbs[:], scalar1=-1.0, scalar2=None, op0=ALU.mult), [kbs_b], [neg_b_])
                yield 1
                G(lambda e: e.tensor_tensor(out=Gm1[:], in0=bcm(c32[:, cSL, :]), in1=bc8(gbt[:, 0:8]), op=ALU.mult),
                  [c32_b, gbt_b], [Gm1_b])
                V(lambda e: e.tensor_tensor(out=Gm2[:], in0=bcm(c32[:, cU, :]), in1=bc8(gbt[:, 0:8]), op=ALU.mult),
                  [c32_b, gbt_b], [Gm2_b])
                yield 1
                pD = a1.pair()
                pT = a1.pair()
                for hh in range(2):
                    g1 = Gm1[:, hh * 4:(hh + 1) * 4, :].rearrange("p a b -> p (a b)")
                    g2 = Gm2[:, hh * 4:(hh + 1) * 4, :].rearrange("p a b -> p (a b)")

                    def fD(e, hh=hh, g1=g1):
                        e.matmul(ps[:, pD + hh, :], c32[:, cU, :], g1, start=True, stop=False)
                        ins = None
                        for q in range(4):
                            ins = e.matmul(ps[:, pD + hh, q * 128:(q + 1) * 128], I16, c16[:, cN1, :], start=False, stop=(q == 3))
                        return ins
                    T(fD, [Gm1_b, c32_b, c16_b], [ps_b[pD + hh]])

                    def fT(e, hh=hh, g2=g2):
                        e.matmul(ps[:, pT + hh, :], c32[:, cSL, :], g2, start=True, stop=False)
                        ins = None
                        for q in range(4):
                            ins = e.matmul(ps[:, pT + hh, q * 128:(q + 1) * 128], I16, c16[:, cN2, :], start=False, stop=(q == 3))
                        return ins
                    T(fT, [Gm2_b, c32_b, c16_b], [ps_b[pT + hh]])
                yield 1
                A(lambda e: e.activation(out=dsm[:].rearrange("p a b -> p (a b)"), in_=P32(pD, 2), func=AF.Exp),
                  [ps_b[pD], ps_b[pD + 1]], [dsm_b])
                pR = a1.pair()
                for hh in range(2):
                    g2 = Gm2[:, hh * 4:(hh + 1) * 4, :].rearrange("p a b -> p (a b)")
                    T(lambda e, hh=hh, g2=g2: e.matmul(ps[:, pR + hh, :], ONES32, g2, start=True, stop=True),
                      [Gm2_b, c32_b], [ps_b[pR + hh]])
                A(lambda e: e.activation(out=dTm[:].rearrange("p a b -> p (a b)"), in_=P32(pT, 2), func=AF.Exp),
                  [ps_b[pT], ps_b[pT + 1]], [dTm_b])
                A(lambda e: e.activation(out=reg[:].rearrange("p a b -> p (a b)"), in_=P32(pR, 2), func=AF.Exp),
                  [ps_b[pR], ps_b[pR + 1]], [reg_b])
                yield 1
                V(lambda e: e.tensor_tensor(out=qt_, in0=R1[:, 0:8, cs], in1=reg[:], op=ALU.mult), qb + [reg_b], [qt_b_])
                pG = a1.pair()
                pP = a1.pair()
                for h in range(H):
                    T(lambda e, h=h: e.matmul(PV(pG)[:, h, :], R1[:, 8 + h, cs], R1[:, 8 + h, cs], start=True, stop=True),
                      [R1_b[8 + h]], [ps_b[pG + h // 4]])
                for h in range(H):
                    T(lambda e, h=h: e.matmul(PV(pP)[:, h, :], R1[:, 8 + h, cs], R1[:, h, cs], start=True, stop=True),
                      [R1_b[8 + h], R1_b[h]], [ps_b[pP + h // 4]])
                yield 1
                for h in range(H):
                    V(lambda e, h=h: e.scalar_tensor_tensor(out=Am[0][:, h, :], in0=PV(pG)[:, h, :], scalar=gbt[:, 8 + h:9 + h],
                                                            in1=dsm[:, h, :], op0=ALU.mult, op1=ALU.mult),
                      [ps_b[pG + h // 4], gbt_b, dsm_b], [Am_b[0]])
                V(lambda e: e.tensor_tensor(out=PTm_, in0=PV(pP), in1=dTm[:], op=ALU.mult),
                  [ps_b[pP], ps_b[pP + 1], dTm_b], [PTm_b_])
                yield 1
                bB = a1.bank()
                for h in range(H):
                    T(lambda e, h=h: e.transpose(out=P16(bB)[:, h * 128:(h + 1) * 128], in_=Am[0][:, h, :], identity=I16),
                      [Am_b[0], c16_b], [ps_b[bB]])
                BP = P16(bB).rearrange("p (a b) -> p a b", a=8)
                if samp:
                    A(lambda e: e.activation(out=Bm[0], in_=BP, func=AF.Copy), [ps_b[bB]], [Bm_b[0]])
                else:
                    A(lambda e: e.activation(out=Bm[1], in_=BP, func=AF.Copy), [ps_b[bB]], [Bm_b[1]])
                    yield 1
                    V(lambda e: e.tensor_tensor(out=Bm[0], in0=Bm[1], in1=bcm(c16[:, 8, :]), op=ALU.mult), [Bm_b[1], c16_b], [Bm_b[0]])
                    V(lambda e: e.tensor_tensor(out=Am[0], in0=Am[0], in1=bcm(c16[:, 8, :]), op=ALU.mult), [Am_b[0], c16_b], [Am_b[0]])
                    G(lambda e: e.tensor_tensor(out=Bo_, in0=Bm[1], in1=bcm(c16[:, 9, :]), op=ALU.mult), [Bm_b[1], c16_b], [Bo_b_])
                yield 1
                V(lambda e: e.tensor_tensor(out=TT, in0=bcm(I32), in1=Bm[0], op=ALU.subtract), [Bm_b[0], c32_b], [TT_b])
                V(lambda e: e.tensor_tensor(out=Pf, in0=bcm(I32), in1=Bm[0], op=ALU.subtract), [Bm_b[0], c32_b], [rstd_b])
                ca = 0
                for lev in range(1, nlev):
                    na = 1 - ca
                    pA = a1.pair()
                    for h in range(H):
                        T(lambda e, h=h: e.matmul(PV(pA)[:, h, :], Bm[ca][:, h, :], Am[ca][:, h, :], start=True, stop=True),
                          [Am_b[ca], Bm_b[ca]], [ps_b[pA + h // 4]])
                    A(lambda e: e.activation(out=Am[na], in_=PV(pA), func=AF.Copy), [ps_b[pA], ps_b[pA + 1]], [Am_b[na]])
                    yield 1
                    if lev < nlev - 1:
                        pB = a1.pair()
                        for h in range(H):
                            T(lambda e, h=h: e.matmul(PV(pB)[:, h, :], Am[ca][:, h, :], Bm[ca][:, h, :], start=True, stop=True),
                              [Am_b[ca], Bm_b[ca]], [ps_b[pB + h // 4]])
                        A(lambda e: e.activation(out=Bm[na], in_=PV(pB), func=AF.Copy), [ps_b[pB], ps_b[pB + 1]], [Bm_b[na]])
                        yield 1
                    pU = a1.pair()
                    for h in range(H):
                        T(lambda e, h=h: e.matmul(PV(pU)[:, h, :], Am[na][:, h, :], TT[:, h, :], start=True, stop=True),
                          [Am_b[na], TT_b], [ps_b[pU + h // 4]])
                    V(lambda e: e.tensor_tensor(out=TT, in0=PV(pU), in1=Pf, op=ALU.add), [ps_b[pU], ps_b[pU + 1], rstd_b], [TT_b])
                    if lev < nlev - 1:
                        V(lambda e: e.tensor_tensor(out=Pf, in0=PV(pU), in1=Pf, op=ALU.add), [ps_b[pU], ps_b[pU + 1], rstd_b], [rstd_b])
                    yield 1
                    ca = na
                yield "S2"
                r32 = on32[:]
                R32b = [on32_b]
                if samp:
                    pK = a2.pair(); pN = a2.pair(); pO = a2.pair()
                    other = [b_ for b_ in range(8) if b_ not in (pK, pK + 1, pN, pN + 1, pO, pO + 1)]
                    V(lambda e: e.tensor_tensor(out=gsel[:], in0=gbt[:, 0:8].unsqueeze(1).to_broadcast([128, 16, 8]),
                                                in1=seq32[:].unsqueeze(2).to_broadcast([128, 16, 8]), op=ALU.mult),
                      [gbt_b, cb32_b], [gsel_b])
                    bE = other[0]
                    T(lambda e: e.matmul(ps[:, bE, 0:128], ONES32, gsel[:].rearrange("p a b -> p (a b)"), start=True, stop=True),
                      [gsel_b, c32_b], [ps_b[bE]])
                    A(lambda e: e.activation(out=egl[:].rearrange("p a b -> p (a b)"), in_=ps[:, bE, 0:128], func=AF.Exp),
                      [ps_b[bE]], [egl_b])
                    for h in range(H):
                        if h % 2 == 0:
                            s0f, s0f_b, s0b, s0b_b = s0f_0, s0f_b0, s0b_0, s0b_b0
                        else:
                            s0f = t32f[:, 0:2048].rearrange("p (a b) -> p a b", a=16); s0f_b = t32_b[0:4]
                            s0b = t16[:, 4096:6144].rearrange("p (a b) -> p a b", a=16); s0b_b = t32_b[4:6]
                        dma("sp", f"s0in{h % 2}", s0f, sd[l, :, h, :, :].rearrange("s k v -> k s v"), [], s0f_b)
                        A(lambda e: e.activation(out=s0b, in_=s0f, func=AF.Copy), s0f_b, s0b_b)
                        V(lambda e, h=h: e.tensor_tensor(out=msk16[:], in0=R1[:, 8 + h, cs].unsqueeze(1).to_broadcast([128, 16, 128]),
                                                         in1=mskc[:], op=ALU.mult), [R1_b[8 + h], cb16_b], [msk16_b])

                        def fks(e, h=h):
                            ins = None
                            for s_ in range(16):
                                ins = e.matmul(PV(pK)[:, h, :], msk16[:, s_, :], s0b[:, s_, :], start=(s_ == 0), stop=(s_ == 15))
                            return ins
                        T(fks, [msk16_b] + s0b_b, [ps_b[pK + h // 4]])
                        V(lambda e, h=h: e.scalar_tensor_tensor(out=r32[:, h, :], in0=PV(pK)[:, h, :], scalar=neg_[:, h:h + 1],
                                                                in1=Vb_[:, h, :], op0=ALU.mult, op1=ALU.add),
                          [ps_b[pK + h // 4], neg_b_, Vb_b_], R32b)
                        V(lambda e, h=h: e.tensor_copy(out=r16[:, h, :], in_=r32[:, h, :]), R32b, [r16_b])
                        T(lambda e, h=h: e.matmul(PV(pN)[:, h, :], TT[:, h, :], r16[:, h, :], start=True, stop=True),
                          [TT_b, r16_b], [ps_b[pN + h // 4]])
                        A(lambda e, h=h: e.activation(out=vn[:, h, :], in_=PV(pN)[:, h, :], func=AF.Copy), [ps_b[pN + h // 4]], [vn_b])
                        V(lambda e, h=h: e.tensor_tensor(out=msk16[:], in0=qt_[:, h, :].unsqueeze(1).to_broadcast([128, 16, 128]),
                                                         in1=mskc[:], op=ALU.mult), [qt_b_, cb16_b], [msk16_b])

                        def fo(e, h=h):
                            for s_ in range(16):
                                e.matmul(PV(pO)[:, h, :], s0b[:, s_, :], msk16[:, s_, :], start=(s_ == 0), stop=False)
                            return e.matmul(PV(pO)[:, h, :], vn[:, h, :], PTm_[:, h, :], start=False, stop=True)
                        T(fo, s0b_b + [msk16_b, vn_b, PTm_b_], [ps_b[pO + h // 4]])
                        V(lambda e, h=h: e.tensor_tensor(out=msk16[:], in0=Kd_[:, h, :].unsqueeze(1).to_broadcast([128, 16, 128]),
                                                         in1=seqs[:].unsqueeze(2).to_broadcast([128, 16, 128]), op=ALU.mult),
                          [Kd_b_, cb16_b], [msk16_b])
                        for q in range(4):
                            bq = other[1 + (h * 4 + q) % (len(other) - 1)]

                            def fs(e, h=h, q=q, bq=bq):
                                ins = None
                                for s4 in range(4):
                                    ins = e.matmul(ps[:, bq, s4 * 128:(s4 + 1) * 128], msk16[:, q * 4 + s4, :], vn[:, h, :],
                                                   start=True, stop=True)
                                return ins
                            T(fs, [msk16_b, vn_b], [ps_b[bq]])
                            for s4 in range(4):
                                s_ = q * 4 + s4
                                V(lambda e, h=h, s_=s_, s4=s4, bq=bq, q=q: e.scalar_tensor_tensor(
                                    out=snew[q % 2][:, s4, :], in0=s0f[:, s_, :], scalar=egl[:, s_, h:h + 1],
                                    in1=ps[:, bq, s4 * 128:(s4 + 1) * 128], op0=ALU.mult, op1=ALU.add),
                                  s0f_b + [egl_b, ps_b[bq]], [snew_b[q % 2]])
                            dma("sp", f"s0out{q % 2}", sd_s[l, q * 4:(q + 1) * 4, h, :, :].rearrange("s k v -> k s v"), snew[q % 2],
                                [snew_b[q % 2]], [])
                    pQ = a2.pair()
                else:
                    pK, pN, pO, pS, pQ = 4, 6, 4, 6, 6
                    pKv = PV(pK); pNv = PV(pN)
                    for h in range(H):
                        T(lambda e, h=h: e.matmul(PV(pK)[:, h, :], R1[:, 8 + h, cs], S16[:, l, h, :], start=True, stop=True),
                          [R1_b[8 + h], S16_b[l]], [ps_b[pK + h // 4]])
                    V(lambda e: e.tensor_tensor(out=r32, in0=pKv, in1=bc8(neg_[:]), op=ALU.mult), [ps_b[pK], ps_b[pK + 1], neg_b_], R32b)
                    yield 1
                    V(lambda e: e.tensor_tensor(out=r32, in0=r32, in1=Vb_, op=ALU.add), R32b + [Vb_b_], R32b)
                    A(lambda e: e.activation(out=r16[:], in_=r32, func=AF.Copy), R32b, [r16_b])
                    yield 1
                    for h in range(H):
                        T(lambda e, h=h: e.matmul(PV(pN)[:, h, :], TT[:, h, :], r16[:, h, :], start=True, stop=True),
                          [TT_b, r16_b], [ps_b[pN + h // 4]])
                    A(lambda e: e.activation(out=vn[:], in_=pNv, func=AF.Copy), [ps_b[pN], ps_b[pN + 1]], [vn_b])
                    yield 1
                    for h in range(H):
                        T(lambda e, h=h: e.matmul(PV(pK)[:, h, :], Bo_[:, h, :], vn[:, h, :], start=True, stop=True),
                          [Bo_b_, vn_b], [ps_b[pK + h // 4]])
                    V(lambda e: e.tensor_tensor(out=r16[:], in0=r32, in1=pKv, op=ALU.subtract), R32b + [ps_b[pK], ps_b[pK + 1]], [r16_b])
                    yield 1
                    for h in range(H):
                        T(lambda e, h=h: e.matmul(PV(pN)[:, h, :], TT[:, h, :], r16[:, h, :], start=True, stop=True),
                          [TT_b, r16_b], [ps_b[pN + h // 4]])
                    A(lambda e: e.activation(out=vn[:], in_=pNv, func=AF.Copy), [ps_b[pN], ps_b[pN + 1]], [vn_b])
                    yield 1
                    for h in range(H):
                        def fo(e, h=h):
                            e.matmul(PV(pO)[:, h, :], S16[:, l, h, :], qt_[:, h, :], start=True, stop=False)
                            return e.matmul(PV(pO)[:, h, :], vn[:, h, :], PTm_[:, h, :], start=False, stop=True)
                        T(fo, [S16_b[l], qt_b_, vn_b, PTm_b_], [ps_b[pO + h // 4]])
                    yield 1
                    for h in range(H):
                        T(lambda e, h=h: e.matmul(PV(pS)[:, h, :], Kd_[:, h, :], vn[:, h, :], start=True, stop=True),
                          [Kd_b_, vn_b], [ps_b[pS + h // 4]])
                    A(lambda e: e.activation(out=sq[0][:], in_=P32(pO, 2), func=AF.Square, scale=float(128 ** -0.5)),
                      [ps_b[pO], ps_b[pO + 1]], [sq_b[0]])
                    yield 1
                    for h in range(H):
                        V(lambda e, h=h: e.scalar_tensor_tensor(out=S32[:, l, h, :], in0=S32[:, l, h, :], scalar=es_[:, 16 + h:17 + h],
                                                                in1=PV(pS)[:, h, :], op0=ALU.mult, op1=ALU.add),
                          [S32_b[l], es_b_, ps_b[pS + h // 4]], [S32_b[l]])
                    A(lambda e: e.activation(out=S16[:, l, :, :], in_=S32[:, l, :, :], func=AF.Copy), [S32_b[l]], [S16_b[l]])
                    yield 1
                if samp:
                    A(lambda e: e.activation(out=sq[0][:], in_=P32(pO, 2), func=AF.Square, scale=float(128 ** -0.5)),
                      [ps_b[pO], ps_b[pO + 1]], [sq_b[0]])
                for hh in range(2):
                    T(lambda e, hh=hh: e.matmul(ps[:, pQ + hh, :], ONES16, sq[0][:, hh * 512:(hh + 1) * 512], start=True, stop=True),
                      [sq_b[0], c16_b], [ps_b[pQ + hh]])
                rsqrt_from_psum(rstd9, P32(pQ, 2), 1.0 / 128, 1e-6, [ps_b[pQ], ps_b[pQ + 1]], t32_b[6:8])
                yield 1
                V(lambda e: e.scalar_tensor_tensor(out=on32[:].rearrange("p a b -> p (a b)"), in0=P32(pO, 2), scalar=dnw[:, l:l + 1],
                                                   in1=rstd9, op0=ALU.mult, op1=ALU.mult),
                  [ps_b[pO], ps_b[pO + 1], dnw_b] + t32_b[6:8], [on32_b])
                G(lambda e: e.tensor_tensor(out=oa[:, :, cs], in0=on32[:], in1=sz[:, :, cs], op=ALU.mult),
                  [on32_b] + sz_b, oa_b)
                yield 1

            cur = None
            for g in [chunk_gen(t) for t in range(NT)] + [None]:
                g_done = g is None
                c_done = cur is None
                while not (g_done and c_done):
                    if not c_done:
                        try:
                            next(cur)
                        except StopIteration:
                            c_done = True
                    if not g_done:
                        if next(g) == "S2":
                            g_done = True
                cur = g

            if (not samp) and last_blk:
                dma("sp", "o_sd", sd_p[l].rearrange("h k v -> k h v"), S32[:, l, :, :], [S32_b[l]], [])

            if stop <= 4:
                return
            def uv_consume(ch, bk):
                A(lambda e: e.activation(out=R1[:, ch, 0:W], in_=ps[:, bk, 0:W], func=AF.Gelu), [ps_b[bk]], [R1_b[ch]])
            proj_chunks("w_in", l, 4112, 16, W, hT, hT_b, uv_consume)
            boff = (l * 2 + (1 if samp else 0)) * 1024
            dma("sp", "lnw", t32f[:, 0:2048], lnw[l].rearrange("b d -> (b d)").partition_broadcast(128), [], t32_b[0:4])
            dma("pool", "prw", prw16[:], prow[:, boff:boff + 1024], [], [prw_b])
            lnt16 = t16[:, 4096:6144].rearrange("p (a b) -> p a b", a=2)
            A(lambda e: e.activation(out=t16[:, 4096:6144], in_=t32f[:, 0:2048], func=AF.Copy), t32_b[0:4], t32_b[4:6])
            dma("sp", "wsp", ws32[:], wsp[l, 1 if samp else 0], [], [ws32_b])
            V(lambda e: e.tensor_tensor(out=wsT[:], in0=ws32[:].rearrange("p (a b) -> p a b", a=8), in1=bcm(c32[:, cU, :]),
                                        op=ALU.mult), [ws32_b, c32_b], [wsT_b])
            for t in range(NT):
                cs = slice(t * 128, (t + 1) * 128)
                bk = bank()
                for g in range(8):
                    T(lambda e, g=g: e.transpose(out=P16(bk)[:, g * 128:(g + 1) * 128], in_=R1[:, 8 + g, cs], identity=I16),
                      [vT_b[g], c16_b], [ps_b[bk]])
                for hh in range(2):
                    V(lambda e, hh=hh: e.bn_stats(out=bnst[:, hh, :], in_=P16(bk)[:, hh * 512:(hh + 1) * 512]), [ps_b[bk]], [bn_b])
                V(lambda e: e.bn_aggr(out=mv[:, 0:2], in_=bnst[:].rearrange("p a b -> p (a b)")), [bn_b], [bn_b])
                A(lambda e: e.activation(out=mv[:, 2:3], in_=mv[:, 1:2], func=AF.Ln, bias=1e-5), [bn_b], [bn_b])
                A(lambda e: e.activation(out=mv[:, 2:3], in_=mv[:, 2:3], func=AF.Exp, scale=-0.5), [bn_b], [bn_b])
                V(lambda e: e.tensor_scalar(out=vnb[:], in0=P16(bk), scalar1=mv[:, 0:1], scalar2=mv[:, 2:3],
                                            op0=ALU.subtract, op1=ALU.mult), [ps_b[bk], bn_b], [vnb_b])
                V(lambda e: e.tensor_tensor(out=vnb[:], in0=vnb[:], in1=lnt16[:, 0, :], op=ALU.mult), [vnb_b] + t32_b[4:6], [vnb_b])
                V(lambda e: e.tensor_tensor(out=vnb[:], in0=vnb[:], in1=lnt16[:, 1, :], op=ALU.add), [vnb_b] + t32_b[4:6], [vnb_b])
                if samp or (last_blk and t == NT - 1):
                    A(lambda e: e.activation(out=lnv32, in_=vnb[:], func=AF.Copy), [vnb_b], [lnv32_b])
                    dma("sp", "o_cv", cv_s[l] if samp else cv_p[l], lnv32, [lnv32_b], [])
                pM = pair()
                for g in range(8):
                    def fm(e, g=g):
                        e.matmul(PV(pM)[:, g, :], vnb[:, g * 128:(g + 1) * 128], wsT[:, g, :], start=True, stop=False)
                        return e.matmul(PV(pM)[:, g, :], c16[0:1, 1, :], prw16[0:1, g * 128:(g + 1) * 128],
                                        start=False, stop=True)
                    T(fm, [vnb_b, wsT_b, c16_b, prw_b], [ps_b[pM + g // 4]])
                V(lambda e: e.tensor_tensor(out=R1[:, 0:8, cs], in0=R1[:, 0:8, cs], in1=PV(pM), op=ALU.mult),
                  uT_b + [ps_b[pM], ps_b[pM + 1]], uT_b)

            if stop <= 5:
                return
            mg, mg_b = sz, sz_b
            for half in range(2):
                sA = load_w(wview("w_in", l, 0, 1024, 6160 + half * 512, 512), 8, 512)
                sPA = load_w(wview("w_pa", l, 0, 1024, half * 512, 512), 8, 512)
                for jj in range(4):
                    d = half * 4 + jj
                    b1 = bank()
                    mm_group(b1, 0, W, lambda k, jj=jj: wsl[sA][:, k, jj * 128:(jj + 1) * 128], lambda k: hT[:, k, 0:W], 8,
                             hT_b, [wsl_b[sA]])
                    b2 = bank()
                    mm_group(b2, 0, W, lambda k, jj=jj: wsl[sPA][:, k, jj * 128:(jj + 1) * 128], lambda k: oa[:, k, 0:W], 8,
                             oa_b, [wsl_b[sPA]])
                    A(lambda e, b1=b1: e.activation(out=acc[0][:, 0:W], in_=ps[:, b1, 0:W], func=AF.Sigmoid), [ps_b[b1]], [acc_b[0]])
                    V(lambda e, d=d, b2=b2: e.tensor_tensor(out=t32[:, d, 0:W], in0=acc[0][:, 0:W], in1=ps[:, b2, 0:W], op=ALU.mult),
                      [acc_b[0], ps_b[b2]], [t32_b[d]])
                w_done(2)
            for half in range(2):
                sB = load_w(wview("w_in", l, 0, 1024, 7184 + half * 512, 512), 8, 512)
                sPB = load_w(wview("w_pb", l, 0, 1024, half * 512, 512), 8, 512)
                for jj in range(4):
                    d = half * 4 + jj
                    b1 = bank()
                    mm_group(b1, 0, W, lambda k, jj=jj: wsl[sB][:, k, jj * 128:(jj + 1) * 128], lambda k: hT[:, k, 0:W], 8,
                             hT_b, [wsl_b[sB]])
                    b2 = bank()
                    mm_group(b2, 0, W, lambda k, jj=jj: wsl[sPB][:, k, jj * 128:(jj + 1) * 128], lambda k: R1[:, k, 0:W], 8,
                             uT_b, [wsl_b[sPB]])
                    A(lambda e, b1=b1: e.activation(out=acc[1][:, 0:W], in_=ps[:, b1, 0:W], func=AF.Sigmoid), [ps_b[b1]], [acc_b[1]])
                    V(lambda e, b2=b2: e.tensor_tensor(out=acc[1][:, 0:W], in0=acc[1][:, 0:W], in1=ps[:, b2, 0:W], op=ALU.mult),
                      [acc_b[1], ps_b[b2]], [acc_b[1]])
                    (V if d % 2 else G)(
                        lambda e, d=d: e.tensor_tensor(out=mg[:, d, 0:W], in0=acc[1][:, 0:W], in1=t32[:, d, 0:W], op=ALU.add),
                        [acc_b[1], t32_b[d]], [mg_b[d]])
                w_done(2)
            proj_chunks("w_o", l, 0, 8, W, mg, mg_b,
                        lambda ch, bk: A(lambda e: e.activation(out=t32[:, ch, 0:W], in_=ps[:, bk, 0:W], func=AF.Copy),
                                         [ps_b[bk]], [t32_b[ch]]))
            rmsnorm(t32, t32_b, W, NW(l, 1), "x", l)
            if stop <= 6:
                return
            rmsnorm(xT, xT_b, W, NW(l, 2), "h", l)
            hid, hid_b = R1, R1_b
            proj_chunks("w_f1", l, 0, 22, W, hT, hT_b,
                        lambda ch, bk: A(lambda e: e.activation(out=hid[:, ch, 0:W], in_=ps[:, bk, 0:W], func=AF.Silu),
                                         [ps_b[bk]], [hid_b[ch]]))
            proj_chunks("w_f1", l, DFF, 22, W, hT, hT_b,
                        lambda ch, bk: V(lambda e: e.tensor_tensor(out=hid[:, ch, 0:W], in0=hid[:, ch, 0:W], in1=ps[:, bk, 0:W],
                                                                   op=ALU.mult), [hid_b[ch], ps_b[bk]], [hid_b[ch]]))
            for half in range(2):
                bks = [bank() for _ in range(4)]
                for kg, (k0, nk) in enumerate(((0, 8), (8, 8), (16, 6))):
                    s = load_w(wview("w_f2", l, k0 * 128, nk * 128, half * 512, 512), nk, 512)
                    for jj in range(4):
                        def ff(e, s=s, jj=jj, k0=k0, nk=nk, kg=kg):
                            ins = None
                            for k in range(nk):
                                ins = e.matmul(ps[:, bks[jj], 0:W], wsl[s][:, k, jj * 128:(jj + 1) * 128], hid[:, k0 + k, 0:W],
                                               start=(kg == 0 and k == 0), stop=(kg == 2 and k == nk - 1))
                            return ins
                        T(ff, hid_b[k0:k0 + nk] + [wsl_b[s]], [ps_b[bks[jj]]])
                    w_done()
                for jj in range(4):
                    d = half * 4 + jj
                    A(lambda e, d=d, jj=jj: e.activation(out=t32[:, d, 0:W], in_=ps[:, bks[jj], 0:W], func=AF.Copy),
                      [ps_b[bks[jj]]], [t32_b[d]])
            rmsnorm(t32, t32_b, W, NW(l, 3), "x", l)

        def load_x(src, W):
            for t in range(W // 128):
                dma("sp", "xin", xin, src[t * 128:(t + 1) * 128, :], [], t32_b[0:2])
                for hh in range(2):
                    bk = bank()
                    for c4 in range(4):
                        c = hh * 4 + c4
                        T(lambda e, c=c, c4=c4, bk=bk: e.transpose(out=ps[:, bk, c4 * 128:(c4 + 1) * 128], in_=xin[:, c * 128:(c + 1) * 128],
                                                                  identity=I32), t32_b[0:2] + [c32_b], [ps_b[bk]])
                    A(lambda e, hh=hh, bk=bk, t=t: e.activation(out=xT[:, hh * 4:(hh + 1) * 4, t * 128:(t + 1) * 128],
                                                               in_=ps[:, bk, :].rearrange("p (a b) -> p a b", a=4), func=AF.Copy),
                      [ps_b[bk]], xT_b[hh * 4:(hh + 1) * 4])

        def store_y(dst, W):
            for t in range(W // 128):
                yo = t % 2
                for hh in range(2):
                    bk = bank()
                    for c4 in range(4):
                        c = hh * 4 + c4
                        T(lambda e, c=c, c4=c4, bk=bk, t=t: e.transpose(out=ps[:, bk, c4 * 128:(c4 + 1) * 128],
                                                                       in_=xT[:, c, t * 128:(t + 1) * 128], identity=I32),
                          [xT_b[c], c32_b], [ps_b[bk]])
                    A(lambda e, hh=hh, bk=bk, yo=yo: e.activation(out=yout[yo][:, hh * 512:(hh + 1) * 512], in_=ps[:, bk, :],
                                                                  func=AF.Copy), [ps_b[bk]], t32_b[2 + 2 * yo:4 + 2 * yo])
                dma("sp", f"yo{yo}", dst[t * 128:(t + 1) * 128, :], yout[yo], t32_b[2 + 2 * yo:4 + 2 * yo], [])

        G(lambda e: e.memset(S32[:].rearrange("p a b c -> p (a b c)"), 0.0), [], S32_b)
        G(lambda e: e.memset(S16[:].rearrange("p a b c -> p (a b c)"), 0.0), [], S16_b)
        G(lambda e: e.memset(carry[:].rearrange("p a b c -> p (a b c)"), 0.0), [], carry_b[0] + carry_b[1])

        for blk in range(nblk):
            load_x(xp[blk * 512:(blk + 1) * 512, :], 512)
            for l in range(DEPTH):
                E.epoch += 1
                layer(l, 512, False, blk == nblk - 1)
            store_y(y_p[blk * 512:(blk + 1) * 512, :], 512)
        E.epoch += 1
        if do_samp:
            if all_reqs is None:
                SAMP_K0[0] = slot_i[0]
            samp_k0[0] = slot_i[0]
            for hf in range(2):
                dma("pool", "cst16", mskc[:, hf * 8:(hf + 1) * 8, :].rearrange("p a b -> p (a b)"), cbd[:, hf * 1024:(hf + 1) * 1024],
                    [], [cb16_b, wsl_b[3], wsl_b[4], msk16_b, scin_b, snew_b[0]])
            mask_loaded[0] = True
            if all_reqs is not None:
                pump()
            load_x(xs, 128)
            for l in range(DEPTH):
                E.epoch += 1
                layer(l, 128, True, False)
            store_y(y_s, 128)

        E.finalize()
        if os.environ.get("KDBG"):
            print("est_span_us", getattr(E, "est_span", None), "n_ops", len(E.ops), flush=True)
        sems = {}
        for e_ in ENGS:
            for ep in range(E.epoch + 1):
                sems[(e_, ep)] = es.enter_context(nc.semaphore(f"s_{e_}_{ep}"))
        chansems = {ch: es.enter_context(nc.semaphore(f"c_{ch}")) for ch in E.chans}
        block = es.enter_context(nc.Block())
        E.emit(block, sems, chansems)
    return nc


def _pack(inp):
    f = lambda a: np.ascontiguousarray(np.asarray(a, dtype=np.float32))
    pp = np.zeros((128, 262), np.float32)
    names = ["norm_pre_mix", "norm_post_mix", "norm_pre_ffn", "norm_post_ffn"]
    for l in range(DEPTH):
        for n, nm in enumerate(names):
            pp[:, (l * 4 + n) * 8:(l * 4 + n) * 8 + 8] = f(inp[nm])[l].reshape(8, 128).T
        cw = f(inp["conv_w"])[l]
        pp[:, 64 + l * 96:64 + (l + 1) * 96] = cw.reshape(4, 24, 128).transpose(2, 0, 1).reshape(128, 96)
        pp[:, 256 + l] = f(inp["delta_norm_w"])[l]
        pp[0:8, 258 + l] = f(inp["a_log"])[l]
        pp[0:8, 260 + l] = f(inp["dt_bias"])[l]
    bs = f(inp["b_spatial"])
    prow = np.zeros((DEPTH, 2, 8, 128), np.float32)
    prow[:, 0] = bs
    prow[:, 1] = np.tile(bs[:, :, :8], (1, 1, 16))
    prow = prow.reshape(1, -1)
    lnw = np.stack([f(inp["sgu_ln_w"]), f(inp["sgu_ln_b"])], axis=1)
    ws = f(inp["w_spatial"])
    wsT = ws.transpose(0, 3, 1, 2)
    wss = np.tile(ws[:, :, :8, :8], (1, 1, 16, 16)).transpose(0, 3, 1, 2)
    wsp = np.ascontiguousarray(np.stack([wsT, wss], axis=1).reshape(DEPTH, 2, 128, 1024))
    shared = dict(
        w_in=f(inp["w_in"]), w_pa=f(inp["w_proj_a"]), w_pb=f(inp["w_proj_b"]), w_o=f(inp["w_out"]),
        w_f1=f(inp["w_ffn_in"]), w_f2=f(inp["w_ffn_out"]), pp=pp, prow=prow, lnw=np.ascontiguousarray(lnw), wsp=wsp,
        cst=_consts()[0], cst16=_consts()[1], cbd=_cb(),
    )
    xpr = f(inp["x_prompt"]); xsm = f(inp["x_sample"]); sdl = f(inp["state_delta"]); scv = f(inp["state_conv"])
    maps = []
    for c in range(NCORES):
        m = dict(shared)
        m["xp"] = xpr[c]
        m["xs"] = np.ascontiguousarray(xsm[16 * c:16 * (c + 1)].reshape(128, D))
        m["sd"] = np.ascontiguousarray(sdl[:, 16 * c:16 * (c + 1)])
        m["sc"] = np.ascontiguousarray(scv[:, 16 * c:16 * (c + 1)].reshape(DEPTH, 48, 3072))
        maps.append(m)
    return maps


def kernel(**inputs):
    maps = _pack(inputs)
    nc = build()
    res = run_bass_kernel_spmd(nc, maps, core_ids=list(range(NCORES)))
    r = res.results
    y_p = np.stack([r[c]["y_p"] for c in range(NCORES)], 0).reshape(8, 2048, D)
    y_s = np.concatenate([r[c]["y_s"].reshape(16, 8, D) for c in range(NCORES)], 0)
    sd_p = np.stack([r[c]["sd_p"] for c in range(NCORES)], 1)
    sc_p = np.stack([r[c]["sc_p"] for c in range(NCORES)], 1)
    cv_p = np.stack([r[c]["cv_p"] for c in range(NCORES)], 1)
    sd_s = np.concatenate([r[c]["sd_s"] for c in range(NCORES)], 1)
    sc_s = np.concatenate([r[c]["sc_s"] for c in range(NCORES)], 1)
    cv_s = np.concatenate([r[c]["cv_s"].reshape(DEPTH, 16, 8, D) for c in range(NCORES)], 1)
    return tuple(np.ascontiguousarray(a.astype(np.float32)) for a in (y_p, y_s, sd_p, sc_p, cv_p, sd_s, sc_s, cv_s))
```

```python
import os
import numpy as np
from contextlib import ExitStack
import concourse.bass as bass
import concourse.mybir as mybir
from concourse.bass_utils import run_bass_kernel_spmd

F32 = mybir.dt.float32
BF16 = mybir.dt.bfloat16
AF = mybir.ActivationFunctionType
ALU = mybir.AluOpType

NCORES = 8
D = 1024
DEPTH = 2
H = 8
DIN = 8208
DFF = 2816
BIG = 30000.0
NSLOT = 3
ENGS = ["pe", "act", "dve", "pool", "sp"]
BNAME = {"pe": "tensor", "act": "scalar", "dve": "vector", "pool": "gpsimd", "sp": "sync"}
NEPOCH = 12
SCHED = True


class Buf:
    __slots__ = ("name", "lw", "rd")

    def __init__(self, name):
        self.name = name
        self.lw = None
        self.rd = []


def bufs(name, n):
    return [Buf(f"{name}{i}") for i in range(n)]


class Rec:
    def __init__(self):
        self.calls = []

    def __getattr__(self, name):
        def f(*a, **k):
            self.calls.append((name, a, k))
            return self
        return f


class Em:
    def __init__(self):
        self.ops = []
        self.chans = {}
        self.epoch = 0

    def op(self, eng, fn, reads=(), writes=(), chan=None, ndma=0):
        idx = len(self.ops)
        deps = set()
        soft = set()
        for b in reads:
            if b.lw is not None:
                deps.add(b.lw)
        for b in writes:
            if b.lw is not None:
                soft.add(b.lw)
            soft.update(b.rd)
        deps |= soft
        val = None
        if chan:
            c = self.chans.setdefault(chan, {"count": 0, "last": None, "eng": eng})
            assert c["eng"] == eng
            if c["last"] is not None:
                deps.add(c["last"])
            c["count"] += 16 * ndma
            c["last"] = idx
            val = c["count"]
        key = chan if chan else eng
        for b in writes:
            b.lw = idx
            b.rd = []
        for b in reads:
            b.rd.append(idx)
        rec = Rec()
        fn(rec)
        assert rec.calls
        self.ops.append(dict(eng=eng, calls=rec.calls, deps=deps, chan=chan, val=val, inc=False, ep=self.epoch))
        return idx

    @staticmethod
    def _est(o):
        eng = o["eng"]
        tot = 0.0
        for name, a, k in o["calls"]:
            out = k.get("out", a[0] if a else None)
            try:
                fs = float(out.free_size())
            except Exception:
                fs = 128.0
            if o["chan"]:
                try:
                    nb = float(out.nbytes())
                except Exception:
                    nb = 1e5
                tot += 2.0 + nb / 1.8e5
            elif eng == "pe":
                lhs = k.get("lhsT", a[1] if len(a) > 1 else None) if name == "matmul" else k.get("in_")
                f32 = getattr(lhs, "dtype", None) == F32
                tot += 0.045 + max(fs, 64.0) * (4.0 if (f32 and name == "matmul") else 1.0) / 2400.0 + (0.05 if fs <= 128 else 0.0)
            elif eng == "dve":
                tot += 0.12 + fs * 1.1e-3
            elif eng == "act":
                tot += 0.17 + fs * 0.9e-3
            else:
                tot += 0.25 + fs * 2.2e-3
        return tot

    def schedule(self):
        import heapq
        ops = self.ops
        n = len(ops)
        succ = [[] for _ in range(n)]
        indeg = [0] * n
        for i, o in enumerate(ops):
            for d in o["deps"]:
                succ[d].append(i)
                indeg[i] += 1
        dur = [self._est(o) for o in ops]
        bl = [0.0] * n
        for i in range(n - 1, -1, -1):
            m_ = 0.0
            for j in succ[i]:
                if bl[j] > m_:
                    m_ = bl[j]
            bl[i] = dur[i] + 0.3 + m_
        PRIO = os.environ.get("KPRIO", "bl")
        WIN = float(os.environ.get("KWIN", "0.1"))
        ready = [0.0] * n
        fin = [0.0] * n
        free = {e: 0.0 for e in ENGS}
        heaps = {e: [] for e in ENGS}
        for i in range(n):
            if indeg[i] == 0:
                heapq.heappush(heaps[ops[i]["eng"]], (0.0, i))
        order = []
        LAT = 0.15
        while len(order) < n:
            best = None
            for e in ENGS:
                hp = heaps[e]
                if not hp:
                    continue
                cand = hp[0]
                st = max(free[e], cand[0])
                key = (st, cand[1])
                if best is None or key < best[0]:
                    best = (key, e)
            (st, i), e = best
            hp = heaps[e]
            pool_ = []
            while hp and hp[0][0] <= st + WIN:
                pool_.append(heapq.heappop(hp))
            if PRIO == "bl":
                pool_.sort(key=lambda x: (-bl[x[1]], x[1]))
            else:
                pool_.sort(key=lambda x: x[1])
            _, i = pool_[0]
            for x in pool_[1:]:
                heapq.heappush(hp, x)
            o = ops[i]
            st = max(st, ready[i])
            fin[i] = st + dur[i]
            free[e] = st + (min(dur[i], 1.0) if o["chan"] else dur[i])
            order.append(i)
            for j in succ[i]:
                r = fin[i] + (LAT if ops[j]["eng"] != e or o["chan"] else 0.1)
                if r > ready[j]:
                    ready[j] = r
                indeg[j] -= 1
                if indeg[j] == 0:
                    heapq.heappush(heaps[ops[j]["eng"]], (ready[j], j))
        pos = {old: new for new, old in enumerate(order)}
        new_ops = []
        for old in order:
            o = ops[old]
            o["deps"] = {pos[d] for d in o["deps"]}
            new_ops.append(o)
        self.ops = new_ops
        self.est_span = max(fin) if fin else 0.0
        if os.environ.get("KDBG"):
            cp = [0.0] * n
            for old in order:
                o = ops[old]
            newdur = [dur[old] for old in order]
            for i_, o in enumerate(new_ops):
                st_ = 0.0
                for d in o["deps"]:
                    st_ = max(st_, cp[d] + LAT)
                cp[i_] = st_ + newdur[i_]
            busy = {}
            for i_, o in enumerate(new_ops):
                busy[o["eng"]] = busy.get(o["eng"], 0.0) + (min(newdur[i_], 1.0) if o["chan"] else newdur[i_])
            print("critical_path_us", max(cp), "busy", {k: round(v) for k, v in busy.items()}, flush=True)
            i_ = max(range(n), key=lambda q: cp[q])
            agg = {}
            seq = []
            while True:
                o = new_ops[i_]
                nm = o["calls"][0][0] + ("/dma" if o["chan"] else "")
                outap = o["calls"][0][2].get("out", o["calls"][0][1][0] if o["calls"][0][1] else None)
                tn = getattr(getattr(outap, "tensor", None), "name", "?")
                key = (o["eng"], nm, tn)
                a_ = agg.setdefault(key, [0, 0.0])
                a_[0] += 1
                a_[1] += newdur[i_] + LAT
                seq.append(key)
                prev = None
                for d in o["deps"]:
                    if prev is None or cp[d] > cp[prev]:
                        prev = d
                if prev is None:
                    break
                i_ = prev
            for k_, v_ in sorted(agg.items(), key=lambda kv: -kv[1][1])[:28]:
                print("   CP", k_, v_[0], round(v_[1], 1), flush=True)

    def prune(self):
        for o in self.ops:
            best = {}
            for d in o["deps"]:
                p = self.ops[d]
                key = ("c", p["chan"]) if p["chan"] else ("e", p["eng"])
                if key not in best or d > best[key]:
                    best[key] = d
            o["deps"] = set(best.values())

    def finalize(self):
        if SCHED:
            self.schedule()
        self.prune()
        for o in self.ops:
            for d in o["deps"]:
                p = self.ops[d]
                if p["chan"] is None:
                    if p["eng"] == "pe" and o["eng"] == "pe" and o["chan"] is None:
                        continue
                    p["inc"] = True
        cnt = {}
        for o in self.ops:
            if o["chan"] is None and o["inc"]:
                k = (o["eng"], o["ep"])
                cnt[k] = cnt.get(k, 0) + 1
                o["val"] = cnt[k]

    def emit(self, block, sems, chansems):
        for e in ENGS:
            ops_e = [o for o in self.ops if o["eng"] == e]

            def body(eng, ops_e=ops_e, e=e):
                seen = {}
                for o in ops_e:
                    for d in sorted(o["deps"]):
                        p = self.ops[d]
                        if p["chan"] is None:
                            if not p["inc"]:
                                continue
                            if p["eng"] == "pe" and e == "pe" and o["chan"] is None:
                                continue
                            key = (p["eng"], p["ep"])
                            sem = sems[key]
                        else:
                            key = p["chan"]
                            sem = chansems[key]
                        v = p["val"]
                        if seen.get(key, 0) >= v:
                            continue
                        seen[key] = v
                        eng.wait_ge(sem, v)
                    ins = None
                    for name, a, k in o["calls"]:
                        ins = getattr(eng, name)(*a, **k)
                        if o["chan"]:
                            ins.then_inc(chansems[o["chan"]], 16)
                    if (not o["chan"]) and o["inc"]:
                        ins.then_inc(sems[(e, o["ep"])], 1)
                for ch, c in self.chans.items():
                    if c["eng"] == e and c["count"] > 0:
                        eng.wait_ge(chansems[ch], c["count"])

            getattr(block, BNAME[e])(body)


def _consts():
    i = np.arange(128)
    m = i[:, None]
    p = i[None, :]
    blk = i // 8
    same = blk[:, None] == blk[None, :]
    c = np.zeros((13, 128, 128), np.float32)
    c[0] = np.eye(128)
    c[1] = 1.0
    c[2] = m <= p
    c[3] = m > p
    c[4] = -BIG * (p >= m)
    c[5] = -BIG * (p < m)
    c[6] = (m <= p) & same
    c[7] = (m > p) & same
    c[8] = -BIG * (~((p < m) & same))
    c[9] = -BIG * (~((p >= m) & same))
    c[10] = same
    s = np.zeros((128, 16, 16), np.float32)
    for j in range(16):
        s[:, j, j] = 1.0
    c[11] = s.reshape(128, 256)[:, :128]
    c[12] = s.reshape(128, 256)[:, 128:]
    c32 = np.ascontiguousarray(c[[0, 1, 2, 3, 6, 7, 10]])
    md = ((m // 64) == (p // 64)).astype(np.float32)
    mo = ((m < 64) & (p >= 64)).astype(np.float32)
    c16 = np.ascontiguousarray(np.concatenate([c[[0, 1, 4, 5, 8, 9, 11, 12]], md[None], mo[None]], 0))
    return c32, c16


def _cb():
    i = np.arange(128)
    maskc = (i[None, :] // 8 == np.arange(16)[:, None]).astype(np.float32)
    maskc = np.broadcast_to(maskc[None], (128, 16, 128)).reshape(128, 2048)
    seqsel = (i[:, None] // 8 == np.arange(16)[None, :]).astype(np.float32)
    return np.concatenate([maskc, seqsel], axis=1).astype(np.float32)


def build(nblk=4, do_samp=True, stop=99):
    reqs = []
    _build(nblk, do_samp, stop, reqs, None)
    return _build(nblk, do_samp, stop, [], reqs)


LOOKAHEAD = 2
SAMP_K0 = [None]


def _build(nblk, do_samp, stop, rec_reqs, all_reqs):
    nc = bass.Bass("TRN2", target_bir_lowering=False)

    def din(name, shape):
        return nc.dram_tensor(name, list(shape), F32, kind="ExternalInput").ap()

    def dout(name, shape):
        return nc.dram_tensor(name, list(shape), F32, kind="ExternalOutput").ap()

    xp = din("xp", [2048, D])
    xs = din("xs", [128, D])
    sd = din("sd", [DEPTH, 16, H, 128, 128])
    sc = din("sc", [DEPTH, 48, 3072])
    w_in = din("w_in", [DEPTH, D, DIN])
    w_pa = din("w_pa", [DEPTH, D, D])
    w_pb = din("w_pb", [DEPTH, D, D])
    w_o = din("w_o", [DEPTH, D, D])
    w_f1 = din("w_f1", [DEPTH, D, 2 * DFF])
    w_f2 = din("w_f2", [DEPTH, DFF, D])
    pp = din("pp", [128, 2 * 32 + 2 * 96 + 2 + 4])
    prow = din("prow", [1, DEPTH * 2 * 1024])
    lnw = din("lnw", [DEPTH, 2, D])
    wsp = din("wsp", [DEPTH, 2, 128, 1024])
    cst = din("cst", [7, 128, 128])
    cst16 = din("cst16", [10, 128, 128])
    cbd = din("cbd", [128, 2064])

    NSCR = 2 * 41
    wscr = nc.dram_tensor("wscr", [NSCR, 128, 4096], BF16, kind="Internal").ap()
    wscr_b = bufs("wscr", NSCR)
    y_p = dout("y_p", [2048, D])
    y_s = dout("y_s", [128, D])
    sd_p = dout("sd_p", [DEPTH, H, 128, 128])
    sc_p = dout("sc_p", [DEPTH, 3, 3072])
    cv_p = dout("cv_p", [DEPTH, 128, D])
    sd_s = dout("sd_s", [DEPTH, 16, H, 128, 128])
    sc_s = dout("sc_s", [DEPTH, 16, 3, 3072])
    cv_s = dout("cv_s", [DEPTH, 128, D])

    E = Em()
    es = ExitStack()
    with es:
        def sb(name, shape, dt):
            return es.enter_context(nc.sbuf_tensor(name, shape, dt))

        xT = sb("xT", [128, 8, 512], F32); xT_b = bufs("xT", 8)
        hT = sb("hT", [128, 8, 512], BF16); hT_b = bufs("hT", 8)
        R1 = sb("R1", [128, 24, 512], BF16); R1_b = bufs("R1", 24)
        sz = sb("sz", [128, 8, 512], BF16); sz_b = bufs("sz", 8)
        oa = sb("oa", [128, 8, 512], BF16); oa_b = bufs("oa", 8)
        uT_b = R1_b[0:8]; vT_b = R1_b[8:16]
        t32 = sb("t32", [128, 8, 512], F32); t32_b = bufs("t32", 8)
        pre = [sb(f"pre{i}", [128, 515], F32) for i in range(2)]; pre_b = bufs("pre", 2)
        acc = [sb(f"acc{i}", [128, 512], F32) for i in range(2)]; acc_b = bufs("acc", 2)
        sq = [sb(f"sq{i}", [128, 1024], BF16) for i in range(2)]; sq_b = bufs("sq", 2)
        rstd = sb("rstd", [128, 1024], F32); rstd_b = Buf("rstd")
        carry = sb("carry", [128, DEPTH, 24, 3], F32); carry_b = [bufs(f"car{l}_", 24) for l in range(DEPTH)]
        wsl = [sb(f"wsl{i}", [128, 8, 512], BF16) for i in range(5)]; wsl_b = bufs("wsl", 5)
        w3f = wsl[3][:].rearrange("p a b -> p (a b)")
        w4f = wsl[4][:].rearrange("p a b -> p (a b)")
        ppt = sb("ppt", [128, 262], F32); ppt_b = Buf("ppt")
        nea = sb("nea", [128, 2], F32); nea_b = Buf("nea")
        dnw = sb("dnw", [128, 2], F32); dnw_b = Buf("dnw")
        c32 = sb("c32", [128, 7, 128], F32); c32_b = Buf("c32")
        c16 = sb("c16", [128, 10, 128], BF16); c16_b = Buf("c16")
        seq32 = sb("seq32", [128, 16], F32); cb32_b = Buf("cb32")
        mskc = w3f[:, 0:2048].rearrange("p (a b) -> p a b", a=16)
        seqs = sb("seqs", [128, 16], BF16); cb16_b = Buf("cb16")
        selT = sb("selT", [16, 16, 128], BF16); selT_b = Buf("selT")
        t32f = t32[:].rearrange("p a b -> p (a b)")
        prw16 = t32f.bitcast(BF16)[0:1, 6144:7168]; prw_b = t32_b[6]
        lnt = t32f[:, 0:2048].rearrange("p (a b) -> p a b", a=2)
        xin = t32f[:, 0:1024]
        yout = [t32f[:, 1024:2048], t32f[:, 2048:3072]]
        scrow = t32f[0:3, 0:3072]
        scin = w4f[:, 0:2304].bitcast(F32).rearrange("p (a b) -> p a b", a=24); scin_b = Buf("scin")
        wsT = sb("wsT", [128, 8, 128], BF16); wsT_b = Buf("wsT")
        gT = pre[0][0:8, 0:512]; gT_b = pre_b[0]
        bT = pre[1][0:8, 0:512]; bT_b = pre_b[1]
        rn = acc[1][0:16, :]; rn_b = acc_b[1]
        rnb = acc[0][0:16, 0:256].bitcast(BF16); rnb_b = acc_b[0]
        rnl = acc[0][0:16, 256:512].bitcast(BF16)
        S32 = sb("S32", [128, DEPTH, H, 128], F32); S32_b = bufs("S32_", DEPTH)
        S16 = sb("S16", [128, DEPTH, H, 128], BF16); S16_b = bufs("S16_", DEPTH)
        s0f_0 = S32[:].rearrange("p a b c -> p (a b) c"); s0f_b0 = S32_b
        s0b_0 = S16[:].rearrange("p a b c -> p (a b) c"); s0b_b0 = S16_b
        gbt = sb("gbt", [128, 16], F32); gbt_b = Buf("gbt")
        es24 = sb("es24", [128, 24], F32); es_b = Buf("es24")
        kbs = sb("kbs", [128, 8], F32); kbs_b = Buf("kbs")
        Gm1 = sb("Gm1", [128, 8, 128], F32); Gm1_b = Buf("Gm1")
        Gm2 = sb("Gm2", [128, 8, 128], F32); Gm2_b = Buf("Gm2")
        ws32 = Gm1[:].rearrange("p a b -> p (a b)"); ws32_b = Gm1_b
        Kd = sb("Kd", [128, 8, 128], F32); Kd_b = Buf("Kd")
        Vb = sb("Vb", [128, 8, 128], BF16); Vb_b = Buf("Vb")
        Bo = sb("Bo", [128, 8, 128], BF16); Bo_b = Buf("Bo")
        r16 = sb("r16", [128, 8, 128], BF16); r16_b = Buf("r16")
        neg = sb("neg", [128, 8], F32); neg_b = Buf("neg")
        dsm = Gm1; dsm_b = Gm1_b
        dTm = Gm2; dTm_b = Gm2_b
        reg = sb("reg", [128, 8, 128], BF16); reg_b = Buf("reg")
        AmT = sb("AmT", [128, 2, 8, 128], BF16); Am = [AmT[:, 0, :, :], AmT[:, 1, :, :]]; Am_b = bufs("Am", 2)
        BmT = sb("BmT", [128, 2, 8, 128], BF16); Bm = [BmT[:, 0, :, :], BmT[:, 1, :, :]]; Bm_b = bufs("Bm", 2)
        vn32 = AmT[:].rearrange("p a b c -> p (a b c)").bitcast(F32).rearrange("p (b c) -> p b c", c=128)
        r32 = BmT[:].rearrange("p a b c -> p (a b c)").bitcast(F32).rearrange("p (b c) -> p b c", c=128)
        Pm0 = sb("Pm0", [128, 8, 128], BF16); Pm_b0 = Buf("Pm")
        Pf = rstd[:].rearrange("p (b c) -> p b c", c=128)
        PTm = sb("PTm", [128, 8, 128], BF16); PTm_b = Buf("PTm")
        vn = sb("vn", [128, 8, 128], BF16); vn_b = Buf("vn")
        qt = sb("qt", [128, 8, 128], BF16); qt_b = Buf("qt")
        on32 = sb("on32", [128, 8, 128], F32); on32_b = Buf("on32")
        lnv32 = on32[:].rearrange("p a b -> p (a b)"); lnv32_b = on32_b
        vnb = sb("vnb", [128, D], BF16); vnb_b = Buf("vnb")
        bnst = sb("bnst", [128, 2, 6], F32); mv = sb("mv", [128, 4], F32); bn_b = Buf("bn")
        t16 = t32f.bitcast(BF16)

        def t16tile(c):
            return t16[:, c * 1024:(c + 1) * 1024].rearrange("p (a b) -> p a b", a=8)
        KdV = Kd[:].rearrange("p a b -> p (a b)").bitcast(BF16).rearrange("p (s a b) -> p s a b", s=2, a=8)
        Kd2 = [KdV[:, 0, :, :], KdV[:, 1, :, :]]; Kd2_b = bufs("Kd2_", 2)
        Vb2 = [Vb[:], t16tile(0)]; Vb2_b = [Vb_b, t32_b[0]]
        qt2 = [qt[:], t16tile(1)]; qt2_b = [qt_b, t32_b[1]]
        PTm2 = [PTm[:], t16tile(2)]; PTm2_b = [PTm_b, t32_b[2]]
        Pm2 = [Pm0[:], t16tile(3)]; Pm2_b = [Pm_b0, t32_b[3]]
        Bo2 = [Bo[:], t16tile(4)]; Bo2_b = [Bo_b, t32_b[4]]
        neg1 = sb("neg1", [128, 8], F32); neg2 = [neg, neg1]; neg2_b = [neg_b, Buf("neg1")]
        es1 = sb("es1", [128, 24], F32); es2 = [es24, es1]; es2_b = [es_b, Buf("es1")]
        rstd9 = t32f[:, 3072:4096]
        msk16 = w3f[:, 2048:4096].rearrange("p (a b) -> p a b", a=16); msk16_b = Buf("msk16")
        egl = sb("egl", [128, 16, 8], F32); egl_b = Buf("egl")
        gsel = sb("gsel", [128, 16, 8], F32); gsel_b = Buf("gsel")
        snew = [w4f[:, 2304:3328].bitcast(F32).rearrange("p (a b) -> p a b", a=4), sb("snew1", [128, 4, 128], F32)[:]]
        snew_b = bufs("snew", 2)

        ps = es.enter_context(nc.psum_tensor("ps", [128, 8, 512], F32))
        ps_b = bufs("ps", 8)
        bptr = [0]

        def bank():
            b = bptr[0] % 8
            bptr[0] = (b + 1) % 8
            return b

        def pair():
            b = bptr[0] % 8
            if b % 2:
                b = (b + 1) % 8
            bptr[0] = (b + 2) % 8
            return b

        def P32(i, n=1):
            return ps[:, i:i + n, :].rearrange("p a b -> p (a b)") if n > 1 else ps[:, i, :]

        def P16(i):
            return ps[:, i, :].bitcast(BF16)

        def V(fn, r, w): return E.op("dve", fn, r, w)
        def A(fn, r, w): return E.op("act", fn, r, w)
        def T(fn, r, w): return E.op("pe", fn, r, w)
        def G(fn, r, w): return E.op("pool", fn, r, w)

        def dma(eng, chan, out, in_, r, w):
            return E.op(eng, lambda e: e.dma_start(out=out, in_=in_), r, w, chan=chan, ndma=1)

        slot_i = [0]
        issued = [0]
        mask_loaded = [False]
        WT = {"w_in": w_in, "w_pa": w_pa, "w_pb": w_pb, "w_o": w_o, "w_f1": w_f1, "w_f2": w_f2}

        scr_idx = {}

        def issue_w(k, desc):
            wn, l_, r0, nr, c0, ncol = desc
            s_ = slot_of(k)
            kc_ = nr // 128
            dst = wsl[s_][:, 0:kc_, 0:ncol]
            if desc not in scr_idx:
                i_ = len(scr_idx)
                scr_idx[desc] = i_
                src = WT[wn][l_, r0:r0 + nr, c0:c0 + ncol].rearrange("(c p) n -> p c n", p=128)
                dma("pool", f"w{s_}", dst, src, [], [wsl_b[s_]])
                sv = wscr[i_, :, 0:kc_ * ncol].rearrange("p (c n) -> p c n", c=kc_)
                dma("sp", f"scrw{i_ % 2}", sv, dst, [wsl_b[s_]], [wscr_b[i_]])
            else:
                i_ = scr_idx[desc]
                sv = wscr[i_, :, 0:kc_ * ncol].rearrange("p (c n) -> p c n", c=kc_)
                dma("sp", f"v{s_}", dst, sv, [wscr_b[i_]], [wsl_b[s_]])

        def load_w(desc, kc=None, ncols=None):
            k = slot_i[0]
            slot_i[0] += 1
            rec_reqs.append(desc)
            if all_reqs is None:
                issue_w(k, desc)
            else:
                assert all_reqs[k] == desc
                pump()
                assert issued[0] > k, "weight slot ring exhausted"
            return slot_of(k)

        samp_k0 = [None]
        if all_reqs is not None:
            samp_k0[0] = SAMP_K0[0]

        def slot_of(k):
            k0 = samp_k0[0]
            if k0 is None or k < k0:
                return k % 5
            return (k - k0) % 3

        def prev_user(k):
            s_ = slot_of(k)
            j = k - 1
            while j >= 0:
                if slot_of(j) == s_:
                    return j
                j -= 1
            return -1

        wdone = [0]

        def pump():
            while issued[0] < len(all_reqs) and prev_user(issued[0]) < wdone[0]:
                issue_w(issued[0], all_reqs[issued[0]])
                issued[0] += 1

        def w_done(n=1):
            wdone[0] += n
            if all_reqs is not None:
                pump()

        def wview(wt, l, r0, nr, c0, ncol):
            return (wt, l, r0, nr, c0, ncol)

        I32 = c32[:, 0, :]; I16 = c16[:, 0, :]; ONES16 = c16[:, 1, :]; ONES32 = c32[:, 1, :]

        def mm_group(bk, col0, ncol, lhs_fn, rhs_fn, nk, rbufs, extra_r=(), mp=128):
            def fn(e):
                ins = None
                for k in range(nk):
                    ins = e.matmul(ps[0:mp, bk, col0:col0 + ncol], lhs_fn(k), rhs_fn(k),
                                   start=(k == 0), stop=(k == nk - 1))
                return ins
            return T(fn, list(rbufs) + list(extra_r), [ps_b[bk]])

        dma("sp", "par", ppt[:], pp, [], [ppt_b])
        dma("sp", "par", c32[:], cst.rearrange("k p n -> p k n"), [], [c32_b])
        dma("sp", "par", seq32[:], cbd[:, 2048:2064], [], [cb32_b])
        dma("pool", "cst16", c16[:], cst16.rearrange("k p n -> p k n"), [], [c16_b])
        V(lambda e: e.tensor_copy(out=seqs[:], in_=seq32[:]), [cb32_b, cb16_b], [cb16_b])
        V(lambda e: e.tensor_copy(out=selT[:], in_=c32[0:16, 0, 0:16].unsqueeze(2).to_broadcast([16, 16, 128])),
          [c32_b], [selT_b])
        def NW(l, n): return ppt[:, (l * 4 + n) * 8:(l * 4 + n) * 8 + 8]
        def CW(l, tap, ch): return ppt[:, 64 + l * 96 + tap * 24 + ch:64 + l * 96 + tap * 24 + ch + 1]
        A(lambda e: e.activation(out=nea[0:8, :], in_=ppt[0:8, 258:260], func=AF.Exp), [ppt_b], [nea_b])
        V(lambda e: e.tensor_scalar(out=nea[0:8, :], in0=nea[0:8, :], scalar1=-1.0, scalar2=None, op0=ALU.mult), [nea_b], [nea_b])
        V(lambda e: e.tensor_scalar(out=dnw[:], in0=ppt[:, 256:258], scalar1=float(128 ** -0.5), scalar2=None, op0=ALU.mult),
          [ppt_b], [dnw_b])
        sel16 = c16[:, 6:8, :].rearrange("p a b -> p (a b)")

        def rsqrt_from_psum(out_ap, in_ap, scale, eps, r, w):
            A(lambda e: e.activation(out=out_ap, in_=in_ap, func=AF.Ln, scale=scale, bias=eps), r, w)
            A(lambda e: e.activation(out=out_ap, in_=out_ap, func=AF.Exp, scale=-0.5), w, w)

        def rmsnorm(src, src_b, W, gain, mode, l):
            bk = bank()
            for c in range(8):
                s = c % 2
                A(lambda e, c=c, s=s: e.activation(out=sq[s][:, 0:W], in_=src[:, c, 0:W], func=AF.Square),
                  [src_b[c]], [sq_b[s]])
                T(lambda e, c=c, s=s: e.matmul(ps[:, bk, 0:W], ONES16, sq[s][:, 0:W], start=(c == 0), stop=(c == 7)),
                  [sq_b[s], c16_b], [ps_b[bk]])
            rsqrt_from_psum(rstd[:, 0:W], ps[:, bk, 0:W], 1.0 / D, 1e-6, [ps_b[bk]], [rstd_b])
            for c in range(8):
                if mode == "h":
                    V(lambda e, c=c: e.scalar_tensor_tensor(out=hT[:, c, 0:W], in0=src[:, c, 0:W], scalar=gain[:, c:c + 1],
                                                            in1=rstd[:, 0:W], op0=ALU.mult, op1=ALU.mult),
                      [src_b[c], rstd_b, ppt_b], [hT_b[c]])
                else:
                    V(lambda e, c=c: e.scalar_tensor_tensor(out=src[:, c, 0:W], in0=src[:, c, 0:W], scalar=gain[:, c:c + 1],
                                                            in1=rstd[:, 0:W], op0=ALU.mult, op1=ALU.mult),
                      [src_b[c], rstd_b, ppt_b], [src_b[c]])
                    (V if c % 2 else G)(
                        lambda e, c=c: e.tensor_tensor(out=xT[:, c, 0:W], in0=xT[:, c, 0:W], in1=src[:, c, 0:W], op=ALU.add),
                        [src_b[c], xT_b[c]], [xT_b[c]])

        def proj_chunks(wt, l, c0, nchunks, W, rhs, rhs_b, consume, kc=8, r0=0):
            j = 0
            pend = [None]
            while j < nchunks:
                nj = min(4, nchunks - j)
                s = load_w(wview(wt, l, r0, kc * 128, c0 + j * 128, nj * 128), kc, nj * 128)
                for jj in range(nj):
                    bk = bank()
                    mm_group(bk, 0, W, lambda k, s=s, jj=jj: wsl[s][:, k, jj * 128:(jj + 1) * 128],
                             lambda k: rhs[:, k, 0:W], kc, rhs_b, [wsl_b[s]])
                    d_ = consume(j + jj, bk)
                    if pend[0] is not None:
                        pend[0]()
                    pend[0] = d_ if callable(d_) else None
                w_done()
                j += nj
            if pend[0] is not None:
                pend[0]()

        def layer(l, W, samp, last_blk):
            NT = W // 128
            if stop <= 0:
                return
            rmsnorm(xT, xT_b, W, NW(l, 0), "h", l)
            if stop <= 1:
                return
            if samp:
                tin = t32f
                dma("sp", "scin", tin[0:48, 0:3072], sc[l], [], t32_b[0:6])
                for q4 in range(6):
                    bk = bank()
                    for cc in range(4):
                        ch = q4 * 4 + cc
                        T(lambda e, ch=ch, cc=cc, bk=bk: e.transpose(out=ps[:, bk, cc * 48:(cc + 1) * 48], in_=tin[0:48, ch * 128:(ch + 1) * 128],
                                                                    identity=c32[0:48, 0, 0:48]), t32_b[0:6] + [c32_b], [ps_b[bk]])
                    V(lambda e, q4=q4, bk=bk: e.tensor_copy(out=scin[:, q4 * 4:(q4 + 1) * 4, :],
                                                            in_=ps[:, bk, 0:192].rearrange("p (a b) -> p a b", a=4)),
                      [ps_b[bk]], [scin_b])
            tok_b = t32_b
            if samp:
                tokm = t32f[:, 0:3072]

            def qkv_consume(ch, bk):
                s = ch % 2
                if samp:
                    pv = pre[s][:, 0:176].rearrange("p (a b) -> p a b", a=16)
                    G(lambda e: e.tensor_copy(out=pv[:, :, 0:3], in_=scin[:, ch, :].rearrange("p (a b) -> p a b", a=16)),
                      [scin_b], [pre_b[s]])
                    A(lambda e: e.activation(out=pv[:, :, 3:11], in_=ps[:, bk, 0:128].rearrange("p (a b) -> p a b", a=16),
                                             func=AF.Copy), [ps_b[bk]], [pre_b[s]])
                    av = acc[s][:, 0:128].rearrange("p (a b) -> p a b", a=16)
                    A(lambda e: e.activation(out=acc[s][:, 0:128], in_=ps[:, bk, 0:128], func=AF.Copy, scale=CW(l, 3, ch)),
                      [ps_b[bk], ppt_b], [acc_b[s]])
                    for tap in (2, 1, 0):
                        V(lambda e, tap=tap: e.scalar_tensor_tensor(out=av, in0=pv[:, :, tap:tap + 8], scalar=CW(l, tap, ch),
                                                                    in1=av, op0=ALU.mult, op1=ALU.add),
                          [pre_b[s], acc_b[s], ppt_b], [acc_b[s]])
                    G(lambda e: e.tensor_copy(out=sq[s][:, 0:256].bitcast(F32).rearrange("p (a b) -> p a b", a=16),
                                              in_=pv[:, :, 3:11]), [pre_b[s]], [sq_b[s]])
                    b2 = bank()
                    T(lambda e: e.transpose(out=ps[:, b2, 0:128], in_=sq[s][:, 0:256].bitcast(F32), identity=I32),
                      [sq_b[s], c32_b], [ps_b[b2]])
                    V(lambda e: e.tensor_copy(out=tokm[:, ch * 128:(ch + 1) * 128], in_=ps[:, b2, 0:128]),
                      [ps_b[b2]], [tok_b[ch // 4]])
                else:
                    G(lambda e: e.tensor_copy(out=pre[s][:, 0:3], in_=carry[:, l, ch, :]), [carry_b[l][ch]], [pre_b[s]])
                    A(lambda e: e.activation(out=pre[s][:, 3:515], in_=ps[:, bk, :], func=AF.Copy), [ps_b[bk]], [pre_b[s]])
                    A(lambda e: e.activation(out=acc[s][:], in_=ps[:, bk, :], func=AF.Copy, scale=CW(l, 3, ch)),
                      [ps_b[bk], ppt_b], [acc_b[s]])
                    G(lambda e: e.tensor_copy(out=carry[:, l, ch, :], in_=pre[s][:, 512:515]), [pre_b[s]], [carry_b[l][ch]])
                    for tap in (2, 1, 0):
                        V(lambda e, tap=tap: e.scalar_tensor_tensor(out=acc[s][:], in0=pre[s][:, tap:tap + 512],
                                                                    scalar=CW(l, tap, ch), in1=acc[s][:],
                                                                    op0=ALU.mult, op1=ALU.add),
                          [pre_b[s], acc_b[s], ppt_b], [acc_b[s]])
                return lambda: A(lambda e: e.activation(out=R1[:, ch, 0:W], in_=acc[s][:, 0:W], func=AF.Silu), [acc_b[s]], [R1_b[ch]])

            proj_chunks("w_in", l, 0, 24, W, hT, hT_b, qkv_consume)
            if samp:
                for j in range(3):
                    dma("sp", "o_sc", sc_s[l, :, j, :], tokm[5 + j:128:8, :], tok_b[0:6], [])
            elif last_blk:
                for q4 in range(6):
                    bk = bank()
                    for cc in range(4):
                        ch = q4 * 4 + cc
                        T(lambda e, ch=ch, cc=cc: e.transpose(out=ps[0:3, bk, cc * 128:(cc + 1) * 128], in_=carry[:, l, ch, :],
                                                              identity=I32), [carry_b[l][ch], c32_b], [ps_b[bk]])
                    V(lambda e, q4=q4: e.tensor_copy(out=scrow[:, q4 * 512:(q4 + 1) * 512], in_=ps[0:3, bk, :]),
                      [ps_b[bk]], t32_b[0:6])
                dma("sp", "o_sc", sc_p[l], scrow, t32_b[0:6], [])

            if stop <= 2:
                return
            proj_chunks("w_in", l, 3072, 8, W, hT, hT_b,
                        lambda ch, bk: A(lambda e: e.activation(out=sz[:, ch, 0:W], in_=ps[:, bk, 0:W], func=AF.Silu),
                                         [ps_b[bk]], [sz_b[ch]]))
            s = load_w(wview("w_in", l, 0, 1024, 4096, 16), 8, 16)
            ba = bank()
            mm_group(ba, 0, W, lambda k: wsl[s][:, k, 0:8], lambda k: hT[:, k, 0:W], 8, hT_b, [wsl_b[s]], mp=8)
            bb = bank()
            mm_group(bb, 0, W, lambda k: wsl[s][:, k, 8:16], lambda k: hT[:, k, 0:W], 8, hT_b, [wsl_b[s]], mp=8)
            w_done()
            A(lambda e: e.activation(out=gT[:, 0:W], in_=ps[0:8, ba, 0:W], func=AF.Exp, bias=ppt[0:8, 260 + l:261 + l]),
              [ps_b[ba], ppt_b], [gT_b])
            A(lambda e: e.activation(out=gT[:, 0:W], in_=gT[:, 0:W], func=AF.Ln, bias=1.0), [gT_b], [gT_b])
            V(lambda e: e.tensor_scalar(out=gT[:, 0:W], in0=gT[:, 0:W], scalar1=nea[0:8, l:l + 1], scalar2=None, op0=ALU.mult),
              [gT_b, nea_b], [gT_b])
            A(lambda e: e.activation(out=bT[:, 0:W], in_=ps[0:8, bb, 0:W], func=AF.Sigmoid), [ps_b[bb]], [bT_b])
            bn_ = bank()
            for j in range(16):
                s2 = j % 2
                A(lambda e, j=j, s2=s2: e.activation(out=sq[s2][:, 0:W], in_=R1[:, j, 0:W], func=AF.Square),
                  [R1_b[j]], [sq_b[s2]])
                T(lambda e, j=j, s2=s2: e.matmul(ps[0:16, bn_, 0:W], sel16[:, j * 16:(j + 1) * 16], sq[s2][:, 0:W],
                                                 start=(j == 0), stop=(j == 15)), [sq_b[s2], c16_b], [ps_b[bn_]])
            rsqrt_from_psum(rn[:, 0:W], ps[0:16, bn_, 0:W], 1.0, 1e-6, [ps_b[bn_]], [rn_b])
            V(lambda e: e.tensor_copy(out=rnb[:, 0:W], in_=rn[:, 0:W]), [rn_b], [rnb_b])
            V(lambda e: e.tensor_tensor(out=rnl[:, 0:W], in0=rn[:, 0:W], in1=rnb[:, 0:W], op=ALU.subtract), [rn_b, rnb_b], [rnb_b])
            for j in range(16):
                bk = bank()
                def fbc(e, j=j, bk=bk):
                    e.matmul(ps[:, bk, 0:W], selT[:, j, :], rnb[:, 0:W], start=True, stop=False)
                    return e.matmul(ps[:, bk, 0:W], selT[:, j, :], rnl[:, 0:W], start=False, stop=True)
                T(fbc, [rnb_b, selT_b], [ps_b[bk]])
                V(lambda e, j=j, bk=bk: e.tensor_tensor(out=R1[:, j, 0:W], in0=R1[:, j, 0:W], in1=ps[:, bk, 0:W], op=ALU.mult),
                  [R1_b[j], ps_b[bk]], [R1_b[j]])

            if stop <= 3:
                return
            cU, cSL, cN1, cN2 = (4, 5, 4, 5) if samp else (2, 3, 2, 3)
            nlev = 3 if samp else 6

            def bc8(ap):
                return ap.unsqueeze(2).to_broadcast([128, 8, 128])

            def bcm(ap):
                return ap.unsqueeze(1).to_broadcast([128, 8, 128])

            def PV(i):
                return ps[:, i:i + 2, :].rearrange("p a (b c) -> p (a b) c", c=128)

            class Alloc:
                def __init__(self, lo, n):
                    self.lo, self.n, self.p = lo, n, 0

                def bank(self):
                    b_ = self.p % self.n
                    self.p = (b_ + 1) % self.n
                    return self.lo + b_

                def pair(self):
                    b_ = self.p % self.n
                    if b_ % 2:
                        b_ = (b_ + 1) % self.n
                    self.p = (b_ + 2) % self.n
                    return self.lo + b_

            a1 = Alloc(0, 4)
            a2 = Alloc(0, 8)
            qb = R1_b[0:8]

            def chunk_gen(t):
                cs = slice(t * 128, (t + 1) * 128)
                st = 0 if samp else t % 2
                Kd_, Kd_b_ = Kd2[st], Kd2_b[st]
                Vb_, Vb_b_ = Vb2[st], Vb2_b[st]
                qt_, qt_b_ = qt2[st], qt2_b[st]
                PTm_, PTm_b_ = PTm2[st], PTm2_b[st]
                TT, TT_b = Pm2[st], Pm2_b[st]
                Bo_, Bo_b_ = Bo2[st], Bo2_b[st]
                neg_, neg_b_ = neg2[st], neg2_b[st]
                es_, es_b_ = es2[st], es2_b[st]
                bK = a1.bank()
                for h in range(H):
                    T(lambda e, h=h: e.transpose(out=P16(bK)[:, h * 128:(h + 1) * 128], in_=R1[:, 8 + h, cs], identity=I16),
                      [R1_b[8 + h], c16_b], [ps_b[bK]])
                bV = a1.bank()
                for h in range(H):
                    T(lambda e, h=h: e.transpose(out=P16(bV)[:, h * 128:(h + 1) * 128], in_=R1[:, 16 + h, cs], identity=I16),
                      [R1_b[16 + h], c16_b], [ps_b[bV]])
                bG = a1.bank()
                T(lambda e: e.transpose(out=ps[:, bG, 0:8], in_=gT[:, cs], identity=c32[0:8, 0, 0:8]), [gT_b, c32_b], [ps_b[bG]])
                T(lambda e: e.transpose(out=ps[:, bG, 8:16], in_=bT[:, cs], identity=c32[0:8, 0, 0:8]), [bT_b, c32_b], [ps_b[bG]])
                V(lambda e: e.tensor_copy(out=gbt[:], in_=ps[:, bG, 0:16]), [ps_b[bG]], [gbt_b])
                yield 1
                bS = a1.bank()
                cLast = 6 if samp else 1
                for i3, ci in enumerate((cU, cSL, cLast)):
                    T(lambda e, i3=i3, ci=ci: e.matmul(ps[:, bS, i3 * 8:(i3 + 1) * 8], c32[:, ci, :], gbt[:, 0:8],
                                                       start=True, stop=True), [gbt_b, c32_b], [ps_b[bS]])
                A(lambda e: e.activation(out=es_[:], in_=ps[:, bS, 0:24], func=AF.Exp), [ps_b[bS]], [es_b_])
                V(lambda e: e.tensor_tensor(out=kbs[:], in0=gbt[:, 8:16], in1=es_[:, 0:8], op=ALU.mult), [gbt_b, es_b_], [kbs_b])
                yield 1
                KP = P16(bK).rearrange("p (a b) -> p a b", a=8)
                VP = P16(bV).rearrange("p (a b) -> p a b", a=8)
                V(lambda e: e.tensor_tensor(out=Kd_, in0=KP, in1=bc8(es_[:, 8:16]), op=ALU.mult), [ps_b[bK], es_b_], [Kd_b_])
                V(lambda e: e.tensor_tensor(out=Vb_, in0=VP, in1=bc8(gbt[:, 8:16]), op=ALU.mult), [ps_b[bV], gbt_b], [Vb_b_])
                V(lambda e: e.tensor_scalar(out=neg_[:], in0=kbs[:], scalar1=-1.0, scalar2=None, op0=ALU.mult), [kbs_b], [neg_b_])
                yield 1
                V(lambda e: e.tensor_tensor(out=Gm1[:], in0=bcm(c32[:, cSL, :]), in1=bc8(gbt[:, 0:8]), op=ALU.mult),
                  [c32_b, gbt_b], [Gm1_b])
                G(lambda e: e.tensor_tensor(out=Gm2[:], in0=bcm(c32[:, cU, :]), in1=bc8(gbt[:, 0:8]), op=ALU.mult),
                  [c32_b, gbt_b], [Gm2_b])
                yield 1
                pD = a1.pair()
                pT = a1.pair()
                for hh in range(2):
                    g1 = Gm1[:, hh * 4:(hh + 1) * 4, :].rearrange("p a b -> p (a b)")
                    g2 = Gm2[:, hh * 4:(hh + 1) * 4, :].rearrange("p a b -> p (a b)")

                    def fD(e, hh=hh, g1=g1):
                        e.matmul(ps[:, pD + hh, :], c32[:, cU, :], g1, start=True, stop=False)
                        ins = None
                        for q in range(4):
                            ins = e.matmul(ps[:, pD + hh, q * 128:(q + 1) * 128], I16, c16[:, cN1, :], start=False, stop=(q == 3))
                        return ins
                    T(fD, [Gm1_b, c32_b, c16_b], [ps_b[pD + hh]])

                    def fT(e, hh=hh, g2=g2):
                        e.matmul(ps[:, pT + hh, :], c32[:, cSL, :], g2, start=True, stop=False)
                        ins = None
                        for q in range(4):
                            ins = e.matmul(ps[:, pT + hh, q * 128:(q + 1) * 128], I16, c16[:, cN2, :], start=False, stop=(q == 3))
                        return ins
                    T(fT, [Gm2_b, c32_b, c16_b], [ps_b[pT + hh]])
                yield 1
                A(lambda e: e.activation(out=dsm[:].rearrange("p a b -> p (a b)"), in_=P32(pD, 2), func=AF.Exp),
                  [ps_b[pD], ps_b[pD + 1]], [dsm_b])
                pR = a1.pair()
                for hh in range(2):
                    g2 = Gm2[:, hh * 4:(hh + 1) * 4, :].rearrange("p a b -> p (a b)")
                    T(lambda e, hh=hh, g2=g2: e.matmul(ps[:, pR + hh, :], ONES32, g2, start=True, stop=True),
                      [Gm2_b, c32_b], [ps_b[pR + hh]])
                A(lambda e: e.activation(out=dTm[:].rearrange("p a b -> p (a b)"), in_=P32(pT, 2), func=AF.Exp),
                  [ps_b[pT], ps_b[pT + 1]], [dTm_b])
                A(lambda e: e.activation(out=reg[:].rearrange("p a b -> p (a b)"), in_=P32(pR, 2), func=AF.Exp),
                  [ps_b[pR], ps_b[pR + 1]], [reg_b])
                yield 1
                G(lambda e: e.tensor_tensor(out=qt_, in0=R1[:, 0:8, cs], in1=reg[:], op=ALU.mult), qb + [reg_b], [qt_b_])
                pG = a1.pair()
                pP = a1.pair()
                for h in range(H):
                    T(lambda e, h=h: e.matmul(PV(pG)[:, h, :], R1[:, 8 + h, cs], R1[:, 8 + h, cs], start=True, stop=True),
                      [R1_b[8 + h]], [ps_b[pG + h // 4]])
                for h in range(H):
                    T(lambda e, h=h: e.matmul(PV(pP)[:, h, :], R1[:, 8 + h, cs], R1[:, h, cs], start=True, stop=True),
                      [R1_b[8 + h], R1_b[h]], [ps_b[pP + h // 4]])
                yield 1
                for h in range(H):
                    V(lambda e, h=h: e.scalar_tensor_tensor(out=Am[0][:, h, :], in0=PV(pG)[:, h, :], scalar=gbt[:, 8 + h:9 + h],
                                                            in1=dsm[:, h, :], op0=ALU.mult, op1=ALU.mult),
                      [ps_b[pG + h // 4], gbt_b, dsm_b], [Am_b[0]])
                V(lambda e: e.tensor_tensor(out=PTm_, in0=PV(pP), in1=dTm[:], op=ALU.mult),
                  [ps_b[pP], ps_b[pP + 1], dTm_b], [PTm_b_])
                yield 1
                bB = a1.bank()
                for h in range(H):
                    T(lambda e, h=h: e.transpose(out=P16(bB)[:, h * 128:(h + 1) * 128], in_=Am[0][:, h, :], identity=I16),
                      [Am_b[0], c16_b], [ps_b[bB]])
                BP = P16(bB).rearrange("p (a b) -> p a b", a=8)
                if samp:
                    A(lambda e: e.activation(out=Bm[0], in_=BP, func=AF.Copy), [ps_b[bB]], [Bm_b[0]])
                else:
                    A(lambda e: e.activation(out=Bm[1], in_=BP, func=AF.Copy), [ps_b[bB]], [Bm_b[1]])
                    yield 1
                    V(lambda e: e.tensor_tensor(out=Bm[0], in0=Bm[1], in1=bcm(c16[:, 8, :]), op=ALU.mult), [Bm_b[1], c16_b], [Bm_b[0]])
                    V(lambda e: e.tensor_tensor(out=Am[0], in0=Am[0], in1=bcm(c16[:, 8, :]), op=ALU.mult), [Am_b[0], c16_b], [Am_b[0]])
                    G(lambda e: e.tensor_tensor(out=Bo_, in0=Bm[1], in1=bcm(c16[:, 9, :]), op=ALU.mult), [Bm_b[1], c16_b], [Bo_b_])
                yield 1
                V(lambda e: e.tensor_tensor(out=TT, in0=bcm(I32), in1=Bm[0], op=ALU.subtract), [Bm_b[0], c32_b], [TT_b])
                V(lambda e: e.tensor_tensor(out=Pf, in0=bcm(I32), in1=Bm[0], op=ALU.subtract), [Bm_b[0], c32_b], [rstd_b])
                ca = 0
                for lev in range(1, nlev):
                    na = 1 - ca
                    pA = a1.pair()
                    for h in range(H):
                        T(lambda e, h=h: e.matmul(PV(pA)[:, h, :], Bm[ca][:, h, :], Am[ca][:, h, :], start=True, stop=True),
                          [Am_b[ca], Bm_b[ca]], [ps_b[pA + h // 4]])
                    A(lambda e: e.activation(out=Am[na], in_=PV(pA), func=AF.Copy), [ps_b[pA], ps_b[pA + 1]], [Am_b[na]])
                    yield 1
                    if lev < nlev - 1:
                        pB = a1.pair()
                        for h in range(H):
                            T(lambda e, h=h: e.matmul(PV(pB)[:, h, :], Am[ca][:, h, :], Bm[ca][:, h, :], start=True, stop=True),
                              [Am_b[ca], Bm_b[ca]], [ps_b[pB + h // 4]])
                        A(lambda e: e.activation(out=Bm[na], in_=PV(pB), func=AF.Copy), [ps_b[pB], ps_b[pB + 1]], [Bm_b[na]])
                        yield 1
                    pU = a1.pair()
                    for h in range(H):
                        T(lambda e, h=h: e.matmul(PV(pU)[:, h, :], Am[na][:, h, :], TT[:, h, :], start=True, stop=True),
                          [Am_b[na], TT_b], [ps_b[pU + h // 4]])
                    V(lambda e: e.tensor_tensor(out=TT, in0=PV(pU), in1=Pf, op=ALU.add), [ps_b[pU], ps_b[pU + 1], rstd_b], [TT_b])
                    if lev < nlev - 1:
                        V(lambda e: e.tensor_tensor(out=Pf, in0=PV(pU), in1=Pf, op=ALU.add), [ps_b[pU], ps_b[pU + 1], rstd_b], [rstd_b])
                    yield 1
                    ca = na
                yield "S2"
                r32 = on32[:]
                R32b = [on32_b]
                if samp:
                    pK = a2.pair(); pN = a2.pair(); pO = a2.pair()
                    other = [b_ for b_ in range(8) if b_ not in (pK, pK + 1, pN, pN + 1, pO, pO + 1)]
                    V(lambda e: e.tensor_tensor(out=gsel[:], in0=gbt[:, 0:8].unsqueeze(1).to_broadcast([128, 16, 8]),
                                                in1=seq32[:].unsqueeze(2).to_broadcast([128, 16, 8]), op=ALU.mult),
                      [gbt_b, cb32_b], [gsel_b])
                    bE = other[0]
                    T(lambda e: e.matmul(ps[:, bE, 0:128], ONES32, gsel[:].rearrange("p a b -> p (a b)"), start=True, stop=True),
                      [gsel_b, c32_b], [ps_b[bE]])
                    A(lambda e: e.activation(out=egl[:].rearrange("p a b -> p (a b)"), in_=ps[:, bE, 0:128], func=AF.Exp),
                      [ps_b[bE]], [egl_b])
                    for h in range(H):
                        if h % 2 == 0:
                            s0f, s0f_b, s0b, s0b_b = s0f_0, s0f_b0, s0b_0, s0b_b0
                        else:
                            s0f = t32f[:, 0:2048].rearrange("p (a b) -> p a b", a=16); s0f_b = t32_b[0:4]
                            s0b = t16[:, 4096:6144].rearrange("p (a b) -> p a b", a=16); s0b_b = t32_b[4:6]
                        dma("sp", f"s0in{h % 2}", s0f, sd[l, :, h, :, :].rearrange("s k v -> k s v"), [], s0f_b)
                        A(lambda e: e.activation(out=s0b, in_=s0f, func=AF.Copy), s0f_b, s0b_b)
                        V(lambda e, h=h: e.tensor_tensor(out=msk16[:], in0=R1[:, 8 + h, cs].unsqueeze(1).to_broadcast([128, 16, 128]),
                                                         in1=mskc[:], op=ALU.mult), [R1_b[8 + h], cb16_b], [msk16_b])

                        def fks(e, h=h):
                            ins = None
                            for s_ in range(16):
                                ins = e.matmul(PV(pK)[:, h, :], msk16[:, s_, :], s0b[:, s_, :], start=(s_ == 0), stop=(s_ == 15))
                            return ins
                        T(fks, [msk16_b] + s0b_b, [ps_b[pK + h // 4]])
                        V(lambda e, h=h: e.scalar_tensor_tensor(out=r32[:, h, :], in0=PV(pK)[:, h, :], scalar=neg_[:, h:h + 1],
                                                                in1=Vb_[:, h, :], op0=ALU.mult, op1=ALU.add),
                          [ps_b[pK + h // 4], neg_b_, Vb_b_], R32b)
                        V(lambda e, h=h: e.tensor_copy(out=r16[:, h, :], in_=r32[:, h, :]), R32b, [r16_b])
                        T(lambda e, h=h: e.matmul(PV(pN)[:, h, :], TT[:, h, :], r16[:, h, :], start=True, stop=True),
                          [TT_b, r16_b], [ps_b[pN + h // 4]])
                        A(lambda e, h=h: e.activation(out=vn[:, h, :], in_=PV(pN)[:, h, :], func=AF.Copy), [ps_b[pN + h // 4]], [vn_b])
                        V(lambda e, h=h: e.tensor_tensor(out=msk16[:], in0=qt_[:, h, :].unsqueeze(1).to_broadcast([128, 16, 128]),
                                                         in1=mskc[:], op=ALU.mult), [qt_b_, cb16_b], [msk16_b])

                        def fo(e, h=h):
                            for s_ in range(16):
                                e.matmul(PV(pO)[:, h, :], s0b[:, s_, :], msk16[:, s_, :], start=(s_ == 0), stop=False)
                            return e.matmul(PV(pO)[:, h, :], vn[:, h, :], PTm_[:, h, :], start=False, stop=True)
                        T(fo, s0b_b + [msk16_b, vn_b, PTm_b_], [ps_b[pO + h // 4]])
                        V(lambda e, h=h: e.tensor_tensor(out=msk16[:], in0=Kd_[:, h, :].unsqueeze(1).to_broadcast([128, 16, 128]),
                                                         in1=seqs[:].unsqueeze(2).to_broadcast([128, 16, 128]), op=ALU.mult),
                          [Kd_b_, cb16_b], [msk16_b])
                        for q in range(4):
                            bq = other[1 + (h * 4 + q) % (len(other) - 1)]

                            def fs(e, h=h, q=q, bq=bq):
                                ins = None
                                for s4 in range(4):
                                    ins = e.matmul(ps[:, bq, s4 * 128:(s4 + 1) * 128], msk16[:, q * 4 + s4, :], vn[:, h, :],
                                                   start=True, stop=True)
                                return ins
                            T(fs, [msk16_b, vn_b], [ps_b[bq]])
                            for s4 in range(4):
                                s_ = q * 4 + s4
                                V(lambda e, h=h, s_=s_, s4=s4, bq=bq, q=q: e.scalar_tensor_tensor(
                                    out=snew[q % 2][:, s4, :], in0=s0f[:, s_, :], scalar=egl[:, s_, h:h + 1],
                                    in1=ps[:, bq, s4 * 128:(s4 + 1) * 128], op0=ALU.mult, op1=ALU.add),
                                  s0f_b + [egl_b, ps_b[bq]], [snew_b[q % 2]])
                            dma("sp", f"s0out{q % 2}", sd_s[l, q * 4:(q + 1) * 4, h, :, :].rearrange("s k v -> k s v"), snew[q % 2],
                                [snew_b[q % 2]], [])
                    pQ = a2.pair()
                else:
                    pK, pN, pO, pS, pQ = 4, 6, 4, 6, 6
                    pKv = PV(pK); pNv = PV(pN)
                    for h in range(H):
                        T(lambda e, h=h: e.matmul(PV(pK)[:, h, :], R1[:, 8 + h, cs], S16[:, l, h, :], start=True, stop=True),
                          [R1_b[8 + h], S16_b[l]], [ps_b[pK + h // 4]])
                    V(lambda e: e.tensor_tensor(out=r32, in0=pKv, in1=bc8(neg_[:]), op=ALU.mult), [ps_b[pK], ps_b[pK + 1], neg_b_], R32b)
                    yield 1
                    V(lambda e: e.tensor_tensor(out=r32, in0=r32, in1=Vb_, op=ALU.add), R32b + [Vb_b_], R32b)
                    A(lambda e: e.activation(out=r16[:], in_=r32, func=AF.Copy), R32b, [r16_b])
                    yield 1
                    for h in range(H):
                        T(lambda e, h=h: e.matmul(PV(pN)[:, h, :], TT[:, h, :], r16[:, h, :], start=True, stop=True),
                          [TT_b, r16_b], [ps_b[pN + h // 4]])
                    A(lambda e: e.activation(out=vn[:], in_=pNv, func=AF.Copy), [ps_b[pN], ps_b[pN + 1]], [vn_b])
                    yield 1
                    for h in range(H):
                        T(lambda e, h=h: e.matmul(PV(pK)[:, h, :], Bo_[:, h, :], vn[:, h, :], start=True, stop=True),
                          [Bo_b_, vn_b], [ps_b[pK + h // 4]])
                    V(lambda e: e.tensor_tensor(out=r16[:], in0=r32, in1=pKv, op=ALU.subtract), R32b + [ps_b[pK], ps_b[pK + 1]], [r16_b])
                    yield 1
                    for h in range(H):
                        T(lambda e, h=h: e.matmul(PV(pN)[:, h, :], TT[:, h, :], r16[:, h, :], start=True, stop=True),
                          [TT_b, r16_b], [ps_b[pN + h // 4]])
                    A(lambda e: e.activation(out=vn[:], in_=pNv, func=AF.Copy), [ps_b[pN], ps_b[pN + 1]], [vn_b])
                    yield 1
                    for h in range(H):
                        def fo(e, h=h):
                            e.matmul(PV(pO)[:, h, :], S16[:, l, h, :], qt_[:, h, :], start=True, stop=False)
                            return e.matmul(PV(pO)[:, h, :], vn[:, h, :], PTm_[:, h, :], start=False, stop=True)
                        T(fo, [S16_b[l], qt_b_, vn_b, PTm_b_], [ps_b[pO + h // 4]])
                    yield 1
                    for h in range(H):
                        T(lambda e, h=h: e.matmul(PV(pS)[:, h, :], Kd_[:, h, :], vn[:, h, :], start=True, stop=True),
                          [Kd_b_, vn_b], [ps_b[pS + h // 4]])
                    A(lambda e: e.activation(out=sq[0][:], in_=P32(pO, 2), func=AF.Square, scale=float(128 ** -0.5)),
                      [ps_b[pO], ps_b[pO + 1]], [sq_b[0]])
                    yield 1
                    for h in range(H):
                        V(lambda e, h=h: e.scalar_tensor_tensor(out=S32[:, l, h, :], in0=S32[:, l, h, :], scalar=es_[:, 16 + h:17 + h],
                                                                in1=PV(pS)[:, h, :], op0=ALU.mult, op1=ALU.add),
                          [S32_b[l], es_b_, ps_b[pS + h // 4]], [S32_b[l]])
                    A(lambda e: e.activation(out=S16[:, l, :, :], in_=S32[:, l, :, :], func=AF.Copy), [S32_b[l]], [S16_b[l]])
                    yield 1
                if samp:
                    A(lambda e: e.activation(out=sq[0][:], in_=P32(pO, 2), func=AF.Square, scale=float(128 ** -0.5)),
                      [ps_b[pO], ps_b[pO + 1]], [sq_b[0]])
                for hh in range(2):
                    T(lambda e, hh=hh: e.matmul(ps[:, pQ + hh, :], ONES16, sq[0][:, hh * 512:(hh + 1) * 512], start=True, stop=True),
                      [sq_b[0], c16_b], [ps_b[pQ + hh]])
                rsqrt_from_psum(rstd9, P32(pQ, 2), 1.0 / 128, 1e-6, [ps_b[pQ], ps_b[pQ + 1]], t32_b[6:8])
                yield 1
                V(lambda e: e.scalar_tensor_tensor(out=on32[:].rearrange("p a b -> p (a b)"), in0=P32(pO, 2), scalar=dnw[:, l:l + 1],
                                                   in1=rstd9, op0=ALU.mult, op1=ALU.mult),
                  [ps_b[pO], ps_b[pO + 1], dnw_b] + t32_b[6:8], [on32_b])
                G(lambda e: e.tensor_tensor(out=oa[:, :, cs], in0=on32[:], in1=sz[:, :, cs], op=ALU.mult),
                  [on32_b] + sz_b, oa_b)
                yield 1

            cur = None
            for g in [chunk_gen(t) for t in range(NT)] + [None]:
                g_done = g is None
                c_done = cur is None
                while not (g_done and c_done):
                    if not c_done:
                        try:
                            next(cur)
                        except StopIteration:
                            c_done = True
                    if not g_done:
                        if next(g) == "S2":
                            g_done = True
                cur = g

            if (not samp) and last_blk:
                dma("sp", "o_sd", sd_p[l].rearrange("h k v -> k h v"), S32[:, l, :, :], [S32_b[l]], [])

            if stop <= 4:
                return
            def uv_consume(ch, bk):
                A(lambda e: e.activation(out=R1[:, ch, 0:W], in_=ps[:, bk, 0:W], func=AF.Gelu), [ps_b[bk]], [R1_b[ch]])
            proj_chunks("w_in", l, 4112, 16, W, hT, hT_b, uv_consume)
            boff = (l * 2 + (1 if samp else 0)) * 1024
            dma("sp", "lnw", t32f[:, 0:2048], lnw[l].rearrange("b d -> (b d)").partition_broadcast(128), [], t32_b[0:4])
            dma("pool", "prw", prw16[:], prow[:, boff:boff + 1024], [], [prw_b])
            lnt16 = t16[:, 4096:6144].rearrange("p (a b) -> p a b", a=2)
            A(lambda e: e.activation(out=t16[:, 4096:6144], in_=t32f[:, 0:2048], func=AF.Copy), t32_b[0:4], t32_b[4:6])
            dma("sp", "wsp", ws32[:], wsp[l, 1 if samp else 0], [], [ws32_b])
            V(lambda e: e.tensor_tensor(out=wsT[:], in0=ws32[:].rearrange("p (a b) -> p a b", a=8), in1=bcm(c32[:, cU, :]),
                                        op=ALU.mult), [ws32_b, c32_b], [wsT_b])
            for t in range(NT):
                cs = slice(t * 128, (t + 1) * 128)
                bk = bank()
                for g in range(8):
                    T(lambda e, g=g: e.transpose(out=P16(bk)[:, g * 128:(g + 1) * 128], in_=R1[:, 8 + g, cs], identity=I16),
                      [vT_b[g], c16_b], [ps_b[bk]])
                for hh in range(2):
                    V(lambda e, hh=hh: e.bn_stats(out=bnst[:, hh, :], in_=P16(bk)[:, hh * 512:(hh + 1) * 512]), [ps_b[bk]], [bn_b])
                V(lambda e: e.bn_aggr(out=mv[:, 0:2], in_=bnst[:].rearrange("p a b -> p (a b)")), [bn_b], [bn_b])
                A(lambda e: e.activation(out=mv[:, 2:3], in_=mv[:, 1:2], func=AF.Ln, bias=1e-5), [bn_b], [bn_b])
                A(lambda e: e.activation(out=mv[:, 2:3], in_=mv[:, 2:3], func=AF.Exp, scale=-0.5), [bn_b], [bn_b])
                V(lambda e: e.tensor_scalar(out=vnb[:], in0=P16(bk), scalar1=mv[:, 0:1], scalar2=mv[:, 2:3],
                                            op0=ALU.subtract, op1=ALU.mult), [ps_b[bk], bn_b], [vnb_b])
                V(lambda e: e.tensor_tensor(out=vnb[:], in0=vnb[:], in1=lnt16[:, 0, :], op=ALU.mult), [vnb_b] + t32_b[4:6], [vnb_b])
                V(lambda e: e.tensor_tensor(out=vnb[:], in0=vnb[:], in1=lnt16[:, 1, :], op=ALU.add), [vnb_b] + t32_b[4:6], [vnb_b])
                if samp or (last_blk and t == NT - 1):
                    A(lambda e: e.activation(out=lnv32, in_=vnb[:], func=AF.Copy), [vnb_b], [lnv32_b])
                    dma("sp", "o_cv", cv_s[l] if samp else cv_p[l], lnv32, [lnv32_b], [])
                pM = pair()
                for g in range(8):
                    def fm(e, g=g):
                        e.matmul(PV(pM)[:, g, :], vnb[:, g * 128:(g + 1) * 128], wsT[:, g, :], start=True, stop=False)
                        return e.matmul(PV(pM)[:, g, :], c16[0:1, 1, :], prw16[0:1, g * 128:(g + 1) * 128],
                                        start=False, stop=True)
                    T(fm, [vnb_b, wsT_b, c16_b, prw_b], [ps_b[pM + g // 4]])
                V(lambda e: e.tensor_tensor(out=R1[:, 0:8, cs], in0=R1[:, 0:8, cs], in1=PV(pM), op=ALU.mult),
                  uT_b + [ps_b[pM], ps_b[pM + 1]], uT_b)

            if stop <= 5:
                return
            mg, mg_b = sz, sz_b
            for half in range(2):
                sA = load_w(wview("w_in", l, 0, 1024, 6160 + half * 512, 512), 8, 512)
                sPA = load_w(wview("w_pa", l, 0, 1024, half * 512, 512), 8, 512)
                for jj in range(4):
                    d = half * 4 + jj
                    b1 = bank()
                    mm_group(b1, 0, W, lambda k, jj=jj: wsl[sA][:, k, jj * 128:(jj + 1) * 128], lambda k: hT[:, k, 0:W], 8,
                             hT_b, [wsl_b[sA]])
                    b2 = bank()
                    mm_group(b2, 0, W, lambda k, jj=jj: wsl[sPA][:, k, jj * 128:(jj + 1) * 128], lambda k: oa[:, k, 0:W], 8,
                             oa_b, [wsl_b[sPA]])
                    A(lambda e, b1=b1: e.activation(out=acc[0][:, 0:W], in_=ps[:, b1, 0:W], func=AF.Sigmoid), [ps_b[b1]], [acc_b[0]])
                    V(lambda e, d=d, b2=b2: e.tensor_tensor(out=t32[:, d, 0:W], in0=acc[0][:, 0:W], in1=ps[:, b2, 0:W], op=ALU.mult),
                      [acc_b[0], ps_b[b2]], [t32_b[d]])
                w_done(2)
            for half in range(2):
                sB = load_w(wview("w_in", l, 0, 1024, 7184 + half * 512, 512), 8, 512)
                sPB = load_w(wview("w_pb", l, 0, 1024, half * 512, 512), 8, 512)
                for jj in range(4):
                    d = half * 4 + jj
                    b1 = bank()
                    mm_group(b1, 0, W, lambda k, jj=jj: wsl[sB][:, k, jj * 128:(jj + 1) * 128], lambda k: hT[:, k, 0:W], 8,
                             hT_b, [wsl_b[sB]])
                    b2 = bank()
                    mm_group(b2, 0, W, lambda k, jj=jj: wsl[sPB][:, k, jj * 128:(jj + 1) * 128], lambda k: R1[:, k, 0:W], 8,
                             uT_b, [wsl_b[sPB]])
                    A(lambda e, b1=b1: e.activation(out=acc[1][:, 0:W], in_=ps[:, b1, 0:W], func=AF.Sigmoid), [ps_b[b1]], [acc_b[1]])
                    V(lambda e, b2=b2: e.tensor_tensor(out=acc[1][:, 0:W], in0=acc[1][:, 0:W], in1=ps[:, b2, 0:W], op=ALU.mult),
                      [acc_b[1], ps_b[b2]], [acc_b[1]])
                    (V if d % 2 else G)(
                        lambda e, d=d: e.tensor_tensor(out=mg[:, d, 0:W], in0=acc[1][:, 0:W], in1=t32[:, d, 0:W], op=ALU.add),
                        [acc_b[1], t32_b[d]], [mg_b[d]])
                w_done(2)
            proj_chunks("w_o", l, 0, 8, W, mg, mg_b,
                        lambda ch, bk: A(lambda e: e.activation(out=t32[:, ch, 0:W], in_=ps[:, bk, 0:W], func=AF.Copy),
                                         [ps_b[bk]], [t32_b[ch]]))
            rmsnorm(t32, t32_b, W, NW(l, 1), "x", l)
            if stop <= 6:
                return
            rmsnorm(xT, xT_b, W, NW(l, 2), "h", l)
            hid, hid_b = R1, R1_b
            proj_chunks("w_f1", l, 0, 22, W, hT, hT_b,
                        lambda ch, bk: A(lambda e: e.activation(out=hid[:, ch, 0:W], in_=ps[:, bk, 0:W], func=AF.Silu),
                                         [ps_b[bk]], [hid_b[ch]]))
            proj_chunks("w_f1", l, DFF, 22, W, hT, hT_b,
                        lambda ch, bk: V(lambda e: e.tensor_tensor(out=hid[:, ch, 0:W], in0=hid[:, ch, 0:W], in1=ps[:, bk, 0:W],
                                                                   op=ALU.mult), [hid_b[ch], ps_b[bk]], [hid_b[ch]]))
            for half in range(2):
                bks = [bank() for _ in range(4)]
                for kg, (k0, nk) in enumerate(((0, 8), (8, 8), (16, 6))):
                    s = load_w(wview("w_f2", l, k0 * 128, nk * 128, half * 512, 512), nk, 512)
                    for jj in range(4):
                        def ff(e, s=s, jj=jj, k0=k0, nk=nk, kg=kg):
                            ins = None
                            for k in range(nk):
                                ins = e.matmul(ps[:, bks[jj], 0:W], wsl[s][:, k, jj * 128:(jj + 1) * 128], hid[:, k0 + k, 0:W],
                                               start=(kg == 0 and k == 0), stop=(kg == 2 and k == nk - 1))
                            return ins
                        T(ff, hid_b[k0:k0 + nk] + [wsl_b[s]], [ps_b[bks[jj]]])
                    w_done()
                for jj in range(4):
                    d = half * 4 + jj
                    A(lambda e, d=d, jj=jj: e.activation(out=t32[:, d, 0:W], in_=ps[:, bks[jj], 0:W], func=AF.Copy),
                      [ps_b[bks[jj]]], [t32_b[d]])
            rmsnorm(t32, t32_b, W, NW(l, 3), "x", l)

        def load_x(src, W):
            for t in range(W // 128):
                dma("sp", "xin", xin, src[t * 128:(t + 1) * 128, :], [], t32_b[0:2])
                for hh in range(2):
                    bk = bank()
                    for c4 in range(4):
                        c = hh * 4 + c4
                        T(lambda e, c=c, c4=c4, bk=bk: e.transpose(out=ps[:, bk, c4 * 128:(c4 + 1) * 128], in_=xin[:, c * 128:(c + 1) * 128],
                                                                  identity=I32), t32_b[0:2] + [c32_b], [ps_b[bk]])
                    A(lambda e, hh=hh, bk=bk, t=t: e.activation(out=xT[:, hh * 4:(hh + 1) * 4, t * 128:(t + 1) * 128],
                                                               in_=ps[:, bk, :].rearrange("p (a b) -> p a b", a=4), func=AF.Copy),
                      [ps_b[bk]], xT_b[hh * 4:(hh + 1) * 4])

        def store_y(dst, W):
            for t in range(W // 128):
                yo = t % 2
                for hh in range(2):
                    bk = bank()
                    for c4 in range(4):
                        c = hh * 4 + c4
                        T(lambda e, c=c, c4=c4, bk=bk, t=t: e.transpose(out=ps[:, bk, c4 * 128:(c4 + 1) * 128],
                                                                       in_=xT[:, c, t * 128:(t + 1) * 128], identity=I32),
                          [xT_b[c], c32_b], [ps_b[bk]])
                    A(lambda e, hh=hh, bk=bk, yo=yo: e.activation(out=yout[yo][:, hh * 512:(hh + 1) * 512], in_=ps[:, bk, :],
                                                                  func=AF.Copy), [ps_b[bk]], t32_b[2 + 2 * yo:4 + 2 * yo])
                dma("sp", f"yo{yo}", dst[t * 128:(t + 1) * 128, :], yout[yo], t32_b[2 + 2 * yo:4 + 2 * yo], [])

        G(lambda e: e.memset(S32[:].rearrange("p a b c -> p (a b c)"), 0.0), [], S32_b)
        G(lambda e: e.memset(S16[:].rearrange("p a b c -> p (a b c)"), 0.0), [], S16_b)
        G(lambda e: e.memset(carry[:].rearrange("p a b c -> p (a b c)"), 0.0), [], carry_b[0] + carry_b[1])

        for blk in range(nblk):
            load_x(xp[blk * 512:(blk + 1) * 512, :], 512)
            for l in range(DEPTH):
                E.epoch += 1
                layer(l, 512, False, blk == nblk - 1)
            store_y(y_p[blk * 512:(blk + 1) * 512, :], 512)
        E.epoch += 1
        if do_samp:
            if all_reqs is None:
                SAMP_K0[0] = slot_i[0]
            samp_k0[0] = slot_i[0]
            for hf in range(2):
                dma("pool", "cst16", mskc[:, hf * 8:(hf + 1) * 8, :].rearrange("p a b -> p (a b)"), cbd[:, hf * 1024:(hf + 1) * 1024],
                    [], [cb16_b, wsl_b[3], wsl_b[4], msk16_b, scin_b, snew_b[0]])
            mask_loaded[0] = True
            if all_reqs is not None:
                pump()
            load_x(xs, 128)
            for l in range(DEPTH):
                E.epoch += 1
                layer(l, 128, True, False)
            store_y(y_s, 128)

        E.finalize()
        if os.environ.get("KDBG"):
            print("est_span_us", getattr(E, "est_span", None), "n_ops", len(E.ops), flush=True)
        sems = {}
        for e_ in ENGS:
            for ep in range(E.epoch + 1):
                sems[(e_, ep)] = es.enter_context(nc.semaphore(f"s_{e_}_{ep}"))
        chansems = {ch: es.enter_context(nc.semaphore(f"c_{ch}")) for ch in E.chans}
        block = es.enter_context(nc.Block())
        E.emit(block, sems, chansems)
    return nc


def _pack(inp):
    f = lambda a: np.ascontiguousarray(np.asarray(a, dtype=np.float32))
    pp = np.zeros((128, 262), np.float32)
    names = ["norm_pre_mix", "norm_post_mix", "norm_pre_ffn", "norm_post_ffn"]
    for l in range(DEPTH):
        for n, nm in enumerate(names):
            pp[:, (l * 4 + n) * 8:(l * 4 + n) * 8 + 8] = f(inp[nm])[l].reshape(8, 128).T
        cw = f(inp["conv_w"])[l]
        pp[:, 64 + l * 96:64 + (l + 1) * 96] = cw.reshape(4, 24, 128).transpose(2, 0, 1).reshape(128, 96)
        pp[:, 256 + l] = f(inp["delta_norm_w"])[l]
        pp[0:8, 258 + l] = f(inp["a_log"])[l]
        pp[0:8, 260 + l] = f(inp["dt_bias"])[l]
    bs = f(inp["b_spatial"])
    prow = np.zeros((DEPTH, 2, 8, 128), np.float32)
    prow[:, 0] = bs
    prow[:, 1] = np.tile(bs[:, :, :8], (1, 1, 16))
    prow = prow.reshape(1, -1)
    lnw = np.stack([f(inp["sgu_ln_w"]), f(inp["sgu_ln_b"])], axis=1)
    ws = f(inp["w_spatial"])
    wsT = ws.transpose(0, 3, 1, 2)
    wss = np.tile(ws[:, :, :8, :8], (1, 1, 16, 16)).transpose(0, 3, 1, 2)
    wsp = np.ascontiguousarray(np.stack([wsT, wss], axis=1).reshape(DEPTH, 2, 128, 1024))
    shared = dict(
        w_in=f(inp["w_in"]), w_pa=f(inp["w_proj_a"]), w_pb=f(inp["w_proj_b"]), w_o=f(inp["w_out"]),
        w_f1=f(inp["w_ffn_in"]), w_f2=f(inp["w_ffn_out"]), pp=pp, prow=prow, lnw=np.ascontiguousarray(lnw), wsp=wsp,
        cst=_consts()[0], cst16=_consts()[1], cbd=_cb(),
    )
    xpr = f(inp["x_prompt"]); xsm = f(inp["x_sample"]); sdl = f(inp["state_delta"]); scv = f(inp["state_conv"])
    maps = []
    for c in range(NCORES):
        m = dict(shared)
        m["xp"] = xpr[c]
        m["xs"] = np.ascontiguousarray(xsm[16 * c:16 * (c + 1)].reshape(128, D))
        m["sd"] = np.ascontiguousarray(sdl[:, 16 * c:16 * (c + 1)])
        m["sc"] = np.ascontiguousarray(scv[:, 16 * c:16 * (c + 1)].reshape(DEPTH, 48, 3072))
        maps.append(m)
    return maps


def kernel(**inputs):
    maps = _pack(inputs)
    nc = build()
    res = run_bass_kernel_spmd(nc, maps, core_ids=list(range(NCORES)))
    r = res.results
    y_p = np.stack([r[c]["y_p"] for c in range(NCORES)], 0).reshape(8, 2048, D)
    y_s = np.concatenate([r[c]["y_s"].reshape(16, 8, D) for c in range(NCORES)], 0)
    sd_p = np.stack([r[c]["sd_p"] for c in range(NCORES)], 1)
    sc_p = np.stack([r[c]["sc_p"] for c in range(NCORES)], 1)
    cv_p = np.stack([r[c]["cv_p"] for c in range(NCORES)], 1)
    sd_s = np.concatenate([r[c]["sd_s"] for c in range(NCORES)], 1)
    sc_s = np.concatenate([r[c]["sc_s"] for c in range(NCORES)], 1)
    cv_s = np.concatenate([r[c]["cv_s"].reshape(DEPTH, 16, 8, D) for c in range(NCORES)], 1)
    return tuple(np.ascontiguousarray(a.astype(np.float32)) for a in (y_p, y_s, sd_p, sc_p, cv_p, sd_s, sc_s, cv_s))
```
